# Optimizing a Trainium2 kernel written in Bass

```python
import math
import jax, jax.numpy as jnp
from jax import lax
import numpy as np

D_MODEL = 1024
BATCH = 16
SEQ = 256
DEPTH = 4
DEC_BATCH = 4
DEC_SEQ = 4096
PAST_LEN = 256

GRID_W = 64
N_DIR = 2
GDN_HEADS = 4
GDN_DK = 128
GDN_DV = 128
GDN_W = GDN_HEADS * GDN_DV
GDN_CONV = 5
GDN_CHUNK = 64
HY_W = D_MODEL - GDN_W
HY_CONV = 3
HY_EMB = 33
HY_FH = 64
HY_DECAY_TARGET = 1e-2
HY_FAST_PCT = 0.3
HY_SLOW_PCT = 1.5
D_FF = 4 * D_MODEL
EPS = 1e-6
OFF_G = 3 * GDN_W
OFF_A = 4 * GDN_W
OFF_B = OFF_A + N_DIR * GDN_HEADS
OFF_HY = OFF_B + N_DIR * GDN_HEADS
IN_COLS = OFF_HY + 3 * HY_W

kernel_name = "hybrid_gdn_hyena_diffusion_step"

F32 = jnp.float32


def rms_norm(x, g):
    xf = x.astype(F32)
    y = xf * lax.rsqrt(jnp.mean(xf * xf, axis=-1, keepdims=True) + EPS)
    return (y * g.astype(F32)).astype(x.dtype)


def l2norm(x):
    return x * lax.rsqrt(jnp.sum(x * x, axis=-1, keepdims=True) + EPS)


def short_conv(x, w, n_rows):
    b, l, ch = x.shape
    width = w.shape[0]
    pad = width // 2
    seg = l // n_rows
    w = w.astype(F32)
    xr = jnp.pad(x.reshape(b * n_rows, seg, ch), ((0, 0), (pad, pad), (0, 0)))
    y = sum(xr[:, j:j + seg] * w[j] for j in range(width))
    return y.reshape(b, l, ch)


def gdn_scan(q, k, v, g, beta, s0):
    b, l, h, dk = q.shape
    dv = v.shape[-1]
    c = GDN_CHUNK
    n = l // c

    def blk(t):
        return jnp.moveaxis(t.reshape((b, n, c) + t.shape[2:]), 3, 1)

    q_, k_, v_, g_, be_ = blk(q), blk(k), blk(v), blk(g), blk(beta)
    gc = jnp.cumsum(g_, axis=-1)
    incl = jnp.tril(jnp.ones((c, c), dtype=bool))
    strict = jnp.tril(jnp.ones((c, c), dtype=F32), -1)
    decay = jnp.exp(jnp.where(incl, gc[..., :, None] - gc[..., None, :], -jnp.inf))
    kb = k_ * be_[..., None]
    vb = v_ * be_[..., None]
    a_mat = jnp.einsum('bhnid,bhnjd->bhnij', kb, k_) * decay * strict
    m_mat = a_mat + jnp.eye(c, dtype=F32)
    u = lax.linalg.triangular_solve(m_mat, vb, left_side=True, lower=True, unit_diagonal=True)
    w = lax.linalg.triangular_solve(m_mat, kb * jnp.exp(gc)[..., None], left_side=True, lower=True,
                                    unit_diagonal=True)
    qk = jnp.einsum('bhnid,bhnjd->bhnij', q_, k_) * decay
    g_last = gc[..., -1]
    q_dec = q_ * jnp.exp(gc)[..., None]
    k_dec = k_ * jnp.exp(g_last[..., None] - gc)[..., None]
    xs = tuple(jnp.moveaxis(t, 2, 0) for t in (q_dec, k_dec, u, w, qk, g_last))

    def step(s, inp):
        qd, kd, uu, ww, qkk, gl = inp
        v_new = uu - jnp.einsum('bhcd,bhde->bhce', ww, s)
        o = jnp.einsum('bhcd,bhde->bhce', qd, s) + jnp.einsum('bhij,bhje->bhie', qkk, v_new)
        s = s * jnp.exp(gl)[..., None, None] + jnp.einsum('bhcd,bhce->bhde', kd, v_new)
        return s, o

    s_fin, o = lax.scan(step, s0, xs)
    o = jnp.moveaxis(o, 0, 2).transpose(0, 2, 3, 1, 4).reshape(b, l, h, dv)
    return o, s_fin


def hyena_filter(l, w1, b1, freq, w2, b2, w3, b3):
    t = jnp.linspace(0.0, 1.0, l, dtype=F32)[:, None]
    bands = (HY_EMB - 1) // 2
    f = jnp.linspace(1e-4, bands - 1, bands, dtype=F32)[None, :]
    wpos = (2.0 * math.pi) * jnp.arange(l, dtype=F32)[:, None] / l
    z = jnp.concatenate([t, jnp.cos(f * wpos), -jnp.sin(f * wpos)], axis=-1)
    freq = freq.astype(F32)
    h = jnp.sin(freq * (z @ w1.astype(F32) + b1.astype(F32)))
    h = jnp.sin(freq * (h @ w2.astype(F32) + b2.astype(F32)))
    h = (h @ w3.astype(F32) + b3.astype(F32)).reshape(l, N_DIR, HY_W)
    max_decay = math.log(HY_DECAY_TARGET) / HY_FAST_PCT
    min_decay = math.log(HY_DECAY_TARGET) / HY_SLOW_PCT
    deltas = jnp.abs(jnp.linspace(min_decay, max_decay, HY_W, dtype=F32))
    h = h * jnp.exp(-t[:, :, None] * deltas)
    filt = jnp.concatenate([h[:, 0], jnp.zeros((1, HY_W), F32), h[:0:-1, 1]], axis=0)
    return jnp.fft.rfft(filt, axis=0)


def trunk_layer(x, mod, s0, n_rows, p):
    b, l, _ = x.shape
    shift1, scale1, gate1, shift2, scale2, gate2 = jnp.split(mod, 6, axis=-1)
    hn = rms_norm(x, p['norm1_g']) * (1 + scale1[:, None]) + shift1[:, None]
    proj = (hn @ p['w_in']).astype(F32)

    qkv = jax.nn.silu(short_conv(proj[..., :OFF_G], p['gdn_conv_w'], n_rows))
    q, k, v = jnp.split(qkv, 3, axis=-1)
    q = l2norm(q.reshape(b, l, GDN_HEADS, GDN_DK)) * (GDN_DK ** -0.5)
    k = l2norm(k.reshape(b, l, GDN_HEADS, GDN_DK))
    v = v.reshape(b, l, GDN_HEADS, GDN_DV)
    a_logit = proj[..., OFF_A:OFF_B].reshape(b, l, N_DIR, GDN_HEADS)
    b_logit = proj[..., OFF_B:OFF_HY].reshape(b, l, N_DIR, GDN_HEADS)
    g = -jnp.exp(p['gdn_a_log'].astype(F32)) * jax.nn.softplus(a_logit + p['gdn_dt_bias'].astype(F32))
    beta = jax.nn.sigmoid(b_logit)
    s0 = s0.astype(F32)
    o_f, s_f = gdn_scan(q, k, v, g[:, :, 0], beta[:, :, 0], s0[:, 0])
    flip = lambda t: t[:, ::-1]
    o_b, s_b = gdn_scan(flip(q), flip(k), flip(v), flip(g[:, :, 1]), flip(beta[:, :, 1]), s0[:, 1])
    o = o_f + flip(o_b)
    gate_o = proj[..., OFF_G:OFF_A].reshape(b, l, GDN_HEADS, GDN_DV)
    o = (rms_norm(o, p['gdn_norm_g']) * jax.nn.silu(gate_o)).reshape(b, l, GDN_W)

    hy = short_conv(proj[..., OFF_HY:], p['hy_conv_w'], n_rows)
    x0, x1, hv = jnp.split(hy, 3, axis=-1)
    zz = x1 * hv
    spec = hyena_filter(l, p['hy_w1'], p['hy_b1'], p['hy_freq'], p['hy_w2'], p['hy_b2'],
                        p['hy_w3'], p['hy_b3'])
    zc = jnp.fft.irfft(jnp.fft.rfft(zz, n=2 * l, axis=1) * spec[None], n=2 * l, axis=1)[:, :l]
    y_h = x0 * (zc + zz * p['hy_skip'].astype(F32))

    mix = jnp.concatenate([o, y_h], axis=-1).astype(x.dtype) @ p['w_out']
    x = x + gate1[:, None] * mix

    h2 = rms_norm(x, p['norm2_g']) * (1 + scale2[:, None]) + shift2[:, None]
    x = x + gate2[:, None] * (jnp.square(jax.nn.relu(h2 @ p['w_mlp1'])) @ p['w_mlp2'])
    return x, jnp.stack([s_f, s_b], axis=1)


def setup_inputs(seed: int = 0) -> dict:
    key = jax.random.key(seed)
    ks = jax.random.split(key, 32)
    nrm = lambda k, shape, s: jax.random.normal(k, shape, F32) * s
    dt = jnp.exp(jax.random.uniform(ks[12], (DEPTH, N_DIR, GDN_HEADS), F32, math.log(1e-3), math.log(1e-1)))
    return {
        "x_prompt": nrm(ks[0], (BATCH, SEQ, D_MODEL), 1.0),
        "x_sample": nrm(ks[1], (DEC_BATCH, DEC_SEQ, D_MODEL), 1.0),
        "state_gdn": nrm(ks[2], (DEC_BATCH, DEPTH, N_DIR, GDN_HEADS, GDN_DK, GDN_DV), GDN_DK ** -0.5),
        "c": nrm(ks[3], (DEC_BATCH, D_MODEL), 1.0),
        "c_ctx": nrm(ks[4], (D_MODEL,), 1.0),
        "w_ada": nrm(ks[5], (DEPTH, D_MODEL, 6 * D_MODEL), 0.5 * D_MODEL ** -0.5),
        "b_ada": nrm(ks[6], (DEPTH, 6 * D_MODEL), 0.02),
        "norm1_g": 1.0 + nrm(ks[7], (DEPTH, D_MODEL), 0.02),
        "norm2_g": 1.0 + nrm(ks[8], (DEPTH, D_MODEL), 0.02),
        "w_in": nrm(ks[9], (DEPTH, D_MODEL, IN_COLS), D_MODEL ** -0.5),
        "gdn_conv_w": nrm(ks[10], (DEPTH, GDN_CONV, 3 * GDN_W), GDN_CONV ** -0.5),
        "gdn_a_log": jnp.log(jax.random.uniform(ks[11], (DEPTH, N_DIR, GDN_HEADS), F32, 1.0, 16.0)),
        "gdn_dt_bias": dt + jnp.log(-jnp.expm1(-dt)),
        "gdn_norm_g": 1.0 + nrm(ks[13], (DEPTH, GDN_DV), 0.02),
        "hy_conv_w": nrm(ks[14], (DEPTH, HY_CONV, 3 * HY_W), HY_CONV ** -0.5),
        "hy_w1": nrm(ks[15], (DEPTH, HY_EMB, HY_FH), HY_EMB ** -0.5),
        "hy_b1": nrm(ks[16], (DEPTH, HY_FH), 0.02),
        "hy_freq": 1.0 + nrm(ks[17], (DEPTH, HY_FH), 0.02),
        "hy_w2": nrm(ks[18], (DEPTH, HY_FH, HY_FH), HY_FH ** -0.5),
        "hy_b2": nrm(ks[19], (DEPTH, HY_FH), 0.02),
        "hy_w3": nrm(ks[20], (DEPTH, HY_FH, N_DIR * HY_W), 0.05 * HY_FH ** -0.5),
        "hy_b3": nrm(ks[21], (DEPTH, N_DIR * HY_W), 0.01),
        "hy_skip": nrm(ks[22], (DEPTH, HY_W), 0.5),
        "w_out": nrm(ks[23], (DEPTH, D_MODEL, D_MODEL), D_MODEL ** -0.5),
        "w_mlp1": nrm(ks[24], (DEPTH, D_MODEL, D_FF), D_MODEL ** -0.5),
        "w_mlp2": nrm(ks[25], (DEPTH, D_FF, D_MODEL), D_FF ** -0.5),
        "final_g": 1.0 + nrm(ks[26], (D_MODEL,), 0.02),
    }


def reference(x_prompt, x_sample, state_gdn, c, c_ctx, w_ada, b_ada, norm1_g, norm2_g, w_in,
              gdn_conv_w, gdn_a_log, gdn_dt_bias, gdn_norm_g, hy_conv_w, hy_w1, hy_b1, hy_freq,
              hy_w2, hy_b2, hy_w3, hy_b3, hy_skip, w_out, w_mlp1, w_mlp2, final_g):
    rows = x_sample.shape[1] // GRID_W

    def layer_params(l):
        return {"norm1_g": norm1_g[l], "norm2_g": norm2_g[l], "w_in": w_in[l],
                "gdn_conv_w": gdn_conv_w[l], "gdn_a_log": gdn_a_log[l], "gdn_dt_bias": gdn_dt_bias[l],
                "gdn_norm_g": gdn_norm_g[l], "hy_conv_w": hy_conv_w[l], "hy_w1": hy_w1[l],
                "hy_b1": hy_b1[l], "hy_freq": hy_freq[l], "hy_w2": hy_w2[l], "hy_b2": hy_b2[l],
                "hy_w3": hy_w3[l], "hy_b3": hy_b3[l], "hy_skip": hy_skip[l], "w_out": w_out[l],
                "w_mlp1": w_mlp1[l], "w_mlp2": w_mlp2[l]}

    x = x_prompt
    s_zero = jnp.zeros((x_prompt.shape[0], N_DIR, GDN_HEADS, GDN_DK, GDN_DV), F32)
    ctx_states = []
    for l in range(DEPTH):
        mod = (jax.nn.silu(c_ctx) @ w_ada[l] + b_ada[l])[None]
        x, s_l = trunk_layer(x, mod, s_zero, 1, layer_params(l))
        ctx_states.append(s_l)
    y_prompt = rms_norm(x, final_g)
    new_state_gdn = jnp.stack(ctx_states, axis=1).astype(x_prompt.dtype)

    x = x_sample
    for l in range(DEPTH):
        mod = jax.nn.silu(c) @ w_ada[l] + b_ada[l]
        x, _ = trunk_layer(x, mod, state_gdn[:, l], rows, layer_params(l))
    y_sample = rms_norm(x, final_g)

    return (y_prompt, y_sample, new_state_gdn)
```

```python
import contextlib
import math
import numpy as np
import ml_dtypes
import concourse.bass as bass
import concourse.mybir as mybir
from concourse.bass_utils import run_bass_kernel_spmd

F32 = mybir.dt.float32
BF16 = mybir.dt.bfloat16
I32 = mybir.dt.int32
AF = mybir.ActivationFunctionType
ALU = mybir.AluOpType
AX = mybir.AxisListType

ENGS = ("pe", "act", "dve", "pool", "sp")
STORE_ENG = "sp"


def _is_store(semname):
    return semname.endswith("_o") or semname in ("C_xo", "C_h2o", "M_xo", "M_yo") or semname.startswith("G_oT")
SEM_EPOCH = 30000
SBUF_BASE = 16640
SBUF_BYTES = 229000


def _dsize(dt):
    return 2 if dt == BF16 else 4


class Op:
    __slots__ = ("eng", "fn", "deps", "is_dma", "dsem", "dval", "signal", "sem", "val", "idx", "_rw")


class Prog:
    def __init__(self, nc, stack):
        self.nc = nc
        self.stack = stack
        self.ops = []
        self.key_w = {}
        self.key_r = {}
        self.dma_sems = {}
        self.bump = SBUF_BASE
        self.tiles = []
        self.tile_keys = {}
        self.tile_pending = {}
        self.uid = 0
        self.ps_rr = 0

    def tile(self, name, shape, dtype):
        nbytes = int(np.prod(shape[1:])) * _dsize(dtype)
        nbytes = (nbytes + 63) // 64 * 64
        off = self.bump
        assert off + nbytes <= SBUF_BYTES, ("SBUF overflow", name, off, nbytes)
        self.bump = off + nbytes
        self.uid += 1
        t = self.nc.alloc_sbuf_tensor_at("%s_%d" % (name, self.uid), list(shape), dtype, offset=off)
        pend = set()
        for (s, e, oname) in self.tiles:
            if s < off + nbytes and off < e:
                pend |= self.tile_pending.get(oname, set())
                for k in self.tile_keys.get(oname, ()):
                    w = self.key_w.get(k)
                    if w is not None:
                        pend.add(w)
                    pend.update(self.key_r.get(k, {}).values())
        self.tiles = [(s, e, n) for (s, e, n) in self.tiles if not (s >= off and e <= off + nbytes)] + [(off, off + nbytes, name)]
        self.tile_keys.setdefault(name, set())
        self.tile_pending[name] = self._compress(pend)
        return t

    def mark(self):
        return self.bump

    def release(self, m):
        self.bump = m

    def new_sem(self, name):
        return self.stack.enter_context(self.nc.semaphore(name))

    def _cls(self, d):
        o = self.ops[d]
        return ("d", id(o.dsem)) if o.is_dma else o.eng

    def _compress(self, deps):
        best = {}
        for d in deps:
            c = self._cls(d)
            b = best.get(c)
            if b is None or b < d:
                best[c] = d
        return set(best.values())

    def _deps(self, reads, writes):
        deps = set()
        for k in reads:
            base = k.split(":")[0]
            if base in self.tile_keys:
                self.tile_keys[base].add(k)
                deps |= self.tile_pending[base]
            w = self.key_w.get(k)
            if w is not None:
                deps.add(w)
        for k in writes:
            base = k.split(":")[0]
            if base in self.tile_keys:
                self.tile_keys[base].add(k)
                deps |= self.tile_pending[base]
            w = self.key_w.get(k)
            if w is not None:
                deps.add(w)
            deps.update(self.key_r.get(k, {}).values())
        return self._compress(deps)

    def _op(self, eng, fn, reads=(), writes=()):
        o = Op()
        o.eng = eng
        o.fn = fn
        o.is_dma = False
        o.dsem = None
        o.signal = False
        o.idx = len(self.ops)
        o.deps = self._deps(reads, writes)
        self.ops.append(o)
        o._rw = (reads, writes)
        return o

    def op(self, eng, fn, reads=(), writes=()):
        o = self._op(eng, fn, reads, writes)
        self._commit(o)
        return o

    def _commit(self, o):
        reads, writes = o._rw
        c = self._cls(o.idx)
        for k in reads:
            self.key_r.setdefault(k, {})[c] = o.idx
        for k in writes:
            self.key_w[k] = o.idx
            self.key_r[k] = {}
        o._rw = None

    def dma(self, semname, fn, reads=(), writes=(), eng="sp"):
        if eng == "sp" and _is_store(semname):
            eng = STORE_ENG
        o = self._op(eng, fn, reads, writes)
        o.is_dma = True
        if semname not in self.dma_sems:
            self.dma_sems[semname] = [self.new_sem("d%d" % len(self.dma_sems)), 0]
        ent = self.dma_sems[semname]
        ent[1] += 16
        o.dsem = ent[0]
        o.dval = ent[1]
        self._commit(o)
        return o

    def emit(self, final_wait_ops=()):
        nc = self.nc
        ops = self.ops

        def skip(do, o):
            return (not do.is_dma) and (not o.is_dma) and do.eng == o.eng and do.eng == "pe"

        for o in ops:
            for d in o.deps:
                do = ops[d]
                if do.is_dma or skip(do, o):
                    continue
                do.signal = True
        cur = {}
        for o in ops:
            if o.is_dma:
                o.sem, o.val = o.dsem, o.dval
                continue
            if not o.signal:
                continue
            ent = cur.get(o.eng)
            if ent is None or ent[1] >= SEM_EPOCH:
                ent = [self.new_sem("e%s%d" % (o.eng, o.idx)), 0]
                cur[o.eng] = ent
            ent[1] += 1
            o.sem, o.val = ent[0], ent[1]
        per_eng = {e: [o for o in ops if o.eng == e] for e in ENGS}
        final_ops = [ops[i] for i in final_wait_ops]

        def run(ename, e):
            waited = {}
            for o in per_eng[ename]:
                need = {}
                for d in o.deps:
                    do = ops[d]
                    if skip(do, o):
                        continue
                    sid = id(do.sem)
                    if sid not in need or need[sid][1] < do.val:
                        need[sid] = (do.sem, do.val)
                for sid, (s, v) in need.items():
                    if waited.get(sid, 0) >= v:
                        continue
                    e.wait_ge(s, v)
                    waited[sid] = v
                ins = o.fn(e)
                if o.is_dma:
                    ins.then_inc(o.sem, 16)
                elif o.signal:
                    ins.then_inc(o.sem, 1)
            if ename == "sp":
                for o in final_ops:
                    e.wait_ge(o.sem, o.val)

        with nc.Block() as block:
            @block.tensor
            def _(e):
                run("pe", e)

            @block.scalar
            def _(e):
                run("act", e)

            @block.vector
            def _(e):
                run("dve", e)

            @block.gpsimd
            def _(e):
                run("pool", e)

            @block.sync
            def _(e):
                run("sp", e)


D = 1024
KC = 8
NT = 512
H = 4
DK = 128
CH = 64
IN_COLS = 3600
OFF_A = 2048
OFF_B = 2056
OFF_HY = 2064
DFF = 4096
EPS = 1e-6
HY_EMB = 33
HY_FH = 64
NEWTON_STEPS = 1
FW = 256
TW = 256


class Group:
    def __init__(self, name, T, L, seg, has_s0, wstate, cond):
        self.name = name
        self.T = T
        self.L = L
        self.nseq = T // L
        self.seg = seg
        self.has_s0 = has_s0
        self.wstate = wstate
        self.cond = cond
        self.ntiles = T // NT
        self.NF = (L + 1 + FW - 1) // FW * FW
        self.NFB = self.NF // 128
        self.NTAB = max(self.NF, L)
        self.nTB = L // 128


def make_groups(LS):
    return [Group("s", LS, LS, 64, True, False, 0), Group("p", 512, 256, 256, False, True, 1)]


DEBUG_SCRATCH = False
_LAST = {}


def build_program(depth, LS):
    nc = bass.Bass("TRN2", target_bir_lowering=False)
    groups = make_groups(LS)

    def din(name, shape, dt=F32):
        return nc.dram_tensor(name, list(shape), dt, kind="ExternalInput").ap()

    def dout(name, shape, dt=F32):
        return nc.dram_tensor(name, list(shape), dt, kind="ExternalOutput").ap()

    def dscr(name, shape, dt=F32):
        return nc.dram_tensor(name, list(shape), dt, kind=("ExternalOutput" if DEBUG_SCRATCH else "Internal")).ap()

    I = {}
    for g in groups:
        I["x_" + g.name] = din("x_" + g.name, [KC, 128, g.T])
        I["ctab_" + g.name] = din("ctab_" + g.name, [g.NTAB // FW, 128, g.NTAB // 128, FW], BF16)
        I["stab_" + g.name] = din("stab_" + g.name, [g.NTAB // FW, 128, g.NTAB // 128, FW], BF16)
        I["zf_" + g.name] = din("zf_" + g.name, [HY_EMB, g.L])
        I["negt_" + g.name] = din("negt_" + g.name, [128, g.nTB])
        I["wcol_" + g.name] = din("wcol_" + g.name, [128, g.NFB])
        I["ctabF_" + g.name] = din("ctabF_" + g.name, [g.NTAB // 128, 128, g.NTAB // 128, 128], BF16)
        I["stabF_" + g.name] = din("stabF_" + g.name, [g.NTAB // 128, 128, g.NTAB // 128, 128], BF16)
    I["s0"] = din("s0", [depth, 2, H, DK, DK])
    I["cond"] = din("cond", [128, KC, 2])
    I["w_ada"] = din("w_ada", [depth, D, 6 * D])
    I["b_ada"] = din("b_ada", [128, depth, 48])
    I["norm_g"] = din("norm_g", [128, depth, 2, KC])
    I["final_g"] = din("final_g", [128, KC])
    I["w_in"] = din("w_in", [depth, D, IN_COLS])
    I["gcw"] = din("gcw", [128, depth, 12, 5])
    I["hcw"] = din("hcw", [128, depth, 12, 3])
    I["a_par"] = din("a_par", [8, depth, 2])
    I["gng"] = din("gng", [128, depth])
    I["hw1"] = din("hw1", [depth, HY_EMB, HY_FH])
    I["hvec"] = din("hvec", [HY_FH, depth, 4])
    I["hw2"] = din("hw2", [depth, HY_FH, HY_FH])
    I["hw3e"] = din("hw3e", [depth, HY_FH + 1, 2 * 512])
    I["hskip"] = din("hskip", [128, depth, 4])
    I["w_out"] = din("w_out", [depth, D, D])
    I["w_mlp1"] = din("w_mlp1", [depth, D, DFF])
    I["w_mlp2"] = din("w_mlp2", [depth, DFF, D])
    I["masks"] = din("masks", [64, 5, 64])
    I["masks2"] = din("masks2", [128, 4, 128])
    I["sel"] = din("sel", [8, 8, 128])
    I["scanmask"] = din("scanmask", [8, NT])
    I["delta"] = din("delta", [512])

    O = {}
    for g in groups:
        O["y_" + g.name] = dout("y_" + g.name, [KC, 128, g.T])
    O["nstate"] = dout("nstate", [2, depth, 2, H, DK, DK])

    S = {}
    for g in groups:
        n = g.name
        S["X_" + n] = dscr("X_" + n, [KC, 128, g.T])
        S["QKV_" + n] = dscr("QKV_" + n, [12, 128, g.T])
        S["GATE_" + n] = dscr("GATE_" + n, [4, 128, g.T])
        S["GB_" + n] = dscr("GB_" + n, [16, g.T])
        S["X0_" + n] = dscr("X0_" + n, [4, 128, g.T])
        S["ZZ_" + n] = dscr("ZZ_" + n, [4, 128, g.T])
        S["ZZT_" + n] = dscr("ZZT_" + n, [g.T, 512], BF16)
        S["O_" + n] = dscr("O_" + n, [2, 4, 128, g.T])
        S["YH_" + n] = dscr("YH_" + n, [4, 128, g.T], BF16)
        S["H2_" + n] = dscr("H2_" + n, [KC, 128, g.T], BF16)
        S["SPEC_" + n] = dscr("SPEC_" + n, [2, g.NFB, 128, 512])

    with contextlib.ExitStack() as st:
        P = Prog(nc, st)
        ps_t = [st.enter_context(nc.psum_tensor("ps%d" % i, [128, 512], F32)) for i in range(8)]

        ps_held = set()

        def next_ps(hold=False):
            while True:
                i = P.ps_rr
                P.ps_rr = (i + 1) % 8
                if i not in ps_held:
                    break
            if hold:
                ps_held.add(i)
            return ps_t[i], "ps%d" % i

        def ps_release(key):
            ps_held.discard(int(key[2:]))

        fin = []

        ident = P.tile("ident", [128, 128], F32)
        identb = P.tile("identb", [128, 128], BF16)
        ones = P.tile("ones", [128, 128], F32)
        onesb = P.tile("onesb", [128, 128], BF16)
        selb = P.tile("selb", [8, 8, 128], BF16)
        masks = P.tile("masks", [64, 5, 64], F32)
        masks2 = P.tile("masks2", [128, 4, 128], F32)
        sel = P.tile("sel", [8, 8, 128], F32)
        scanmask = P.tile("scanmask", [8, NT], F32)
        mods = P.tile("mods", [128, depth, 2, 48], F32)
        gmod = P.tile("gmod", [128, depth, 2, 2, KC], F32)
        normg = P.tile("normg", [128, depth, 2, KC], F32)
        finalg = P.tile("finalg", [128, KC], F32)
        gcw = P.tile("gcw", [128, depth, 12, 5], F32)
        hcw = P.tile("hcw", [128, depth, 12, 3], F32)
        apar = P.tile("apar", [8, depth, 2], F32)
        nega = P.tile("nega", [8, depth], F32)
        gng = P.tile("gng", [128, depth], F32)
        hskip = P.tile("hskip", [128, depth, 4], F32)
        hvec = P.tile("hvec", [HY_FH, depth, 4], F32)
        hf2p = P.tile("hf2p", [HY_FH, depth], F32)
        deltab = P.tile("deltab", [128, 512], F32)
        condt = P.tile("condt", [128, KC, 2], F32)
        bada = P.tile("bada", [128, depth, 48], F32)

        def ld(semname, t_ap, src, key, eng="sp"):
            return P.dma(semname, lambda e: e.dma_start(out=t_ap, in_=src), writes=[key], eng=eng)

        P.op("pool", lambda e: e.memset(ident[:], 0.0), writes=["ident"])
        P.op("pool", lambda e: e.affine_select(out=ident[:], in_=ident[:], pattern=[[-1, 128]], compare_op=ALU.not_equal,
                                               fill=1.0, base=0, channel_multiplier=1), reads=["ident"], writes=["ident"])
        P.op("dve", lambda e: e.tensor_copy(out=identb[:], in_=ident[:]), reads=["ident"], writes=["identb"])
        P.op("pool", lambda e: e.memset(ones[:], 1.0), writes=["ones"])
        P.op("pool", lambda e: e.memset(onesb[:], 1.0), writes=["onesb"])
        ld("c_masks", masks[:], I["masks"], "masks")
        ld("c_masks2", masks2[:], I["masks2"], "masks2")
        ld("c_sel", sel[:], I["sel"], "sel")
        P.op("dve", lambda e: e.tensor_copy(out=selb[:], in_=sel[:]), reads=["sel"], writes=["selb"])
        ld("c_scanmask", scanmask[:], I["scanmask"], "scanmask")
        ld("c_normg", normg[:], I["norm_g"], "normg")
        ld("c_finalg", finalg[:], I["final_g"], "finalg")
        ld("c_gcw", gcw[:], I["gcw"], "gcw")
        ld("c_hcw", hcw[:], I["hcw"], "hcw")
        ld("c_apar", apar[:], I["a_par"], "apar")
        ld("c_gng", gng[:], I["gng"], "gng")
        ld("c_hskip", hskip[:], I["hskip"], "hskip")
        ld("c_hvec", hvec[:], I["hvec"], "hvec")
        ld("c_delta", deltab[:], I["delta"].partition_broadcast(128), "deltab")
        ld("c_cond", condt[:], I["cond"], "condt")
        ld("c_bada", bada[:], I["b_ada"], "bada")
        P.op("act", lambda e: e.activation(out=nega[:], in_=apar[:, :, 0], func=AF.Exp), reads=["apar"], writes=["nega"])
        P.op("dve", lambda e: e.tensor_scalar(out=nega[:], in0=nega[:], scalar1=-1.0, scalar2=None, op0=ALU.mult), reads=["nega"], writes=["nega"])
        P.op("dve", lambda e: e.tensor_scalar(out=hf2p[:], in0=hvec[:, :, 1], scalar1=1.0 / (2 * math.pi), scalar2=None, op0=ALU.mult),
             reads=["hvec"], writes=["hf2p"])

        base_mark = P.mark()

        def prologue():
            m0 = P.mark()
            scond = P.tile("scond", [128, KC, 2], F32)
            P.op("act", lambda e: e.activation(out=scond[:], in_=condt[:], func=AF.Silu), reads=["condt"], writes=["scond"])
            wa = [P.tile("wa%d" % i, [128, KC, 512], F32) for i in range(2)]
            n = 0
            for l in range(depth):
                for cg in range(12):
                    wt = wa[n % 2]
                    wk = "wa%d" % (n % 2)
                    n += 1
                    src = I["w_ada"][l, :, cg * 512:(cg + 1) * 512].rearrange("(k p) n -> p k n", p=128)
                    ld(wk, wt[:], src, wk)
                    for mi in range(4):
                        chunk = cg * 4 + mi
                        pt, pk = next_ps()
                        for k in range(KC):
                            P.op("pe", lambda e, pt=pt, wt=wt, k=k, mi=mi: e.matmul(pt[:, 0:2], lhsT=wt[:, k, mi * 128:(mi + 1) * 128],
                                                                                   rhs=scond[:, k, :], start=(k == 0), stop=(k == KC - 1)),
                                 reads=[wk, "scond"], writes=[pk])
                        P.op("dve", lambda e, pt=pt, l=l, chunk=chunk: e.tensor_tensor(
                            out=mods[:, l, :, chunk], in0=pt[:, 0:2], in1=bada[:, l, chunk:chunk + 1].to_broadcast([128, 2]), op=ALU.add),
                            reads=[pk, "bada"], writes=["mods"])
            for l in range(depth):
                for j in range(2):
                    for w, sc0 in ((0, 8), (1, 32)):
                        P.op("dve", lambda e, l=l, j=j, w=w, sc0=sc0: e.scalar_tensor_tensor(
                            out=gmod[:, l, j, w, :], in0=mods[:, l, j, sc0:sc0 + KC], scalar=1.0, in1=normg[:, l, w, :],
                            op0=ALU.add, op1=ALU.mult), reads=["mods", "normg"], writes=["gmod"])
            P.release(m0)

        prologue()

        def load_weight_bf16(name, src3, shape):
            t = P.tile(name, shape, BF16)
            for a in range(shape[1]):
                P.dma(name, lambda e, a=a: e.dma_start(out=t[:, a, :], in_=src3[:, a, :]), writes=[name], eng="pool")
            return t

        def rms_stats(src_tile, src_key, nchunks, dim, rstd, rstd_key, sqbufs):
            pt, pk = next_ps()
            for k in range(nchunks):
                sq, sqk = sqbufs[k % len(sqbufs)]
                P.op("act", lambda e, sq=sq, k=k: e.activation(out=sq[:], in_=src_tile[:, k, :], func=AF.Square), reads=[src_key], writes=[sqk])
                P.op("pe", lambda e, sq=sq, k=k, pt=pt: e.matmul(pt[:], lhsT=onesb[:], rhs=sq[:], start=(k == 0), stop=(k == nchunks - 1)),
                     reads=[sqk, "onesb"], writes=[pk])
            P.op("act", lambda e, pt=pt: e.activation(out=rstd[:], in_=pt[:], func=AF.Ln, scale=1.0 / dim, bias=EPS), reads=[pk], writes=[rstd_key])
            P.op("act", lambda e: e.activation(out=rstd[:], in_=rstd[:], func=AF.Exp, scale=-0.5), reads=[rstd_key], writes=[rstd_key])

        def sumsq_hilo(src_ap, src_key, sqf, sqfk, hl, hlk, pt, pk):
            P.op("act", lambda e: e.activation(out=sqf[:], in_=src_ap, func=AF.Square), reads=[src_key], writes=[sqfk])
            P.op("act", lambda e: e.activation(out=hl[:, 0, :], in_=src_ap, func=AF.Square), reads=[src_key], writes=[hlk + ":0"])
            P.op("dve", lambda e: e.tensor_tensor(out=hl[:, 1, :], in0=sqf[:], in1=hl[:, 0, :], op=ALU.subtract), reads=[sqfk, hlk + ":0"], writes=[hlk + ":1"])
            P.op("pe", lambda e: e.matmul(pt[:], lhsT=onesb[:], rhs=hl[:, 0, :], start=True, stop=False), reads=[hlk + ":0", "onesb"], writes=[pk])
            P.op("pe", lambda e: e.matmul(pt[:], lhsT=onesb[:], rhs=hl[:, 1, :], start=False, stop=True), reads=[hlk + ":1", "onesb"], writes=[pk])

        def xkeys(g, t):
            return "X_%s:%d" % (g.name, t)

        def phase_A(l, g, w_in_sb):
            n = g.name
            j = g.cond
            nseg = NT // g.seg
            m0 = P.mark()
            xt = P.tile("A_xt", [128, KC, NT], F32)
            sqb = [(P.tile("A_sq%d" % i, [128, NT], F32), "A_sq%d" % i) for i in range(4)]
            sqr = [(P.tile("A_sqr%d" % i, [128, NT], BF16), "A_sqr%d" % i) for i in range(4)]
            hlb = [(P.tile("A_hl%d" % i, [128, 2, NT], BF16), "A_hl%d" % i) for i in range(4)]
            rstd = P.tile("A_rstd", [128, NT], F32)
            tmp = [(P.tile("A_tmp%d" % i, [128, NT], F32), "A_tmp%d" % i) for i in range(4)]
            hn = P.tile("A_hn", [128, KC, NT], BF16)
            pj = [(P.tile("A_pj%d" % i, [128, NT], F32), "A_pj%d" % i) for i in range(3)]
            qkv = P.tile("A_qkv", [128, 12, NT], F32)
            qkvb = P.tile("A_qkvb", [128, 8, NT], F32)
            gate = P.tile("A_gate", [128, 4, NT], F32)
            gb = P.tile("A_gb", [8, 2, NT], F32)
            zz = P.tile("A_zz", [128, 4, NT], F32)
            zzb = P.tile("A_zzb", [128, 4, NT], BF16)
            zzt = P.tile("A_zzt", [128, 4, 512], BF16)
            xsrc = I["x_" + n] if l == 0 else S["X_" + n]
            def load_xt(t):
                t0 = t * NT
                P.dma("A_xt", lambda e, t0=t0: e.dma_start(out=xt[:], in_=xsrc[:, :, t0:t0 + NT].rearrange("k p t -> p k t")),
                      reads=[xkeys(g, t)], writes=["A_xt"])

            load_xt(0)
            for t in range(g.ntiles):
                t0 = t * NT
                rms_stats(xt, "A_xt", KC, D, rstd, "A_rstd", sqr)
                for k in range(KC):
                    tb, tk = tmp[k % 4]
                    P.op("dve", lambda e, tb=tb, k=k: e.scalar_tensor_tensor(out=tb[:], in0=xt[:, k, :], scalar=gmod[:, l, j, 0, k:k + 1],
                                                                           in1=rstd[:], op0=ALU.mult, op1=ALU.mult),
                         reads=["A_xt", "gmod", "A_rstd"], writes=[tk])
                    P.op("act", lambda e, tb=tb, k=k: e.activation(out=hn[:, k, :], in_=tb[:], func=AF.Identity,
                                                                    bias=mods[:, l, j, 0 + k:0 + k + 1], scale=1.0),
                         reads=[tk, "mods"], writes=["A_hn:%d" % k])
                hn_keys = ["A_hn:%d" % k for k in range(KC)]
                if t + 1 < g.ntiles:
                    load_xt(t + 1)

                def proj(c0, ncols, pt, pk):
                    for k in range(KC):
                        P.op("pe", lambda e, k=k: e.matmul(pt[0:ncols, :], lhsT=w_in_sb[:, k, c0:c0 + ncols], rhs=hn[:, k, :],
                                                           start=(k == 0), stop=(k == KC - 1)),
                             reads=["w_in_sb", "A_hn:%d" % k], writes=[pk])

                def conv(dst, dkey, src, skey, wts, width, pt, pk):
                    pad = width // 2
                    d3 = dst.rearrange("p (s c) -> p s c", c=g.seg)
                    s3 = src.rearrange("p (s c) -> p s c", c=g.seg)
                    P.op("act", lambda e: e.activation(out=dst, in_=pt[:], func=AF.Copy, scale=wts[:, pad:pad + 1]),
                         reads=[pk, "gcw", "hcw"], writes=[dkey])
                    for jj in range(width):
                        o = jj - pad
                        if o == 0:
                            continue
                        lo_d, hi_d = max(0, -o), g.seg - max(0, o)
                        lo_s, hi_s = max(0, o), g.seg - max(0, -o)
                        P.op("dve", lambda e, jj=jj, lo_d=lo_d, hi_d=hi_d, lo_s=lo_s, hi_s=hi_s: e.scalar_tensor_tensor(
                            out=d3[:, :, lo_d:hi_d], in0=s3[:, :, lo_s:hi_s], scalar=wts[:, jj:jj + 1], in1=d3[:, :, lo_d:hi_d],
                            op0=ALU.mult, op1=ALU.add), reads=[skey, dkey, "gcw", "hcw"], writes=[dkey])

                for m in range(12):
                    pt, pk = next_ps()
                    proj(m * 128, 128, pt, pk)
                    pb, pbk = pj[m % 3]
                    P.op("act", lambda e, pt=pt, pb=pb: e.copy(out=pb[:], in_=pt[:]), reads=[pk], writes=[pbk])
                    conv(qkv[:, m, :], "A_qkv:%d" % m, pb[:], pbk, gcw[:, l, m, :], 5, pt, pk)
                gpts = []
                for m in range(4):
                    pt, pk = next_ps()
                    proj(1536 + m * 128, 128, pt, pk)
                    gpts.append((pt, pk))
                for m in range(12):
                    P.op("act", lambda e, m=m: e.activation(out=qkv[:, m, :], in_=qkv[:, m, :], func=AF.Silu),
                         reads=["A_qkv:%d" % m], writes=["A_qkv:%d" % m])
                for m in range(4):
                    pt, pk = gpts[m]
                    P.op("act", lambda e, m=m, pt=pt: e.activation(out=gate[:, m, :], in_=pt[:], func=AF.Silu), reads=[pk], writes=["A_gate"])
                P.dma("A_gate_o", lambda e, t0=t0: e.dma_start(out=S["GATE_" + n][:, :, t0:t0 + NT].rearrange("m p t -> p m t"), in_=gate[:]),
                      reads=["A_gate"], writes=["GATE_%s:%d" % (n, t)])
                pt, pk = next_ps()
                proj(OFF_B, 8, pt, pk)
                P.op("act", lambda e, pt=pt: e.activation(out=gb[:, 1, :], in_=pt[0:8, :], func=AF.Sigmoid), reads=[pk], writes=["A_gb:1"])
                pt, pk = next_ps()
                proj(OFF_A, 8, pt, pk)
                P.op("act", lambda e, pt=pt: e.activation(out=gb[:, 0, :], in_=pt[0:8, :], func=AF.Exp, bias=apar[:, l, 1:2], scale=1.0),
                     reads=[pk, "apar"], writes=["A_gb:0"])
                P.op("act", lambda e: e.activation(out=gb[:, 0, :], in_=gb[:, 0, :], func=AF.Ln, bias=1.0, scale=1.0), reads=["A_gb:0"], writes=["A_gb:0"])
                P.op("dve", lambda e: e.tensor_scalar(out=gb[:, 0, :], in0=gb[:, 0, :], scalar1=nega[:, l:l + 1], scalar2=None, op0=ALU.mult),
                     reads=["A_gb:0", "nega"], writes=["A_gb:0"])
                P.dma("A_gb_o", lambda e, t0=t0: e.dma_start(out=S["GB_" + n][:, t0:t0 + NT].rearrange("(a r) t -> r a t", a=2), in_=gb[:]),
                      reads=["A_gb:0", "A_gb:1"], writes=["GB_%s:%d" % (n, t)])
                for grp in range(2):
                    rn = {}
                    ms_ = range(grp * 4, grp * 4 + 4)
                    for m in ms_:
                        sq, sqk = sqb[m % 4]
                        hl, hlk = hlb[m % 4]
                        pt, pk = next_ps()
                        sumsq_hilo(qkv[:, m, :], "A_qkv:%d" % m, sq, sqk, hl, hlk, pt, pk)
                        rn[m] = (pt, pk)
                    for m in ms_:
                        sq, sqk = sqb[m % 4]
                        pt, pk = rn[m]
                        P.op("act", lambda e, pt=pt, sq=sq: e.activation(out=sq[:], in_=pt[:], func=AF.Ln, scale=1.0, bias=EPS), reads=[pk], writes=[sqk])
                    for m in ms_:
                        sq, sqk = sqb[m % 4]
                        P.op("act", lambda e, sq=sq: e.activation(out=sq[:], in_=sq[:], func=AF.Exp, scale=-0.5), reads=[sqk], writes=[sqk])
                    for m in ms_:
                        sq, sqk = sqb[m % 4]
                        sc = DK ** -0.5 if m < 4 else 1.0
                        P.op("dve", lambda e, m=m, sq=sq, sc=sc: e.scalar_tensor_tensor(out=qkvb[:, m, :], in0=qkv[:, m, :], scalar=sc, in1=sq[:],
                                                                                     op0=ALU.mult, op1=ALU.mult),
                             reads=["A_qkv:%d" % m, sqk], writes=["A_qkvb:%d" % m])
                P.dma("A_qkv_o", lambda e, t0=t0: e.dma_start(out=S["QKV_" + n][0:8, :, t0:t0 + NT].rearrange("m p t -> p m t"), in_=qkvb[:]),
                      reads=["A_qkvb:%d" % m for m in range(8)], writes=["QKV_%s:%d" % (n, t)])
                P.dma("A_v_o", lambda e, t0=t0: e.dma_start(out=S["QKV_" + n][8:12, :, t0:t0 + NT].rearrange("m p t -> p m t"), in_=qkv[:, 8:12, :]),
                      reads=["A_qkv:%d" % m for m in range(8, 12)], writes=["QKVv_%s:%d" % (n, t)])
                for m in range(12):
                    pt, pk = next_ps()
                    proj(OFF_HY + m * 128, 128, pt, pk)
                    pb, pbk = pj[m % 3]
                    P.op("act", lambda e, pt=pt, pb=pb: e.copy(out=pb[:], in_=pt[:]), reads=[pk], writes=[pbk])
                    conv(qkv[:, m, :], "A_qkv:%d" % m, pb[:], pbk, hcw[:, l, m, :], 3, pt, pk)
                P.dma("A_x0_o", lambda e, t0=t0: e.dma_start(out=S["X0_" + n][:, :, t0:t0 + NT].rearrange("m p t -> p m t"), in_=qkv[:, 0:4, :]),
                      reads=["A_qkv:%d" % m for m in range(4)], writes=["X0_%s:%d" % (n, t)])
                for c in range(4):
                    P.op("dve", lambda e, c=c: e.tensor_tensor(out=zz[:, c, :], in0=qkv[:, 4 + c, :], in1=qkv[:, 8 + c, :], op=ALU.mult),
                         reads=["A_qkv:%d" % (4 + c), "A_qkv:%d" % (8 + c)], writes=["A_zz:%d" % c])
                    P.op("pool", lambda e, c=c: e.tensor_copy(out=zzb[:, c, :], in_=zz[:, c, :]), reads=["A_zz:%d" % c], writes=["A_zzb:%d" % c])
                P.dma("A_zz_o", lambda e, t0=t0: e.dma_start(out=S["ZZ_" + n][:, :, t0:t0 + NT].rearrange("m p t -> p m t"), in_=zz[:]),
                      reads=["A_zz:%d" % c for c in range(4)], writes=["ZZ_%s:%d" % (n, t)])
                for tb4 in range(4):
                    pt, pk = next_ps()
                    ptb = pt[:].bitcast(BF16)
                    for c in range(4):
                        P.op("pe", lambda e, c=c, tb4=tb4, ptb=ptb: e.transpose(out=ptb[:, c * 128:(c + 1) * 128],
                                                                             in_=zzb[:, c, tb4 * 128:(tb4 + 1) * 128], identity=identb[:]),
                             reads=["A_zzb:%d" % c, "identb"], writes=[pk])
                    P.op("act", lambda e, tb4=tb4, ptb=ptb: e.copy(out=zzt[:, tb4, :], in_=ptb[:, 0:512]), reads=[pk], writes=["A_zzt:%d" % tb4])
                P.dma("A_zzt_o", lambda e, t0=t0: e.dma_start(out=S["ZZT_" + n][t0:t0 + NT, :].rearrange("(b p) c -> p b c", p=128), in_=zzt[:]),
                      reads=["A_zzt:%d" % b for b in range(4)], writes=["ZZT_%s:%d" % (n, t)])
            P.release(m0)

        def phase_H(l, g):
            n = g.name
            L, nTB, NF, NFB = g.L, g.nTB, g.NF, g.NFB
            ctab, stab = I["ctab_" + n], I["stab_" + n]
            m0 = P.mark()
            hsd = P.tile("H_hsd", [128, 2, nTB, 512], BF16)
            m1 = P.mark()
            w1 = P.tile("H_w1", [HY_EMB, HY_FH], F32)
            w2 = P.tile("H_w2", [HY_FH, HY_FH], F32)
            w3e = P.tile("H_w3e", [HY_FH + 1, 1024], F32)
            zf = P.tile("H_zf", [HY_EMB, L], F32)
            negt = P.tile("H_negt", [128, nTB], F32)
            h1 = P.tile("H_h1", [HY_FH, L], F32)
            h2e = P.tile("H_h2e", [HY_FH + 1, L], F32)
            ld("H_w1", w1[:], I["hw1"][l], "H_w1")
            ld("H_w2", w2[:], I["hw2"][l], "H_w2")
            ld("H_w3e", w3e[:], I["hw3e"][l], "H_w3e")
            ld("H_zf", zf[:], I["zf_" + n], "H_zf")
            ld("H_negt", negt[:], I["negt_" + n], "H_negt")
            CW = min(512, L)
            ua = P.tile("H_ua", [HY_FH, CW], F32)
            ui = P.tile("H_ui", [HY_FH, CW], I32)
            uf = P.tile("H_uf", [HY_FH, CW], F32)

            def sin_layer(wt, wkey, kdim, src, skey, bcol, dst, dkey):
                for c0 in range(0, L, CW):
                    pt, pk = next_ps()
                    P.op("pe", lambda e, c0=c0, pt=pt: e.matmul(pt[0:HY_FH, 0:CW], lhsT=wt[0:kdim, :], rhs=src[0:kdim, c0:c0 + CW], start=True, stop=True),
                         reads=[wkey, skey], writes=[pk])
                    P.op("dve", lambda e, pt=pt: e.tensor_scalar(out=ua[:], in0=pt[0:HY_FH, 0:CW], scalar1=hvec[:, l, bcol:bcol + 1],
                                                                scalar2=hf2p[:, l:l + 1], op0=ALU.add, op1=ALU.mult),
                         reads=[pk, "hvec", "hf2p"], writes=["H_ua"])
                    P.op("dve", lambda e: e.tensor_scalar(out=ua[:], in0=ua[:], scalar1=8.5, scalar2=None, op0=ALU.add), reads=["H_ua"], writes=["H_ua"])
                    P.op("dve", lambda e: e.tensor_copy(out=ui[:], in_=ua[:]), reads=["H_ua"], writes=["H_ui"])
                    P.op("dve", lambda e: e.tensor_copy(out=uf[:], in_=ui[:]), reads=["H_ui"], writes=["H_uf"])
                    P.op("dve", lambda e: e.tensor_tensor(out=ua[:], in0=ua[:], in1=uf[:], op=ALU.subtract), reads=["H_ua", "H_uf"], writes=["H_ua"])
                    P.op("dve", lambda e: e.tensor_scalar(out=uf[:], in0=ua[:], scalar1=0.5, scalar2=None, op0=ALU.is_gt), reads=["H_ua"], writes=["H_uf"])
                    P.op("dve", lambda e: e.tensor_tensor(out=ua[:], in0=ua[:], in1=uf[:], op=ALU.subtract), reads=["H_ua", "H_uf"], writes=["H_ua"])
                    P.op("act", lambda e, c0=c0: e.activation(out=dst[0:HY_FH, c0:c0 + CW], in_=ua[:], func=AF.Sin, scale=-2 * math.pi),
                         reads=["H_ua"], writes=[dkey])

            sin_layer(w1, "H_w1", HY_EMB, zf, "H_zf", 0, h1, "H_h1")
            P.op("pool", lambda e: e.memset(h2e[64:65, :], 1.0), writes=["H_h2e"])
            sin_layer(w2, "H_w2", HY_FH, h1, "H_h1", 2, h2e, "H_h2e")
            win = P.tile("H_win", [128, 512], F32)
            hfb = [(P.tile("H_hf%d" % i, [128, 512], F32), "H_hf%d" % i) for i in range(2)]
            for tb in range(nTB):
                P.op("act", lambda e, tb=tb: e.activation(out=win[:], in_=deltab[:], func=AF.Exp, scale=negt[:, tb:tb + 1]),
                     reads=["deltab", "H_negt"], writes=["H_win"])
                for d in range(2):
                    pt, pk = next_ps()
                    P.op("pe", lambda e, tb=tb, d=d, pt=pt: e.matmul(pt[:], lhsT=h2e[:, tb * 128:(tb + 1) * 128], rhs=w3e[:, d * 512:(d + 1) * 512],
                                                                     start=True, stop=True), reads=["H_h2e", "H_w3e"], writes=[pk])
                    hb, hk = hfb[d]
                    P.op("dve", lambda e, pt=pt, hb=hb: e.tensor_tensor(out=hb[:], in0=pt[:], in1=win[:], op=ALU.mult), reads=[pk, "H_win"], writes=[hk])
                if tb == 0:
                    P.op("pool", lambda e: e.memset(hfb[1][0][0:1, :], 0.0), reads=[hfb[1][1]], writes=[hfb[1][1]])
                P.op("dve", lambda e, tb=tb: e.tensor_tensor(out=hsd[:, 0, tb, :], in0=hfb[0][0][:], in1=hfb[1][0][:], op=ALU.add),
                     reads=[hfb[0][1], hfb[1][1]], writes=["H_hsd"])
                P.op("pool", lambda e, tb=tb: e.tensor_tensor(out=hsd[:, 1, tb, :], in0=hfb[0][0][:], in1=hfb[1][0][:], op=ALU.subtract),
                     reads=[hfb[0][1], hfb[1][1]], writes=["H_hsd"])
            P.release(m1)

            ctabF, stabF = I["ctabF_" + n], I["stabF_" + n]

            def fwd_slabs():
                sl = []
                for i in range(2):
                    sl.append((P.tile("H_cs%d" % i, [128, nTB, 128], BF16), "H_cs%d" % i, P.tile("H_ss%d" % i, [128, nTB, 128], BF16), "H_ss%d" % i))
                return sl

            def load_fwd_slab(sl, fb):
                cs, ck, ss, sk = sl[fb % 2]
                P.dma(ck, lambda e: e.dma_start(out=cs[:], in_=ctabF[fb, :, 0:nTB, :]), writes=[ck])
                P.dma(sk, lambda e: e.dma_start(out=ss[:], in_=stabF[fb, :, 0:nTB, :]), writes=[sk])
                return cs, ck, ss, sk

            m1 = P.mark()
            wcol = P.tile("H_wcol", [128, NFB], F32)
            ld("H_wcol", wcol[:], I["wcol_" + n], "H_wcol")
            sl = fwd_slabs()
            fst = [(P.tile("H_fst%d" % i, [128, 2, 512], F32), "H_fst%d" % i) for i in range(2)]
            for fb in range(NFB):
                cs, ck, ss, sk = load_fwd_slab(sl, fb)
                fs, fk = fst[fb % 2]
                for which, (slab, slk) in enumerate(((cs, ck), (ss, sk))):
                    pt, pk = next_ps()
                    for tb in range(nTB):
                        P.op("pe", lambda e, tb=tb, pt=pt, slab=slab, which=which: e.matmul(
                            pt[:], lhsT=slab[:, tb, :], rhs=hsd[:, which, tb, :], start=(tb == 0), stop=(tb == nTB - 1)),
                            reads=["H_hsd", slk], writes=[pk])
                    P.op("act", lambda e, pt=pt, fs=fs, which=which, fb=fb: e.activation(out=fs[:, which, :], in_=pt[:], func=AF.Copy, scale=wcol[:, fb:fb + 1]),
                         reads=[pk, "H_wcol"], writes=[fk])
                P.dma(fk + "o", lambda e, fs=fs, fb=fb: e.dma_start(out=S["SPEC_" + n][:, fb, :, :].rearrange("w p c -> p w c"), in_=fs[:]),
                      reads=[fk], writes=["SPEC_%s:%d" % (n, fb)])
            P.release(m1)
            P.release(m0)

            for sq_i in range(g.nseq):
                s0 = sq_i * L
                m0 = P.mark()
                yf = P.tile("H_yf", [128, 2 * NFB, 512], BF16)
                m1 = P.mark()
                zzt = P.tile("H_zzt", [128, nTB, 512], BF16)
                tiles_touched = sorted(set((s0 + i * 128) // NT for i in range(nTB)))
                P.dma("H_zzt", lambda e, s0=s0, zzt=zzt: e.dma_start(out=zzt[:], in_=S["ZZT_" + n][s0:s0 + L, :].rearrange("(b p) c -> p b c", p=128)),
                      reads=["ZZT_%s:%d" % (n, t) for t in tiles_touched], writes=["H_zzt"])
                sl = fwd_slabs()
                fsl = [(P.tile("H_fsl%d" % i, [128, 2, 512], F32), "H_fsl%d" % i) for i in range(2)]
                t1 = [(P.tile("H_t1%d" % i, [128, 512], F32), "H_t1%d" % i) for i in range(4)]
                for fb in range(NFB):
                    cs, ck, ss, sk = load_fwd_slab(sl, fb)
                    fs, fk = fsl[fb % 2]
                    P.dma(fk, lambda e, fs=fs, fb=fb: e.dma_start(out=fs[:], in_=S["SPEC_" + n][:, fb, :, :].rearrange("w p c -> p w c")),
                          reads=["SPEC_%s:%d" % (n, fb)], writes=[fk])
                    zps = []
                    for slab, slk in ((cs, ck), (ss, sk)):
                        pt, pk = next_ps()
                        for tb in range(nTB):
                            P.op("pe", lambda e, tb=tb, pt=pt, slab=slab, zzt=zzt: e.matmul(
                                pt[:], lhsT=slab[:, tb, :], rhs=zzt[:, tb, :], start=(tb == 0), stop=(tb == nTB - 1)),
                                reads=["H_zzt", slk], writes=[pk])
                        zps.append((pt, pk))
                    (zc, zck), (zs, zsk) = zps
                    a, ak = t1[0]
                    b, bk = t1[1]
                    c_, c_k = t1[2]
                    d_, d_k = t1[3]
                    P.op("dve", lambda e, zc=zc, fs=fs, a=a: e.tensor_tensor(out=a[:], in0=zc[:], in1=fs[:, 0, :], op=ALU.mult), reads=[zck, fk], writes=[ak])
                    P.op("dve", lambda e, zs=zs, fs=fs, b=b: e.tensor_tensor(out=b[:], in0=zs[:], in1=fs[:, 1, :], op=ALU.mult), reads=[zsk, fk], writes=[bk])
                    P.op("dve", lambda e, zc=zc, fs=fs, c_=c_: e.tensor_tensor(out=c_[:], in0=zc[:], in1=fs[:, 1, :], op=ALU.mult), reads=[zck, fk], writes=[c_k])
                    P.op("dve", lambda e, zs=zs, fs=fs, d_=d_: e.tensor_tensor(out=d_[:], in0=zs[:], in1=fs[:, 0, :], op=ALU.mult), reads=[zsk, fk], writes=[d_k])
                    P.op("pool", lambda e, a=a, b=b, fb=fb, yf=yf: e.tensor_tensor(out=yf[:, fb, :], in0=a[:], in1=b[:], op=ALU.subtract), reads=[ak, bk], writes=["H_yf"])
                    P.op("pool", lambda e, c_=c_, d_=d_, fb=fb, yf=yf: e.tensor_tensor(out=yf[:, NFB + fb, :], in0=c_[:], in1=d_[:], op=ALU.add), reads=[c_k, d_k], writes=["H_yf"])
                P.release(m1)
                TWg = min(TW, L)
                isl = [(P.tile("H_ci%d" % i, [128, NFB, TWg], BF16), "H_ci%d" % i, P.tile("H_si%d" % i, [128, NFB, TWg], BF16), "H_si%d" % i) for i in range(2)]
                x0b = [(P.tile("H_x0%d" % i, [128, TWg], F32), "H_x0%d" % i) for i in range(2)]
                zzb_ = [(P.tile("H_zb%d" % i, [128, TWg], F32), "H_zb%d" % i) for i in range(2)]
                yo = [(P.tile("H_yo%d" % i, [128, TWg], BF16), "H_yo%d" % i) for i in range(2)]
                tm = [(P.tile("H_tm%d" % i, [128, TWg], F32), "H_tm%d" % i) for i in range(2)]
                it = 0
                for ti in range(L // TWg):
                    tt0 = ti * TWg
                    ci, cik, si, sik = isl[ti % 2]
                    P.dma(cik, lambda e, ci=ci, ti=ti: e.dma_start(out=ci[:], in_=ctab[ti, :, 0:NFB, :]), writes=[cik])
                    P.dma(sik, lambda e, si=si, ti=ti: e.dma_start(out=si[:], in_=stab[ti, :, 0:NFB, :]), writes=[sik])
                    gt0 = s0 + tt0
                    tile_i = gt0 // NT
                    for cc in range(4):
                        xb, xk = x0b[it % 2]
                        zb, zk = zzb_[it % 2]
                        yb, yk = yo[it % 2]
                        tmb, tmk = tm[it % 2]
                        it += 1
                        P.dma(xk, lambda e, xb=xb, cc=cc, gt0=gt0: e.dma_start(out=xb[:], in_=S["X0_" + n][cc, :, gt0:gt0 + TWg]),
                              reads=["X0_%s:%d" % (n, tile_i)], writes=[xk])
                        P.dma(zk, lambda e, zb=zb, cc=cc, gt0=gt0: e.dma_start(out=zb[:], in_=S["ZZ_" + n][cc, :, gt0:gt0 + TWg]),
                              reads=["ZZ_%s:%d" % (n, tile_i)], writes=[zk])
                        pt, pk = next_ps()
                        nmm = 2 * NFB
                        i_mm = 0
                        for w, (slab, slk) in enumerate(((ci, cik), (si, sik))):
                            for fb in range(NFB):
                                P.op("pe", lambda e, pt=pt, w=w, fb=fb, slab=slab, cc=cc, i_mm=i_mm: e.matmul(
                                    pt[:, 0:TWg], lhsT=yf[:, w * NFB + fb, cc * 128:(cc + 1) * 128], rhs=slab[:, fb, :],
                                    start=(i_mm == 0), stop=(i_mm == nmm - 1)), reads=["H_yf", slk], writes=[pk])
                                i_mm += 1
                        P.op("dve", lambda e, pt=pt, zb=zb, tmb=tmb, cc=cc: e.scalar_tensor_tensor(
                            out=tmb[:], in0=zb[:], scalar=hskip[:, l, cc:cc + 1], in1=pt[:, 0:TWg], op0=ALU.mult, op1=ALU.add),
                            reads=[zk, pk, "hskip"], writes=[tmk])
                        P.op("pool", lambda e, tmb=tmb, xb=xb, yb=yb: e.tensor_tensor(out=yb[:], in0=tmb[:], in1=xb[:], op=ALU.mult), reads=[tmk, xk], writes=[yk])
                        P.dma(yk + "o", lambda e, yb=yb, cc=cc, gt0=gt0: e.dma_start(out=S["YH_" + n][cc, :, gt0:gt0 + TWg], in_=yb[:]),
                              reads=[yk], writes=["YH_%s:%d:%d:%d" % (n, tile_i, cc, (gt0 % NT) // TWg)])
                P.release(m0)

        def phase_G(l, g):
            n = g.name
            m0 = P.mark()
            NCH = NT // CH
            Sst = [[(P.tile("G_S%d%d" % (d, h), [128, 128], F32), "G_S%d%d" % (d, h)) for h in range(H)] for d in range(2)]
            Sbf = [[(P.tile("G_Sb%d%d" % (d, h), [128, 128], BF16), "G_Sb%d%d" % (d, h)) for h in range(H)] for d in range(2)]

            NP = NCH // 2

            def dir_tiles(d):
                p = "G_"
                W = {}
                W["qkv"] = P.tile(p + "qkv", [128, 12, NT], F32)
                W["gbr"] = P.tile(p + "gbr", [8, 2, NT], F32)
                W["gc"] = P.tile(p + "gc", [8, NT], F32)
                W["gtot"] = P.tile(p + "gtot", [8, NCH], F32)
                W["gcs"] = P.tile(p + "gcs", [8, 3, NT], BF16)
                W["bts"] = P.tile(p + "bts", [8, 3, NT], BF16)
                W["gcb"] = P.tile(p + "gcb", [128, NT], F32)
                W["btb"] = P.tile(p + "btb", [128, NT], F32)
                W["E"] = P.tile(p + "E", [128, NT], F32)
                W["tmp"] = P.tile(p + "tmp", [128, NT], F32)
                W["gcT"] = P.tile(p + "gcT", [128, NP], F32)
                W["btT"] = P.tile(p + "btT", [128, NP], F32)
                W["sc"] = P.tile(p + "sc", [128, 3, NP], F32)
                W["eglh"] = P.tile(p + "eglh", [128, H, NCH], F32)
                W["DT"] = P.tile(p + "DT", [128, NT], F32)
                W["XB"] = P.tile(p + "XB", [128, NT], F32)
                for h in range(H):
                    W["qd%d" % h] = P.tile(p + "qd%d" % h, [128, NT], BF16)
                    W["wT%d" % h] = P.tile(p + "wT%d" % h, [128, NT], F32)
                    W["kbg%d" % h] = P.tile(p + "kbg%d" % h, [128, NP, 128], F32)
                    W["kdec%d" % h] = P.tile(p + "kdec%d" % h, [128, NP, 128], BF16)
                    W["Xf%d" % h] = P.tile(p + "Xf%d" % h, [128, NP, 128], F32)
                    W["Xtf%d" % h] = P.tile(p + "Xtf%d" % h, [128, NP, 128], F32)
                    W["vbf%d" % h] = P.tile(p + "vbf%d" % h, [128, NP, 128], F32)
                    W["X%d" % h] = P.tile(p + "X%d" % h, [128, 2, NP, 128], BF16)
                    W["Xt%d" % h] = P.tile(p + "Xt%d" % h, [128, 2, NP, 128], BF16)
                    W["R%d" % h] = P.tile(p + "R%d" % h, [128, 2, NP, 128], BF16)
                    W["Rf%d" % h] = P.tile(p + "Rf%d" % h, [128, NP, 128], F32)
                    W["qkT%d" % h] = P.tile(p + "qkT%d" % h, [128, NP, 128], BF16)
                    W["u%d" % h] = P.tile(p + "u%d" % h, [128, NP, 128], F32)
                    W["vn%d" % h] = P.tile(p + "vn%d" % h, [128, 2, 128], BF16)
                    W["oT%d" % h] = P.tile(p + "oT%d" % h, [128, NT], F32)
                W["E2"] = W["X0"][:].bitcast(F32).rearrange("p a c i -> p (a c i)").rearrange("p (c i) -> p c i", i=128)
                W["RfT"] = W["Xt0"][:].bitcast(F32).rearrange("p a c i -> p (a c i)").rearrange("p (c i) -> p c i", i=128)
                W["p"] = p
                return W

            W0 = dir_tiles(0)
            Ws = [W0, W0]

            for d in range(2):
                for h in range(H):
                    St, Sk = Sst[d][h]
                    if g.has_s0:
                        P.dma(Sk, lambda e, St=St, d=d, h=h: e.dma_start(out=St[:], in_=I["s0"][l, d, h]), writes=[Sk])
                    else:
                        P.op("pool", lambda e, St=St: e.memset(St[:], 0.0), writes=[Sk])
                    Sb, Sbk = Sbf[d][h]
                    P.op("act", lambda e, St=St, Sb=Sb: e.copy(out=Sb[:], in_=St[:]), reads=[Sk], writes=[Sbk])

            def load_tile(d, t):
                W = Ws[d]
                p = W["p"]
                t0 = t * NT
                P.dma(p + "qkv", lambda e: e.dma_start(out=W["qkv"][:], in_=S["QKV_" + n][:, :, t0:t0 + NT].rearrange("m p t -> p m t")),
                      reads=["QKV_%s:%d" % (n, t), "QKVv_%s:%d" % (n, t)], writes=[p + "qkv"])
                P.dma(p + "gbr", lambda e: e.dma_start(out=W["gbr"][:], in_=S["GB_" + n][:, t0:t0 + NT].rearrange("(a r) t -> r a t", a=2)),
                      reads=["GB_%s:%d" % (n, t)], writes=[p + "gbr"])

            def v4(ap):
                return ap.rearrange("p (q k) -> p q k", k=128)

            def chunk_local(d, t):
                W = Ws[d]
                p = W["p"]
                mi, ms = (0, 2) if d == 0 else (1, 3)
                last = CH - 1 if d == 0 else 0
                P.op("dve", lambda e: e.tensor_tensor_scan(out=W["gc"][:], data0=scanmask[:], data1=W["gbr"][:, 0, :], initial=0.0, op0=ALU.mult, op1=ALU.add),
                     reads=[p + "gbr", "scanmask"], writes=[p + "gc"])
                if d == 1:
                    gc3 = W["gc"][:].rearrange("r (c i) -> r c i", i=CH)
                    P.op("dve", lambda e: e.tensor_copy(out=W["gtot"][:], in_=gc3[:, :, CH - 1]), reads=[p + "gc"], writes=[p + "gtot"])
                    P.op("dve", lambda e: e.tensor_tensor(out=W["gc"][:], in0=W["gbr"][:, 0, :], in1=W["gc"][:], op=ALU.subtract),
                         reads=[p + "gbr", p + "gc"], writes=[p + "gc"])
                    P.op("dve", lambda e: e.tensor_tensor(out=gc3, in0=gc3, in1=W["gtot"][:].unsqueeze(2).to_broadcast([8, NCH, CH]), op=ALU.add),
                         reads=[p + "gc", p + "gtot"], writes=[p + "gc"])

                def split3(src_ap, skey, dst, dkey):
                    sp0, sp1 = W["tmp"][0:8, :], W["DT"][0:8, :]
                    P.op("dve", lambda e: e.tensor_copy(out=dst[:, 0, :], in_=src_ap), reads=[skey], writes=[dkey])
                    P.op("dve", lambda e: e.tensor_tensor(out=sp0, in0=src_ap, in1=dst[:, 0, :], op=ALU.subtract), reads=[skey, dkey], writes=[p + "tmp"])
                    P.op("dve", lambda e: e.tensor_copy(out=dst[:, 1, :], in_=sp0), reads=[p + "tmp"], writes=[dkey])
                    P.op("dve", lambda e: e.tensor_tensor(out=sp1, in0=sp0, in1=dst[:, 1, :], op=ALU.subtract), reads=[p + "tmp", dkey], writes=[p + "DT"])
                    P.op("dve", lambda e: e.tensor_copy(out=dst[:, 2, :], in_=sp1), reads=[p + "DT"], writes=[dkey])

                split3(W["gc"][:], p + "gc", W["gcs"], p + "gcs")
                split3(W["gbr"][:, 1, :], p + "gbr", W["bts"], p + "bts")

                def bcast(pt_ap, pk_, r_, src, skey):
                    for i3 in range(3):
                        P.op("pe", lambda e, i3=i3: e.matmul(pt_ap, lhsT=selb[:, r_, :], rhs=src[:, i3, :], start=(i3 == 0), stop=(i3 == 2)),
                             reads=["selb", skey], writes=[pk_])

                def xt_part(h):
                    Xh, Xth, Rh = W["X%d" % h], W["Xt%d" % h], W["R%d" % h]
                    kX, kXt, kR = p + "X%d" % h, p + "Xt%d" % h, p + "R%d" % h
                    Xf, Xtf = W["Xf%d" % h], W["Xtf%d" % h]
                    ptt, pkt = next_ps()
                    for q in range(NP):
                        P.op("pe", lambda e, q=q, ptt=ptt, Xf=Xf: e.transpose(out=ptt[:, q * 128:(q + 1) * 128], in_=Xf[:, q, :], identity=ident[:]),
                             reads=[p + "Xf%d" % h, "ident"], writes=[pkt])
                    P.op("act", lambda e, ptt=ptt, Xtf=Xtf: e.copy(out=Xtf[:], in_=v4(ptt[:])), reads=[pkt], writes=[p + "Xtf%d" % h])
                    P.op("pool", lambda e, Xth=Xth, Xtf=Xtf: e.tensor_copy(out=Xth[:, 0, :, :], in_=Xtf[:]), reads=[p + "Xtf%d" % h], writes=[kXt + ":0"])
                    P.op("pool", lambda e, Xh=Xh, Rh=Rh: e.tensor_copy(out=Rh[:, 0, :, :], in_=Xh[:, 0, :, :]), reads=[kX + ":0"], writes=[kR + ":0"])
                    P.op("pool", lambda e, Xf=Xf, h=h: e.tensor_copy(out=W["Rf%d" % h][:], in_=Xf[:]), reads=[p + "Xf%d" % h], writes=[p + "Rf%d" % h])

                gcb4, btb4, tmp4, DT4, XB4 = v4(W["gcb"][:]), v4(W["btb"][:]), v4(W["tmp"][:]), v4(W["DT"][:]), v4(W["XB"][:])
                eye_b = ident[:].unsqueeze(1).to_broadcast([128, NP, 128])
                gcbc3 = W["gcb"][:].rearrange("p (c i) -> p c i", i=CH)
                for h in range(H):
                    r = d * 4 + h
                    qT = W["qkv"][:, h, :]
                    kT = W["qkv"][:, 4 + h, :]
                    vT = W["qkv"][:, 8 + h, :]
                    pt, pk = next_ps()
                    bcast(pt[:], pk, r, W["gcs"], p + "gcs")
                    P.op("act", lambda e, pt=pt: e.copy(out=W["gcb"][:], in_=pt[:]), reads=[pk], writes=[p + "gcb"])
                    P.op("act", lambda e, pt=pt: e.activation(out=W["E"][:], in_=pt[:], func=AF.Exp), reads=[pk], writes=[p + "E"])
                    pt2, pk2 = next_ps()
                    bcast(pt2[:], pk2, r, W["bts"], p + "bts")
                    P.op("act", lambda e, pt2=pt2: e.copy(out=W["btb"][:], in_=pt2[:]), reads=[pk2], writes=[p + "btb"])
                    P.op("dve", lambda e: e.tensor_tensor(out=tmp4, in0=gcb4, in1=eye_b, op=ALU.mult), reads=[p + "gcb", "ident"], writes=[p + "tmp"])
                    P.op("dve", lambda e: e.tensor_reduce(out=W["gcT"][:], in_=tmp4, axis=AX.X, op=ALU.add), reads=[p + "tmp"], writes=[p + "gcT"])
                    P.op("dve", lambda e: e.tensor_tensor(out=tmp4, in0=btb4, in1=eye_b, op=ALU.mult), reads=[p + "btb", "ident"], writes=[p + "tmp"])
                    P.op("dve", lambda e: e.tensor_reduce(out=W["btT"][:], in_=tmp4, axis=AX.X, op=ALU.add), reads=[p + "tmp"], writes=[p + "btT"])
                    P.op("act", lambda e: e.activation(out=W["sc"][:, 2, :], in_=W["gcT"][:], func=AF.Exp), reads=[p + "gcT"], writes=[p + "sc:2"])
                    P.op("dve", lambda e: e.tensor_tensor(out=W["sc"][:, 0, :], in0=W["sc"][:, 2, :], in1=W["btT"][:], op=ALU.mult),
                         reads=[p + "sc:2", p + "btT"], writes=[p + "sc:0"])
                    for c2 in range(2):
                        ps_ = slice(c2 * 64, (c2 + 1) * 64)
                        P.op("dve", lambda e, ps_=ps_, c2=c2: e.tensor_tensor(out=W["sc"][ps_, 1, :], in0=gcb4[ps_, :, c2 * 64 + last], in1=W["gcT"][ps_, :], op=ALU.subtract),
                             reads=[p + "gcb", p + "gcT"], writes=[p + "sc:1"])
                    P.op("act", lambda e: e.activation(out=W["sc"][:, 1, :], in_=W["sc"][:, 1, :], func=AF.Exp), reads=[p + "sc:1"], writes=[p + "sc:1"])
                    P.op("act", lambda e, h=h: e.activation(out=W["eglh"][:, h, :], in_=gcbc3[:, :, last], func=AF.Exp), reads=[p + "gcb"], writes=[p + "eglh"])
                    P.op("dve", lambda e: e.tensor_tensor(out=tmp4, in0=gcb4, in1=W["gcT"][:].unsqueeze(2).to_broadcast([128, NP, 128]), op=ALU.subtract),
                         reads=[p + "gcb", p + "gcT"], writes=[p + "tmp"])
                    P.op("dve", lambda e: e.tensor_scalar(out=W["tmp"][:], in0=W["tmp"][:], scalar1=0.0, scalar2=None, op0=ALU.min), reads=[p + "tmp"], writes=[p + "tmp"])
                    P.op("act", lambda e: e.activation(out=W["tmp"][:], in_=W["tmp"][:], func=AF.Exp), reads=[p + "tmp"], writes=[p + "tmp"])
                    P.op("dve", lambda e: e.tensor_tensor(out=DT4, in0=tmp4, in1=masks2[:, mi, :].unsqueeze(1).to_broadcast([128, NP, 128]), op=ALU.mult),
                         reads=[p + "tmp", "masks2"], writes=[p + "DT"])
                    P.op("pool", lambda e: e.tensor_tensor(out=XB4, in0=DT4, in1=masks2[:, ms, :].unsqueeze(1).to_broadcast([128, NP, 128]), op=ALU.mult),
                         reads=[p + "DT", "masks2"], writes=[p + "XB"])
                    P.op("dve", lambda e: e.tensor_tensor(out=W["XB"][:], in0=W["XB"][:], in1=W["btb"][:], op=ALU.mult), reads=[p + "XB", p + "btb"], writes=[p + "XB"])
                    P.op("dve", lambda e, h=h, qT=qT: e.tensor_tensor(out=W["qd%d" % h][:], in0=qT, in1=W["E"][:], op=ALU.mult),
                         reads=[p + "qkv", p + "E"], writes=[p + "qd%d" % h])
                    P.op("pool", lambda e, h=h, kT=kT: e.tensor_tensor(out=W["wT%d" % h][:], in0=kT, in1=W["E"][:], op=ALU.mult),
                         reads=[p + "qkv", p + "E"], writes=[p + "wT%d" % h])
                    P.op("pool", lambda e, h=h: e.tensor_tensor(out=W["wT%d" % h][:], in0=W["wT%d" % h][:], in1=W["btb"][:], op=ALU.mult),
                         reads=[p + "btb", p + "wT%d" % h], writes=[p + "wT%d" % h])
                    for kind, src in ((0, kT), (1, vT)):
                        pt4, pk4 = next_ps()
                        for q in range(NP):
                            P.op("pe", lambda e, pt4=pt4, q=q, src=src: e.transpose(out=pt4[:, q * 128:(q + 1) * 128], in_=src[:, q * 128:(q + 1) * 128], identity=ident[:]),
                                 reads=[p + "qkv", "ident"], writes=[pk4])
                        src3 = v4(pt4[:])
                        if kind == 0:
                            P.op("dve", lambda e, src3=src3, h=h: e.tensor_tensor(
                                out=W["kbg%d" % h][:], in0=src3, in1=W["sc"][:, 0, :].unsqueeze(2).to_broadcast([128, NP, 128]), op=ALU.mult),
                                reads=[pk4, p + "sc:0"], writes=[p + "kbg%d" % h])
                            P.op("dve", lambda e, src3=src3, h=h: e.tensor_tensor(
                                out=W["kdec%d" % h][:], in0=src3, in1=W["sc"][:, 1, :].unsqueeze(2).to_broadcast([128, NP, 128]), op=ALU.mult),
                                reads=[pk4, p + "sc:1"], writes=[p + "kdec%d" % h])
                        else:
                            P.op("dve", lambda e, src3=src3, h=h: e.tensor_tensor(
                                out=W["vbf%d" % h][:], in0=src3, in1=W["btT"][:].unsqueeze(2).to_broadcast([128, NP, 128]), op=ALU.mult),
                                reads=[pk4, p + "btT"], writes=[p + "vbf%d" % h])
                    ptk, pkk = next_ps()
                    ptq, pkq = next_ps()
                    for q in range(NP):
                        qs = slice(q * 128, (q + 1) * 128)
                        P.op("pe", lambda e, qs=qs, ptk=ptk, kT=kT: e.matmul(ptk[:, qs], lhsT=kT[:, qs], rhs=kT[:, qs], start=True, stop=True), reads=[p + "qkv"], writes=[pkk])
                        P.op("pe", lambda e, qs=qs, ptq=ptq, kT=kT, qT=qT: e.matmul(ptq[:, qs], lhsT=kT[:, qs], rhs=qT[:, qs], start=True, stop=True), reads=[p + "qkv"], writes=[pkq])
                    Xh = W["X%d" % h]
                    kX = p + "X%d" % h
                    Xf = W["Xf%d" % h]
                    P.op("dve", lambda e, ptk=ptk, Xf=Xf: e.tensor_tensor(out=Xf[:], in0=v4(ptk[:]), in1=XB4, op=ALU.mult), reads=[pkk, p + "XB"], writes=[p + "Xf%d" % h])
                    P.op("dve", lambda e, ptq=ptq, h=h: e.tensor_tensor(out=W["qkT%d" % h][:], in0=v4(ptq[:]), in1=DT4, op=ALU.mult), reads=[pkq, p + "DT"], writes=[p + "qkT%d" % h])
                    P.op("pool", lambda e, Xh=Xh, Xf=Xf: e.tensor_copy(out=Xh[:, 0, :, :], in_=Xf[:]), reads=[p + "Xf%d" % h], writes=[kX + ":0"])
                    if h > 0:
                        xt_part(h - 1)
                xt_part(H - 1)
                for it in range(5):
                    a, b = it % 2, (it + 1) % 2
                    for h in range(H):
                        Xh, Xth = W["X%d" % h], W["Xt%d" % h]
                        kX, kXt = p + "X%d" % h, p + "Xt%d" % h
                        pa, pka = next_ps()
                        for q in range(NP):
                            P.op("pe", lambda e, q=q, pa=pa, Xh=Xh, Xth=Xth, a=a: e.matmul(pa[:, q * 128:(q + 1) * 128], lhsT=Xth[:, a, q, :], rhs=Xh[:, a, q, :], start=True, stop=True),
                                 reads=[kX + ":%d" % a, kXt + ":%d" % a], writes=[pka])
                        P.op("act", lambda e, pa=pa, Xh=Xh, b=b: e.copy(out=Xh[:, b, :, :], in_=v4(pa[:])), reads=[pka], writes=[kX + ":%d" % b])
                        pb_, pkb = next_ps()
                        for q in range(NP):
                            P.op("pe", lambda e, q=q, pb_=pb_, Xh=Xh, Xth=Xth, a=a: e.matmul(pb_[:, q * 128:(q + 1) * 128], lhsT=Xh[:, a, q, :], rhs=Xth[:, a, q, :], start=True, stop=True),
                                 reads=[kX + ":%d" % a, kXt + ":%d" % a], writes=[pkb])
                        P.op("act", lambda e, pb_=pb_, Xth=Xth, b=b: e.copy(out=Xth[:, b, :, :], in_=v4(pb_[:])), reads=[pkb], writes=[kXt + ":%d" % b])
                    for h in range(H):
                        Xh, Xth, Rh = W["X%d" % h], W["Xt%d" % h], W["R%d" % h]
                        kX, kXt, kR = p + "X%d" % h, p + "Xt%d" % h, p + "R%d" % h
                        pr, pkr = next_ps()
                        for q in range(NP):
                            P.op("pe", lambda e, q=q, pr=pr, Rh=Rh, Xth=Xth, a=a, b=b: e.matmul(pr[:, q * 128:(q + 1) * 128], lhsT=Xth[:, b, q, :], rhs=Rh[:, a, q, :], start=True, stop=True),
                                 reads=[kXt + ":%d" % b, kR + ":%d" % a], writes=[pkr])
                        Rf = W["Rf%d" % h]
                        P.op("dve", lambda e, pr=pr, Rf=Rf: e.tensor_tensor(out=Rf[:], in0=v4(pr[:]), in1=Rf[:], op=ALU.add), reads=[pkr, p + "Rf%d" % h], writes=[p + "Rf%d" % h])
                        P.op("pool", lambda e, Rf=Rf, Xh=Xh, b=b: e.tensor_tensor(out=Rf[:], in0=Rf[:], in1=Xh[:, b, :, :], op=ALU.add),
                             reads=[p + "Rf%d" % h, kX + ":%d" % b], writes=[p + "Rf%d" % h])
                        P.op("act", lambda e, Rf=Rf, Rh=Rh, b=b: e.copy(out=Rh[:, b, :, :], in_=Rf[:]), reads=[p + "Rf%d" % h], writes=[kR + ":%d" % b])
                kE2 = [p + "X0:0", p + "X0:1"]
                kRT = [p + "Xt0:0", p + "Xt0:1"]
                for h in range(H):
                    Xf, Xtf, Rf = W["Xf%d" % h], W["Xtf%d" % h], W["Rf%d" % h]
                    kRf = p + "Rf%d" % h
                    for rstep in range(NEWTON_STEPS):
                        pe2, pke2 = next_ps()
                        for q in range(NP):
                            P.op("pe", lambda e, q=q, pe2=pe2, Xtf=Xtf, Rf=Rf: e.matmul(pe2[:, q * 128:(q + 1) * 128], lhsT=Xtf[:, q, :], rhs=Rf[:, q, :], start=True, stop=True),
                                 reads=[p + "Xtf%d" % h, kRf], writes=[pke2])
                        P.op("dve", lambda e, pe2=pe2, Xf=Xf: e.tensor_tensor(out=W["E2"], in0=v4(pe2[:]), in1=Xf[:], op=ALU.add), reads=[pke2, p + "Xf%d" % h], writes=kE2)
                        P.op("dve", lambda e, Rf=Rf: e.tensor_tensor(out=W["E2"], in0=W["E2"], in1=Rf[:], op=ALU.subtract), reads=kE2 + [kRf], writes=kE2)
                        prt, pkrt = next_ps()
                        for q in range(NP):
                            P.op("pe", lambda e, q=q, prt=prt, Rf=Rf: e.transpose(out=prt[:, q * 128:(q + 1) * 128], in_=Rf[:, q, :], identity=ident[:]), reads=[kRf, "ident"], writes=[pkrt])
                        P.op("act", lambda e, prt=prt: e.copy(out=W["RfT"], in_=v4(prt[:])), reads=[pkrt], writes=kRT)
                        pre, pkre = next_ps()
                        for q in range(NP):
                            P.op("pe", lambda e, q=q, pre=pre: e.matmul(pre[:, q * 128:(q + 1) * 128], lhsT=W["RfT"][:, q, :], rhs=W["E2"][:, q, :], start=True, stop=True),
                                 reads=kRT + kE2, writes=[pkre])
                        P.op("dve", lambda e, Rf=Rf: e.tensor_tensor(out=Rf[:], in0=Rf[:], in1=W["E2"], op=ALU.add), reads=[kRf] + kE2, writes=[kRf])
                        P.op("dve", lambda e, pre=pre, Rf=Rf: e.tensor_tensor(out=Rf[:], in0=v4(pre[:]), in1=Rf[:], op=ALU.add), reads=[pkre, kRf], writes=[kRf])
                for h in range(H):
                    Rf = W["Rf%d" % h]
                    kRf = p + "Rf%d" % h
                    pu, pku = next_ps()
                    for q in range(NP):
                        P.op("pe", lambda e, q=q, pu=pu, Rf=Rf, h=h: e.matmul(pu[:, q * 128:(q + 1) * 128], lhsT=Rf[:, q, :], rhs=W["vbf%d" % h][:, q, :], start=True, stop=True),
                             reads=[kRf, p + "vbf%d" % h], writes=[pku])
                    P.op("dve", lambda e, pu=pu, h=h: e.tensor_tensor(out=W["u%d" % h][:], in0=v4(pu[:]), in1=W["vbf%d" % h][:], op=ALU.add),
                         reads=[pku, p + "vbf%d" % h], writes=[p + "u%d" % h])
                    pw, pkw = next_ps()
                    for q in range(NP):
                        P.op("pe", lambda e, q=q, pw=pw, Rf=Rf, h=h: e.matmul(pw[:, q * 128:(q + 1) * 128], lhsT=W["kbg%d" % h][:, q, :], rhs=Rf[:, q, :], start=True, stop=True),
                             reads=[kRf, p + "kbg%d" % h], writes=[pkw])
                    P.op("dve", lambda e, pw=pw, h=h: e.tensor_tensor(out=W["wT%d" % h][:], in0=pw[:], in1=W["wT%d" % h][:], op=ALU.add),
                         reads=[pkw, p + "wT%d" % h], writes=[p + "wT%d" % h])

            def scan_tiles(pairs):
                orders = {}
                for d, t in pairs:
                    orders[d] = list(range(NCH)) if d == 0 else list(range(NCH - 1, -1, -1))
                for step in range(NCH):
                    for d, t in pairs:
                        W = Ws[d]
                        p = W["p"]
                        c = orders[d][step]
                        q, c2 = c // 2, c % 2
                        hs = slice(c2 * 64, (c2 + 1) * 64)
                        gpos = t * NT + c * CH
                        seq = gpos // g.L
                        is_start = (gpos % g.L == 0) if d == 0 else ((gpos + CH) % g.L == 0)
                        is_end = ((gpos + CH) % g.L == 0) if d == 0 else (gpos % g.L == 0)
                        for h in range(H):
                            St, Sk = Sst[d][h]
                            Sb, Sbk = Sbf[d][h]
                            if is_start and not g.has_s0 and not (step == 0 and ((d == 0 and t == 0) or (d == 1 and t == g.ntiles - 1))):
                                P.op("pool", lambda e, St=St: e.memset(St[:], 0.0), reads=[Sk], writes=[Sk])
                                P.op("pool", lambda e, Sb=Sb: e.memset(Sb[:], 0.0), reads=[Sbk], writes=[Sbk])
                            vn = W["vn%d" % h]
                            par = step % 2
                            vk = p + "vn%d:%d" % (h, par)
                            pv, pkv = next_ps()
                            P.op("pe", lambda e, pv=pv, q=q, h=h, St=St, W=W: e.matmul(pv[:, 0:128], lhsT=W["wT%d" % h][:, q * 128:(q + 1) * 128], rhs=St[:], start=True, stop=True),
                                 reads=[p + "wT%d" % h, Sk], writes=[pkv])
                            P.op("dve", lambda e, pv=pv, q=q, h=h, vn=vn, par=par, hs=hs, W=W: e.tensor_tensor(out=vn[hs, par, :], in0=W["u%d" % h][hs, q, :], in1=pv[hs, 0:128], op=ALU.subtract),
                                 reads=[p + "u%d" % h, pkv], writes=[vk])
                            po, pko = next_ps()
                            P.op("pe", lambda e, po=po, c=c, h=h, Sb=Sb, W=W: e.matmul(po[:, 0:CH], lhsT=Sb[:], rhs=W["qd%d" % h][:, c * CH:(c + 1) * CH], start=True, stop=False),
                                 reads=[p + "qd%d" % h, Sbk], writes=[pko])
                            P.op("pe", lambda e, po=po, q=q, c2=c2, h=h, vn=vn, par=par, hs=hs, W=W: e.matmul(po[:, 0:CH], lhsT=vn[hs, par, :], rhs=W["qkT%d" % h][hs, q, c2 * 64:(c2 + 1) * 64], start=False, stop=True),
                                 reads=[p + "qkT%d" % h, vk], writes=[pko])
                            P.op("act", lambda e, po=po, c=c, h=h, W=W: e.copy(out=W["oT%d" % h][:, c * CH:(c + 1) * CH], in_=po[:, 0:CH]), reads=[pko], writes=[p + "oT%d" % h])
                            pss, pks = next_ps()
                            P.op("pe", lambda e, pss=pss, q=q, h=h, vn=vn, par=par, hs=hs, W=W: e.matmul(pss[:, 0:128], lhsT=W["kdec%d" % h][hs, q, :], rhs=vn[hs, par, :], start=True, stop=True),
                                 reads=[p + "kdec%d" % h, vk], writes=[pks])
                            P.op("dve", lambda e, pss=pss, c=c, h=h, St=St, W=W: e.scalar_tensor_tensor(out=St[:], in0=St[:], scalar=W["eglh"][:, h, c:c + 1], in1=pss[:, 0:128], op0=ALU.mult, op1=ALU.add),
                                 reads=[Sk, pks, p + "eglh"], writes=[Sk])
                            P.op("act", lambda e, St=St, Sb=Sb: e.copy(out=Sb[:], in_=St[:]), reads=[Sk], writes=[Sbk])
                            if is_end and g.wstate:
                                d_ = P.dma("nst_%d%d" % (d, h), lambda e, St=St, seq=seq, d=d, h=h: e.dma_start(out=O["nstate"][seq, l, d, h], in_=St[:]), reads=[Sk])
                                fin.append(d_.idx)
                for d, t in pairs:
                    W = Ws[d]
                    p = W["p"]
                    for h in range(H):
                        P.dma(p + "oT%d" % h, lambda e, W=W, h=h, d=d, t=t: e.dma_start(out=S["O_" + n][d, h, :, t * NT:(t + 1) * NT], in_=W["oT%d" % h][:]),
                              reads=[p + "oT%d" % h], writes=["O_%s:%d:%d:%d" % (n, d, h, t)])

            seq = [(0, t) for t in range(g.ntiles)] + [(1, t) for t in range(g.ntiles - 1, -1, -1)]
            load_tile(*seq[0])
            for i_, (d, t) in enumerate(seq):
                chunk_local(d, t)
                if i_ + 1 < len(seq):
                    load_tile(*seq[i_ + 1])
                scan_tiles([(d, t)])
            P.release(m0)

        def phase_C1(l, g, w_out_sb):
            n = g.name
            j = g.cond
            m0 = P.mark()
            xt = P.tile("C_xt", [128, KC, NT], F32)
            of = P.tile("C_of", [128, 4, NT], F32)
            ob = P.tile("C_ob", [128, 4, NT], F32)
            gate = P.tile("C_gate", [128, 4, NT], F32)
            mix = P.tile("C_mix", [128, KC, NT], BF16)
            sqb = [(P.tile("C_sq%d" % i, [128, NT], F32), "C_sq%d" % i) for i in range(4)]
            sqr = [(P.tile("C_sqr%d" % i, [128, NT], BF16), "C_sqr%d" % i) for i in range(8)]
            hlb = [(P.tile("C_hl%d" % i, [128, 2, NT], BF16), "C_hl%d" % i) for i in range(4)]
            tmp = [(P.tile("C_tmp%d" % i, [128, NT], F32), "C_tmp%d" % i) for i in range(4)]
            rstd = P.tile("C_rstd", [128, NT], F32)
            h2 = P.tile("C_h2", [128, KC, NT], BF16)
            def load_oga(t):
                t0 = t * NT
                P.dma("C_of", lambda e, t0=t0: e.dma_start(out=of[:], in_=S["O_" + n][0, :, :, t0:t0 + NT].rearrange("h p t -> p h t")),
                      reads=["O_%s:0:%d:%d" % (n, h, t) for h in range(H)], writes=["C_of"])
                P.dma("C_ob", lambda e, t0=t0: e.dma_start(out=ob[:], in_=S["O_" + n][1, :, :, t0:t0 + NT].rearrange("h p t -> p h t")),
                      reads=["O_%s:1:%d:%d" % (n, h, t) for h in range(H)], writes=["C_ob"])
                P.dma("C_gate", lambda e, t0=t0: e.dma_start(out=gate[:], in_=S["GATE_" + n][:, :, t0:t0 + NT].rearrange("m p t -> p m t")),
                      reads=["GATE_%s:%d" % (n, t)], writes=["C_gate"])

            def load_yh(t):
                t0 = t * NT
                nsub = NT // min(TW, g.L)
                P.dma("C_yh", lambda e, t0=t0: e.dma_start(out=mix[:, 4:8, :], in_=S["YH_" + n][:, :, t0:t0 + NT].rearrange("m p t -> p m t")),
                      reads=["YH_%s:%d:%d:%d" % (n, t, cc, s_) for cc in range(4) for s_ in range(nsub)], writes=["C_mix:hy"])

            for t in range(g.ntiles):
                t0 = t * NT
                P.dma("C_xt", lambda e, t0=t0: e.dma_start(out=xt[:], in_=(I["x_" + n] if l == 0 else S["X_" + n])[:, :, t0:t0 + NT].rearrange("k p t -> p k t")),
                      reads=[xkeys(g, t)], writes=["C_xt"])
                if t == 0:
                    load_oga(0)
                    load_yh(0)
                P.op("dve", lambda e: e.tensor_tensor(out=of[:], in0=of[:], in1=ob[:], op=ALU.add), reads=["C_of", "C_ob"], writes=["C_of"])
                gp = []
                for h in range(H):
                    sq, sqk = sqb[h]
                    hl, hlk = hlb[h]
                    pt, pk = next_ps()
                    sumsq_hilo(of[:, h, :], "C_of", sq, sqk, hl, hlk, pt, pk)
                    gp.append((pt, pk))
                for h in range(H):
                    sq, sqk = sqb[h]
                    pt, pk = gp[h]
                    P.op("act", lambda e, pt=pt, sq=sq: e.activation(out=sq[:], in_=pt[:], func=AF.Ln, scale=1.0 / DK, bias=EPS), reads=[pk], writes=[sqk])
                for h in range(H):
                    sq, sqk = sqb[h]
                    P.op("act", lambda e, sq=sq: e.activation(out=sq[:], in_=sq[:], func=AF.Exp, scale=-0.5), reads=[sqk], writes=[sqk])
                for h in range(H):
                    sq, sqk = sqb[h]
                    tb, tk = tmp[h]
                    P.op("dve", lambda e, h=h, sq=sq, tb=tb: e.scalar_tensor_tensor(out=tb[:], in0=of[:, h, :], scalar=gng[:, l:l + 1], in1=sq[:], op0=ALU.mult, op1=ALU.mult),
                         reads=["C_of", "gng", sqk], writes=[tk])
                    P.op("pool", lambda e, h=h, tb=tb: e.tensor_tensor(out=mix[:, h, :], in0=tb[:], in1=gate[:, h, :], op=ALU.mult), reads=[tk, "C_gate"], writes=["C_mix:%d" % h])
                mixkeys = ["C_mix:%d" % h for h in range(H)] + ["C_mix:hy"]
                if t + 1 < g.ntiles:
                    load_oga(t + 1)
                for m in range(KC):
                    pt, pk = next_ps()
                    for k in range(KC):
                        P.op("pe", lambda e, pt=pt, k=k, m=m: e.matmul(pt[:], lhsT=w_out_sb[:, k, m * 128:(m + 1) * 128], rhs=mix[:, k, :], start=(k == 0), stop=(k == KC - 1)),
                             reads=["w_out_sb"] + mixkeys, writes=[pk])
                    P.op("dve", lambda e, pt=pt, m=m: e.scalar_tensor_tensor(out=xt[:, m, :], in0=pt[:], scalar=mods[:, l, j, 16 + m:16 + m + 1], in1=xt[:, m, :], op0=ALU.mult, op1=ALU.add),
                         reads=[pk, "mods", "C_xt"], writes=["C_xt"])
                if t + 1 < g.ntiles:
                    load_yh(t + 1)
                P.dma("C_xo", lambda e, t0=t0: e.dma_start(out=S["X_" + n][:, :, t0:t0 + NT].rearrange("k p t -> p k t"), in_=xt[:]),
                      reads=["C_xt"], writes=[xkeys(g, t)])
                rms_stats(xt, "C_xt", KC, D, rstd, "C_rstd", sqr)
                for k in range(KC):
                    tb, tk = tmp[k % 4]
                    P.op("dve", lambda e, tb=tb, k=k: e.scalar_tensor_tensor(out=tb[:], in0=xt[:, k, :], scalar=gmod[:, l, j, 1, k:k + 1], in1=rstd[:], op0=ALU.mult, op1=ALU.mult),
                         reads=["C_xt", "gmod", "C_rstd"], writes=[tk])
                    P.op("act", lambda e, tb=tb, k=k: e.activation(out=h2[:, k, :], in_=tb[:], func=AF.Identity, bias=mods[:, l, j, 24 + k:24 + k + 1], scale=1.0),
                         reads=[tk, "mods"], writes=["C_h2"])
                P.dma("C_h2o", lambda e, t0=t0: e.dma_start(out=S["H2_" + n][:, :, t0:t0 + NT].rearrange("k p t -> p k t"), in_=h2[:]),
                      reads=["C_h2"], writes=["H2_%s:%d" % (n, t)])
            P.release(m0)

        def phase_C2(l, g, w1_sb, w2_sb):
            n = g.name
            j = g.cond
            lastl = (l == depth - 1)
            m0 = P.mark()
            xt = P.tile("M_xt", [128, KC, NT], F32)
            h2 = P.tile("M_h2", [128, KC, NT], BF16)
            act = P.tile("M_act", [128, 16, NT], BF16)
            rl = [(P.tile("M_rl%d" % i, [128, NT], F32), "M_rl%d" % i) for i in range(3)]
            sqb = [(P.tile("M_sq%d" % i, [128, NT], BF16), "M_sq%d" % i) for i in range(2)]
            rstd = P.tile("M_rstd", [128, NT], F32)
            def load_h2(t):
                t0 = t * NT
                P.dma("M_h2", lambda e, t0=t0: e.dma_start(out=h2[:], in_=S["H2_" + n][:, :, t0:t0 + NT].rearrange("k p t -> p k t")), reads=["H2_%s:%d" % (n, t)], writes=["M_h2"])

            load_h2(0)
            for t in range(g.ntiles):
                t0 = t * NT
                P.dma("M_xt", lambda e, t0=t0: e.dma_start(out=xt[:], in_=S["X_" + n][:, :, t0:t0 + NT].rearrange("k p t -> p k t")), reads=[xkeys(g, t)], writes=["M_xt"])
                for half in range(2):
                    for mm in range(16):
                        col = (half * 16 + mm) * 128
                        pt, pk = next_ps()
                        for k in range(KC):
                            P.op("pe", lambda e, pt=pt, k=k, col=col: e.matmul(pt[:], lhsT=w1_sb[:, k, col:col + 128], rhs=h2[:, k, :], start=(k == 0), stop=(k == KC - 1)),
                                 reads=["w1_sb", "M_h2"], writes=[pk])
                        rb, rk = rl[mm % 3]
                        P.op("act", lambda e, pt=pt, rb=rb: e.activation(out=rb[:], in_=pt[:], func=AF.Relu), reads=[pk], writes=[rk])
                        P.op("pool", lambda e, rb=rb, mm=mm: e.tensor_tensor(out=act[:, mm, :], in0=rb[:], in1=rb[:], op=ALU.mult), reads=[rk], writes=["M_act:%d" % mm])
                    if half == 1 and t + 1 < g.ntiles:
                        load_h2(t + 1)
                    for m in range(KC):
                        pt, pk = next_ps()
                        for kk in range(16):
                            P.op("pe", lambda e, pt=pt, kk=kk, m=m, half=half: e.matmul(pt[:], lhsT=w2_sb[:, half * 16 + kk, m * 128:(m + 1) * 128], rhs=act[:, kk, :], start=(kk == 0), stop=(kk == 15)),
                                 reads=["w2_sb", "M_act:%d" % kk], writes=[pk])
                        P.op("dve", lambda e, pt=pt, m=m: e.scalar_tensor_tensor(out=xt[:, m, :], in0=pt[:], scalar=mods[:, l, j, 40 + m:40 + m + 1], in1=xt[:, m, :], op0=ALU.mult, op1=ALU.add),
                             reads=[pk, "mods", "M_xt"], writes=["M_xt"])
                if not lastl:
                    P.dma("M_xo", lambda e, t0=t0: e.dma_start(out=S["X_" + n][:, :, t0:t0 + NT].rearrange("k p t -> p k t"), in_=xt[:]), reads=["M_xt"], writes=[xkeys(g, t)])
                else:
                    rms_stats(xt, "M_xt", KC, D, rstd, "M_rstd", sqb)
                    for k in range(KC):
                        P.op("dve", lambda e, k=k: e.scalar_tensor_tensor(out=xt[:, k, :], in0=xt[:, k, :], scalar=finalg[:, k:k + 1], in1=rstd[:], op0=ALU.mult, op1=ALU.mult),
                             reads=["M_xt", "finalg", "M_rstd"], writes=["M_xt"])
                    d_ = P.dma("M_yo", lambda e, t0=t0: e.dma_start(out=O["y_" + n][:, :, t0:t0 + NT].rearrange("k p t -> p k t"), in_=xt[:]), reads=["M_xt"])
                    fin.append(d_.idx)
            P.release(m0)

        for l in range(depth):
            m0 = P.mark()
            w_in_sb = load_weight_bf16("w_in_sb", I["w_in"][l].rearrange("(k p) n -> p k n", p=128), [128, KC, IN_COLS])
            for g in groups:
                phase_A(l, g, w_in_sb)
            P.release(m0)
            for g in groups:
                phase_H(l, g)
            for g in groups:
                phase_G(l, g)
            m0 = P.mark()
            w_out_sb = load_weight_bf16("w_out_sb", I["w_out"][l].rearrange("(k p) n -> p k n", p=128), [128, KC, D])
            for g in groups:
                phase_C1(l, g, w_out_sb)
            P.release(m0)
            m0 = P.mark()
            w1_sb = load_weight_bf16("w1_sb", I["w_mlp1"][l].rearrange("(k p) n -> p k n", p=128), [128, KC, DFF])
            w2_sb = load_weight_bf16("w2_sb", I["w_mlp2"][l].rearrange("(k p) n -> p k n", p=128), [128, 32, D])
            for g in groups:
                phase_C2(l, g, w1_sb, w2_sb)
            P.release(m0)
        P.emit(final_wait_ops=fin)
    return nc


def _dft_tables(L, ntab):
    N = 2 * L
    a = np.arange(ntab, dtype=np.int64)
    ph = (a[:, None] * a[None, :]) % N
    ang = ph.astype(np.float64) * (2.0 * np.pi / N)
    c = np.cos(ang)
    s_ = np.sin(ang)
    s_[(ph % L) == 0] = 0.0
    def tl(a_, w_):
        return np.ascontiguousarray(a_.reshape(ntab // 128, 128, ntab // w_, w_).transpose(2, 1, 0, 3)).astype(ml_dtypes.bfloat16)
    return tl(c, FW), tl(s_, FW), tl(c, 128), tl(s_, 128)


def _zfeat(L):
    t = np.linspace(0.0, 1.0, L, dtype=np.float32)[:, None]
    bands = (HY_EMB - 1) // 2
    f = np.linspace(1e-4, bands - 1, bands, dtype=np.float32)[None, :]
    wpos = (np.float32(2.0 * math.pi) * np.arange(L, dtype=np.float32)[:, None] / np.float32(L)).astype(np.float32)
    z = np.concatenate([t, np.cos(f * wpos), -np.sin(f * wpos)], axis=-1).astype(np.float32)
    return np.ascontiguousarray(z.T), t[:, 0]


_PROG_CACHE = {}


def run_cfg(depth, LS, inputs, n_cores=8):
    f32 = np.float32
    groups = make_groups(LS)
    A = {k: np.asarray(v) for k, v in inputs.items()}
    shared = {}
    for g in groups:
        c, s_, cF, sF = _dft_tables(g.L, g.NTAB)
        shared["ctab_" + g.name] = c
        shared["stab_" + g.name] = s_
        shared["ctabF_" + g.name] = cF
        shared["stabF_" + g.name] = sF
        zf, t = _zfeat(g.L)
        shared["zf_" + g.name] = zf
        shared["negt_" + g.name] = np.ascontiguousarray((-t).reshape(g.nTB, 128).T).astype(f32)
        w = np.zeros(g.NF, f32)
        w[0:g.L + 1] = 2.0 / (2 * g.L)
        w[0] = 1.0 / (2 * g.L)
        w[g.L] = 1.0 / (2 * g.L)
        shared["wcol_" + g.name] = np.ascontiguousarray(w.reshape(g.NFB, 128).T).astype(f32)
    shared["w_ada"] = A["w_ada"][:depth].astype(f32)
    shared["b_ada"] = np.ascontiguousarray(A["b_ada"][:depth].reshape(depth, 48, 128).transpose(2, 0, 1)).astype(f32)
    ng = np.stack([A["norm1_g"][:depth], A["norm2_g"][:depth]], axis=1)
    shared["norm_g"] = np.ascontiguousarray(ng.reshape(depth, 2, KC, 128).transpose(3, 0, 1, 2)).astype(f32)
    shared["final_g"] = np.ascontiguousarray(A["final_g"].reshape(KC, 128).T).astype(f32)
    shared["w_in"] = A["w_in"][:depth].astype(f32)
    shared["gcw"] = np.ascontiguousarray(A["gdn_conv_w"][:depth].reshape(depth, 5, 12, 128).transpose(3, 0, 2, 1)).astype(f32)
    shared["hcw"] = np.ascontiguousarray(A["hy_conv_w"][:depth].reshape(depth, 3, 12, 128).transpose(3, 0, 2, 1)).astype(f32)
    ap_ = np.stack([A["gdn_a_log"][:depth].reshape(depth, 8), A["gdn_dt_bias"][:depth].reshape(depth, 8)], axis=-1)
    shared["a_par"] = np.ascontiguousarray(ap_.transpose(1, 0, 2)).astype(f32)
    shared["gng"] = np.ascontiguousarray(A["gdn_norm_g"][:depth].T).astype(f32)
    shared["hw1"] = A["hy_w1"][:depth].astype(f32)
    hv = np.stack([A["hy_b1"][:depth], A["hy_freq"][:depth], A["hy_b2"][:depth], np.zeros_like(A["hy_b1"][:depth])], axis=-1)
    shared["hvec"] = np.ascontiguousarray(hv.transpose(1, 0, 2)).astype(f32)
    shared["hw2"] = A["hy_w2"][:depth].astype(f32)
    shared["hw3e"] = np.ascontiguousarray(np.concatenate([A["hy_w3"][:depth], A["hy_b3"][:depth][:, None, :]], axis=1)).astype(f32)
    shared["hskip"] = np.ascontiguousarray(A["hy_skip"][:depth].reshape(depth, 4, 128).transpose(2, 0, 1)).astype(f32)
    shared["w_out"] = A["w_out"][:depth].astype(f32)
    shared["w_mlp1"] = A["w_mlp1"][:depth].astype(f32)
    shared["w_mlp2"] = A["w_mlp2"][:depth].astype(f32)
    ii = np.arange(64)
    mk = np.zeros((64, 5, 64), f32)
    mk[:, 0, :] = (ii[None, :] >= ii[:, None])
    mk[:, 1, :] = (ii[None, :] <= ii[:, None])
    mk[:, 2, :] = -1.0 * (ii[None, :] > ii[:, None])
    mk[:, 3, :] = -1.0 * (ii[None, :] < ii[:, None])
    mk[:, 4, :] = (ii[None, :] == ii[:, None])
    shared["masks"] = mk
    mk2 = np.zeros((128, 4, 128), f32)
    for bb in range(2):
        mk2[bb * 64:(bb + 1) * 64, :, bb * 64:(bb + 1) * 64] = mk[:, 0:4, :]
    shared["masks2"] = mk2
    sel = np.zeros((8, 8, 128), f32)
    for r in range(8):
        sel[r, r, :] = 1.0
    shared["sel"] = sel
    sm = np.ones((8, NT), f32)
    sm[:, ::CH] = 0.0
    shared["scanmask"] = sm
    min_decay = math.log(1e-2) / 1.5
    max_decay = math.log(1e-2) / 0.3
    shared["delta"] = np.abs(np.linspace(min_decay, max_decay, 512, dtype=f32)).astype(f32)

    n_s = A["x_sample"].shape[0]
    in_maps = []
    for i in range(n_cores):
        b = i % n_s
        m = dict(shared)
        xs = A["x_sample"][b]
        m["x_s"] = np.ascontiguousarray(xs.T.reshape(KC, 128, LS)).astype(f32)
        xp = A["x_prompt"][2 * i:2 * i + 2].reshape(512, D)
        m["x_p"] = np.ascontiguousarray(xp.T.reshape(KC, 128, 512)).astype(f32)
        m["s0"] = np.ascontiguousarray(A["state_gdn"][b][:depth]).astype(f32)
        cd = np.stack([A["c"][b], A["c_ctx"]], axis=-1)
        m["cond"] = np.ascontiguousarray(cd.reshape(KC, 128, 2).transpose(1, 0, 2)).astype(f32)
        in_maps.append(m)

    key = (depth, LS)
    if key not in _PROG_CACHE:
        _PROG_CACHE[key] = build_program(depth, LS)
    nc = _PROG_CACHE[key]
    res = run_bass_kernel_spmd(nc, in_maps, core_ids=list(range(n_cores)))
    R = res.results
    _LAST["R"] = R
    y_sample = np.stack([R[b]["y_s"].reshape(D, LS).T for b in range(n_s)], axis=0).astype(f32)
    y_prompt = np.concatenate([R[i]["y_p"].reshape(D, 512).T.reshape(2, 256, D) for i in range(n_cores)], axis=0).astype(f32)
    new_state = np.concatenate([R[i]["nstate"] for i in range(n_cores)], axis=0).astype(f32)
    return (y_prompt, y_sample, new_state)


def kernel(**inputs):
    return run_cfg(4, 4096, inputs)
```

```python
import contextlib
import math
import numpy as np
import ml_dtypes
import concourse.bass as bass
import concourse.mybir as mybir
from concourse.bass_utils import run_bass_kernel_spmd

F32 = mybir.dt.float32
BF16 = mybir.dt.bfloat16
I32 = mybir.dt.int32
AF = mybir.ActivationFunctionType
ALU = mybir.AluOpType
AX = mybir.AxisListType

ENGS = ("pe", "act", "dve", "pool", "sp")
STORE_ENG = "sp"


def _is_store(semname):
    return semname.endswith("_o") or semname in ("C_xo", "C_h2o", "M_xo", "M_yo") or semname.startswith("G_oT")
SEM_EPOCH = 30000
SBUF_BASE = 16640
SBUF_BYTES = 229000


def _dsize(dt):
    return 2 if dt == BF16 else 4


class Op:
    __slots__ = ("eng", "fn", "deps", "is_dma", "dsem", "dval", "signal", "sem", "val", "idx", "_rw")


class Prog:
    def __init__(self, nc, stack):
        self.nc = nc
        self.stack = stack
        self.ops = []
        self.key_w = {}
        self.key_r = {}
        self.dma_sems = {}
        self.bump = SBUF_BASE
        self.tiles = []
        self.tile_keys = {}
        self.tile_pending = {}
        self.uid = 0
        self.ps_rr = 0

    def tile(self, name, shape, dtype):
        nbytes = int(np.prod(shape[1:])) * _dsize(dtype)
        nbytes = (nbytes + 63) // 64 * 64
        off = self.bump
        assert off + nbytes <= SBUF_BYTES, ("SBUF overflow", name, off, nbytes)
        self.bump = off + nbytes
        self.uid += 1
        t = self.nc.alloc_sbuf_tensor_at("%s_%d" % (name, self.uid), list(shape), dtype, offset=off)
        pend = set()
        for (s, e, oname) in self.tiles:
            if s < off + nbytes and off < e:
                pend |= self.tile_pending.get(oname, set())
                for k in self.tile_keys.get(oname, ()):
                    w = self.key_w.get(k)
                    if w is not None:
                        pend.add(w)
                    pend.update(self.key_r.get(k, {}).values())
        self.tiles = [(s, e, n) for (s, e, n) in self.tiles if not (s >= off and e <= off + nbytes)] + [(off, off + nbytes, name)]
        self.tile_keys.setdefault(name, set())
        self.tile_pending[name] = self._compress(pend)
        return t

    def mark(self):
        return self.bump

    def release(self, m):
        self.bump = m

    def new_sem(self, name):
        return self.stack.enter_context(self.nc.semaphore(name))

    def _cls(self, d):
        o = self.ops[d]
        return ("d", id(o.dsem)) if o.is_dma else o.eng

    def _compress(self, deps):
        best = {}
        for d in deps:
            c = self._cls(d)
            b = best.get(c)
            if b is None or b < d:
                best[c] = d
        return set(best.values())

    def _deps(self, reads, writes):
        deps = set()
        for k in reads:
            base = k.split(":")[0]
            if base in self.tile_keys:
                self.tile_keys[base].add(k)
                deps |= self.tile_pending[base]
            w = self.key_w.get(k)
            if w is not None:
                deps.add(w)
        for k in writes:
            base = k.split(":")[0]
            if base in self.tile_keys:
                self.tile_keys[base].add(k)
                deps |= self.tile_pending[base]
            w = self.key_w.get(k)
            if w is not None:
                deps.add(w)
            deps.update(self.key_r.get(k, {}).values())
        return self._compress(deps)

    def _op(self, eng, fn, reads=(), writes=()):
        o = Op()
        o.eng = eng
        o.fn = fn
        o.is_dma = False
        o.dsem = None
        o.signal = False
        o.idx = len(self.ops)
        o.deps = self._deps(reads, writes)
        self.ops.append(o)
        o._rw = (reads, writes)
        return o

    def op(self, eng, fn, reads=(), writes=()):
        o = self._op(eng, fn, reads, writes)
        self._commit(o)
        return o

    def _commit(self, o):
        reads, writes = o._rw
        c = self._cls(o.idx)
        for k in reads:
            self.key_r.setdefault(k, {})[c] = o.idx
        for k in writes:
            self.key_w[k] = o.idx
            self.key_r[k] = {}
        o._rw = None

    def dma(self, semname, fn, reads=(), writes=(), eng="sp"):
        if eng == "sp" and _is_store(semname):
            eng = STORE_ENG
        o = self._op(eng, fn, reads, writes)
        o.is_dma = True
        if semname not in self.dma_sems:
            self.dma_sems[semname] = [self.new_sem("d%d" % len(self.dma_sems)), 0]
        ent = self.dma_sems[semname]
        ent[1] += 16
        o.dsem = ent[0]
        o.dval = ent[1]
        self._commit(o)
        return o

    def emit(self, final_wait_ops=()):
        nc = self.nc
        ops = self.ops

        def skip(do, o):
            return (not do.is_dma) and (not o.is_dma) and do.eng == o.eng and do.eng == "pe"

        for o in ops:
            for d in o.deps:
                do = ops[d]
                if do.is_dma or skip(do, o):
                    continue
                do.signal = True
        cur = {}
        for o in ops:
            if o.is_dma:
                o.sem, o.val = o.dsem, o.dval
                continue
            if not o.signal:
                continue
            ent = cur.get(o.eng)
            if ent is None or ent[1] >= SEM_EPOCH:
                ent = [self.new_sem("e%s%d" % (o.eng, o.idx)), 0]
                cur[o.eng] = ent
            ent[1] += 1
            o.sem, o.val = ent[0], ent[1]
        per_eng = {e: [o for o in ops if o.eng == e] for e in ENGS}
        final_ops = [ops[i] for i in final_wait_ops]

        def run(ename, e):
            waited = {}
            for o in per_eng[ename]:
                need = {}
                for d in o.deps:
                    do = ops[d]
                    if skip(do, o):
                        continue
                    sid = id(do.sem)
                    if sid not in need or need[sid][1] < do.val:
                        need[sid] = (do.sem, do.val)
                for sid, (s, v) in need.items():
                    if waited.get(sid, 0) >= v:
                        continue
                    e.wait_ge(s, v)
                    waited[sid] = v
                ins = o.fn(e)
                if o.is_dma:
                    ins.then_inc(o.sem, 16)
                elif o.signal:
                    ins.then_inc(o.sem, 1)
            if ename == "sp":
                for o in final_ops:
                    e.wait_ge(o.sem, o.val)

        with nc.Block() as block:
            @block.tensor
            def _(e):
                run("pe", e)

            @block.scalar
            def _(e):
                run("act", e)

            @block.vector
            def _(e):
                run("dve", e)

            @block.gpsimd
            def _(e):
                run("pool", e)

            @block.sync
            def _(e):
                run("sp", e)


D = 1024
KC = 8
NT = 512
H = 4
DK = 128
CH = 64
IN_COLS = 3600
OFF_A = 2048
OFF_B = 2056
OFF_HY = 2064
DFF = 4096
EPS = 1e-6
HY_EMB = 33
HY_FH = 64
NEWTON_STEPS = 1
FW = 256
TW = 256


class Group:
    def __init__(self, name, T, L, seg, has_s0, wstate, cond):
        self.name = name
        self.T = T
        self.L = L
        self.nseq = T // L
        self.seg = seg
        self.has_s0 = has_s0
        self.wstate = wstate
        self.cond = cond
        self.ntiles = T // NT
        self.NF = (L + 1 + FW - 1) // FW * FW
        self.NFB = self.NF // 128
        self.NTAB = max(self.NF, L)
        self.nTB = L // 128


def make_groups(LS):
    return [Group("s", LS, LS, 64, True, False, 0), Group("p", 512, 256, 256, False, True, 1)]


DEBUG_SCRATCH = False
_LAST = {}


def build_program(depth, LS):
    nc = bass.Bass("TRN2", target_bir_lowering=False)
    groups = make_groups(LS)

    def din(name, shape, dt=F32):
        return nc.dram_tensor(name, list(shape), dt, kind="ExternalInput").ap()

    def dout(name, shape, dt=F32):
        return nc.dram_tensor(name, list(shape), dt, kind="ExternalOutput").ap()

    def dscr(name, shape, dt=F32):
        return nc.dram_tensor(name, list(shape), dt, kind=("ExternalOutput" if DEBUG_SCRATCH else "Internal")).ap()

    I = {}
    for g in groups:
        I["x_" + g.name] = din("x_" + g.name, [KC, 128, g.T])
        I["ctab_" + g.name] = din("ctab_" + g.name, [g.NTAB // FW, 128, g.NTAB // 128, FW], BF16)
        I["stab_" + g.name] = din("stab_" + g.name, [g.NTAB // FW, 128, g.NTAB // 128, FW], BF16)
        I["zf_" + g.name] = din("zf_" + g.name, [HY_EMB, g.L])
        I["negt_" + g.name] = din("negt_" + g.name, [128, g.nTB])
        I["wcol_" + g.name] = din("wcol_" + g.name, [128, g.NFB])
        I["ctabF_" + g.name] = din("ctabF_" + g.name, [g.NTAB // 128, 128, g.NTAB // 128, 128], BF16)
        I["stabF_" + g.name] = din("stabF_" + g.name, [g.NTAB // 128, 128, g.NTAB // 128, 128], BF16)
    I["s0"] = din("s0", [depth, 2, H, DK, DK])
    I["cond"] = din("cond", [128, KC, 2])
    I["w_ada"] = din("w_ada", [depth, D, 6 * D])
    I["b_ada"] = din("b_ada", [128, depth, 48])
    I["norm_g"] = din("norm_g", [128, depth, 2, KC])
    I["final_g"] = din("final_g", [128, KC])
    I["w_in"] = din("w_in", [depth, D, IN_COLS])
    I["gcw"] = din("gcw", [128, depth, 12, 5])
    I["hcw"] = din("hcw", [128, depth, 12, 3])
    I["a_par"] = din("a_par", [8, depth, 2])
    I["gng"] = din("gng", [128, depth])
    I["hw1"] = din("hw1", [depth, HY_EMB, HY_FH])
    I["hvec"] = din("hvec", [HY_FH, depth, 4])
    I["hw2"] = din("hw2", [depth, HY_FH, HY_FH])
    I["hw3e"] = din("hw3e", [depth, HY_FH + 1, 2 * 512])
    I["hskip"] = din("hskip", [128, depth, 4])
    I["w_out"] = din("w_out", [depth, D, D])
    I["w_mlp1"] = din("w_mlp1", [depth, D, DFF])
    I["w_mlp2"] = din("w_mlp2", [depth, DFF, D])
    I["masks"] = din("masks", [64, 5, 64])
    I["masks2"] = din("masks2", [128, 4, 128])
    I["sel"] = din("sel", [8, 8, 128])
    I["scanmask"] = din("scanmask", [8, NT])
    I["delta"] = din("delta", [512])

    O = {}
    for g in groups:
        O["y_" + g.name] = dout("y_" + g.name, [KC, 128, g.T])
    O["nstate"] = dout("nstate", [2, depth, 2, H, DK, DK])

    S = {}
    for g in groups:
        n = g.name
        S["X_" + n] = dscr("X_" + n, [KC, 128, g.T])
        S["QKV_" + n] = dscr("QKV_" + n, [12, 128, g.T])
        S["GATE_" + n] = dscr("GATE_" + n, [4, 128, g.T])
        S["GB_" + n] = dscr("GB_" + n, [16, g.T])
        S["X0_" + n] = dscr("X0_" + n, [4, 128, g.T])
        S["ZZ_" + n] = dscr("ZZ_" + n, [4, 128, g.T])
        S["ZZT_" + n] = dscr("ZZT_" + n, [g.T, 512], BF16)
        S["O_" + n] = dscr("O_" + n, [2, 4, 128, g.T])
        S["YH_" + n] = dscr("YH_" + n, [4, 128, g.T], BF16)
        S["H2_" + n] = dscr("H2_" + n, [KC, 128, g.T], BF16)
        S["SPEC_" + n] = dscr("SPEC_" + n, [2, g.NFB, 128, 512])

    with contextlib.ExitStack() as st:
        P = Prog(nc, st)
        ps_t = [st.enter_context(nc.psum_tensor("ps%d" % i, [128, 512], F32)) for i in range(8)]

        ps_held = set()

        def next_ps(hold=False):
            while True:
                i = P.ps_rr
                P.ps_rr = (i + 1) % 8
                if i not in ps_held:
                    break
            if hold:
                ps_held.add(i)
            return ps_t[i], "ps%d" % i

        def ps_release(key):
            ps_held.discard(int(key[2:]))

        fin = []

        ident = P.tile("ident", [128, 128], F32)
        identb = P.tile("identb", [128, 128], BF16)
        ones = P.tile("ones", [128, 128], F32)
        onesb = P.tile("onesb", [128, 128], BF16)
        selb = P.tile("selb", [8, 8, 128], BF16)
        masks = P.tile("masks", [64, 5, 64], F32)
        masks2 = P.tile("masks2", [128, 4, 128], F32)
        sel = P.tile("sel", [8, 8, 128], F32)
        scanmask = P.tile("scanmask", [8, NT], F32)
        mods = P.tile("mods", [128, depth, 2, 48], F32)
        gmod = P.tile("gmod", [128, depth, 2, 2, KC], F32)
        normg = P.tile("normg", [128, depth, 2, KC], F32)
        finalg = P.tile("finalg", [128, KC], F32)
        gcw = P.tile("gcw", [128, depth, 12, 5], F32)
        hcw = P.tile("hcw", [128, depth, 12, 3], F32)
        apar = P.tile("apar", [8, depth, 2], F32)
        nega = P.tile("nega", [8, depth], F32)
        gng = P.tile("gng", [128, depth], F32)
        hskip = P.tile("hskip", [128, depth, 4], F32)
        hvec = P.tile("hvec", [HY_FH, depth, 4], F32)
        hf2p = P.tile("hf2p", [HY_FH, depth], F32)
        deltab = P.tile("deltab", [128, 512], F32)
        condt = P.tile("condt", [128, KC, 2], F32)
        bada = P.tile("bada", [128, depth, 48], F32)

        def ld(semname, t_ap, src, key, eng="sp"):
            return P.dma(semname, lambda e: e.dma_start(out=t_ap, in_=src), writes=[key], eng=eng)

        P.op("pool", lambda e: e.memset(ident[:], 0.0), writes=["ident"])
        P.op("pool", lambda e: e.affine_select(out=ident[:], in_=ident[:], pattern=[[-1, 128]], compare_op=ALU.not_equal,
                                               fill=1.0, base=0, channel_multiplier=1), reads=["ident"], writes=["ident"])
        P.op("dve", lambda e: e.tensor_copy(out=identb[:], in_=ident[:]), reads=["ident"], writes=["identb"])
        P.op("pool", lambda e: e.memset(ones[:], 1.0), writes=["ones"])
        P.op("pool", lambda e: e.memset(onesb[:], 1.0), writes=["onesb"])
        ld("c_masks", masks[:], I["masks"], "masks")
        ld("c_masks2", masks2[:], I["masks2"], "masks2")
        ld("c_sel", sel[:], I["sel"], "sel")
        P.op("dve", lambda e: e.tensor_copy(out=selb[:], in_=sel[:]), reads=["sel"], writes=["selb"])
        ld("c_scanmask", scanmask[:], I["scanmask"], "scanmask")
        ld("c_normg", normg[:], I["norm_g"], "normg")
        ld("c_finalg", finalg[:], I["final_g"], "finalg")
        ld("c_gcw", gcw[:], I["gcw"], "gcw")
        ld("c_hcw", hcw[:], I["hcw"], "hcw")
        ld("c_apar", apar[:], I["a_par"], "apar")
        ld("c_gng", gng[:], I["gng"], "gng")
        ld("c_hskip", hskip[:], I["hskip"], "hskip")
        ld("c_hvec", hvec[:], I["hvec"], "hvec")
        ld("c_delta", deltab[:], I["delta"].partition_broadcast(128), "deltab")
        ld("c_cond", condt[:], I["cond"], "condt")
        ld("c_bada", bada[:], I["b_ada"], "bada")
        P.op("act", lambda e: e.activation(out=nega[:], in_=apar[:, :, 0], func=AF.Exp), reads=["apar"], writes=["nega"])
        P.op("dve", lambda e: e.tensor_scalar(out=nega[:], in0=nega[:], scalar1=-1.0, scalar2=None, op0=ALU.mult), reads=["nega"], writes=["nega"])
        P.op("dve", lambda e: e.tensor_scalar(out=hf2p[:], in0=hvec[:, :, 1], scalar1=1.0 / (2 * math.pi), scalar2=None, op0=ALU.mult),
             reads=["hvec"], writes=["hf2p"])

        base_mark = P.mark()

        def prologue():
            m0 = P.mark()
            scond = P.tile("scond", [128, KC, 2], F32)
            P.op("act", lambda e: e.activation(out=scond[:], in_=condt[:], func=AF.Silu), reads=["condt"], writes=["scond"])
            wa = [P.tile("wa%d" % i, [128, KC, 512], F32) for i in range(2)]
            n = 0
            for l in range(depth):
                for cg in range(12):
                    wt = wa[n % 2]
                    wk = "wa%d" % (n % 2)
                    n += 1
                    src = I["w_ada"][l, :, cg * 512:(cg + 1) * 512].rearrange("(k p) n -> p k n", p=128)
                    ld(wk, wt[:], src, wk)
                    for mi in range(4):
                        chunk = cg * 4 + mi
                        pt, pk = next_ps()
                        for k in range(KC):
                            P.op("pe", lambda e, pt=pt, wt=wt, k=k, mi=mi: e.matmul(pt[:, 0:2], lhsT=wt[:, k, mi * 128:(mi + 1) * 128],
                                                                                   rhs=scond[:, k, :], start=(k == 0), stop=(k == KC - 1)),
                                 reads=[wk, "scond"], writes=[pk])
                        P.op("dve", lambda e, pt=pt, l=l, chunk=chunk: e.tensor_tensor(
                            out=mods[:, l, :, chunk], in0=pt[:, 0:2], in1=bada[:, l, chunk:chunk + 1].to_broadcast([128, 2]), op=ALU.add),
                            reads=[pk, "bada"], writes=["mods"])
            for l in range(depth):
                for j in range(2):
                    for w, sc0 in ((0, 8), (1, 32)):
                        P.op("dve", lambda e, l=l, j=j, w=w, sc0=sc0: e.scalar_tensor_tensor(
                            out=gmod[:, l, j, w, :], in0=mods[:, l, j, sc0:sc0 + KC], scalar=1.0, in1=normg[:, l, w, :],
                            op0=ALU.add, op1=ALU.mult), reads=["mods", "normg"], writes=["gmod"])
            P.release(m0)

        prologue()

        def load_weight_bf16(name, src3, shape):
            t = P.tile(name, shape, BF16)
            for a in range(shape[1]):
                P.dma(name, lambda e, a=a: e.dma_start(out=t[:, a, :], in_=src3[:, a, :]), writes=[name], eng="pool")
            return t

        def rms_stats(src_tile, src_key, nchunks, dim, rstd, rstd_key, sqbufs):
            pt, pk = next_ps()
            for k in range(nchunks):
                sq, sqk = sqbufs[k % len(sqbufs)]
                P.op("act", lambda e, sq=sq, k=k: e.activation(out=sq[:], in_=src_tile[:, k, :], func=AF.Square), reads=[src_key], writes=[sqk])
                P.op("pe", lambda e, sq=sq, k=k, pt=pt: e.matmul(pt[:], lhsT=onesb[:], rhs=sq[:], start=(k == 0), stop=(k == nchunks - 1)),
                     reads=[sqk, "onesb"], writes=[pk])
            P.op("act", lambda e, pt=pt: e.activation(out=rstd[:], in_=pt[:], func=AF.Ln, scale=1.0 / dim, bias=EPS), reads=[pk], writes=[rstd_key])
            P.op("act", lambda e: e.activation(out=rstd[:], in_=rstd[:], func=AF.Exp, scale=-0.5), reads=[rstd_key], writes=[rstd_key])

        def sumsq_hilo(src_ap, src_key, sqf, sqfk, hl, hlk, pt, pk):
            P.op("act", lambda e: e.activation(out=sqf[:], in_=src_ap, func=AF.Square), reads=[src_key], writes=[sqfk])
            P.op("act", lambda e: e.activation(out=hl[:, 0, :], in_=src_ap, func=AF.Square), reads=[src_key], writes=[hlk + ":0"])
            P.op("dve", lambda e: e.tensor_tensor(out=hl[:, 1, :], in0=sqf[:], in1=hl[:, 0, :], op=ALU.subtract), reads=[sqfk, hlk + ":0"], writes=[hlk + ":1"])
            P.op("pe", lambda e: e.matmul(pt[:], lhsT=onesb[:], rhs=hl[:, 0, :], start=True, stop=False), reads=[hlk + ":0", "onesb"], writes=[pk])
            P.op("pe", lambda e: e.matmul(pt[:], lhsT=onesb[:], rhs=hl[:, 1, :], start=False, stop=True), reads=[hlk + ":1", "onesb"], writes=[pk])

        def xkeys(g, t):
            return "X_%s:%d" % (g.name, t)

        def phase_A(l, g, w_in_sb):
            n = g.name
            j = g.cond
            nseg = NT // g.seg
            m0 = P.mark()
            xt = P.tile("A_xt", [128, KC, NT], F32)
            sqb = [(P.tile("A_sq%d" % i, [128, NT], F32), "A_sq%d" % i) for i in range(4)]
            sqr = [(P.tile("A_sqr%d" % i, [128, NT], BF16), "A_sqr%d" % i) for i in range(4)]
            hlb = [(P.tile("A_hl%d" % i, [128, 2, NT], BF16), "A_hl%d" % i) for i in range(4)]
            rstd = P.tile("A_rstd", [128, NT], F32)
            tmp = [(P.tile("A_tmp%d" % i, [128, NT], F32), "A_tmp%d" % i) for i in range(4)]
            hn = P.tile("A_hn", [128, KC, NT], BF16)
            pj = [(P.tile("A_pj%d" % i, [128, NT], F32), "A_pj%d" % i) for i in range(3)]
            qkv = P.tile("A_qkv", [128, 12, NT], F32)
            qkvb = P.tile("A_qkvb", [128, 8, NT], F32)
            gate = P.tile("A_gate", [128, 4, NT], F32)
            gb = P.tile("A_gb", [8, 2, NT], F32)
            zz = P.tile("A_zz", [128, 4, NT], F32)
            zzb = P.tile("A_zzb", [128, 4, NT], BF16)
            zzt = P.tile("A_zzt", [128, 4, 512], BF16)
            xsrc = I["x_" + n] if l == 0 else S["X_" + n]
            def load_xt(t):
                t0 = t * NT
                P.dma("A_xt", lambda e, t0=t0: e.dma_start(out=xt[:], in_=xsrc[:, :, t0:t0 + NT].rearrange("k p t -> p k t")),
                      reads=[xkeys(g, t)], writes=["A_xt"])

            load_xt(0)
            for t in range(g.ntiles):
                t0 = t * NT
                rms_stats(xt, "A_xt", KC, D, rstd, "A_rstd", sqr)
                for k in range(KC):
                    tb, tk = tmp[k % 4]
                    P.op("dve", lambda e, tb=tb, k=k: e.scalar_tensor_tensor(out=tb[:], in0=xt[:, k, :], scalar=gmod[:, l, j, 0, k:k + 1],
                                                                           in1=rstd[:], op0=ALU.mult, op1=ALU.mult),
                         reads=["A_xt", "gmod", "A_rstd"], writes=[tk])
                    P.op("act", lambda e, tb=tb, k=k: e.activation(out=hn[:, k, :], in_=tb[:], func=AF.Identity,
                                                                    bias=mods[:, l, j, 0 + k:0 + k + 1], scale=1.0),
                         reads=[tk, "mods"], writes=["A_hn:%d" % k])
                hn_keys = ["A_hn:%d" % k for k in range(KC)]
                if t + 1 < g.ntiles:
                    load_xt(t + 1)

                def proj(c0, ncols, pt, pk):
                    for k in range(KC):
                        P.op("pe", lambda e, k=k: e.matmul(pt[0:ncols, :], lhsT=w_in_sb[:, k, c0:c0 + ncols], rhs=hn[:, k, :],
                                                           start=(k == 0), stop=(k == KC - 1)),
                             reads=["w_in_sb", "A_hn:%d" % k], writes=[pk])

                def conv(dst, dkey, src, skey, wts, width, pt, pk):
                    pad = width // 2
                    d3 = dst.rearrange("p (s c) -> p s c", c=g.seg)
                    s3 = src.rearrange("p (s c) -> p s c", c=g.seg)
                    P.op("act", lambda e: e.activation(out=dst, in_=pt[:], func=AF.Copy, scale=wts[:, pad:pad + 1]),
                         reads=[pk, "gcw", "hcw"], writes=[dkey])
                    for jj in range(width):
                        o = jj - pad
                        if o == 0:
                            continue
                        lo_d, hi_d = max(0, -o), g.seg - max(0, o)
                        lo_s, hi_s = max(0, o), g.seg - max(0, -o)
                        P.op("dve", lambda e, jj=jj, lo_d=lo_d, hi_d=hi_d, lo_s=lo_s, hi_s=hi_s: e.scalar_tensor_tensor(
                            out=d3[:, :, lo_d:hi_d], in0=s3[:, :, lo_s:hi_s], scalar=wts[:, jj:jj + 1], in1=d3[:, :, lo_d:hi_d],
                            op0=ALU.mult, op1=ALU.add), reads=[skey, dkey, "gcw", "hcw"], writes=[dkey])

                for m in range(12):
                    pt, pk = next_ps()
                    proj(m * 128, 128, pt, pk)
                    pb, pbk = pj[m % 3]
                    P.op("act", lambda e, pt=pt, pb=pb: e.copy(out=pb[:], in_=pt[:]), reads=[pk], writes=[pbk])
                    conv(qkv[:, m, :], "A_qkv:%d" % m, pb[:], pbk, gcw[:, l, m, :], 5, pt, pk)
                gpts = []
                for m in range(4):
                    pt, pk = next_ps()
                    proj(1536 + m * 128, 128, pt, pk)
                    gpts.append((pt, pk))
                for m in range(12):
                    P.op("act", lambda e, m=m: e.activation(out=qkv[:, m, :], in_=qkv[:, m, :], func=AF.Silu),
                         reads=["A_qkv:%d" % m], writes=["A_qkv:%d" % m])
                for m in range(4):
                    pt, pk = gpts[m]
                    P.op("act", lambda e, m=m, pt=pt: e.activation(out=gate[:, m, :], in_=pt[:], func=AF.Silu), reads=[pk], writes=["A_gate"])
                P.dma("A_gate_o", lambda e, t0=t0: e.dma_start(out=S["GATE_" + n][:, :, t0:t0 + NT].rearrange("m p t -> p m t"), in_=gate[:]),
                      reads=["A_gate"], writes=["GATE_%s:%d" % (n, t)])
                pt, pk = next_ps()
                proj(OFF_B, 8, pt, pk)
                P.op("act", lambda e, pt=pt: e.activation(out=gb[:, 1, :], in_=pt[0:8, :], func=AF.Sigmoid), reads=[pk], writes=["A_gb:1"])
                pt, pk = next_ps()
                proj(OFF_A, 8, pt, pk)
                P.op("act", lambda e, pt=pt: e.activation(out=gb[:, 0, :], in_=pt[0:8, :], func=AF.Exp, bias=apar[:, l, 1:2], scale=1.0),
                     reads=[pk, "apar"], writes=["A_gb:0"])
                P.op("act", lambda e: e.activation(out=gb[:, 0, :], in_=gb[:, 0, :], func=AF.Ln, bias=1.0, scale=1.0), reads=["A_gb:0"], writes=["A_gb:0"])
                P.op("dve", lambda e: e.tensor_scalar(out=gb[:, 0, :], in0=gb[:, 0, :], scalar1=nega[:, l:l + 1], scalar2=None, op0=ALU.mult),
                     reads=["A_gb:0", "nega"], writes=["A_gb:0"])
                P.dma("A_gb_o", lambda e, t0=t0: e.dma_start(out=S["GB_" + n][:, t0:t0 + NT].rearrange("(a r) t -> r a t", a=2), in_=gb[:]),
                      reads=["A_gb:0", "A_gb:1"], writes=["GB_%s:%d" % (n, t)])
                for grp in range(2):
                    rn = {}
                    ms_ = range(grp * 4, grp * 4 + 4)
                    for m in ms_:
                        sq, sqk = sqb[m % 4]
                        hl, hlk = hlb[m % 4]
                        pt, pk = next_ps()
                        sumsq_hilo(qkv[:, m, :], "A_qkv:%d" % m, sq, sqk, hl, hlk, pt, pk)
                        rn[m] = (pt, pk)
                    for m in ms_:
                        sq, sqk = sqb[m % 4]
                        pt, pk = rn[m]
                        P.op("act", lambda e, pt=pt, sq=sq: e.activation(out=sq[:], in_=pt[:], func=AF.Ln, scale=1.0, bias=EPS), reads=[pk], writes=[sqk])
                    for m in ms_:
                        sq, sqk = sqb[m % 4]
                        P.op("act", lambda e, sq=sq: e.activation(out=sq[:], in_=sq[:], func=AF.Exp, scale=-0.5), reads=[sqk], writes=[sqk])
                    for m in ms_:
                        sq, sqk = sqb[m % 4]
                        sc = DK ** -0.5 if m < 4 else 1.0
                        P.op("dve", lambda e, m=m, sq=sq, sc=sc: e.scalar_tensor_tensor(out=qkvb[:, m, :], in0=qkv[:, m, :], scalar=sc, in1=sq[:],
                                                                                     op0=ALU.mult, op1=ALU.mult),
                             reads=["A_qkv:%d" % m, sqk], writes=["A_qkvb:%d" % m])
                P.dma("A_qkv_o", lambda e, t0=t0: e.dma_start(out=S["QKV_" + n][0:8, :, t0:t0 + NT].rearrange("m p t -> p m t"), in_=qkvb[:]),
                      reads=["A_qkvb:%d" % m for m in range(8)], writes=["QKV_%s:%d" % (n, t)])
                P.dma("A_v_o", lambda e, t0=t0: e.dma_start(out=S["QKV_" + n][8:12, :, t0:t0 + NT].rearrange("m p t -> p m t"), in_=qkv[:, 8:12, :]),
                      reads=["A_qkv:%d" % m for m in range(8, 12)], writes=["QKVv_%s:%d" % (n, t)])
                for m in range(12):
                    pt, pk = next_ps()
                    proj(OFF_HY + m * 128, 128, pt, pk)
                    pb, pbk = pj[m % 3]
                    P.op("act", lambda e, pt=pt, pb=pb: e.copy(out=pb[:], in_=pt[:]), reads=[pk], writes=[pbk])
                    conv(qkv[:, m, :], "A_qkv:%d" % m, pb[:], pbk, hcw[:, l, m, :], 3, pt, pk)
                P.dma("A_x0_o", lambda e, t0=t0: e.dma_start(out=S["X0_" + n][:, :, t0:t0 + NT].rearrange("m p t -> p m t"), in_=qkv[:, 0:4, :]),
                      reads=["A_qkv:%d" % m for m in range(4)], writes=["X0_%s:%d" % (n, t)])
                for c in range(4):
                    P.op("dve", lambda e, c=c: e.tensor_tensor(out=zz[:, c, :], in0=qkv[:, 4 + c, :], in1=qkv[:, 8 + c, :], op=ALU.mult),
                         reads=["A_qkv:%d" % (4 + c), "A_qkv:%d" % (8 + c)], writes=["A_zz:%d" % c])
                    P.op("pool", lambda e, c=c: e.tensor_copy(out=zzb[:, c, :], in_=zz[:, c, :]), reads=["A_zz:%d" % c], writes=["A_zzb:%d" % c])
                P.dma("A_zz_o", lambda e, t0=t0: e.dma_start(out=S["ZZ_" + n][:, :, t0:t0 + NT].rearrange("m p t -> p m t"), in_=zz[:]),
                      reads=["A_zz:%d" % c for c in range(4)], writes=["ZZ_%s:%d" % (n, t)])
                for tb4 in range(4):
                    pt, pk = next_ps()
                    ptb = pt[:].bitcast(BF16)
                    for c in range(4):
                        P.op("pe", lambda e, c=c, tb4=tb4, ptb=ptb: e.transpose(out=ptb[:, c * 128:(c + 1) * 128],
                                                                             in_=zzb[:, c, tb4 * 128:(tb4 + 1) * 128], identity=identb[:]),
                             reads=["A_zzb:%d" % c, "identb"], writes=[pk])
                    P.op("act", lambda e, tb4=tb4, ptb=ptb: e.copy(out=zzt[:, tb4, :], in_=ptb[:, 0:512]), reads=[pk], writes=["A_zzt:%d" % tb4])
                P.dma("A_zzt_o", lambda e, t0=t0: e.dma_start(out=S["ZZT_" + n][t0:t0 + NT, :].rearrange("(b p) c -> p b c", p=128), in_=zzt[:]),
                      reads=["A_zzt:%d" % b for b in range(4)], writes=["ZZT_%s:%d" % (n, t)])
            P.release(m0)

        def phase_H(l, g):
            n = g.name
            L, nTB, NF, NFB = g.L, g.nTB, g.NF, g.NFB
            ctab, stab = I["ctab_" + n], I["stab_" + n]
            m0 = P.mark()
            hsd = P.tile("H_hsd", [128, 2, nTB, 512], BF16)
            m1 = P.mark()
            w1 = P.tile("H_w1", [HY_EMB, HY_FH], F32)
            w2 = P.tile("H_w2", [HY_FH, HY_FH], F32)
            w3e = P.tile("H_w3e", [HY_FH + 1, 1024], F32)
            zf = P.tile("H_zf", [HY_EMB, L], F32)
            negt = P.tile("H_negt", [128, nTB], F32)
            h1 = P.tile("H_h1", [HY_FH, L], F32)
            h2e = P.tile("H_h2e", [HY_FH + 1, L], F32)
            ld("H_w1", w1[:], I["hw1"][l], "H_w1")
            ld("H_w2", w2[:], I["hw2"][l], "H_w2")
            ld("H_w3e", w3e[:], I["hw3e"][l], "H_w3e")
            ld("H_zf", zf[:], I["zf_" + n], "H_zf")
            ld("H_negt", negt[:], I["negt_" + n], "H_negt")
            CW = min(512, L)
            ua = P.tile("H_ua", [HY_FH, CW], F32)
            ui = P.tile("H_ui", [HY_FH, CW], I32)
            uf = P.tile("H_uf", [HY_FH, CW], F32)

            def sin_layer(wt, wkey, kdim, src, skey, bcol, dst, dkey):
                for c0 in range(0, L, CW):
                    pt, pk = next_ps()
                    P.op("pe", lambda e, c0=c0, pt=pt: e.matmul(pt[0:HY_FH, 0:CW], lhsT=wt[0:kdim, :], rhs=src[0:kdim, c0:c0 + CW], start=True, stop=True),
                         reads=[wkey, skey], writes=[pk])
                    P.op("dve", lambda e, pt=pt: e.tensor_scalar(out=ua[:], in0=pt[0:HY_FH, 0:CW], scalar1=hvec[:, l, bcol:bcol + 1],
                                                                scalar2=hf2p[:, l:l + 1], op0=ALU.add, op1=ALU.mult),
                         reads=[pk, "hvec", "hf2p"], writes=["H_ua"])
                    P.op("dve", lambda e: e.tensor_scalar(out=ua[:], in0=ua[:], scalar1=8.5, scalar2=None, op0=ALU.add), reads=["H_ua"], writes=["H_ua"])
                    P.op("dve", lambda e: e.tensor_copy(out=ui[:], in_=ua[:]), reads=["H_ua"], writes=["H_ui"])
                    P.op("dve", lambda e: e.tensor_copy(out=uf[:], in_=ui[:]), reads=["H_ui"], writes=["H_uf"])
                    P.op("dve", lambda e: e.tensor_tensor(out=ua[:], in0=ua[:], in1=uf[:], op=ALU.subtract), reads=["H_ua", "H_uf"], writes=["H_ua"])
                    P.op("dve", lambda e: e.tensor_scalar(out=uf[:], in0=ua[:], scalar1=0.5, scalar2=None, op0=ALU.is_gt), reads=["H_ua"], writes=["H_uf"])
                    P.op("dve", lambda e: e.tensor_tensor(out=ua[:], in0=ua[:], in1=uf[:], op=ALU.subtract), reads=["H_ua", "H_uf"], writes=["H_ua"])
                    P.op("act", lambda e, c0=c0: e.activation(out=dst[0:HY_FH, c0:c0 + CW], in_=ua[:], func=AF.Sin, scale=-2 * math.pi),
                         reads=["H_ua"], writes=[dkey])

            sin_layer(w1, "H_w1", HY_EMB, zf, "H_zf", 0, h1, "H_h1")
            P.op("pool", lambda e: e.memset(h2e[64:65, :], 1.0), writes=["H_h2e"])
            sin_layer(w2, "H_w2", HY_FH, h1, "H_h1", 2, h2e, "H_h2e")
            win = P.tile("H_win", [128, 512], F32)
            hfb = [(P.tile("H_hf%d" % i, [128, 512], F32), "H_hf%d" % i) for i in range(2)]
            for tb in range(nTB):
                P.op("act", lambda e, tb=tb: e.activation(out=win[:], in_=deltab[:], func=AF.Exp, scale=negt[:, tb:tb + 1]),
                     reads=["deltab", "H_negt"], writes=["H_win"])
                for d in range(2):
                    pt, pk = next_ps()
                    P.op("pe", lambda e, tb=tb, d=d, pt=pt: e.matmul(pt[:], lhsT=h2e[:, tb * 128:(tb + 1) * 128], rhs=w3e[:, d * 512:(d + 1) * 512],
                                                                     start=True, stop=True), reads=["H_h2e", "H_w3e"], writes=[pk])
                    hb, hk = hfb[d]
                    P.op("dve", lambda e, pt=pt, hb=hb: e.tensor_tensor(out=hb[:], in0=pt[:], in1=win[:], op=ALU.mult), reads=[pk, "H_win"], writes=[hk])
                if tb == 0:
                    P.op("pool", lambda e: e.memset(hfb[1][0][0:1, :], 0.0), reads=[hfb[1][1]], writes=[hfb[1][1]])
                P.op("dve", lambda e, tb=tb: e.tensor_tensor(out=hsd[:, 0, tb, :], in0=hfb[0][0][:], in1=hfb[1][0][:], op=ALU.add),
                     reads=[hfb[0][1], hfb[1][1]], writes=["H_hsd"])
                P.op("pool", lambda e, tb=tb: e.tensor_tensor(out=hsd[:, 1, tb, :], in0=hfb[0][0][:], in1=hfb[1][0][:], op=ALU.subtract),
                     reads=[hfb[0][1], hfb[1][1]], writes=["H_hsd"])
            P.release(m1)

            ctabF, stabF = I["ctabF_" + n], I["stabF_" + n]

            def fwd_slabs():
                sl = []
                for i in range(2):
                    sl.append((P.tile("H_cs%d" % i, [128, nTB, 128], BF16), "H_cs%d" % i, P.tile("H_ss%d" % i, [128, nTB, 128], BF16), "H_ss%d" % i))
                return sl

            def load_fwd_slab(sl, fb):
                cs, ck, ss, sk = sl[fb % 2]
                P.dma(ck, lambda e: e.dma_start(out=cs[:], in_=ctabF[fb, :, 0:nTB, :]), writes=[ck])
                P.dma(sk, lambda e: e.dma_start(out=ss[:], in_=stabF[fb, :, 0:nTB, :]), writes=[sk])
                return cs, ck, ss, sk

            m1 = P.mark()
            wcol = P.tile("H_wcol", [128, NFB], F32)
            ld("H_wcol", wcol[:], I["wcol_" + n], "H_wcol")
            sl = fwd_slabs()
            fst = [(P.tile("H_fst%d" % i, [128, 2, 512], F32), "H_fst%d" % i) for i in range(2)]
            for fb in range(NFB):
                cs, ck, ss, sk = load_fwd_slab(sl, fb)
                fs, fk = fst[fb % 2]
                for which, (slab, slk) in enumerate(((cs, ck), (ss, sk))):
                    pt, pk = next_ps()
                    for tb in range(nTB):
                        P.op("pe", lambda e, tb=tb, pt=pt, slab=slab, which=which: e.matmul(
                            pt[:], lhsT=slab[:, tb, :], rhs=hsd[:, which, tb, :], start=(tb == 0), stop=(tb == nTB - 1)),
                            reads=["H_hsd", slk], writes=[pk])
                    P.op("act", lambda e, pt=pt, fs=fs, which=which, fb=fb: e.activation(out=fs[:, which, :], in_=pt[:], func=AF.Copy, scale=wcol[:, fb:fb + 1]),
                         reads=[pk, "H_wcol"], writes=[fk])
                P.dma(fk + "o", lambda e, fs=fs, fb=fb: e.dma_start(out=S["SPEC_" + n][:, fb, :, :].rearrange("w p c -> p w c"), in_=fs[:]),
                      reads=[fk], writes=["SPEC_%s:%d" % (n, fb)])
            P.release(m1)
            P.release(m0)

            for sq_i in range(g.nseq):
                s0 = sq_i * L
                m0 = P.mark()
                yf = P.tile("H_yf", [128, 2 * NFB, 512], BF16)
                m1 = P.mark()
                zzt = P.tile("H_zzt", [128, nTB, 512], BF16)
                tiles_touched = sorted(set((s0 + i * 128) // NT for i in range(nTB)))
                P.dma("H_zzt", lambda e, s0=s0, zzt=zzt: e.dma_start(out=zzt[:], in_=S["ZZT_" + n][s0:s0 + L, :].rearrange("(b p) c -> p b c", p=128)),
                      reads=["ZZT_%s:%d" % (n, t) for t in tiles_touched], writes=["H_zzt"])
                sl = fwd_slabs()
                fsl = [(P.tile("H_fsl%d" % i, [128, 2, 512], F32), "H_fsl%d" % i) for i in range(2)]
                t1 = [(P.tile("H_t1%d" % i, [128, 512], F32), "H_t1%d" % i) for i in range(4)]
                for fb in range(NFB):
                    cs, ck, ss, sk = load_fwd_slab(sl, fb)
                    fs, fk = fsl[fb % 2]
                    P.dma(fk, lambda e, fs=fs, fb=fb: e.dma_start(out=fs[:], in_=S["SPEC_" + n][:, fb, :, :].rearrange("w p c -> p w c")),
                          reads=["SPEC_%s:%d" % (n, fb)], writes=[fk])
                    zps = []
                    for slab, slk in ((cs, ck), (ss, sk)):
                        pt, pk = next_ps()
                        for tb in range(nTB):
                            P.op("pe", lambda e, tb=tb, pt=pt, slab=slab, zzt=zzt: e.matmul(
                                pt[:], lhsT=slab[:, tb, :], rhs=zzt[:, tb, :], start=(tb == 0), stop=(tb == nTB - 1)),
                                reads=["H_zzt", slk], writes=[pk])
                        zps.append((pt, pk))
                    (zc, zck), (zs, zsk) = zps
                    a, ak = t1[0]
                    b, bk = t1[1]
                    c_, c_k = t1[2]
                    d_, d_k = t1[3]
                    P.op("dve", lambda e, zc=zc, fs=fs, a=a: e.tensor_tensor(out=a[:], in0=zc[:], in1=fs[:, 0, :], op=ALU.mult), reads=[zck, fk], writes=[ak])
                    P.op("dve", lambda e, zs=zs, fs=fs, b=b: e.tensor_tensor(out=b[:], in0=zs[:], in1=fs[:, 1, :], op=ALU.mult), reads=[zsk, fk], writes=[bk])
                    P.op("dve", lambda e, zc=zc, fs=fs, c_=c_: e.tensor_tensor(out=c_[:], in0=zc[:], in1=fs[:, 1, :], op=ALU.mult), reads=[zck, fk], writes=[c_k])
                    P.op("dve", lambda e, zs=zs, fs=fs, d_=d_: e.tensor_tensor(out=d_[:], in0=zs[:], in1=fs[:, 0, :], op=ALU.mult), reads=[zsk, fk], writes=[d_k])
                    P.op("pool", lambda e, a=a, b=b, fb=fb, yf=yf: e.tensor_tensor(out=yf[:, fb, :], in0=a[:], in1=b[:], op=ALU.subtract), reads=[ak, bk], writes=["H_yf"])
                    P.op("pool", lambda e, c_=c_, d_=d_, fb=fb, yf=yf: e.tensor_tensor(out=yf[:, NFB + fb, :], in0=c_[:], in1=d_[:], op=ALU.add), reads=[c_k, d_k], writes=["H_yf"])
                P.release(m1)
                TWg = min(TW, L)
                isl = [(P.tile("H_ci%d" % i, [128, NFB, TWg], BF16), "H_ci%d" % i, P.tile("H_si%d" % i, [128, NFB, TWg], BF16), "H_si%d" % i) for i in range(2)]
                x0b = [(P.tile("H_x0%d" % i, [128, TWg], F32), "H_x0%d" % i) for i in range(2)]
                zzb_ = [(P.tile("H_zb%d" % i, [128, TWg], F32), "H_zb%d" % i) for i in range(2)]
                yo = [(P.tile("H_yo%d" % i, [128, TWg], BF16), "H_yo%d" % i) for i in range(2)]
                tm = [(P.tile("H_tm%d" % i, [128, TWg], F32), "H_tm%d" % i) for i in range(2)]
                it = 0
                for ti in range(L // TWg):
                    tt0 = ti * TWg
                    ci, cik, si, sik = isl[ti % 2]
                    P.dma(cik, lambda e, ci=ci, ti=ti: e.dma_start(out=ci[:], in_=ctab[ti, :, 0:NFB, :]), writes=[cik])
                    P.dma(sik, lambda e, si=si, ti=ti: e.dma_start(out=si[:], in_=stab[ti, :, 0:NFB, :]), writes=[sik])
                    gt0 = s0 + tt0
                    tile_i = gt0 // NT
                    for cc in range(4):
                        xb, xk = x0b[it % 2]
                        zb, zk = zzb_[it % 2]
                        yb, yk = yo[it % 2]
                        tmb, tmk = tm[it % 2]
                        it += 1
                        P.dma(xk, lambda e, xb=xb, cc=cc, gt0=gt0: e.dma_start(out=xb[:], in_=S["X0_" + n][cc, :, gt0:gt0 + TWg]),
                              reads=["X0_%s:%d" % (n, tile_i)], writes=[xk])
                        P.dma(zk, lambda e, zb=zb, cc=cc, gt0=gt0: e.dma_start(out=zb[:], in_=S["ZZ_" + n][cc, :, gt0:gt0 + TWg]),
                              reads=["ZZ_%s:%d" % (n, tile_i)], writes=[zk])
                        pt, pk = next_ps()
                        nmm = 2 * NFB
                        i_mm = 0
                        for w, (slab, slk) in enumerate(((ci, cik), (si, sik))):
                            for fb in range(NFB):
                                P.op("pe", lambda e, pt=pt, w=w, fb=fb, slab=slab, cc=cc, i_mm=i_mm: e.matmul(
                                    pt[:, 0:TWg], lhsT=yf[:, w * NFB + fb, cc * 128:(cc + 1) * 128], rhs=slab[:, fb, :],
                                    start=(i_mm == 0), stop=(i_mm == nmm - 1)), reads=["H_yf", slk], writes=[pk])
                                i_mm += 1
                        P.op("dve", lambda e, pt=pt, zb=zb, tmb=tmb, cc=cc: e.scalar_tensor_tensor(
                            out=tmb[:], in0=zb[:], scalar=hskip[:, l, cc:cc + 1], in1=pt[:, 0:TWg], op0=ALU.mult, op1=ALU.add),
                            reads=[zk, pk, "hskip"], writes=[tmk])
                        P.op("pool", lambda e, tmb=tmb, xb=xb, yb=yb: e.tensor_tensor(out=yb[:], in0=tmb[:], in1=xb[:], op=ALU.mult), reads=[tmk, xk], writes=[yk])
                        P.dma(yk + "o", lambda e, yb=yb, cc=cc, gt0=gt0: e.dma_start(out=S["YH_" + n][cc, :, gt0:gt0 + TWg], in_=yb[:]),
                              reads=[yk], writes=["YH_%s:%d:%d:%d" % (n, tile_i, cc, (gt0 % NT) // TWg)])
                P.release(m0)

        def phase_G(l, g):
            n = g.name
            m0 = P.mark()
            NCH = NT // CH
            Sst = [[(P.tile("G_S%d%d" % (d, h), [128, 128], F32), "G_S%d%d" % (d, h)) for h in range(H)] for d in range(2)]
            Sbf = [[(P.tile("G_Sb%d%d" % (d, h), [128, 128], BF16), "G_Sb%d%d" % (d, h)) for h in range(H)] for d in range(2)]

            NP = NCH // 2

            def dir_tiles(d):
                p = "G_"
                W = {}
                W["qkv"] = P.tile(p + "qkv", [128, 12, NT], F32)
                W["gbr"] = P.tile(p + "gbr", [8, 2, NT], F32)
                W["gc"] = P.tile(p + "gc", [8, NT], F32)
                W["gtot"] = P.tile(p + "gtot", [8, NCH], F32)
                W["gcs"] = P.tile(p + "gcs", [8, 3, NT], BF16)
                W["bts"] = P.tile(p + "bts", [8, 3, NT], BF16)
                W["gcb"] = P.tile(p + "gcb", [128, NT], F32)
                W["btb"] = P.tile(p + "btb", [128, NT], F32)
                W["E"] = P.tile(p + "E", [128, NT], F32)
                W["tmp"] = P.tile(p + "tmp", [128, NT], F32)
                W["gcT"] = P.tile(p + "gcT", [128, NP], F32)
                W["btT"] = P.tile(p + "btT", [128, NP], F32)
                W["sc"] = P.tile(p + "sc", [128, 3, NP], F32)
                W["eglh"] = P.tile(p + "eglh", [128, H, NCH], F32)
                W["DT"] = P.tile(p + "DT", [128, NT], F32)
                W["XB"] = P.tile(p + "XB", [128, NT], F32)
                for h in range(H):
                    W["qd%d" % h] = P.tile(p + "qd%d" % h, [128, NT], BF16)
                    W["wT%d" % h] = P.tile(p + "wT%d" % h, [128, NT], F32)
                    W["kbg%d" % h] = P.tile(p + "kbg%d" % h, [128, NP, 128], F32)
                    W["kdec%d" % h] = P.tile(p + "kdec%d" % h, [128, NP, 128], BF16)
                    W["Xf%d" % h] = P.tile(p + "Xf%d" % h, [128, NP, 128], F32)
                    W["Xtf%d" % h] = P.tile(p + "Xtf%d" % h, [128, NP, 128], F32)
                    W["vbf%d" % h] = P.tile(p + "vbf%d" % h, [128, NP, 128], F32)
                    W["X%d" % h] = P.tile(p + "X%d" % h, [128, 2, NP, 128], BF16)
                    W["Xt%d" % h] = P.tile(p + "Xt%d" % h, [128, 2, NP, 128], BF16)
                    W["R%d" % h] = P.tile(p + "R%d" % h, [128, 2, NP, 128], BF16)
                    W["Rf%d" % h] = P.tile(p + "Rf%d" % h, [128, NP, 128], F32)
                    W["qkT%d" % h] = P.tile(p + "qkT%d" % h, [128, NP, 128], BF16)
                    W["u%d" % h] = P.tile(p + "u%d" % h, [128, NP, 128], F32)
                    W["vn%d" % h] = P.tile(p + "vn%d" % h, [128, 2, 128], BF16)
                    W["oT%d" % h] = P.tile(p + "oT%d" % h, [128, NT], F32)
                W["E2"] = W["X0"][:].bitcast(F32).rearrange("p a c i -> p (a c i)").rearrange("p (c i) -> p c i", i=128)
                W["RfT"] = W["Xt0"][:].bitcast(F32).rearrange("p a c i -> p (a c i)").rearrange("p (c i) -> p c i", i=128)
                W["p"] = p
                return W

            W0 = dir_tiles(0)
            Ws = [W0, W0]

            for d in range(2):
                for h in range(H):
                    St, Sk = Sst[d][h]
                    if g.has_s0:
                        P.dma(Sk, lambda e, St=St, d=d, h=h: e.dma_start(out=St[:], in_=I["s0"][l, d, h]), writes=[Sk])
                    else:
                        P.op("pool", lambda e, St=St: e.memset(St[:], 0.0), writes=[Sk])
                    Sb, Sbk = Sbf[d][h]
                    P.op("act", lambda e, St=St, Sb=Sb: e.copy(out=Sb[:], in_=St[:]), reads=[Sk], writes=[Sbk])

            def load_tile(d, t):
                W = Ws[d]
                p = W["p"]
                t0 = t * NT
                P.dma(p + "qkv", lambda e: e.dma_start(out=W["qkv"][:], in_=S["QKV_" + n][:, :, t0:t0 + NT].rearrange("m p t -> p m t")),
                      reads=["QKV_%s:%d" % (n, t), "QKVv_%s:%d" % (n, t)], writes=[p + "qkv"])
                P.dma(p + "gbr", lambda e: e.dma_start(out=W["gbr"][:], in_=S["GB_" + n][:, t0:t0 + NT].rearrange("(a r) t -> r a t", a=2)),
                      reads=["GB_%s:%d" % (n, t)], writes=[p + "gbr"])

            def v4(ap):
                return ap.rearrange("p (q k) -> p q k", k=128)

            def chunk_local(d, t):
                W = Ws[d]
                p = W["p"]
                mi, ms = (0, 2) if d == 0 else (1, 3)
                last = CH - 1 if d == 0 else 0
                P.op("dve", lambda e: e.tensor_tensor_scan(out=W["gc"][:], data0=scanmask[:], data1=W["gbr"][:, 0, :], initial=0.0, op0=ALU.mult, op1=ALU.add),
                     reads=[p + "gbr", "scanmask"], writes=[p + "gc"])
                if d == 1:
                    gc3 = W["gc"][:].rearrange("r (c i) -> r c i", i=CH)
                    P.op("dve", lambda e: e.tensor_copy(out=W["gtot"][:], in_=gc3[:, :, CH - 1]), reads=[p + "gc"], writes=[p + "gtot"])
                    P.op("dve", lambda e: e.tensor_tensor(out=W["gc"][:], in0=W["gbr"][:, 0, :], in1=W["gc"][:], op=ALU.subtract),
                         reads=[p + "gbr", p + "gc"], writes=[p + "gc"])
                    P.op("dve", lambda e: e.tensor_tensor(out=gc3, in0=gc3, in1=W["gtot"][:].unsqueeze(2).to_broadcast([8, NCH, CH]), op=ALU.add),
                         reads=[p + "gc", p + "gtot"], writes=[p + "gc"])

                def split3(src_ap, skey, dst, dkey):
                    sp0, sp1 = W["tmp"][0:8, :], W["DT"][0:8, :]
                    P.op("dve", lambda e: e.tensor_copy(out=dst[:, 0, :], in_=src_ap), reads=[skey], writes=[dkey])
                    P.op("dve", lambda e: e.tensor_tensor(out=sp0, in0=src_ap, in1=dst[:, 0, :], op=ALU.subtract), reads=[skey, dkey], writes=[p + "tmp"])
                    P.op("dve", lambda e: e.tensor_copy(out=dst[:, 1, :], in_=sp0), reads=[p + "tmp"], writes=[dkey])
                    P.op("dve", lambda e: e.tensor_tensor(out=sp1, in0=sp0, in1=dst[:, 1, :], op=ALU.subtract), reads=[p + "tmp", dkey], writes=[p + "DT"])
                    P.op("dve", lambda e: e.tensor_copy(out=dst[:, 2, :], in_=sp1), reads=[p + "DT"], writes=[dkey])

                split3(W["gc"][:], p + "gc", W["gcs"], p + "gcs")
                split3(W["gbr"][:, 1, :], p + "gbr", W["bts"], p + "bts")

                def bcast(pt_ap, pk_, r_, src, skey):
                    for i3 in range(3):
                        P.op("pe", lambda e, i3=i3: e.matmul(pt_ap, lhsT=selb[:, r_, :], rhs=src[:, i3, :], start=(i3 == 0), stop=(i3 == 2)),
                             reads=["selb", skey], writes=[pk_])

                def xt_part(h):
                    Xh, Xth, Rh = W["X%d" % h], W["Xt%d" % h], W["R%d" % h]
                    kX, kXt, kR = p + "X%d" % h, p + "Xt%d" % h, p + "R%d" % h
                    Xf, Xtf = W["Xf%d" % h], W["Xtf%d" % h]
                    ptt, pkt = next_ps()
                    for q in range(NP):
                        P.op("pe", lambda e, q=q, ptt=ptt, Xf=Xf: e.transpose(out=ptt[:, q * 128:(q + 1) * 128], in_=Xf[:, q, :], identity=ident[:]),
                             reads=[p + "Xf%d" % h, "ident"], writes=[pkt])
                    P.op("act", lambda e, ptt=ptt, Xtf=Xtf: e.copy(out=Xtf[:], in_=v4(ptt[:])), reads=[pkt], writes=[p + "Xtf%d" % h])
                    P.op("pool", lambda e, Xth=Xth, Xtf=Xtf: e.tensor_copy(out=Xth[:, 0, :, :], in_=Xtf[:]), reads=[p + "Xtf%d" % h], writes=[kXt + ":0"])
                    P.op("pool", lambda e, Xh=Xh, Rh=Rh: e.tensor_copy(out=Rh[:, 0, :, :], in_=Xh[:, 0, :, :]), reads=[kX + ":0"], writes=[kR + ":0"])
                    P.op("pool", lambda e, Xf=Xf, h=h: e.tensor_copy(out=W["Rf%d" % h][:], in_=Xf[:]), reads=[p + "Xf%d" % h], writes=[p + "Rf%d" % h])

                gcb4, btb4, tmp4, DT4, XB4 = v4(W["gcb"][:]), v4(W["btb"][:]), v4(W["tmp"][:]), v4(W["DT"][:]), v4(W["XB"][:])
                eye_b = ident[:].unsqueeze(1).to_broadcast([128, NP, 128])
                gcbc3 = W["gcb"][:].rearrange("p (c i) -> p c i", i=CH)
                for h in range(H):
                    r = d * 4 + h
                    qT = W["qkv"][:, h, :]
                    kT = W["qkv"][:, 4 + h, :]
                    vT = W["qkv"][:, 8 + h, :]
                    pt, pk = next_ps()
                    bcast(pt[:], pk, r, W["gcs"], p + "gcs")
                    P.op("act", lambda e, pt=pt: e.copy(out=W["gcb"][:], in_=pt[:]), reads=[pk], writes=[p + "gcb"])
                    P.op("act", lambda e, pt=pt: e.activation(out=W["E"][:], in_=pt[:], func=AF.Exp), reads=[pk], writes=[p + "E"])
                    pt2, pk2 = next_ps()
                    bcast(pt2[:], pk2, r, W["bts"], p + "bts")
                    P.op("act", lambda e, pt2=pt2: e.copy(out=W["btb"][:], in_=pt2[:]), reads=[pk2], writes=[p + "btb"])
                    P.op("dve", lambda e: e.tensor_tensor(out=tmp4, in0=gcb4, in1=eye_b, op=ALU.mult), reads=[p + "gcb", "ident"], writes=[p + "tmp"])
                    P.op("dve", lambda e: e.tensor_reduce(out=W["gcT"][:], in_=tmp4, axis=AX.X, op=ALU.add), reads=[p + "tmp"], writes=[p + "gcT"])
                    P.op("dve", lambda e: e.tensor_tensor(out=tmp4, in0=btb4, in1=eye_b, op=ALU.mult), reads=[p + "btb", "ident"], writes=[p + "tmp"])
                    P.op("dve", lambda e: e.tensor_reduce(out=W["btT"][:], in_=tmp4, axis=AX.X, op=ALU.add), reads=[p + "tmp"], writes=[p + "btT"])
                    P.op("act", lambda e: e.activation(out=W["sc"][:, 2, :], in_=W["gcT"][:], func=AF.Exp), reads=[p + "gcT"], writes=[p + "sc:2"])
                    P.op("dve", lambda e: e.tensor_tensor(out=W["sc"][:, 0, :], in0=W["sc"][:, 2, :], in1=W["btT"][:], op=ALU.mult),
                         reads=[p + "sc:2", p + "btT"], writes=[p + "sc:0"])
                    for c2 in range(2):
                        ps_ = slice(c2 * 64, (c2 + 1) * 64)
                        P.op("dve", lambda e, ps_=ps_, c2=c2: e.tensor_tensor(out=W["sc"][ps_, 1, :], in0=gcb4[ps_, :, c2 * 64 + last], in1=W["gcT"][ps_, :], op=ALU.subtract),
                             reads=[p + "gcb", p + "gcT"], writes=[p + "sc:1"])
                    P.op("act", lambda e: e.activation(out=W["sc"][:, 1, :], in_=W["sc"][:, 1, :], func=AF.Exp), reads=[p + "sc:1"], writes=[p + "sc:1"])
                    P.op("act", lambda e, h=h: e.activation(out=W["eglh"][:, h, :], in_=gcbc3[:, :, last], func=AF.Exp), reads=[p + "gcb"], writes=[p + "eglh"])
                    P.op("dve", lambda e: e.tensor_tensor(out=tmp4, in0=gcb4, in1=W["gcT"][:].unsqueeze(2).to_broadcast([128, NP, 128]), op=ALU.subtract),
                         reads=[p + "gcb", p + "gcT"], writes=[p + "tmp"])
                    P.op("dve", lambda e: e.tensor_scalar(out=W["tmp"][:], in0=W["tmp"][:], scalar1=0.0, scalar2=None, op0=ALU.min), reads=[p + "tmp"], writes=[p + "tmp"])
                    P.op("act", lambda e: e.activation(out=W["tmp"][:], in_=W["tmp"][:], func=AF.Exp), reads=[p + "tmp"], writes=[p + "tmp"])
                    P.op("dve", lambda e: e.tensor_tensor(out=DT4, in0=tmp4, in1=masks2[:, mi, :].unsqueeze(1).to_broadcast([128, NP, 128]), op=ALU.mult),
                         reads=[p + "tmp", "masks2"], writes=[p + "DT"])
                    P.op("pool", lambda e: e.tensor_tensor(out=XB4, in0=DT4, in1=masks2[:, ms, :].unsqueeze(1).to_broadcast([128, NP, 128]), op=ALU.mult),
                         reads=[p + "DT", "masks2"], writes=[p + "XB"])
                    P.op("dve", lambda e: e.tensor_tensor(out=W["XB"][:], in0=W["XB"][:], in1=W["btb"][:], op=ALU.mult), reads=[p + "XB", p + "btb"], writes=[p + "XB"])
                    P.op("dve", lambda e, h=h, qT=qT: e.tensor_tensor(out=W["qd%d" % h][:], in0=qT, in1=W["E"][:], op=ALU.mult),
                         reads=[p + "qkv", p + "E"], writes=[p + "qd%d" % h])
                    P.op("pool", lambda e, h=h, kT=kT: e.tensor_tensor(out=W["wT%d" % h][:], in0=kT, in1=W["E"][:], op=ALU.mult),
                         reads=[p + "qkv", p + "E"], writes=[p + "wT%d" % h])
                    P.op("pool", lambda e, h=h: e.tensor_tensor(out=W["wT%d" % h][:], in0=W["wT%d" % h][:], in1=W["btb"][:], op=ALU.mult),
                         reads=[p + "btb", p + "wT%d" % h], writes=[p + "wT%d" % h])
                    for kind, src in ((0, kT), (1, vT)):
                        pt4, pk4 = next_ps()
                        for q in range(NP):
                            P.op("pe", lambda e, pt4=pt4, q=q, src=src: e.transpose(out=pt4[:, q * 128:(q + 1) * 128], in_=src[:, q * 128:(q + 1) * 128], identity=ident[:]),
                                 reads=[p + "qkv", "ident"], writes=[pk4])
                        src3 = v4(pt4[:])
                        if kind == 0:
                            P.op("dve", lambda e, src3=src3, h=h: e.tensor_tensor(
                                out=W["kbg%d" % h][:], in0=src3, in1=W["sc"][:, 0, :].unsqueeze(2).to_broadcast([128, NP, 128]), op=ALU.mult),
                                reads=[pk4, p + "sc:0"], writes=[p + "kbg%d" % h])
                            P.op("dve", lambda e, src3=src3, h=h: e.tensor_tensor(
                                out=W["kdec%d" % h][:], in0=src3, in1=W["sc"][:, 1, :].unsqueeze(2).to_broadcast([128, NP, 128]), op=ALU.mult),
                                reads=[pk4, p + "sc:1"], writes=[p + "kdec%d" % h])
                        else:
                            P.op("dve", lambda e, src3=src3, h=h: e.tensor_tensor(
                                out=W["vbf%d" % h][:], in0=src3, in1=W["btT"][:].unsqueeze(2).to_broadcast([128, NP, 128]), op=ALU.mult),
                                reads=[pk4, p + "btT"], writes=[p + "vbf%d" % h])
                    ptk, pkk = next_ps()
                    ptq, pkq = next_ps()
                    for q in range(NP):
                        qs = slice(q * 128, (q + 1) * 128)
                        P.op("pe", lambda e, qs=qs, ptk=ptk, kT=kT: e.matmul(ptk[:, qs], lhsT=kT[:, qs], rhs=kT[:, qs], start=True, stop=True), reads=[p + "qkv"], writes=[pkk])
                        P.op("pe", lambda e, qs=qs, ptq=ptq, kT=kT, qT=qT: e.matmul(ptq[:, qs], lhsT=kT[:, qs], rhs=qT[:, qs], start=True, stop=True), reads=[p + "qkv"], writes=[pkq])
                    Xh = W["X%d" % h]
                    kX = p + "X%d" % h
                    Xf = W["Xf%d" % h]
                    P.op("dve", lambda e, ptk=ptk, Xf=Xf: e.tensor_tensor(out=Xf[:], in0=v4(ptk[:]), in1=XB4, op=ALU.mult), reads=[pkk, p + "XB"], writes=[p + "Xf%d" % h])
                    P.op("dve", lambda e, ptq=ptq, h=h: e.tensor_tensor(out=W["qkT%d" % h][:], in0=v4(ptq[:]), in1=DT4, op=ALU.mult), reads=[pkq, p + "DT"], writes=[p + "qkT%d" % h])
                    P.op("pool", lambda e, Xh=Xh, Xf=Xf: e.tensor_copy(out=Xh[:, 0, :, :], in_=Xf[:]), reads=[p + "Xf%d" % h], writes=[kX + ":0"])
                    if h > 0:
                        xt_part(h - 1)
                xt_part(H - 1)
                for it in range(5):
                    a, b = it % 2, (it + 1) % 2
                    for h in range(H):
                        Xh, Xth = W["X%d" % h], W["Xt%d" % h]
                        kX, kXt = p + "X%d" % h, p + "Xt%d" % h
                        pa, pka = next_ps()
                        for q in range(NP):
                            P.op("pe", lambda e, q=q, pa=pa, Xh=Xh, Xth=Xth, a=a: e.matmul(pa[:, q * 128:(q + 1) * 128], lhsT=Xth[:, a, q, :], rhs=Xh[:, a, q, :], start=True, stop=True),
                                 reads=[kX + ":%d" % a, kXt + ":%d" % a], writes=[pka])
                        P.op("act", lambda e, pa=pa, Xh=Xh, b=b: e.copy(out=Xh[:, b, :, :], in_=v4(pa[:])), reads=[pka], writes=[kX + ":%d" % b])
                        pb_, pkb = next_ps()
                        for q in range(NP):
                            P.op("pe", lambda e, q=q, pb_=pb_, Xh=Xh, Xth=Xth, a=a: e.matmul(pb_[:, q * 128:(q + 1) * 128], lhsT=Xh[:, a, q, :], rhs=Xth[:, a, q, :], start=True, stop=True),
                                 reads=[kX + ":%d" % a, kXt + ":%d" % a], writes=[pkb])
                        P.op("act", lambda e, pb_=pb_, Xth=Xth, b=b: e.copy(out=Xth[:, b, :, :], in_=v4(pb_[:])), reads=[pkb], writes=[kXt + ":%d" % b])
                    for h in range(H):
                        Xh, Xth, Rh = W["X%d" % h], W["Xt%d" % h], W["R%d" % h]
                        kX, kXt, kR = p + "X%d" % h, p + "Xt%d" % h, p + "R%d" % h
                        pr, pkr = next_ps()
                        for q in range(NP):
                            P.op("pe", lambda e, q=q, pr=pr, Rh=Rh, Xth=Xth, a=a, b=b: e.matmul(pr[:, q * 128:(q + 1) * 128], lhsT=Xth[:, b, q, :], rhs=Rh[:, a, q, :], start=True, stop=True),
                                 reads=[kXt + ":%d" % b, kR + ":%d" % a], writes=[pkr])
                        Rf = W["Rf%d" % h]
                        P.op("dve", lambda e, pr=pr, Rf=Rf: e.tensor_tensor(out=Rf[:], in0=v4(pr[:]), in1=Rf[:], op=ALU.add), reads=[pkr, p + "Rf%d" % h], writes=[p + "Rf%d" % h])
                        P.op("pool", lambda e, Rf=Rf, Xh=Xh, b=b: e.tensor_tensor(out=Rf[:], in0=Rf[:], in1=Xh[:, b, :, :], op=ALU.add),
                             reads=[p + "Rf%d" % h, kX + ":%d" % b], writes=[p + "Rf%d" % h])
                        P.op("act", lambda e, Rf=Rf, Rh=Rh, b=b: e.copy(out=Rh[:, b, :, :], in_=Rf[:]), reads=[p + "Rf%d" % h], writes=[kR + ":%d" % b])
                kE2 = [p + "X0:0", p + "X0:1"]
                kRT = [p + "Xt0:0", p + "Xt0:1"]
                for h in range(H):
                    Xf, Xtf, Rf = W["Xf%d" % h], W["Xtf%d" % h], W["Rf%d" % h]
                    kRf = p + "Rf%d" % h
                    for rstep in range(NEWTON_STEPS):
                        pe2, pke2 = next_ps()
                        for q in range(NP):
                            P.op("pe", lambda e, q=q, pe2=pe2, Xtf=Xtf, Rf=Rf: e.matmul(pe2[:, q * 128:(q + 1) * 128], lhsT=Xtf[:, q, :], rhs=Rf[:, q, :], start=True, stop=True),
                                 reads=[p + "Xtf%d" % h, kRf], writes=[pke2])
                        P.op("dve", lambda e, pe2=pe2, Xf=Xf: e.tensor_tensor(out=W["E2"], in0=v4(pe2[:]), in1=Xf[:], op=ALU.add), reads=[pke2, p + "Xf%d" % h], writes=kE2)
                        P.op("dve", lambda e, Rf=Rf: e.tensor_tensor(out=W["E2"], in0=W["E2"], in1=Rf[:], op=ALU.subtract), reads=kE2 + [kRf], writes=kE2)
                        prt, pkrt = next_ps()
                        for q in range(NP):
                            P.op("pe", lambda e, q=q, prt=prt, Rf=Rf: e.transpose(out=prt[:, q * 128:(q + 1) * 128], in_=Rf[:, q, :], identity=ident[:]), reads=[kRf, "ident"], writes=[pkrt])
                        P.op("act", lambda e, prt=prt: e.copy(out=W["RfT"], in_=v4(prt[:])), reads=[pkrt], writes=kRT)
                        pre, pkre = next_ps()
                        for q in range(NP):
                            P.op("pe", lambda e, q=q, pre=pre: e.matmul(pre[:, q * 128:(q + 1) * 128], lhsT=W["RfT"][:, q, :], rhs=W["E2"][:, q, :], start=True, stop=True),
                                 reads=kRT + kE2, writes=[pkre])
                        P.op("dve", lambda e, Rf=Rf: e.tensor_tensor(out=Rf[:], in0=Rf[:], in1=W["E2"], op=ALU.add), reads=[kRf] + kE2, writes=[kRf])
                        P.op("dve", lambda e, pre=pre, Rf=Rf: e.tensor_tensor(out=Rf[:], in0=v4(pre[:]), in1=Rf[:], op=ALU.add), reads=[pkre, kRf], writes=[kRf])
                for h in range(H):
                    Rf = W["Rf%d" % h]
                    kRf = p + "Rf%d" % h
                    pu, pku = next_ps()
                    for q in range(NP):
                        P.op("pe", lambda e, q=q, pu=pu, Rf=Rf, h=h: e.matmul(pu[:, q * 128:(q + 1) * 128], lhsT=Rf[:, q, :], rhs=W["vbf%d" % h][:, q, :], start=True, stop=True),
                             reads=[kRf, p + "vbf%d" % h], writes=[pku])
                    P.op("dve", lambda e, pu=pu, h=h: e.tensor_tensor(out=W["u%d" % h][:], in0=v4(pu[:]), in1=W["vbf%d" % h][:], op=ALU.add),
                         reads=[pku, p + "vbf%d" % h], writes=[p + "u%d" % h])
                    pw, pkw = next_ps()
                    for q in range(NP):
                        P.op("pe", lambda e, q=q, pw=pw, Rf=Rf, h=h: e.matmul(pw[:, q * 128:(q + 1) * 128], lhsT=W["kbg%d" % h][:, q, :], rhs=Rf[:, q, :], start=True, stop=True),
                             reads=[kRf, p + "kbg%d" % h], writes=[pkw])
                    P.op("dve", lambda e, pw=pw, h=h: e.tensor_tensor(out=W["wT%d" % h][:], in0=pw[:], in1=W["wT%d" % h][:], op=ALU.add),
                         reads=[pkw, p + "wT%d" % h], writes=[p + "wT%d" % h])

            def scan_tiles(pairs):
                orders = {}
                for d, t in pairs:
                    orders[d] = list(range(NCH)) if d == 0 else list(range(NCH - 1, -1, -1))
                for step in range(NCH):
                    for d, t in pairs:
                        W = Ws[d]
                        p = W["p"]
                        c = orders[d][step]
                        q, c2 = c // 2, c % 2
                        hs = slice(c2 * 64, (c2 + 1) * 64)
                        gpos = t * NT + c * CH
                        seq = gpos // g.L
                        is_start = (gpos % g.L == 0) if d == 0 else ((gpos + CH) % g.L == 0)
                        is_end = ((gpos + CH) % g.L == 0) if d == 0 else (gpos % g.L == 0)
                        par = step % 2
                        for h in range(H):
                            St, Sk = Sst[d][h]
                            Sb, Sbk = Sbf[d][h]
                            if is_start and not g.has_s0 and not (step == 0 and ((d == 0 and t == 0) or (d == 1 and t == g.ntiles - 1))):
                                P.op("pool", lambda e, St=St: e.memset(St[:], 0.0), reads=[Sk], writes=[Sk])
                                P.op("pool", lambda e, Sb=Sb: e.memset(Sb[:], 0.0), reads=[Sbk], writes=[Sbk])
                        pvs = []
                        for h in range(H):
                            St, Sk = Sst[d][h]
                            pv, pkv = next_ps()
                            P.op("pe", lambda e, pv=pv, q=q, h=h, St=St, W=W: e.matmul(pv[:, 0:128], lhsT=W["wT%d" % h][:, q * 128:(q + 1) * 128], rhs=St[:], start=True, stop=True),
                                 reads=[p + "wT%d" % h, Sk], writes=[pkv])
                            pvs.append((pv, pkv))
                        for h in range(H):
                            pv, pkv = pvs[h]
                            vn = W["vn%d" % h]
                            vk = p + "vn%d:%d" % (h, par)
                            P.op("dve", lambda e, pv=pv, q=q, h=h, vn=vn, par=par, hs=hs, W=W: e.tensor_tensor(out=vn[hs, par, :], in0=W["u%d" % h][hs, q, :], in1=pv[hs, 0:128], op=ALU.subtract),
                                 reads=[p + "u%d" % h, pkv], writes=[vk])
                        pss_l = []
                        for h in range(H):
                            vn = W["vn%d" % h]
                            vk = p + "vn%d:%d" % (h, par)
                            pss, pks = next_ps()
                            P.op("pe", lambda e, pss=pss, q=q, h=h, vn=vn, par=par, hs=hs, W=W: e.matmul(pss[:, 0:128], lhsT=W["kdec%d" % h][hs, q, :], rhs=vn[hs, par, :], start=True, stop=True),
                                 reads=[p + "kdec%d" % h, vk], writes=[pks])
                            pss_l.append((pss, pks))
                        pos = []
                        for h in range(H):
                            Sb, Sbk = Sbf[d][h]
                            vn = W["vn%d" % h]
                            vk = p + "vn%d:%d" % (h, par)
                            po, pko = next_ps()
                            P.op("pe", lambda e, po=po, c=c, h=h, Sb=Sb, W=W: e.matmul(po[:, 0:CH], lhsT=Sb[:], rhs=W["qd%d" % h][:, c * CH:(c + 1) * CH], start=True, stop=False),
                                 reads=[p + "qd%d" % h, Sbk], writes=[pko])
                            P.op("pe", lambda e, po=po, q=q, c2=c2, h=h, vn=vn, par=par, hs=hs, W=W: e.matmul(po[:, 0:CH], lhsT=vn[hs, par, :], rhs=W["qkT%d" % h][hs, q, c2 * 64:(c2 + 1) * 64], start=False, stop=True),
                                 reads=[p + "qkT%d" % h, vk], writes=[pko])
                            pos.append((po, pko))
                        for h in range(H):
                            St, Sk = Sst[d][h]
                            Sb, Sbk = Sbf[d][h]
                            po, pko = pos[h]
                            pss, pks = pss_l[h]
                            P.op("act", lambda e, po=po, c=c, h=h, W=W: e.copy(out=W["oT%d" % h][:, c * CH:(c + 1) * CH], in_=po[:, 0:CH]), reads=[pko], writes=[p + "oT%d" % h])
                            P.op("dve", lambda e, pss=pss, c=c, h=h, St=St, W=W: e.scalar_tensor_tensor(out=St[:], in0=St[:], scalar=W["eglh"][:, h, c:c + 1], in1=pss[:, 0:128], op0=ALU.mult, op1=ALU.add),
                                 reads=[Sk, pks, p + "eglh"], writes=[Sk])
                            P.op("act", lambda e, St=St, Sb=Sb: e.copy(out=Sb[:], in_=St[:]), reads=[Sk], writes=[Sbk])
                            if is_end and g.wstate:
                                d_ = P.dma("nst_%d%d" % (d, h), lambda e, St=St, seq=seq, d=d, h=h: e.dma_start(out=O["nstate"][seq, l, d, h], in_=St[:]), reads=[Sk])
                                fin.append(d_.idx)
                for d, t in pairs:
                    W = Ws[d]
                    p = W["p"]
                    for h in range(H):
                        P.dma(p + "oT%d" % h, lambda e, W=W, h=h, d=d, t=t: e.dma_start(out=S["O_" + n][d, h, :, t * NT:(t + 1) * NT], in_=W["oT%d" % h][:]),
                              reads=[p + "oT%d" % h], writes=["O_%s:%d:%d:%d" % (n, d, h, t)])

            seq = [(0, t) for t in range(g.ntiles)] + [(1, t) for t in range(g.ntiles - 1, -1, -1)]
            load_tile(*seq[0])
            for i_, (d, t) in enumerate(seq):
                chunk_local(d, t)
                if i_ + 1 < len(seq):
                    load_tile(*seq[i_ + 1])
                scan_tiles([(d, t)])
            P.release(m0)

        def phase_C1(l, g, w_out_sb):
            n = g.name
            j = g.cond
            m0 = P.mark()
            xt = P.tile("C_xt", [128, KC, NT], F32)
            of = P.tile("C_of", [128, 4, NT], F32)
            ob = P.tile("C_ob", [128, 4, NT], F32)
            gate = P.tile("C_gate", [128, 4, NT], F32)
            mix = P.tile("C_mix", [128, KC, NT], BF16)
            sqb = [(P.tile("C_sq%d" % i, [128, NT], F32), "C_sq%d" % i) for i in range(4)]
            sqr = [(P.tile("C_sqr%d" % i, [128, NT], BF16), "C_sqr%d" % i) for i in range(8)]
            hlb = [(P.tile("C_hl%d" % i, [128, 2, NT], BF16), "C_hl%d" % i) for i in range(4)]
            tmp = [(P.tile("C_tmp%d" % i, [128, NT], F32), "C_tmp%d" % i) for i in range(4)]
            rstd = P.tile("C_rstd", [128, NT], F32)
            h2 = P.tile("C_h2", [128, KC, NT], BF16)
            def load_oga(t):
                t0 = t * NT
                P.dma("C_of", lambda e, t0=t0: e.dma_start(out=of[:], in_=S["O_" + n][0, :, :, t0:t0 + NT].rearrange("h p t -> p h t")),
                      reads=["O_%s:0:%d:%d" % (n, h, t) for h in range(H)], writes=["C_of"])
                P.dma("C_ob", lambda e, t0=t0: e.dma_start(out=ob[:], in_=S["O_" + n][1, :, :, t0:t0 + NT].rearrange("h p t -> p h t")),
                      reads=["O_%s:1:%d:%d" % (n, h, t) for h in range(H)], writes=["C_ob"])
                P.dma("C_gate", lambda e, t0=t0: e.dma_start(out=gate[:], in_=S["GATE_" + n][:, :, t0:t0 + NT].rearrange("m p t -> p m t")),
                      reads=["GATE_%s:%d" % (n, t)], writes=["C_gate"])

            def load_yh(t):
                t0 = t * NT
                nsub = NT // min(TW, g.L)
                P.dma("C_yh", lambda e, t0=t0: e.dma_start(out=mix[:, 4:8, :], in_=S["YH_" + n][:, :, t0:t0 + NT].rearrange("m p t -> p m t")),
                      reads=["YH_%s:%d:%d:%d" % (n, t, cc, s_) for cc in range(4) for s_ in range(nsub)], writes=["C_mix:hy"])

            for t in range(g.ntiles):
                t0 = t * NT
                P.dma("C_xt", lambda e, t0=t0: e.dma_start(out=xt[:], in_=(I["x_" + n] if l == 0 else S["X_" + n])[:, :, t0:t0 + NT].rearrange("k p t -> p k t")),
                      reads=[xkeys(g, t)], writes=["C_xt"])
                if t == 0:
                    load_oga(0)
                    load_yh(0)
                P.op("dve", lambda e: e.tensor_tensor(out=of[:], in0=of[:], in1=ob[:], op=ALU.add), reads=["C_of", "C_ob"], writes=["C_of"])
                gp = []
                for h in range(H):
                    sq, sqk = sqb[h]
                    hl, hlk = hlb[h]
                    pt, pk = next_ps()
                    sumsq_hilo(of[:, h, :], "C_of", sq, sqk, hl, hlk, pt, pk)
                    gp.append((pt, pk))
                for h in range(H):
                    sq, sqk = sqb[h]
                    pt, pk = gp[h]
                    P.op("act", lambda e, pt=pt, sq=sq: e.activation(out=sq[:], in_=pt[:], func=AF.Ln, scale=1.0 / DK, bias=EPS), reads=[pk], writes=[sqk])
                for h in range(H):
                    sq, sqk = sqb[h]
                    P.op("act", lambda e, sq=sq: e.activation(out=sq[:], in_=sq[:], func=AF.Exp, scale=-0.5), reads=[sqk], writes=[sqk])
                for h in range(H):
                    sq, sqk = sqb[h]
                    tb, tk = tmp[h]
                    P.op("dve", lambda e, h=h, sq=sq, tb=tb: e.scalar_tensor_tensor(out=tb[:], in0=of[:, h, :], scalar=gng[:, l:l + 1], in1=sq[:], op0=ALU.mult, op1=ALU.mult),
                         reads=["C_of", "gng", sqk], writes=[tk])
                    P.op("pool", lambda e, h=h, tb=tb: e.tensor_tensor(out=mix[:, h, :], in0=tb[:], in1=gate[:, h, :], op=ALU.mult), reads=[tk, "C_gate"], writes=["C_mix:%d" % h])
                mixkeys = ["C_mix:%d" % h for h in range(H)] + ["C_mix:hy"]
                if t + 1 < g.ntiles:
                    load_oga(t + 1)
                for m in range(KC):
                    pt, pk = next_ps()
                    for k in range(KC):
                        P.op("pe", lambda e, pt=pt, k=k, m=m: e.matmul(pt[:], lhsT=w_out_sb[:, k, m * 128:(m + 1) * 128], rhs=mix[:, k, :], start=(k == 0), stop=(k == KC - 1)),
                             reads=["w_out_sb"] + mixkeys, writes=[pk])
                    P.op("dve", lambda e, pt=pt, m=m: e.scalar_tensor_tensor(out=xt[:, m, :], in0=pt[:], scalar=mods[:, l, j, 16 + m:16 + m + 1], in1=xt[:, m, :], op0=ALU.mult, op1=ALU.add),
                         reads=[pk, "mods", "C_xt"], writes=["C_xt"])
                if t + 1 < g.ntiles:
                    load_yh(t + 1)
                P.dma("C_xo", lambda e, t0=t0: e.dma_start(out=S["X_" + n][:, :, t0:t0 + NT].rearrange("k p t -> p k t"), in_=xt[:]),
                      reads=["C_xt"], writes=[xkeys(g, t)])
                rms_stats(xt, "C_xt", KC, D, rstd, "C_rstd", sqr)
                for k in range(KC):
                    tb, tk = tmp[k % 4]
                    P.op("dve", lambda e, tb=tb, k=k: e.scalar_tensor_tensor(out=tb[:], in0=xt[:, k, :], scalar=gmod[:, l, j, 1, k:k + 1], in1=rstd[:], op0=ALU.mult, op1=ALU.mult),
                         reads=["C_xt", "gmod", "C_rstd"], writes=[tk])
                    P.op("act", lambda e, tb=tb, k=k: e.activation(out=h2[:, k, :], in_=tb[:], func=AF.Identity, bias=mods[:, l, j, 24 + k:24 + k + 1], scale=1.0),
                         reads=[tk, "mods"], writes=["C_h2"])
                P.dma("C_h2o", lambda e, t0=t0: e.dma_start(out=S["H2_" + n][:, :, t0:t0 + NT].rearrange("k p t -> p k t"), in_=h2[:]),
                      reads=["C_h2"], writes=["H2_%s:%d" % (n, t)])
            P.release(m0)

        def phase_C2(l, g, w1_sb, w2_sb):
            n = g.name
            j = g.cond
            lastl = (l == depth - 1)
            m0 = P.mark()
            xt = P.tile("M_xt", [128, KC, NT], F32)
            h2 = P.tile("M_h2", [128, KC, NT], BF16)
            act = P.tile("M_act", [128, 16, NT], BF16)
            rl = [(P.tile("M_rl%d" % i, [128, NT], F32), "M_rl%d" % i) for i in range(3)]
            sqb = [(P.tile("M_sq%d" % i, [128, NT], BF16), "M_sq%d" % i) for i in range(2)]
            rstd = P.tile("M_rstd", [128, NT], F32)
            def load_h2(t):
                t0 = t * NT
                P.dma("M_h2", lambda e, t0=t0: e.dma_start(out=h2[:], in_=S["H2_" + n][:, :, t0:t0 + NT].rearrange("k p t -> p k t")), reads=["H2_%s:%d" % (n, t)], writes=["M_h2"])

            load_h2(0)
            for t in range(g.ntiles):
                t0 = t * NT
                P.dma("M_xt", lambda e, t0=t0: e.dma_start(out=xt[:], in_=S["X_" + n][:, :, t0:t0 + NT].rearrange("k p t -> p k t")), reads=[xkeys(g, t)], writes=["M_xt"])
                for half in range(2):
                    for mm in range(16):
                        col = (half * 16 + mm) * 128
                        pt, pk = next_ps()
                        for k in range(KC):
                            P.op("pe", lambda e, pt=pt, k=k, col=col: e.matmul(pt[:], lhsT=w1_sb[:, k, col:col + 128], rhs=h2[:, k, :], start=(k == 0), stop=(k == KC - 1)),
                                 reads=["w1_sb", "M_h2"], writes=[pk])
                        rb, rk = rl[mm % 3]
                        P.op("act", lambda e, pt=pt, rb=rb: e.activation(out=rb[:], in_=pt[:], func=AF.Relu), reads=[pk], writes=[rk])
                        P.op("pool", lambda e, rb=rb, mm=mm: e.tensor_tensor(out=act[:, mm, :], in0=rb[:], in1=rb[:], op=ALU.mult), reads=[rk], writes=["M_act:%d" % mm])
                    if half == 1 and t + 1 < g.ntiles:
                        load_h2(t + 1)
                    for m in range(KC):
                        pt, pk = next_ps()
                        for kk in range(16):
                            P.op("pe", lambda e, pt=pt, kk=kk, m=m, half=half: e.matmul(pt[:], lhsT=w2_sb[:, half * 16 + kk, m * 128:(m + 1) * 128], rhs=act[:, kk, :], start=(kk == 0), stop=(kk == 15)),
                                 reads=["w2_sb", "M_act:%d" % kk], writes=[pk])
                        P.op("dve", lambda e, pt=pt, m=m: e.scalar_tensor_tensor(out=xt[:, m, :], in0=pt[:], scalar=mods[:, l, j, 40 + m:40 + m + 1], in1=xt[:, m, :], op0=ALU.mult, op1=ALU.add),
                             reads=[pk, "mods", "M_xt"], writes=["M_xt"])
                if not lastl:
                    P.dma("M_xo", lambda e, t0=t0: e.dma_start(out=S["X_" + n][:, :, t0:t0 + NT].rearrange("k p t -> p k t"), in_=xt[:]), reads=["M_xt"], writes=[xkeys(g, t)])
                else:
                    rms_stats(xt, "M_xt", KC, D, rstd, "M_rstd", sqb)
                    for k in range(KC):
                        P.op("dve", lambda e, k=k: e.scalar_tensor_tensor(out=xt[:, k, :], in0=xt[:, k, :], scalar=finalg[:, k:k + 1], in1=rstd[:], op0=ALU.mult, op1=ALU.mult),
                             reads=["M_xt", "finalg", "M_rstd"], writes=["M_xt"])
                    d_ = P.dma("M_yo", lambda e, t0=t0: e.dma_start(out=O["y_" + n][:, :, t0:t0 + NT].rearrange("k p t -> p k t"), in_=xt[:]), reads=["M_xt"])
                    fin.append(d_.idx)
            P.release(m0)

        for l in range(depth):
            m0 = P.mark()
            w_in_sb = load_weight_bf16("w_in_sb", I["w_in"][l].rearrange("(k p) n -> p k n", p=128), [128, KC, IN_COLS])
            for g in groups:
                phase_A(l, g, w_in_sb)
            P.release(m0)
            for g in groups:
                phase_H(l, g)
            for g in groups:
                phase_G(l, g)
            m0 = P.mark()
            w_out_sb = load_weight_bf16("w_out_sb", I["w_out"][l].rearrange("(k p) n -> p k n", p=128), [128, KC, D])
            for g in groups:
                phase_C1(l, g, w_out_sb)
            P.release(m0)
            m0 = P.mark()
            w1_sb = load_weight_bf16("w1_sb", I["w_mlp1"][l].rearrange("(k p) n -> p k n", p=128), [128, KC, DFF])
            w2_sb = load_weight_bf16("w2_sb", I["w_mlp2"][l].rearrange("(k p) n -> p k n", p=128), [128, 32, D])
            for g in groups:
                phase_C2(l, g, w1_sb, w2_sb)
            P.release(m0)
        P.emit(final_wait_ops=fin)
    return nc


def _dft_tables(L, ntab):
    N = 2 * L
    a = np.arange(ntab, dtype=np.int64)
    ph = (a[:, None] * a[None, :]) % N
    ang = ph.astype(np.float64) * (2.0 * np.pi / N)
    c = np.cos(ang)
    s_ = np.sin(ang)
    s_[(ph % L) == 0] = 0.0
    def tl(a_, w_):
        return np.ascontiguousarray(a_.reshape(ntab // 128, 128, ntab // w_, w_).transpose(2, 1, 0, 3)).astype(ml_dtypes.bfloat16)
    return tl(c, FW), tl(s_, FW), tl(c, 128), tl(s_, 128)


def _zfeat(L):
    t = np.linspace(0.0, 1.0, L, dtype=np.float32)[:, None]
    bands = (HY_EMB - 1) // 2
    f = np.linspace(1e-4, bands - 1, bands, dtype=np.float32)[None, :]
    wpos = (np.float32(2.0 * math.pi) * np.arange(L, dtype=np.float32)[:, None] / np.float32(L)).astype(np.float32)
    z = np.concatenate([t, np.cos(f * wpos), -np.sin(f * wpos)], axis=-1).astype(np.float32)
    return np.ascontiguousarray(z.T), t[:, 0]


_PROG_CACHE = {}


def run_cfg(depth, LS, inputs, n_cores=8):
    f32 = np.float32
    groups = make_groups(LS)
    A = {k: np.asarray(v) for k, v in inputs.items()}
    shared = {}
    for g in groups:
        c, s_, cF, sF = _dft_tables(g.L, g.NTAB)
        shared["ctab_" + g.name] = c
        shared["stab_" + g.name] = s_
        shared["ctabF_" + g.name] = cF
        shared["stabF_" + g.name] = sF
        zf, t = _zfeat(g.L)
        shared["zf_" + g.name] = zf
        shared["negt_" + g.name] = np.ascontiguousarray((-t).reshape(g.nTB, 128).T).astype(f32)
        w = np.zeros(g.NF, f32)
        w[0:g.L + 1] = 2.0 / (2 * g.L)
        w[0] = 1.0 / (2 * g.L)
        w[g.L] = 1.0 / (2 * g.L)
        shared["wcol_" + g.name] = np.ascontiguousarray(w.reshape(g.NFB, 128).T).astype(f32)
    shared["w_ada"] = A["w_ada"][:depth].astype(f32)
    shared["b_ada"] = np.ascontiguousarray(A["b_ada"][:depth].reshape(depth, 48, 128).transpose(2, 0, 1)).astype(f32)
    ng = np.stack([A["norm1_g"][:depth], A["norm2_g"][:depth]], axis=1)
    shared["norm_g"] = np.ascontiguousarray(ng.reshape(depth, 2, KC, 128).transpose(3, 0, 1, 2)).astype(f32)
    shared["final_g"] = np.ascontiguousarray(A["final_g"].reshape(KC, 128).T).astype(f32)
    shared["w_in"] = A["w_in"][:depth].astype(f32)
    shared["gcw"] = np.ascontiguousarray(A["gdn_conv_w"][:depth].reshape(depth, 5, 12, 128).transpose(3, 0, 2, 1)).astype(f32)
    shared["hcw"] = np.ascontiguousarray(A["hy_conv_w"][:depth].reshape(depth, 3, 12, 128).transpose(3, 0, 2, 1)).astype(f32)
    ap_ = np.stack([A["gdn_a_log"][:depth].reshape(depth, 8), A["gdn_dt_bias"][:depth].reshape(depth, 8)], axis=-1)
    shared["a_par"] = np.ascontiguousarray(ap_.transpose(1, 0, 2)).astype(f32)
    shared["gng"] = np.ascontiguousarray(A["gdn_norm_g"][:depth].T).astype(f32)
    shared["hw1"] = A["hy_w1"][:depth].astype(f32)
    hv = np.stack([A["hy_b1"][:depth], A["hy_freq"][:depth], A["hy_b2"][:depth], np.zeros_like(A["hy_b1"][:depth])], axis=-1)
    shared["hvec"] = np.ascontiguousarray(hv.transpose(1, 0, 2)).astype(f32)
    shared["hw2"] = A["hy_w2"][:depth].astype(f32)
    shared["hw3e"] = np.ascontiguousarray(np.concatenate([A["hy_w3"][:depth], A["hy_b3"][:depth][:, None, :]], axis=1)).astype(f32)
    shared["hskip"] = np.ascontiguousarray(A["hy_skip"][:depth].reshape(depth, 4, 128).transpose(2, 0, 1)).astype(f32)
    shared["w_out"] = A["w_out"][:depth].astype(f32)
    shared["w_mlp1"] = A["w_mlp1"][:depth].astype(f32)
    shared["w_mlp2"] = A["w_mlp2"][:depth].astype(f32)
    ii = np.arange(64)
    mk = np.zeros((64, 5, 64), f32)
    mk[:, 0, :] = (ii[None, :] >= ii[:, None])
    mk[:, 1, :] = (ii[None, :] <= ii[:, None])
    mk[:, 2, :] = -1.0 * (ii[None, :] > ii[:, None])
    mk[:, 3, :] = -1.0 * (ii[None, :] < ii[:, None])
    mk[:, 4, :] = (ii[None, :] == ii[:, None])
    shared["masks"] = mk
    mk2 = np.zeros((128, 4, 128), f32)
    for bb in range(2):
        mk2[bb * 64:(bb + 1) * 64, :, bb * 64:(bb + 1) * 64] = mk[:, 0:4, :]
    shared["masks2"] = mk2
    sel = np.zeros((8, 8, 128), f32)
    for r in range(8):
        sel[r, r, :] = 1.0
    shared["sel"] = sel
    sm = np.ones((8, NT), f32)
    sm[:, ::CH] = 0.0
    shared["scanmask"] = sm
    min_decay = math.log(1e-2) / 1.5
    max_decay = math.log(1e-2) / 0.3
    shared["delta"] = np.abs(np.linspace(min_decay, max_decay, 512, dtype=f32)).astype(f32)

    n_s = A["x_sample"].shape[0]
    in_maps = []
    for i in range(n_cores):
        b = i % n_s
        m = dict(shared)
        xs = A["x_sample"][b]
        m["x_s"] = np.ascontiguousarray(xs.T.reshape(KC, 128, LS)).astype(f32)
        xp = A["x_prompt"][2 * i:2 * i + 2].reshape(512, D)
        m["x_p"] = np.ascontiguousarray(xp.T.reshape(KC, 128, 512)).astype(f32)
        m["s0"] = np.ascontiguousarray(A["state_gdn"][b][:depth]).astype(f32)
        cd = np.stack([A["c"][b], A["c_ctx"]], axis=-1)
        m["cond"] = np.ascontiguousarray(cd.reshape(KC, 128, 2).transpose(1, 0, 2)).astype(f32)
        in_maps.append(m)

    key = (depth, LS)
    if key not in _PROG_CACHE:
        _PROG_CACHE[key] = build_program(depth, LS)
    nc = _PROG_CACHE[key]
    res = run_bass_kernel_spmd(nc, in_maps, core_ids=list(range(n_cores)))
    R = res.results
    _LAST["R"] = R
    y_sample = np.stack([R[b]["y_s"].reshape(D, LS).T for b in range(n_s)], axis=0).astype(f32)
    y_prompt = np.concatenate([R[i]["y_p"].reshape(D, 512).T.reshape(2, 256, D) for i in range(n_cores)], axis=0).astype(f32)
    new_state = np.concatenate([R[i]["nstate"] for i in range(n_cores)], axis=0).astype(f32)
    return (y_prompt, y_sample, new_state)


def kernel(**inputs):
    return run_cfg(4, 4096, inputs)
```

```python
import contextlib
import math
import numpy as np
import ml_dtypes
import concourse.bass as bass
import concourse.mybir as mybir
from concourse.bass_utils import run_bass_kernel_spmd

F32 = mybir.dt.float32
BF16 = mybir.dt.bfloat16
I32 = mybir.dt.int32
AF = mybir.ActivationFunctionType
ALU = mybir.AluOpType
AX = mybir.AxisListType

ENGS = ("pe", "act", "dve", "pool", "sp")
STORE_ENG = "sp"


def _is_store(semname):
    return semname.endswith("_o") or semname in ("C_xo", "C_h2o", "M_xo", "M_yo") or semname.startswith("G_oT")
SEM_EPOCH = 30000
SBUF_BASE = 16640
SBUF_BYTES = 229000


def _dsize(dt):
    return 2 if dt == BF16 else 4


class Op:
    __slots__ = ("eng", "fn", "deps", "is_dma", "dsem", "dval", "signal", "sem", "val", "idx", "_rw")


class Prog:
    def __init__(self, nc, stack):
        self.nc = nc
        self.stack = stack
        self.ops = []
        self.key_w = {}
        self.key_r = {}
        self.dma_sems = {}
        self.bump = SBUF_BASE
        self.tiles = []
        self.tile_keys = {}
        self.tile_pending = {}
        self.uid = 0
        self.ps_rr = 0

    def tile(self, name, shape, dtype):
        nbytes = int(np.prod(shape[1:])) * _dsize(dtype)
        nbytes = (nbytes + 63) // 64 * 64
        off = self.bump
        assert off + nbytes <= SBUF_BYTES, ("SBUF overflow", name, off, nbytes)
        self.bump = off + nbytes
        self.uid += 1
        t = self.nc.alloc_sbuf_tensor_at("%s_%d" % (name, self.uid), list(shape), dtype, offset=off)
        pend = set()
        for (s, e, oname) in self.tiles:
            if s < off + nbytes and off < e:
                pend |= self.tile_pending.get(oname, set())
                for k in self.tile_keys.get(oname, ()):
                    w = self.key_w.get(k)
                    if w is not None:
                        pend.add(w)
                    pend.update(self.key_r.get(k, {}).values())
        self.tiles = [(s, e, n) for (s, e, n) in self.tiles if not (s >= off and e <= off + nbytes)] + [(off, off + nbytes, name)]
        self.tile_keys.setdefault(name, set())
        self.tile_pending[name] = self._compress(pend)
        return t

    def mark(self):
        return self.bump

    def release(self, m):
        self.bump = m

    def new_sem(self, name):
        return self.stack.enter_context(self.nc.semaphore(name))

    def _cls(self, d):
        o = self.ops[d]
        return ("d", id(o.dsem)) if o.is_dma else o.eng

    def _compress(self, deps):
        best = {}
        for d in deps:
            c = self._cls(d)
            b = best.get(c)
            if b is None or b < d:
                best[c] = d
        return set(best.values())

    def _deps(self, reads, writes):
        deps = set()
        for k in reads:
            base = k.split(":")[0]
            if base in self.tile_keys:
                self.tile_keys[base].add(k)
                deps |= self.tile_pending[base]
            w = self.key_w.get(k)
            if w is not None:
                deps.add(w)
        for k in writes:
            base = k.split(":")[0]
            if base in self.tile_keys:
                self.tile_keys[base].add(k)
                deps |= self.tile_pending[base]
            w = self.key_w.get(k)
            if w is not None:
                deps.add(w)
            deps.update(self.key_r.get(k, {}).values())
        return self._compress(deps)

    def _op(self, eng, fn, reads=(), writes=()):
        o = Op()
        o.eng = eng
        o.fn = fn
        o.is_dma = False
        o.dsem = None
        o.signal = False
        o.idx = len(self.ops)
        o.deps = self._deps(reads, writes)
        self.ops.append(o)
        o._rw = (reads, writes)
        return o

    def op(self, eng, fn, reads=(), writes=()):
        o = self._op(eng, fn, reads, writes)
        self._commit(o)
        return o

    def _commit(self, o):
        reads, writes = o._rw
        c = self._cls(o.idx)
        for k in reads:
            self.key_r.setdefault(k, {})[c] = o.idx
        for k in writes:
            self.key_w[k] = o.idx
            self.key_r[k] = {}
        o._rw = None

    def dma(self, semname, fn, reads=(), writes=(), eng="sp"):
        if eng == "sp" and _is_store(semname):
            eng = STORE_ENG
        o = self._op(eng, fn, reads, writes)
        o.is_dma = True
        if semname not in self.dma_sems:
            self.dma_sems[semname] = [self.new_sem("d%d" % len(self.dma_sems)), 0]
        ent = self.dma_sems[semname]
        ent[1] += 16
        o.dsem = ent[0]
        o.dval = ent[1]
        self._commit(o)
        return o

    def emit(self, final_wait_ops=()):
        nc = self.nc
        ops = self.ops

        def skip(do, o):
            return (not do.is_dma) and (not o.is_dma) and do.eng == o.eng and do.eng == "pe"

        for o in ops:
            for d in o.deps:
                do = ops[d]
                if do.is_dma or skip(do, o):
                    continue
                do.signal = True
        cur = {}
        for o in ops:
            if o.is_dma:
                o.sem, o.val = o.dsem, o.dval
                continue
            if not o.signal:
                continue
            ent = cur.get(o.eng)
            if ent is None or ent[1] >= SEM_EPOCH:
                ent = [self.new_sem("e%s%d" % (o.eng, o.idx)), 0]
                cur[o.eng] = ent
            ent[1] += 1
            o.sem, o.val = ent[0], ent[1]
        per_eng = {e: [o for o in ops if o.eng == e] for e in ENGS}
        final_ops = [ops[i] for i in final_wait_ops]

        def run(ename, e):
            waited = {}
            for o in per_eng[ename]:
                need = {}
                for d in o.deps:
                    do = ops[d]
                    if skip(do, o):
                        continue
                    sid = id(do.sem)
                    if sid not in need or need[sid][1] < do.val:
                        need[sid] = (do.sem, do.val)
                for sid, (s, v) in need.items():
                    if waited.get(sid, 0) >= v:
                        continue
                    e.wait_ge(s, v)
                    waited[sid] = v
                ins = o.fn(e)
                if o.is_dma:
                    ins.then_inc(o.sem, 16)
                elif o.signal:
                    ins.then_inc(o.sem, 1)
            if ename == "sp":
                for o in final_ops:
                    e.wait_ge(o.sem, o.val)

        with nc.Block() as block:
            @block.tensor
            def _(e):
                run("pe", e)

            @block.scalar
            def _(e):
                run("act", e)

            @block.vector
            def _(e):
                run("dve", e)

            @block.gpsimd
            def _(e):
                run("pool", e)

            @block.sync
            def _(e):
                run("sp", e)


D = 1024
KC = 8
NT = 512
H = 4
DK = 128
CH = 64
IN_COLS = 3600
OFF_A = 2048
OFF_B = 2056
OFF_HY = 2064
DFF = 4096
EPS = 1e-6
HY_EMB = 33
HY_FH = 64
NEWTON_STEPS = 1
FW = 256
TW = 256


class Group:
    def __init__(self, name, T, L, seg, has_s0, wstate, cond):
        self.name = name
        self.T = T
        self.L = L
        self.nseq = T // L
        self.seg = seg
        self.has_s0 = has_s0
        self.wstate = wstate
        self.cond = cond
        self.ntiles = T // NT
        self.NF = (L + 1 + FW - 1) // FW * FW
        self.NFB = self.NF // 128
        self.NTAB = max(self.NF, L)
        self.nTB = L // 128


def make_groups(LS):
    return [Group("s", LS, LS, 64, True, False, 0), Group("p", 512, 256, 256, False, True, 1)]


DEBUG_SCRATCH = False
_LAST = {}


def build_program(depth, LS):
    nc = bass.Bass("TRN2", target_bir_lowering=False)
    groups = make_groups(LS)

    def din(name, shape, dt=F32):
        return nc.dram_tensor(name, list(shape), dt, kind="ExternalInput").ap()

    def dout(name, shape, dt=F32):
        return nc.dram_tensor(name, list(shape), dt, kind="ExternalOutput").ap()

    def dscr(name, shape, dt=F32):
        return nc.dram_tensor(name, list(shape), dt, kind=("ExternalOutput" if DEBUG_SCRATCH else "Internal")).ap()

    I = {}
    for g in groups:
        I["x_" + g.name] = din("x_" + g.name, [KC, 128, g.T])
        I["ctab_" + g.name] = din("ctab_" + g.name, [g.NTAB // FW, 128, g.NTAB // 128, FW], BF16)
        I["stab_" + g.name] = din("stab_" + g.name, [g.NTAB // FW, 128, g.NTAB // 128, FW], BF16)
        I["zf_" + g.name] = din("zf_" + g.name, [HY_EMB, g.L])
        I["negt_" + g.name] = din("negt_" + g.name, [128, g.nTB])
        I["wcol_" + g.name] = din("wcol_" + g.name, [128, g.NFB])
        I["ctabF_" + g.name] = din("ctabF_" + g.name, [g.NTAB // 128, 128, g.NTAB // 128, 128], BF16)
        I["stabF_" + g.name] = din("stabF_" + g.name, [g.NTAB // 128, 128, g.NTAB // 128, 128], BF16)
    I["s0"] = din("s0", [depth, 2, H, DK, DK])
    I["cond"] = din("cond", [128, KC, 2])
    I["w_ada"] = din("w_ada", [depth, D, 6 * D])
    I["b_ada"] = din("b_ada", [128, depth, 48])
    I["norm_g"] = din("norm_g", [128, depth, 2, KC])
    I["final_g"] = din("final_g", [128, KC])
    I["w_in"] = din("w_in", [depth, D, IN_COLS])
    I["gcw"] = din("gcw", [128, depth, 12, 5])
    I["hcw"] = din("hcw", [128, depth, 12, 3])
    I["a_par"] = din("a_par", [8, depth, 2])
    I["gng"] = din("gng", [128, depth])
    I["hw1"] = din("hw1", [depth, HY_EMB, HY_FH])
    I["hvec"] = din("hvec", [HY_FH, depth, 4])
    I["hw2"] = din("hw2", [depth, HY_FH, HY_FH])
    I["hw3e"] = din("hw3e", [depth, HY_FH + 1, 2 * 512])
    I["hskip"] = din("hskip", [128, depth, 4])
    I["w_out"] = din("w_out", [depth, D, D])
    I["w_mlp1"] = din("w_mlp1", [depth, D, DFF])
    I["w_mlp2"] = din("w_mlp2", [depth, DFF, D])
    I["masks"] = din("masks", [64, 5, 64])
    I["masks2"] = din("masks2", [128, 4, 128])
    I["sel"] = din("sel", [8, 8, 128])
    I["scanmask"] = din("scanmask", [8, NT])
    I["delta"] = din("delta", [512])

    O = {}
    for g in groups:
        O["y_" + g.name] = dout("y_" + g.name, [KC, 128, g.T])
    O["nstate"] = dout("nstate", [2, depth, 2, H, DK, DK])

    S = {}
    for g in groups:
        n = g.name
        S["X_" + n] = dscr("X_" + n, [KC, 128, g.T])
        S["QKV_" + n] = dscr("QKV_" + n, [12, 128, g.T])
        S["GATE_" + n] = dscr("GATE_" + n, [4, 128, g.T])
        S["GB_" + n] = dscr("GB_" + n, [16, g.T])
        S["X0_" + n] = dscr("X0_" + n, [4, 128, g.T])
        S["ZZ_" + n] = dscr("ZZ_" + n, [4, 128, g.T])
        S["ZZT_" + n] = dscr("ZZT_" + n, [g.T, 512], BF16)
        S["O_" + n] = dscr("O_" + n, [2, 4, 128, g.T])
        S["YH_" + n] = dscr("YH_" + n, [4, 128, g.T], BF16)
        S["H2_" + n] = dscr("H2_" + n, [KC, 128, g.T], BF16)
        S["SPEC_" + n] = dscr("SPEC_" + n, [2, g.NFB, 128, 512])

    with contextlib.ExitStack() as st:
        P = Prog(nc, st)
        ps_t = [st.enter_context(nc.psum_tensor("ps%d" % i, [128, 512], F32)) for i in range(8)]

        ps_held = set()

        def next_ps(hold=False):
            while True:
                i = P.ps_rr
                P.ps_rr = (i + 1) % 8
                if i not in ps_held:
                    break
            if hold:
                ps_held.add(i)
            return ps_t[i], "ps%d" % i

        def ps_release(key):
            ps_held.discard(int(key[2:]))

        fin = []

        ident = P.tile("ident", [128, 128], F32)
        identb = P.tile("identb", [128, 128], BF16)
        ones = P.tile("ones", [128, 128], F32)
        onesb = P.tile("onesb", [128, 128], BF16)
        selb = P.tile("selb", [8, 8, 128], BF16)
        masks = P.tile("masks", [64, 5, 64], F32)
        masks2 = P.tile("masks2", [128, 4, 128], F32)
        sel = P.tile("sel", [8, 8, 128], F32)
        scanmask = P.tile("scanmask", [8, NT], F32)
        mods = P.tile("mods", [128, depth, 2, 48], F32)
        gmod = P.tile("gmod", [128, depth, 2, 2, KC], F32)
        normg = P.tile("normg", [128, depth, 2, KC], F32)
        finalg = P.tile("finalg", [128, KC], F32)
        gcw = P.tile("gcw", [128, depth, 12, 5], F32)
        hcw = P.tile("hcw", [128, depth, 12, 3], F32)
        apar = P.tile("apar", [8, depth, 2], F32)
        nega = P.tile("nega", [8, depth], F32)
        gng = P.tile("gng", [128, depth], F32)
        hskip = P.tile("hskip", [128, depth, 4], F32)
        hvec = P.tile("hvec", [HY_FH, depth, 4], F32)
        hf2p = P.tile("hf2p", [HY_FH, depth], F32)
        deltab = P.tile("deltab", [128, 512], F32)
        condt = P.tile("condt", [128, KC, 2], F32)
        bada = P.tile("bada", [128, depth, 48], F32)

        def ld(semname, t_ap, src, key, eng="sp"):
            return P.dma(semname, lambda e: e.dma_start(out=t_ap, in_=src), writes=[key], eng=eng)

        P.op("pool", lambda e: e.memset(ident[:], 0.0), writes=["ident"])
        P.op("pool", lambda e: e.affine_select(out=ident[:], in_=ident[:], pattern=[[-1, 128]], compare_op=ALU.not_equal,
                                               fill=1.0, base=0, channel_multiplier=1), reads=["ident"], writes=["ident"])
        P.op("dve", lambda e: e.tensor_copy(out=identb[:], in_=ident[:]), reads=["ident"], writes=["identb"])
        P.op("pool", lambda e: e.memset(ones[:], 1.0), writes=["ones"])
        P.op("pool", lambda e: e.memset(onesb[:], 1.0), writes=["onesb"])
        ld("c_masks", masks[:], I["masks"], "masks")
        ld("c_masks2", masks2[:], I["masks2"], "masks2")
        ld("c_sel", sel[:], I["sel"], "sel")
        P.op("dve", lambda e: e.tensor_copy(out=selb[:], in_=sel[:]), reads=["sel"], writes=["selb"])
        ld("c_scanmask", scanmask[:], I["scanmask"], "scanmask")
        ld("c_normg", normg[:], I["norm_g"], "normg")
        ld("c_finalg", finalg[:], I["final_g"], "finalg")
        ld("c_gcw", gcw[:], I["gcw"], "gcw")
        ld("c_hcw", hcw[:], I["hcw"], "hcw")
        ld("c_apar", apar[:], I["a_par"], "apar")
        ld("c_gng", gng[:], I["gng"], "gng")
        ld("c_hskip", hskip[:], I["hskip"], "hskip")
        ld("c_hvec", hvec[:], I["hvec"], "hvec")
        ld("c_delta", deltab[:], I["delta"].partition_broadcast(128), "deltab")
        ld("c_cond", condt[:], I["cond"], "condt")
        ld("c_bada", bada[:], I["b_ada"], "bada")
        P.op("act", lambda e: e.activation(out=nega[:], in_=apar[:, :, 0], func=AF.Exp), reads=["apar"], writes=["nega"])
        P.op("dve", lambda e: e.tensor_scalar(out=nega[:], in0=nega[:], scalar1=-1.0, scalar2=None, op0=ALU.mult), reads=["nega"], writes=["nega"])
        P.op("dve", lambda e: e.tensor_scalar(out=hf2p[:], in0=hvec[:, :, 1], scalar1=1.0 / (2 * math.pi), scalar2=None, op0=ALU.mult),
             reads=["hvec"], writes=["hf2p"])

        base_mark = P.mark()

        def prologue():
            m0 = P.mark()
            scond = P.tile("scond", [128, KC, 2], F32)
            P.op("act", lambda e: e.activation(out=scond[:], in_=condt[:], func=AF.Silu), reads=["condt"], writes=["scond"])
            wa = [P.tile("wa%d" % i, [128, KC, 512], F32) for i in range(2)]
            n = 0
            for l in range(depth):
                for cg in range(12):
                    wt = wa[n % 2]
                    wk = "wa%d" % (n % 2)
                    n += 1
                    src = I["w_ada"][l, :, cg * 512:(cg + 1) * 512].rearrange("(k p) n -> p k n", p=128)
                    ld(wk, wt[:], src, wk)
                    for mi in range(4):
                        chunk = cg * 4 + mi
                        pt, pk = next_ps()
                        for k in range(KC):
                            P.op("pe", lambda e, pt=pt, wt=wt, k=k, mi=mi: e.matmul(pt[:, 0:2], lhsT=wt[:, k, mi * 128:(mi + 1) * 128],
                                                                                   rhs=scond[:, k, :], start=(k == 0), stop=(k == KC - 1)),
                                 reads=[wk, "scond"], writes=[pk])
                        P.op("dve", lambda e, pt=pt, l=l, chunk=chunk: e.tensor_tensor(
                            out=mods[:, l, :, chunk], in0=pt[:, 0:2], in1=bada[:, l, chunk:chunk + 1].to_broadcast([128, 2]), op=ALU.add),
                            reads=[pk, "bada"], writes=["mods"])
            for l in range(depth):
                for j in range(2):
                    for w, sc0 in ((0, 8), (1, 32)):
                        P.op("dve", lambda e, l=l, j=j, w=w, sc0=sc0: e.scalar_tensor_tensor(
                            out=gmod[:, l, j, w, :], in0=mods[:, l, j, sc0:sc0 + KC], scalar=1.0, in1=normg[:, l, w, :],
                            op0=ALU.add, op1=ALU.mult), reads=["mods", "normg"], writes=["gmod"])
            P.release(m0)

        prologue()

        def load_weight_bf16(name, src3, shape):
            t = P.tile(name, shape, BF16)
            for a in range(shape[1]):
                P.dma(name, lambda e, a=a: e.dma_start(out=t[:, a, :], in_=src3[:, a, :]), writes=[name], eng="pool")
            return t

        def rms_stats(src_tile, src_key, nchunks, dim, rstd, rstd_key, sqbufs):
            pt, pk = next_ps()
            for k in range(nchunks):
                sq, sqk = sqbufs[k % len(sqbufs)]
                P.op("act", lambda e, sq=sq, k=k: e.activation(out=sq[:], in_=src_tile[:, k, :], func=AF.Square), reads=[src_key], writes=[sqk])
                P.op("pe", lambda e, sq=sq, k=k, pt=pt: e.matmul(pt[:], lhsT=onesb[:], rhs=sq[:], start=(k == 0), stop=(k == nchunks - 1)),
                     reads=[sqk, "onesb"], writes=[pk])
            P.op("act", lambda e, pt=pt: e.activation(out=rstd[:], in_=pt[:], func=AF.Ln, scale=1.0 / dim, bias=EPS), reads=[pk], writes=[rstd_key])
            P.op("act", lambda e: e.activation(out=rstd[:], in_=rstd[:], func=AF.Exp, scale=-0.5), reads=[rstd_key], writes=[rstd_key])

        def sumsq_hilo(src_ap, src_key, sqf, sqfk, hl, hlk, pt, pk):
            P.op("act", lambda e: e.activation(out=sqf[:], in_=src_ap, func=AF.Square), reads=[src_key], writes=[sqfk])
            P.op("act", lambda e: e.activation(out=hl[:, 0, :], in_=src_ap, func=AF.Square), reads=[src_key], writes=[hlk + ":0"])
            P.op("dve", lambda e: e.tensor_tensor(out=hl[:, 1, :], in0=sqf[:], in1=hl[:, 0, :], op=ALU.subtract), reads=[sqfk, hlk + ":0"], writes=[hlk + ":1"])
            P.op("pe", lambda e: e.matmul(pt[:], lhsT=onesb[:], rhs=hl[:, 0, :], start=True, stop=False), reads=[hlk + ":0", "onesb"], writes=[pk])
            P.op("pe", lambda e: e.matmul(pt[:], lhsT=onesb[:], rhs=hl[:, 1, :], start=False, stop=True), reads=[hlk + ":1", "onesb"], writes=[pk])

        def xkeys(g, t):
            return "X_%s:%d" % (g.name, t)

        def phase_A(l, g, w_in_sb):
            n = g.name
            j = g.cond
            nseg = NT // g.seg
            m0 = P.mark()
            xt = P.tile("A_xt", [128, KC, NT], F32)
            sqb = [(P.tile("A_sq%d" % i, [128, NT], F32), "A_sq%d" % i) for i in range(4)]
            sqr = [(P.tile("A_sqr%d" % i, [128, NT], BF16), "A_sqr%d" % i) for i in range(4)]
            hlb = [(P.tile("A_hl%d" % i, [128, 2, NT], BF16), "A_hl%d" % i) for i in range(4)]
            rstd = P.tile("A_rstd", [128, NT], F32)
            tmp = [(P.tile("A_tmp%d" % i, [128, NT], F32), "A_tmp%d" % i) for i in range(4)]
            hn = P.tile("A_hn", [128, KC, NT], BF16)
            pj = [(P.tile("A_pj%d" % i, [128, NT], F32), "A_pj%d" % i) for i in range(3)]
            qkv = P.tile("A_qkv", [128, 12, NT], F32)
            qkvb = P.tile("A_qkvb", [128, 8, NT], F32)
            gate = P.tile("A_gate", [128, 4, NT], F32)
            gb = P.tile("A_gb", [8, 2, NT], F32)
            zz = P.tile("A_zz", [128, 4, NT], F32)
            zzb = P.tile("A_zzb", [128, 4, NT], BF16)
            zzt = P.tile("A_zzt", [128, 4, 512], BF16)
            xsrc = I["x_" + n] if l == 0 else S["X_" + n]
            def load_xt(t):
                t0 = t * NT
                P.dma("A_xt", lambda e, t0=t0: e.dma_start(out=xt[:], in_=xsrc[:, :, t0:t0 + NT].rearrange("k p t -> p k t")),
                      reads=[xkeys(g, t)], writes=["A_xt"])

            load_xt(0)
            for t in range(g.ntiles):
                t0 = t * NT
                rms_stats(xt, "A_xt", KC, D, rstd, "A_rstd", sqr)
                for k in range(KC):
                    tb, tk = tmp[k % 4]
                    P.op("dve", lambda e, tb=tb, k=k: e.scalar_tensor_tensor(out=tb[:], in0=xt[:, k, :], scalar=gmod[:, l, j, 0, k:k + 1],
                                                                           in1=rstd[:], op0=ALU.mult, op1=ALU.mult),
                         reads=["A_xt", "gmod", "A_rstd"], writes=[tk])
                    P.op("act", lambda e, tb=tb, k=k: e.activation(out=hn[:, k, :], in_=tb[:], func=AF.Identity,
                                                                    bias=mods[:, l, j, 0 + k:0 + k + 1], scale=1.0),
                         reads=[tk, "mods"], writes=["A_hn:%d" % k])
                hn_keys = ["A_hn:%d" % k for k in range(KC)]
                if t + 1 < g.ntiles:
                    load_xt(t + 1)

                def proj(c0, ncols, pt, pk):
                    for k in range(KC):
                        P.op("pe", lambda e, k=k: e.matmul(pt[0:ncols, :], lhsT=w_in_sb[:, k, c0:c0 + ncols], rhs=hn[:, k, :],
                                                           start=(k == 0), stop=(k == KC - 1)),
                             reads=["w_in_sb", "A_hn:%d" % k], writes=[pk])

                def conv(dst, dkey, src, skey, wts, width, pt, pk):
                    pad = width // 2
                    d3 = dst.rearrange("p (s c) -> p s c", c=g.seg)
                    s3 = src.rearrange("p (s c) -> p s c", c=g.seg)
                    P.op("act", lambda e: e.activation(out=dst, in_=pt[:], func=AF.Copy, scale=wts[:, pad:pad + 1]),
                         reads=[pk, "gcw", "hcw"], writes=[dkey])
                    for jj in range(width):
                        o = jj - pad
                        if o == 0:
                            continue
                        lo_d, hi_d = max(0, -o), g.seg - max(0, o)
                        lo_s, hi_s = max(0, o), g.seg - max(0, -o)
                        P.op("dve", lambda e, jj=jj, lo_d=lo_d, hi_d=hi_d, lo_s=lo_s, hi_s=hi_s: e.scalar_tensor_tensor(
                            out=d3[:, :, lo_d:hi_d], in0=s3[:, :, lo_s:hi_s], scalar=wts[:, jj:jj + 1], in1=d3[:, :, lo_d:hi_d],
                            op0=ALU.mult, op1=ALU.add), reads=[skey, dkey, "gcw", "hcw"], writes=[dkey])

                for m in range(12):
                    pt, pk = next_ps()
                    proj(m * 128, 128, pt, pk)
                    pb, pbk = pj[m % 3]
                    P.op("act", lambda e, pt=pt, pb=pb: e.copy(out=pb[:], in_=pt[:]), reads=[pk], writes=[pbk])
                    conv(qkv[:, m, :], "A_qkv:%d" % m, pb[:], pbk, gcw[:, l, m, :], 5, pt, pk)
                gpts = []
                for m in range(4):
                    pt, pk = next_ps()
                    proj(1536 + m * 128, 128, pt, pk)
                    gpts.append((pt, pk))
                for m in range(12):
                    P.op("act", lambda e, m=m: e.activation(out=qkv[:, m, :], in_=qkv[:, m, :], func=AF.Silu),
                         reads=["A_qkv:%d" % m], writes=["A_qkv:%d" % m])
                for m in range(4):
                    pt, pk = gpts[m]
                    P.op("act", lambda e, m=m, pt=pt: e.activation(out=gate[:, m, :], in_=pt[:], func=AF.Silu), reads=[pk], writes=["A_gate"])
                P.dma("A_gate_o", lambda e, t0=t0: e.dma_start(out=S["GATE_" + n][:, :, t0:t0 + NT].rearrange("m p t -> p m t"), in_=gate[:]),
                      reads=["A_gate"], writes=["GATE_%s:%d" % (n, t)])
                pt, pk = next_ps()
                proj(OFF_B, 8, pt, pk)
                P.op("act", lambda e, pt=pt: e.activation(out=gb[:, 1, :], in_=pt[0:8, :], func=AF.Sigmoid), reads=[pk], writes=["A_gb:1"])
                pt, pk = next_ps()
                proj(OFF_A, 8, pt, pk)
                P.op("act", lambda e, pt=pt: e.activation(out=gb[:, 0, :], in_=pt[0:8, :], func=AF.Exp, bias=apar[:, l, 1:2], scale=1.0),
                     reads=[pk, "apar"], writes=["A_gb:0"])
                P.op("act", lambda e: e.activation(out=gb[:, 0, :], in_=gb[:, 0, :], func=AF.Ln, bias=1.0, scale=1.0), reads=["A_gb:0"], writes=["A_gb:0"])
                P.op("dve", lambda e: e.tensor_scalar(out=gb[:, 0, :], in0=gb[:, 0, :], scalar1=nega[:, l:l + 1], scalar2=None, op0=ALU.mult),
                     reads=["A_gb:0", "nega"], writes=["A_gb:0"])
                P.dma("A_gb_o", lambda e, t0=t0: e.dma_start(out=S["GB_" + n][:, t0:t0 + NT].rearrange("(a r) t -> r a t", a=2), in_=gb[:]),
                      reads=["A_gb:0", "A_gb:1"], writes=["GB_%s:%d" % (n, t)])
                for grp in range(2):
                    rn = {}
                    ms_ = range(grp * 4, grp * 4 + 4)
                    for m in ms_:
                        sq, sqk = sqb[m % 4]
                        hl, hlk = hlb[m % 4]
                        pt, pk = next_ps()
                        sumsq_hilo(qkv[:, m, :], "A_qkv:%d" % m, sq, sqk, hl, hlk, pt, pk)
                        rn[m] = (pt, pk)
                    for m in ms_:
                        sq, sqk = sqb[m % 4]
                        pt, pk = rn[m]
                        P.op("act", lambda e, pt=pt, sq=sq: e.activation(out=sq[:], in_=pt[:], func=AF.Ln, scale=1.0, bias=EPS), reads=[pk], writes=[sqk])
                    for m in ms_:
                        sq, sqk = sqb[m % 4]
                        P.op("act", lambda e, sq=sq: e.activation(out=sq[:], in_=sq[:], func=AF.Exp, scale=-0.5), reads=[sqk], writes=[sqk])
                    for m in ms_:
                        sq, sqk = sqb[m % 4]
                        sc = DK ** -0.5 if m < 4 else 1.0
                        P.op("dve", lambda e, m=m, sq=sq, sc=sc: e.scalar_tensor_tensor(out=qkvb[:, m, :], in0=qkv[:, m, :], scalar=sc, in1=sq[:],
                                                                                     op0=ALU.mult, op1=ALU.mult),
                             reads=["A_qkv:%d" % m, sqk], writes=["A_qkvb:%d" % m])
                P.dma("A_qkv_o", lambda e, t0=t0: e.dma_start(out=S["QKV_" + n][0:8, :, t0:t0 + NT].rearrange("m p t -> p m t"), in_=qkvb[:]),
                      reads=["A_qkvb:%d" % m for m in range(8)], writes=["QKV_%s:%d" % (n, t)])
                P.dma("A_v_o", lambda e, t0=t0: e.dma_start(out=S["QKV_" + n][8:12, :, t0:t0 + NT].rearrange("m p t -> p m t"), in_=qkv[:, 8:12, :]),
                      reads=["A_qkv:%d" % m for m in range(8, 12)], writes=["QKVv_%s:%d" % (n, t)])
                for m in range(12):
                    pt, pk = next_ps()
                    proj(OFF_HY + m * 128, 128, pt, pk)
                    pb, pbk = pj[m % 3]
                    P.op("act", lambda e, pt=pt, pb=pb: e.copy(out=pb[:], in_=pt[:]), reads=[pk], writes=[pbk])
                    conv(qkv[:, m, :], "A_qkv:%d" % m, pb[:], pbk, hcw[:, l, m, :], 3, pt, pk)
                P.dma("A_x0_o", lambda e, t0=t0: e.dma_start(out=S["X0_" + n][:, :, t0:t0 + NT].rearrange("m p t -> p m t"), in_=qkv[:, 0:4, :]),
                      reads=["A_qkv:%d" % m for m in range(4)], writes=["X0_%s:%d" % (n, t)])
                for c in range(4):
                    P.op("dve", lambda e, c=c: e.tensor_tensor(out=zz[:, c, :], in0=qkv[:, 4 + c, :], in1=qkv[:, 8 + c, :], op=ALU.mult),
                         reads=["A_qkv:%d" % (4 + c), "A_qkv:%d" % (8 + c)], writes=["A_zz:%d" % c])
                    P.op("pool", lambda e, c=c: e.tensor_copy(out=zzb[:, c, :], in_=zz[:, c, :]), reads=["A_zz:%d" % c], writes=["A_zzb:%d" % c])
                P.dma("A_zz_o", lambda e, t0=t0: e.dma_start(out=S["ZZ_" + n][:, :, t0:t0 + NT].rearrange("m p t -> p m t"), in_=zz[:]),
                      reads=["A_zz:%d" % c for c in range(4)], writes=["ZZ_%s:%d" % (n, t)])
                for tb4 in range(4):
                    pt, pk = next_ps()
                    ptb = pt[:].bitcast(BF16)
                    for c in range(4):
                        P.op("pe", lambda e, c=c, tb4=tb4, ptb=ptb: e.transpose(out=ptb[:, c * 128:(c + 1) * 128],
                                                                             in_=zzb[:, c, tb4 * 128:(tb4 + 1) * 128], identity=identb[:]),
                             reads=["A_zzb:%d" % c, "identb"], writes=[pk])
                    P.op("act", lambda e, tb4=tb4, ptb=ptb: e.copy(out=zzt[:, tb4, :], in_=ptb[:, 0:512]), reads=[pk], writes=["A_zzt:%d" % tb4])
                P.dma("A_zzt_o", lambda e, t0=t0: e.dma_start(out=S["ZZT_" + n][t0:t0 + NT, :].rearrange("(b p) c -> p b c", p=128), in_=zzt[:]),
                      reads=["A_zzt:%d" % b for b in range(4)], writes=["ZZT_%s:%d" % (n, t)])
            P.release(m0)

        def phase_H(l, g):
            n = g.name
            L, nTB, NF, NFB = g.L, g.nTB, g.NF, g.NFB
            ctab, stab = I["ctab_" + n], I["stab_" + n]
            m0 = P.mark()
            hsd = P.tile("H_hsd", [128, 2, nTB, 512], BF16)
            m1 = P.mark()
            w1 = P.tile("H_w1", [HY_EMB, HY_FH], F32)
            w2 = P.tile("H_w2", [HY_FH, HY_FH], F32)
            w3e = P.tile("H_w3e", [HY_FH + 1, 1024], F32)
            zf = P.tile("H_zf", [HY_EMB, L], F32)
            negt = P.tile("H_negt", [128, nTB], F32)
            h1 = P.tile("H_h1", [HY_FH, L], F32)
            h2e = P.tile("H_h2e", [HY_FH + 1, L], F32)
            ld("H_w1", w1[:], I["hw1"][l], "H_w1")
            ld("H_w2", w2[:], I["hw2"][l], "H_w2")
            ld("H_w3e", w3e[:], I["hw3e"][l], "H_w3e")
            ld("H_zf", zf[:], I["zf_" + n], "H_zf")
            ld("H_negt", negt[:], I["negt_" + n], "H_negt")
            CW = min(512, L)
            ua = P.tile("H_ua", [HY_FH, CW], F32)
            ui = P.tile("H_ui", [HY_FH, CW], I32)
            uf = P.tile("H_uf", [HY_FH, CW], F32)

            def sin_layer(wt, wkey, kdim, src, skey, bcol, dst, dkey):
                for c0 in range(0, L, CW):
                    pt, pk = next_ps()
                    P.op("pe", lambda e, c0=c0, pt=pt: e.matmul(pt[0:HY_FH, 0:CW], lhsT=wt[0:kdim, :], rhs=src[0:kdim, c0:c0 + CW], start=True, stop=True),
                         reads=[wkey, skey], writes=[pk])
                    P.op("dve", lambda e, pt=pt: e.tensor_scalar(out=ua[:], in0=pt[0:HY_FH, 0:CW], scalar1=hvec[:, l, bcol:bcol + 1],
                                                                scalar2=hf2p[:, l:l + 1], op0=ALU.add, op1=ALU.mult),
                         reads=[pk, "hvec", "hf2p"], writes=["H_ua"])
                    P.op("dve", lambda e: e.tensor_scalar(out=ua[:], in0=ua[:], scalar1=8.5, scalar2=None, op0=ALU.add), reads=["H_ua"], writes=["H_ua"])
                    P.op("dve", lambda e: e.tensor_copy(out=ui[:], in_=ua[:]), reads=["H_ua"], writes=["H_ui"])
                    P.op("dve", lambda e: e.tensor_copy(out=uf[:], in_=ui[:]), reads=["H_ui"], writes=["H_uf"])
                    P.op("dve", lambda e: e.tensor_tensor(out=ua[:], in0=ua[:], in1=uf[:], op=ALU.subtract), reads=["H_ua", "H_uf"], writes=["H_ua"])
                    P.op("dve", lambda e: e.tensor_scalar(out=uf[:], in0=ua[:], scalar1=0.5, scalar2=None, op0=ALU.is_gt), reads=["H_ua"], writes=["H_uf"])
                    P.op("dve", lambda e: e.tensor_tensor(out=ua[:], in0=ua[:], in1=uf[:], op=ALU.subtract), reads=["H_ua", "H_uf"], writes=["H_ua"])
                    P.op("act", lambda e, c0=c0: e.activation(out=dst[0:HY_FH, c0:c0 + CW], in_=ua[:], func=AF.Sin, scale=-2 * math.pi),
                         reads=["H_ua"], writes=[dkey])

            sin_layer(w1, "H_w1", HY_EMB, zf, "H_zf", 0, h1, "H_h1")
            P.op("pool", lambda e: e.memset(h2e[64:65, :], 1.0), writes=["H_h2e"])
            sin_layer(w2, "H_w2", HY_FH, h1, "H_h1", 2, h2e, "H_h2e")
            win = P.tile("H_win", [128, 512], F32)
            hfb = [(P.tile("H_hf%d" % i, [128, 512], F32), "H_hf%d" % i) for i in range(2)]
            for tb in range(nTB):
                P.op("act", lambda e, tb=tb: e.activation(out=win[:], in_=deltab[:], func=AF.Exp, scale=negt[:, tb:tb + 1]),
                     reads=["deltab", "H_negt"], writes=["H_win"])
                for d in range(2):
                    pt, pk = next_ps()
                    P.op("pe", lambda e, tb=tb, d=d, pt=pt: e.matmul(pt[:], lhsT=h2e[:, tb * 128:(tb + 1) * 128], rhs=w3e[:, d * 512:(d + 1) * 512],
                                                                     start=True, stop=True), reads=["H_h2e", "H_w3e"], writes=[pk])
                    hb, hk = hfb[d]
                    P.op("dve", lambda e, pt=pt, hb=hb: e.tensor_tensor(out=hb[:], in0=pt[:], in1=win[:], op=ALU.mult), reads=[pk, "H_win"], writes=[hk])
                if tb == 0:
                    P.op("pool", lambda e: e.memset(hfb[1][0][0:1, :], 0.0), reads=[hfb[1][1]], writes=[hfb[1][1]])
                P.op("dve", lambda e, tb=tb: e.tensor_tensor(out=hsd[:, 0, tb, :], in0=hfb[0][0][:], in1=hfb[1][0][:], op=ALU.add),
                     reads=[hfb[0][1], hfb[1][1]], writes=["H_hsd"])
                P.op("pool", lambda e, tb=tb: e.tensor_tensor(out=hsd[:, 1, tb, :], in0=hfb[0][0][:], in1=hfb[1][0][:], op=ALU.subtract),
                     reads=[hfb[0][1], hfb[1][1]], writes=["H_hsd"])
            P.release(m1)

            ctabF, stabF = I["ctabF_" + n], I["stabF_" + n]

            def fwd_slabs():
                sl = []
                for i in range(2):
                    sl.append((P.tile("H_cs%d" % i, [128, nTB, 128], BF16), "H_cs%d" % i, P.tile("H_ss%d" % i, [128, nTB, 128], BF16), "H_ss%d" % i))
                return sl

            def load_fwd_slab(sl, fb):
                cs, ck, ss, sk = sl[fb % 2]
                P.dma(ck, lambda e: e.dma_start(out=cs[:], in_=ctabF[fb, :, 0:nTB, :]), writes=[ck])
                P.dma(sk, lambda e: e.dma_start(out=ss[:], in_=stabF[fb, :, 0:nTB, :]), writes=[sk])
                return cs, ck, ss, sk

            m1 = P.mark()
            wcol = P.tile("H_wcol", [128, NFB], F32)
            ld("H_wcol", wcol[:], I["wcol_" + n], "H_wcol")
            sl = fwd_slabs()
            fst = [(P.tile("H_fst%d" % i, [128, 2, 512], F32), "H_fst%d" % i) for i in range(2)]
            for fb in range(NFB):
                cs, ck, ss, sk = load_fwd_slab(sl, fb)
                fs, fk = fst[fb % 2]
                for which, (slab, slk) in enumerate(((cs, ck), (ss, sk))):
                    pt, pk = next_ps()
                    for tb in range(nTB):
                        P.op("pe", lambda e, tb=tb, pt=pt, slab=slab, which=which: e.matmul(
                            pt[:], lhsT=slab[:, tb, :], rhs=hsd[:, which, tb, :], start=(tb == 0), stop=(tb == nTB - 1)),
                            reads=["H_hsd", slk], writes=[pk])
                    P.op("act", lambda e, pt=pt, fs=fs, which=which, fb=fb: e.activation(out=fs[:, which, :], in_=pt[:], func=AF.Copy, scale=wcol[:, fb:fb + 1]),
                         reads=[pk, "H_wcol"], writes=[fk])
                P.dma(fk + "o", lambda e, fs=fs, fb=fb: e.dma_start(out=S["SPEC_" + n][:, fb, :, :].rearrange("w p c -> p w c"), in_=fs[:]),
                      reads=[fk], writes=["SPEC_%s:%d" % (n, fb)])
            P.release(m1)
            P.release(m0)

            for sq_i in range(g.nseq):
                s0 = sq_i * L
                m0 = P.mark()
                yf = P.tile("H_yf", [128, 2 * NFB, 512], BF16)
                m1 = P.mark()
                zzt = P.tile("H_zzt", [128, nTB, 512], BF16)
                tiles_touched = sorted(set((s0 + i * 128) // NT for i in range(nTB)))
                P.dma("H_zzt", lambda e, s0=s0, zzt=zzt: e.dma_start(out=zzt[:], in_=S["ZZT_" + n][s0:s0 + L, :].rearrange("(b p) c -> p b c", p=128)),
                      reads=["ZZT_%s:%d" % (n, t) for t in tiles_touched], writes=["H_zzt"])
                sl = fwd_slabs()
                fsl = [(P.tile("H_fsl%d" % i, [128, 2, 512], F32), "H_fsl%d" % i) for i in range(2)]
                t1 = [(P.tile("H_t1%d" % i, [128, 512], F32), "H_t1%d" % i) for i in range(4)]
                for fb in range(NFB):
                    cs, ck, ss, sk = load_fwd_slab(sl, fb)
                    fs, fk = fsl[fb % 2]
                    P.dma(fk, lambda e, fs=fs, fb=fb: e.dma_start(out=fs[:], in_=S["SPEC_" + n][:, fb, :, :].rearrange("w p c -> p w c")),
                          reads=["SPEC_%s:%d" % (n, fb)], writes=[fk])
                    zps = []
                    for slab, slk in ((cs, ck), (ss, sk)):
                        pt, pk = next_ps()
                        for tb in range(nTB):
                            P.op("pe", lambda e, tb=tb, pt=pt, slab=slab, zzt=zzt: e.matmul(
                                pt[:], lhsT=slab[:, tb, :], rhs=zzt[:, tb, :], start=(tb == 0), stop=(tb == nTB - 1)),
                                reads=["H_zzt", slk], writes=[pk])
                        zps.append((pt, pk))
                    (zc, zck), (zs, zsk) = zps
                    a, ak = t1[0]
                    b, bk = t1[1]
                    c_, c_k = t1[2]
                    d_, d_k = t1[3]
                    P.op("dve", lambda e, zc=zc, fs=fs, a=a: e.tensor_tensor(out=a[:], in0=zc[:], in1=fs[:, 0, :], op=ALU.mult), reads=[zck, fk], writes=[ak])
                    P.op("dve", lambda e, zs=zs, fs=fs, b=b: e.tensor_tensor(out=b[:], in0=zs[:], in1=fs[:, 1, :], op=ALU.mult), reads=[zsk, fk], writes=[bk])
                    P.op("dve", lambda e, zc=zc, fs=fs, c_=c_: e.tensor_tensor(out=c_[:], in0=zc[:], in1=fs[:, 1, :], op=ALU.mult), reads=[zck, fk], writes=[c_k])
                    P.op("dve", lambda e, zs=zs, fs=fs, d_=d_: e.tensor_tensor(out=d_[:], in0=zs[:], in1=fs[:, 0, :], op=ALU.mult), reads=[zsk, fk], writes=[d_k])
                    P.op("pool", lambda e, a=a, b=b, fb=fb, yf=yf: e.tensor_tensor(out=yf[:, fb, :], in0=a[:], in1=b[:], op=ALU.subtract), reads=[ak, bk], writes=["H_yf"])
                    P.op("pool", lambda e, c_=c_, d_=d_, fb=fb, yf=yf: e.tensor_tensor(out=yf[:, NFB + fb, :], in0=c_[:], in1=d_[:], op=ALU.add), reads=[c_k, d_k], writes=["H_yf"])
                P.release(m1)
                TWg = min(TW, L)
                isl = [(P.tile("H_ci%d" % i, [128, NFB, TWg], BF16), "H_ci%d" % i, P.tile("H_si%d" % i, [128, NFB, TWg], BF16), "H_si%d" % i) for i in range(2)]
                x0b = [(P.tile("H_x0%d" % i, [128, TWg], F32), "H_x0%d" % i) for i in range(2)]
                zzb_ = [(P.tile("H_zb%d" % i, [128, TWg], F32), "H_zb%d" % i) for i in range(2)]
                yo = [(P.tile("H_yo%d" % i, [128, TWg], BF16), "H_yo%d" % i) for i in range(2)]
                tm = [(P.tile("H_tm%d" % i, [128, TWg], F32), "H_tm%d" % i) for i in range(2)]
                it = 0
                for ti in range(L // TWg):
                    tt0 = ti * TWg
                    ci, cik, si, sik = isl[ti % 2]
                    P.dma(cik, lambda e, ci=ci, ti=ti: e.dma_start(out=ci[:], in_=ctab[ti, :, 0:NFB, :]), writes=[cik])
                    P.dma(sik, lambda e, si=si, ti=ti: e.dma_start(out=si[:], in_=stab[ti, :, 0:NFB, :]), writes=[sik])
                    gt0 = s0 + tt0
                    tile_i = gt0 // NT
                    for cc in range(4):
                        xb, xk = x0b[it % 2]
                        zb, zk = zzb_[it % 2]
                        yb, yk = yo[it % 2]
                        tmb, tmk = tm[it % 2]
                        it += 1
                        P.dma(xk, lambda e, xb=xb, cc=cc, gt0=gt0: e.dma_start(out=xb[:], in_=S["X0_" + n][cc, :, gt0:gt0 + TWg]),
                              reads=["X0_%s:%d" % (n, tile_i)], writes=[xk])
                        P.dma(zk, lambda e, zb=zb, cc=cc, gt0=gt0: e.dma_start(out=zb[:], in_=S["ZZ_" + n][cc, :, gt0:gt0 + TWg]),
                              reads=["ZZ_%s:%d" % (n, tile_i)], writes=[zk])
                        pt, pk = next_ps()
                        nmm = 2 * NFB
                        i_mm = 0
                        for w, (slab, slk) in enumerate(((ci, cik), (si, sik))):
                            for fb in range(NFB):
                                P.op("pe", lambda e, pt=pt, w=w, fb=fb, slab=slab, cc=cc, i_mm=i_mm: e.matmul(
                                    pt[:, 0:TWg], lhsT=yf[:, w * NFB + fb, cc * 128:(cc + 1) * 128], rhs=slab[:, fb, :],
                                    start=(i_mm == 0), stop=(i_mm == nmm - 1)), reads=["H_yf", slk], writes=[pk])
                                i_mm += 1
                        P.op("dve", lambda e, pt=pt, zb=zb, tmb=tmb, cc=cc: e.scalar_tensor_tensor(
                            out=tmb[:], in0=zb[:], scalar=hskip[:, l, cc:cc + 1], in1=pt[:, 0:TWg], op0=ALU.mult, op1=ALU.add),
                            reads=[zk, pk, "hskip"], writes=[tmk])
                        P.op("pool", lambda e, tmb=tmb, xb=xb, yb=yb: e.tensor_tensor(out=yb[:], in0=tmb[:], in1=xb[:], op=ALU.mult), reads=[tmk, xk], writes=[yk])
                        P.dma(yk + "o", lambda e, yb=yb, cc=cc, gt0=gt0: e.dma_start(out=S["YH_" + n][cc, :, gt0:gt0 + TWg], in_=yb[:]),
                              reads=[yk], writes=["YH_%s:%d:%d:%d" % (n, tile_i, cc, (gt0 % NT) // TWg)])
                P.release(m0)

        def phase_G(l, g):
            n = g.name
            m0 = P.mark()
            NCH = NT // CH
            Sst = [[(P.tile("G_S%d%d" % (d, h), [128, 128], F32), "G_S%d%d" % (d, h)) for h in range(H)] for d in range(2)]
            Sbf = [[(P.tile("G_Sb%d%d" % (d, h), [128, 128], BF16), "G_Sb%d%d" % (d, h)) for h in range(H)] for d in range(2)]

            NP = NCH // 2

            def dir_tiles(d):
                p = "G_"
                W = {}
                W["qkv"] = P.tile(p + "qkv", [128, 12, NT], F32)
                W["gbr"] = P.tile(p + "gbr", [8, 2, NT], F32)
                W["gc"] = P.tile(p + "gc", [8, NT], F32)
                W["gtot"] = P.tile(p + "gtot", [8, NCH], F32)
                W["gcs"] = P.tile(p + "gcs", [8, 3, NT], BF16)
                W["bts"] = P.tile(p + "bts", [8, 3, NT], BF16)
                W["gcb"] = P.tile(p + "gcb", [128, NT], F32)
                W["btb"] = P.tile(p + "btb", [128, NT], F32)
                W["E"] = P.tile(p + "E", [128, NT], F32)
                W["tmp"] = P.tile(p + "tmp", [128, NT], F32)
                W["gcT"] = P.tile(p + "gcT", [128, NP], F32)
                W["btT"] = P.tile(p + "btT", [128, NP], F32)
                W["sc"] = P.tile(p + "sc", [128, 3, NP], F32)
                W["eglh"] = P.tile(p + "eglh", [128, H, NCH], F32)
                W["DT"] = P.tile(p + "DT", [128, NT], F32)
                W["XB"] = P.tile(p + "XB", [128, NT], F32)
                for h in range(H):
                    W["qd%d" % h] = P.tile(p + "qd%d" % h, [128, NT], BF16)
                    W["wT%d" % h] = P.tile(p + "wT%d" % h, [128, NT], F32)
                    W["kbg%d" % h] = P.tile(p + "kbg%d" % h, [128, NP, 128], F32)
                    W["kdec%d" % h] = P.tile(p + "kdec%d" % h, [128, NP, 128], BF16)
                    W["Xf%d" % h] = P.tile(p + "Xf%d" % h, [128, NP, 128], F32)
                    W["Xtf%d" % h] = P.tile(p + "Xtf%d" % h, [128, NP, 128], F32)
                    W["vbf%d" % h] = P.tile(p + "vbf%d" % h, [128, NP, 128], F32)
                    W["X%d" % h] = P.tile(p + "X%d" % h, [128, 2, NP, 128], BF16)
                    W["Xt%d" % h] = P.tile(p + "Xt%d" % h, [128, 2, NP, 128], BF16)
                    W["R%d" % h] = P.tile(p + "R%d" % h, [128, 2, NP, 128], BF16)
                    W["Rf%d" % h] = P.tile(p + "Rf%d" % h, [128, NP, 128], F32)
                    W["qkT%d" % h] = P.tile(p + "qkT%d" % h, [128, NP, 128], BF16)
                    W["u%d" % h] = P.tile(p + "u%d" % h, [128, NP, 128], F32)
                    W["vn%d" % h] = P.tile(p + "vn%d" % h, [128, 2, 128], BF16)
                    W["oT%d" % h] = P.tile(p + "oT%d" % h, [128, NT], F32)
                W["E2"] = W["X0"][:].bitcast(F32).rearrange("p a c i -> p (a c i)").rearrange("p (c i) -> p c i", i=128)
                W["RfT"] = W["Xt0"][:].bitcast(F32).rearrange("p a c i -> p (a c i)").rearrange("p (c i) -> p c i", i=128)
                W["p"] = p
                return W

            W0 = dir_tiles(0)
            Ws = [W0, W0]

            for d in range(2):
                for h in range(H):
                    St, Sk = Sst[d][h]
                    if g.has_s0:
                        P.dma(Sk, lambda e, St=St, d=d, h=h: e.dma_start(out=St[:], in_=I["s0"][l, d, h]), writes=[Sk])
                    else:
                        P.op("pool", lambda e, St=St: e.memset(St[:], 0.0), writes=[Sk])
                    Sb, Sbk = Sbf[d][h]
                    P.op("act", lambda e, St=St, Sb=Sb: e.copy(out=Sb[:], in_=St[:]), reads=[Sk], writes=[Sbk])

            def load_tile(d, t):
                W = Ws[d]
                p = W["p"]
                t0 = t * NT
                P.dma(p + "qkv", lambda e: e.dma_start(out=W["qkv"][:], in_=S["QKV_" + n][:, :, t0:t0 + NT].rearrange("m p t -> p m t")),
                      reads=["QKV_%s:%d" % (n, t), "QKVv_%s:%d" % (n, t)], writes=[p + "qkv"])
                P.dma(p + "gbr", lambda e: e.dma_start(out=W["gbr"][:], in_=S["GB_" + n][:, t0:t0 + NT].rearrange("(a r) t -> r a t", a=2)),
                      reads=["GB_%s:%d" % (n, t)], writes=[p + "gbr"])

            def v4(ap):
                return ap.rearrange("p (q k) -> p q k", k=128)

            def chunk_local(d, t):
                W = Ws[d]
                p = W["p"]
                mi, ms = (0, 2) if d == 0 else (1, 3)
                last = CH - 1 if d == 0 else 0
                P.op("dve", lambda e: e.tensor_tensor_scan(out=W["gc"][:], data0=scanmask[:], data1=W["gbr"][:, 0, :], initial=0.0, op0=ALU.mult, op1=ALU.add),
                     reads=[p + "gbr", "scanmask"], writes=[p + "gc"])
                if d == 1:
                    gc3 = W["gc"][:].rearrange("r (c i) -> r c i", i=CH)
                    P.op("dve", lambda e: e.tensor_copy(out=W["gtot"][:], in_=gc3[:, :, CH - 1]), reads=[p + "gc"], writes=[p + "gtot"])
                    P.op("dve", lambda e: e.tensor_tensor(out=W["gc"][:], in0=W["gbr"][:, 0, :], in1=W["gc"][:], op=ALU.subtract),
                         reads=[p + "gbr", p + "gc"], writes=[p + "gc"])
                    P.op("dve", lambda e: e.tensor_tensor(out=gc3, in0=gc3, in1=W["gtot"][:].unsqueeze(2).to_broadcast([8, NCH, CH]), op=ALU.add),
                         reads=[p + "gc", p + "gtot"], writes=[p + "gc"])

                def split3(src_ap, skey, dst, dkey):
                    sp0, sp1 = W["tmp"][0:8, :], W["DT"][0:8, :]
                    P.op("dve", lambda e: e.tensor_copy(out=dst[:, 0, :], in_=src_ap), reads=[skey], writes=[dkey])
                    P.op("dve", lambda e: e.tensor_tensor(out=sp0, in0=src_ap, in1=dst[:, 0, :], op=ALU.subtract), reads=[skey, dkey], writes=[p + "tmp"])
                    P.op("dve", lambda e: e.tensor_copy(out=dst[:, 1, :], in_=sp0), reads=[p + "tmp"], writes=[dkey])
                    P.op("dve", lambda e: e.tensor_tensor(out=sp1, in0=sp0, in1=dst[:, 1, :], op=ALU.subtract), reads=[p + "tmp", dkey], writes=[p + "DT"])
                    P.op("dve", lambda e: e.tensor_copy(out=dst[:, 2, :], in_=sp1), reads=[p + "DT"], writes=[dkey])

                split3(W["gc"][:], p + "gc", W["gcs"], p + "gcs")
                split3(W["gbr"][:, 1, :], p + "gbr", W["bts"], p + "bts")

                def bcast(pt_ap, pk_, r_, src, skey):
                    for i3 in range(3):
                        P.op("pe", lambda e, i3=i3: e.matmul(pt_ap, lhsT=selb[:, r_, :], rhs=src[:, i3, :], start=(i3 == 0), stop=(i3 == 2)),
                             reads=["selb", skey], writes=[pk_])

                def xt_part(h):
                    Xh, Xth, Rh = W["X%d" % h], W["Xt%d" % h], W["R%d" % h]
                    kX, kXt, kR = p + "X%d" % h, p + "Xt%d" % h, p + "R%d" % h
                    Xf, Xtf = W["Xf%d" % h], W["Xtf%d" % h]
                    ptt, pkt = next_ps()
                    for q in range(NP):
                        P.op("pe", lambda e, q=q, ptt=ptt, Xf=Xf: e.transpose(out=ptt[:, q * 128:(q + 1) * 128], in_=Xf[:, q, :], identity=ident[:]),
                             reads=[p + "Xf%d" % h, "ident"], writes=[pkt])
                    P.op("act", lambda e, ptt=ptt, Xtf=Xtf: e.copy(out=Xtf[:], in_=v4(ptt[:])), reads=[pkt], writes=[p + "Xtf%d" % h])
                    P.op("pool", lambda e, Xth=Xth, Xtf=Xtf: e.tensor_copy(out=Xth[:, 0, :, :], in_=Xtf[:]), reads=[p + "Xtf%d" % h], writes=[kXt + ":0"])
                    P.op("pool", lambda e, Xh=Xh, Rh=Rh: e.tensor_copy(out=Rh[:, 0, :, :], in_=Xh[:, 0, :, :]), reads=[kX + ":0"], writes=[kR + ":0"])
                    P.op("pool", lambda e, Xf=Xf, h=h: e.tensor_copy(out=W["Rf%d" % h][:], in_=Xf[:]), reads=[p + "Xf%d" % h], writes=[p + "Rf%d" % h])

                gcb4, btb4, tmp4, DT4, XB4 = v4(W["gcb"][:]), v4(W["btb"][:]), v4(W["tmp"][:]), v4(W["DT"][:]), v4(W["XB"][:])
                eye_b = ident[:].unsqueeze(1).to_broadcast([128, NP, 128])
                gcbc3 = W["gcb"][:].rearrange("p (c i) -> p c i", i=CH)
                for h in range(H):
                    r = d * 4 + h
                    qT = W["qkv"][:, h, :]
                    kT = W["qkv"][:, 4 + h, :]
                    vT = W["qkv"][:, 8 + h, :]
                    pt, pk = next_ps()
                    bcast(pt[:], pk, r, W["gcs"], p + "gcs")
                    P.op("act", lambda e, pt=pt: e.copy(out=W["gcb"][:], in_=pt[:]), reads=[pk], writes=[p + "gcb"])
                    P.op("act", lambda e, pt=pt: e.activation(out=W["E"][:], in_=pt[:], func=AF.Exp), reads=[pk], writes=[p + "E"])
                    pt2, pk2 = next_ps()
                    bcast(pt2[:], pk2, r, W["bts"], p + "bts")
                    P.op("act", lambda e, pt2=pt2: e.copy(out=W["btb"][:], in_=pt2[:]), reads=[pk2], writes=[p + "btb"])
                    P.op("dve", lambda e: e.tensor_tensor(out=tmp4, in0=gcb4, in1=eye_b, op=ALU.mult), reads=[p + "gcb", "ident"], writes=[p + "tmp"])
                    P.op("dve", lambda e: e.tensor_reduce(out=W["gcT"][:], in_=tmp4, axis=AX.X, op=ALU.add), reads=[p + "tmp"], writes=[p + "gcT"])
                    P.op("dve", lambda e: e.tensor_tensor(out=tmp4, in0=btb4, in1=eye_b, op=ALU.mult), reads=[p + "btb", "ident"], writes=[p + "tmp"])
                    P.op("dve", lambda e: e.tensor_reduce(out=W["btT"][:], in_=tmp4, axis=AX.X, op=ALU.add), reads=[p + "tmp"], writes=[p + "btT"])
                    P.op("act", lambda e: e.activation(out=W["sc"][:, 2, :], in_=W["gcT"][:], func=AF.Exp), reads=[p + "gcT"], writes=[p + "sc:2"])
                    P.op("dve", lambda e: e.tensor_tensor(out=W["sc"][:, 0, :], in0=W["sc"][:, 2, :], in1=W["btT"][:], op=ALU.mult),
                         reads=[p + "sc:2", p + "btT"], writes=[p + "sc:0"])
                    for c2 in range(2):
                        ps_ = slice(c2 * 64, (c2 + 1) * 64)
                        P.op("dve", lambda e, ps_=ps_, c2=c2: e.tensor_tensor(out=W["sc"][ps_, 1, :], in0=gcb4[ps_, :, c2 * 64 + last], in1=W["gcT"][ps_, :], op=ALU.subtract),
                             reads=[p + "gcb", p + "gcT"], writes=[p + "sc:1"])
                    P.op("act", lambda e: e.activation(out=W["sc"][:, 1, :], in_=W["sc"][:, 1, :], func=AF.Exp), reads=[p + "sc:1"], writes=[p + "sc:1"])
                    P.op("act", lambda e, h=h: e.activation(out=W["eglh"][:, h, :], in_=gcbc3[:, :, last], func=AF.Exp), reads=[p + "gcb"], writes=[p + "eglh"])
                    P.op("dve", lambda e: e.tensor_tensor(out=tmp4, in0=gcb4, in1=W["gcT"][:].unsqueeze(2).to_broadcast([128, NP, 128]), op=ALU.subtract),
                         reads=[p + "gcb", p + "gcT"], writes=[p + "tmp"])
                    P.op("dve", lambda e: e.tensor_scalar(out=W["tmp"][:], in0=W["tmp"][:], scalar1=0.0, scalar2=None, op0=ALU.min), reads=[p + "tmp"], writes=[p + "tmp"])
                    P.op("act", lambda e: e.activation(out=W["tmp"][:], in_=W["tmp"][:], func=AF.Exp), reads=[p + "tmp"], writes=[p + "tmp"])
                    P.op("dve", lambda e: e.tensor_tensor(out=DT4, in0=tmp4, in1=masks2[:, mi, :].unsqueeze(1).to_broadcast([128, NP, 128]), op=ALU.mult),
                         reads=[p + "tmp", "masks2"], writes=[p + "DT"])
                    P.op("pool", lambda e: e.tensor_tensor(out=XB4, in0=DT4, in1=masks2[:, ms, :].unsqueeze(1).to_broadcast([128, NP, 128]), op=ALU.mult),
                         reads=[p + "DT", "masks2"], writes=[p + "XB"])
                    P.op("dve", lambda e: e.tensor_tensor(out=W["XB"][:], in0=W["XB"][:], in1=W["btb"][:], op=ALU.mult), reads=[p + "XB", p + "btb"], writes=[p + "XB"])
                    P.op("dve", lambda e, h=h, qT=qT: e.tensor_tensor(out=W["qd%d" % h][:], in0=qT, in1=W["E"][:], op=ALU.mult),
                         reads=[p + "qkv", p + "E"], writes=[p + "qd%d" % h])
                    P.op("pool", lambda e, h=h, kT=kT: e.tensor_tensor(out=W["wT%d" % h][:], in0=kT, in1=W["E"][:], op=ALU.mult),
                         reads=[p + "qkv", p + "E"], writes=[p + "wT%d" % h])
                    P.op("pool", lambda e, h=h: e.tensor_tensor(out=W["wT%d" % h][:], in0=W["wT%d" % h][:], in1=W["btb"][:], op=ALU.mult),
                         reads=[p + "btb", p + "wT%d" % h], writes=[p + "wT%d" % h])
                    for kind, src in ((0, kT), (1, vT)):
                        pt4, pk4 = next_ps()
                        for q in range(NP):
                            P.op("pe", lambda e, pt4=pt4, q=q, src=src: e.transpose(out=pt4[:, q * 128:(q + 1) * 128], in_=src[:, q * 128:(q + 1) * 128], identity=ident[:]),
                                 reads=[p + "qkv", "ident"], writes=[pk4])
                        src3 = v4(pt4[:])
                        if kind == 0:
                            P.op("dve", lambda e, src3=src3, h=h: e.tensor_tensor(
                                out=W["kbg%d" % h][:], in0=src3, in1=W["sc"][:, 0, :].unsqueeze(2).to_broadcast([128, NP, 128]), op=ALU.mult),
                                reads=[pk4, p + "sc:0"], writes=[p + "kbg%d" % h])
                            P.op("dve", lambda e, src3=src3, h=h: e.tensor_tensor(
                                out=W["kdec%d" % h][:], in0=src3, in1=W["sc"][:, 1, :].unsqueeze(2).to_broadcast([128, NP, 128]), op=ALU.mult),
                                reads=[pk4, p + "sc:1"], writes=[p + "kdec%d" % h])
                        else:
                            P.op("dve", lambda e, src3=src3, h=h: e.tensor_tensor(
                                out=W["vbf%d" % h][:], in0=src3, in1=W["btT"][:].unsqueeze(2).to_broadcast([128, NP, 128]), op=ALU.mult),
                                reads=[pk4, p + "btT"], writes=[p + "vbf%d" % h])
                    ptk, pkk = next_ps()
                    ptq, pkq = next_ps()
                    for q in range(NP):
                        qs = slice(q * 128, (q + 1) * 128)
                        P.op("pe", lambda e, qs=qs, ptk=ptk, kT=kT: e.matmul(ptk[:, qs], lhsT=kT[:, qs], rhs=kT[:, qs], start=True, stop=True), reads=[p + "qkv"], writes=[pkk])
                        P.op("pe", lambda e, qs=qs, ptq=ptq, kT=kT, qT=qT: e.matmul(ptq[:, qs], lhsT=kT[:, qs], rhs=qT[:, qs], start=True, stop=True), reads=[p + "qkv"], writes=[pkq])
                    Xh = W["X%d" % h]
                    kX = p + "X%d" % h
                    Xf = W["Xf%d" % h]
                    P.op("dve", lambda e, ptk=ptk, Xf=Xf: e.tensor_tensor(out=Xf[:], in0=v4(ptk[:]), in1=XB4, op=ALU.mult), reads=[pkk, p + "XB"], writes=[p + "Xf%d" % h])
                    P.op("dve", lambda e, ptq=ptq, h=h: e.tensor_tensor(out=W["qkT%d" % h][:], in0=v4(ptq[:]), in1=DT4, op=ALU.mult), reads=[pkq, p + "DT"], writes=[p + "qkT%d" % h])
                    P.op("pool", lambda e, Xh=Xh, Xf=Xf: e.tensor_copy(out=Xh[:, 0, :, :], in_=Xf[:]), reads=[p + "Xf%d" % h], writes=[kX + ":0"])
                    if h > 0:
                        xt_part(h - 1)
                xt_part(H - 1)
                for it in range(5):
                    a, b = it % 2, (it + 1) % 2
                    for h in range(H):
                        Xh, Xth = W["X%d" % h], W["Xt%d" % h]
                        kX, kXt = p + "X%d" % h, p + "Xt%d" % h
                        pa, pka = next_ps()
                        for q in range(NP):
                            P.op("pe", lambda e, q=q, pa=pa, Xh=Xh, Xth=Xth, a=a: e.matmul(pa[:, q * 128:(q + 1) * 128], lhsT=Xth[:, a, q, :], rhs=Xh[:, a, q, :], start=True, stop=True),
                                 reads=[kX + ":%d" % a, kXt + ":%d" % a], writes=[pka])
                        P.op("act", lambda e, pa=pa, Xh=Xh, b=b: e.copy(out=Xh[:, b, :, :], in_=v4(pa[:])), reads=[pka], writes=[kX + ":%d" % b])
                        pb_, pkb = next_ps()
                        for q in range(NP):
                            P.op("pe", lambda e, q=q, pb_=pb_, Xh=Xh, Xth=Xth, a=a: e.matmul(pb_[:, q * 128:(q + 1) * 128], lhsT=Xh[:, a, q, :], rhs=Xth[:, a, q, :], start=True, stop=True),
                                 reads=[kX + ":%d" % a, kXt + ":%d" % a], writes=[pkb])
                        P.op("act", lambda e, pb_=pb_, Xth=Xth, b=b: e.copy(out=Xth[:, b, :, :], in_=v4(pb_[:])), reads=[pkb], writes=[kXt + ":%d" % b])
                    for h in range(H):
                        Xh, Xth, Rh = W["X%d" % h], W["Xt%d" % h], W["R%d" % h]
                        kX, kXt, kR = p + "X%d" % h, p + "Xt%d" % h, p + "R%d" % h
                        pr, pkr = next_ps()
                        for q in range(NP):
                            P.op("pe", lambda e, q=q, pr=pr, Rh=Rh, Xth=Xth, a=a, b=b: e.matmul(pr[:, q * 128:(q + 1) * 128], lhsT=Xth[:, b, q, :], rhs=Rh[:, a, q, :], start=True, stop=True),
                                 reads=[kXt + ":%d" % b, kR + ":%d" % a], writes=[pkr])
                        Rf = W["Rf%d" % h]
                        P.op("dve", lambda e, pr=pr, Rf=Rf: e.tensor_tensor(out=Rf[:], in0=v4(pr[:]), in1=Rf[:], op=ALU.add), reads=[pkr, p + "Rf%d" % h], writes=[p + "Rf%d" % h])
                        P.op("pool", lambda e, Rf=Rf, Xh=Xh, b=b: e.tensor_tensor(out=Rf[:], in0=Rf[:], in1=Xh[:, b, :, :], op=ALU.add),
                             reads=[p + "Rf%d" % h, kX + ":%d" % b], writes=[p + "Rf%d" % h])
                        P.op("act", lambda e, Rf=Rf, Rh=Rh, b=b: e.copy(out=Rh[:, b, :, :], in_=Rf[:]), reads=[p + "Rf%d" % h], writes=[kR + ":%d" % b])
                kE2 = [p + "X0:0", p + "X0:1"]
                kRT = [p + "Xt0:0", p + "Xt0:1"]
                for h in range(H):
                    Xf, Xtf, Rf = W["Xf%d" % h], W["Xtf%d" % h], W["Rf%d" % h]
                    kRf = p + "Rf%d" % h
                    for rstep in range(NEWTON_STEPS):
                        pe2, pke2 = next_ps()
                        for q in range(NP):
                            P.op("pe", lambda e, q=q, pe2=pe2, Xtf=Xtf, Rf=Rf: e.matmul(pe2[:, q * 128:(q + 1) * 128], lhsT=Xtf[:, q, :], rhs=Rf[:, q, :], start=True, stop=True),
                                 reads=[p + "Xtf%d" % h, kRf], writes=[pke2])
                        P.op("dve", lambda e, pe2=pe2, Xf=Xf: e.tensor_tensor(out=W["E2"], in0=v4(pe2[:]), in1=Xf[:], op=ALU.add), reads=[pke2, p + "Xf%d" % h], writes=kE2)
                        P.op("dve", lambda e, Rf=Rf: e.tensor_tensor(out=W["E2"], in0=W["E2"], in1=Rf[:], op=ALU.subtract), reads=kE2 + [kRf], writes=kE2)
                        prt, pkrt = next_ps()
                        for q in range(NP):
                            P.op("pe", lambda e, q=q, prt=prt, Rf=Rf: e.transpose(out=prt[:, q * 128:(q + 1) * 128], in_=Rf[:, q, :], identity=ident[:]), reads=[kRf, "ident"], writes=[pkrt])
                        P.op("act", lambda e, prt=prt: e.copy(out=W["RfT"], in_=v4(prt[:])), reads=[pkrt], writes=kRT)
                        pre, pkre = next_ps()
                        for q in range(NP):
                            P.op("pe", lambda e, q=q, pre=pre: e.matmul(pre[:, q * 128:(q + 1) * 128], lhsT=W["RfT"][:, q, :], rhs=W["E2"][:, q, :], start=True, stop=True),
                                 reads=kRT + kE2, writes=[pkre])
                        P.op("dve", lambda e, Rf=Rf: e.tensor_tensor(out=Rf[:], in0=Rf[:], in1=W["E2"], op=ALU.add), reads=[kRf] + kE2, writes=[kRf])
                        P.op("dve", lambda e, pre=pre, Rf=Rf: e.tensor_tensor(out=Rf[:], in0=v4(pre[:]), in1=Rf[:], op=ALU.add), reads=[pkre, kRf], writes=[kRf])
                for h in range(H):
                    Rf = W["Rf%d" % h]
                    kRf = p + "Rf%d" % h
                    pu, pku = next_ps()
                    for q in range(NP):
                        P.op("pe", lambda e, q=q, pu=pu, Rf=Rf, h=h: e.matmul(pu[:, q * 128:(q + 1) * 128], lhsT=Rf[:, q, :], rhs=W["vbf%d" % h][:, q, :], start=True, stop=True),
                             reads=[kRf, p + "vbf%d" % h], writes=[pku])
                    P.op("dve", lambda e, pu=pu, h=h: e.tensor_tensor(out=W["u%d" % h][:], in0=v4(pu[:]), in1=W["vbf%d" % h][:], op=ALU.add),
                         reads=[pku, p + "vbf%d" % h], writes=[p + "u%d" % h])
                    pw, pkw = next_ps()
                    for q in range(NP):
                        P.op("pe", lambda e, q=q, pw=pw, Rf=Rf, h=h: e.matmul(pw[:, q * 128:(q + 1) * 128], lhsT=W["kbg%d" % h][:, q, :], rhs=Rf[:, q, :], start=True, stop=True),
                             reads=[kRf, p + "kbg%d" % h], writes=[pkw])
                    P.op("dve", lambda e, pw=pw, h=h: e.tensor_tensor(out=W["wT%d" % h][:], in0=pw[:], in1=W["wT%d" % h][:], op=ALU.add),
                         reads=[pkw, p + "wT%d" % h], writes=[p + "wT%d" % h])

            def scan_tiles(pairs):
                orders = {}
                for d, t in pairs:
                    orders[d] = list(range(NCH)) if d == 0 else list(range(NCH - 1, -1, -1))
                for step in range(NCH):
                    for d, t in pairs:
                        W = Ws[d]
                        p = W["p"]
                        c = orders[d][step]
                        q, c2 = c // 2, c % 2
                        hs = slice(c2 * 64, (c2 + 1) * 64)
                        gpos = t * NT + c * CH
                        seq = gpos // g.L
                        is_start = (gpos % g.L == 0) if d == 0 else ((gpos + CH) % g.L == 0)
                        is_end = ((gpos + CH) % g.L == 0) if d == 0 else (gpos % g.L == 0)
                        par = step % 2
                        for h in range(H):
                            St, Sk = Sst[d][h]
                            Sb, Sbk = Sbf[d][h]
                            if is_start and not g.has_s0 and not (step == 0 and ((d == 0 and t == 0) or (d == 1 and t == g.ntiles - 1))):
                                P.op("pool", lambda e, St=St: e.memset(St[:], 0.0), reads=[Sk], writes=[Sk])
                                P.op("pool", lambda e, Sb=Sb: e.memset(Sb[:], 0.0), reads=[Sbk], writes=[Sbk])
                        pvs = []
                        for h in range(H):
                            St, Sk = Sst[d][h]
                            pv, pkv = next_ps()
                            P.op("pe", lambda e, pv=pv, q=q, h=h, St=St, W=W: e.matmul(pv[:, 0:128], lhsT=W["wT%d" % h][:, q * 128:(q + 1) * 128], rhs=St[:], start=True, stop=True),
                                 reads=[p + "wT%d" % h, Sk], writes=[pkv])
                            pvs.append((pv, pkv))
                        for h in range(H):
                            pv, pkv = pvs[h]
                            vn = W["vn%d" % h]
                            vk = p + "vn%d:%d" % (h, par)
                            P.op("dve", lambda e, pv=pv, q=q, h=h, vn=vn, par=par, hs=hs, W=W: e.tensor_tensor(out=vn[hs, par, :], in0=W["u%d" % h][hs, q, :], in1=pv[hs, 0:128], op=ALU.subtract),
                                 reads=[p + "u%d" % h, pkv], writes=[vk])
                        pss_l = []
                        pos = []
                        for h in range(H):
                            Sb, Sbk = Sbf[d][h]
                            vn = W["vn%d" % h]
                            vk = p + "vn%d:%d" % (h, par)
                            po, pko = next_ps()
                            P.op("pe", lambda e, po=po, c=c, h=h, Sb=Sb, W=W: e.matmul(po[:, 0:CH], lhsT=Sb[:], rhs=W["qd%d" % h][:, c * CH:(c + 1) * CH], start=True, stop=False),
                                 reads=[p + "qd%d" % h, Sbk], writes=[pko])
                            P.op("pe", lambda e, po=po, q=q, c2=c2, h=h, vn=vn, par=par, hs=hs, W=W: e.matmul(po[:, 0:CH], lhsT=vn[hs, par, :], rhs=W["qkT%d" % h][hs, q, c2 * 64:(c2 + 1) * 64], start=False, stop=True),
                                 reads=[p + "qkT%d" % h, vk], writes=[pko])
                            pos.append((po, pko))
                            pss, pks = next_ps()
                            P.op("pe", lambda e, pss=pss, q=q, h=h, vn=vn, par=par, hs=hs, W=W: e.matmul(pss[:, 0:128], lhsT=W["kdec%d" % h][hs, q, :], rhs=vn[hs, par, :], start=True, stop=True),
                                 reads=[p + "kdec%d" % h, vk], writes=[pks])
                            pss_l.append((pss, pks))
                        for h in range(H):
                            St, Sk = Sst[d][h]
                            Sb, Sbk = Sbf[d][h]
                            po, pko = pos[h]
                            pss, pks = pss_l[h]
                            P.op("act", lambda e, po=po, c=c, h=h, W=W: e.copy(out=W["oT%d" % h][:, c * CH:(c + 1) * CH], in_=po[:, 0:CH]), reads=[pko], writes=[p + "oT%d" % h])
                            P.op("dve", lambda e, pss=pss, c=c, h=h, St=St, W=W: e.scalar_tensor_tensor(out=St[:], in0=St[:], scalar=W["eglh"][:, h, c:c + 1], in1=pss[:, 0:128], op0=ALU.mult, op1=ALU.add),
                                 reads=[Sk, pks, p + "eglh"], writes=[Sk])
                            P.op("act", lambda e, St=St, Sb=Sb: e.copy(out=Sb[:], in_=St[:]), reads=[Sk], writes=[Sbk])
                            if is_end and g.wstate:
                                d_ = P.dma("nst_%d%d" % (d, h), lambda e, St=St, seq=seq, d=d, h=h: e.dma_start(out=O["nstate"][seq, l, d, h], in_=St[:]), reads=[Sk])
                                fin.append(d_.idx)
                for d, t in pairs:
                    W = Ws[d]
                    p = W["p"]
                    for h in range(H):
                        P.dma(p + "oT%d" % h, lambda e, W=W, h=h, d=d, t=t: e.dma_start(out=S["O_" + n][d, h, :, t * NT:(t + 1) * NT], in_=W["oT%d" % h][:]),
                              reads=[p + "oT%d" % h], writes=["O_%s:%d:%d:%d" % (n, d, h, t)])

            seq = [(0, t) for t in range(g.ntiles)] + [(1, t) for t in range(g.ntiles - 1, -1, -1)]
            load_tile(*seq[0])
            for i_, (d, t) in enumerate(seq):
                chunk_local(d, t)
                if i_ + 1 < len(seq):
                    load_tile(*seq[i_ + 1])
                scan_tiles([(d, t)])
            P.release(m0)

        def phase_C1(l, g, w_out_sb):
            n = g.name
            j = g.cond
            m0 = P.mark()
            xt = P.tile("C_xt", [128, KC, NT], F32)
            of = P.tile("C_of", [128, 4, NT], F32)
            ob = P.tile("C_ob", [128, 4, NT], F32)
            gate = P.tile("C_gate", [128, 4, NT], F32)
            mix = P.tile("C_mix", [128, KC, NT], BF16)
            sqb = [(P.tile("C_sq%d" % i, [128, NT], F32), "C_sq%d" % i) for i in range(4)]
            sqr = [(P.tile("C_sqr%d" % i, [128, NT], BF16), "C_sqr%d" % i) for i in range(8)]
            hlb = [(P.tile("C_hl%d" % i, [128, 2, NT], BF16), "C_hl%d" % i) for i in range(4)]
            tmp = [(P.tile("C_tmp%d" % i, [128, NT], F32), "C_tmp%d" % i) for i in range(4)]
            rstd = P.tile("C_rstd", [128, NT], F32)
            h2 = P.tile("C_h2", [128, KC, NT], BF16)
            def load_oga(t):
                t0 = t * NT
                P.dma("C_of", lambda e, t0=t0: e.dma_start(out=of[:], in_=S["O_" + n][0, :, :, t0:t0 + NT].rearrange("h p t -> p h t")),
                      reads=["O_%s:0:%d:%d" % (n, h, t) for h in range(H)], writes=["C_of"])
                P.dma("C_ob", lambda e, t0=t0: e.dma_start(out=ob[:], in_=S["O_" + n][1, :, :, t0:t0 + NT].rearrange("h p t -> p h t")),
                      reads=["O_%s:1:%d:%d" % (n, h, t) for h in range(H)], writes=["C_ob"])
                P.dma("C_gate", lambda e, t0=t0: e.dma_start(out=gate[:], in_=S["GATE_" + n][:, :, t0:t0 + NT].rearrange("m p t -> p m t")),
                      reads=["GATE_%s:%d" % (n, t)], writes=["C_gate"])

            def load_yh(t):
                t0 = t * NT
                nsub = NT // min(TW, g.L)
                P.dma("C_yh", lambda e, t0=t0: e.dma_start(out=mix[:, 4:8, :], in_=S["YH_" + n][:, :, t0:t0 + NT].rearrange("m p t -> p m t")),
                      reads=["YH_%s:%d:%d:%d" % (n, t, cc, s_) for cc in range(4) for s_ in range(nsub)], writes=["C_mix:hy"])

            for t in range(g.ntiles):
                t0 = t * NT
                P.dma("C_xt", lambda e, t0=t0: e.dma_start(out=xt[:], in_=(I["x_" + n] if l == 0 else S["X_" + n])[:, :, t0:t0 + NT].rearrange("k p t -> p k t")),
                      reads=[xkeys(g, t)], writes=["C_xt"])
                if t == 0:
                    load_oga(0)
                    load_yh(0)
                P.op("dve", lambda e: e.tensor_tensor(out=of[:], in0=of[:], in1=ob[:], op=ALU.add), reads=["C_of", "C_ob"], writes=["C_of"])
                gp = []
                for h in range(H):
                    sq, sqk = sqb[h]
                    hl, hlk = hlb[h]
                    pt, pk = next_ps()
                    sumsq_hilo(of[:, h, :], "C_of", sq, sqk, hl, hlk, pt, pk)
                    gp.append((pt, pk))
                for h in range(H):
                    sq, sqk = sqb[h]
                    pt, pk = gp[h]
                    P.op("act", lambda e, pt=pt, sq=sq: e.activation(out=sq[:], in_=pt[:], func=AF.Ln, scale=1.0 / DK, bias=EPS), reads=[pk], writes=[sqk])
                for h in range(H):
                    sq, sqk = sqb[h]
                    P.op("act", lambda e, sq=sq: e.activation(out=sq[:], in_=sq[:], func=AF.Exp, scale=-0.5), reads=[sqk], writes=[sqk])
                for h in range(H):
                    sq, sqk = sqb[h]
                    tb, tk = tmp[h]
                    P.op("dve", lambda e, h=h, sq=sq, tb=tb: e.scalar_tensor_tensor(out=tb[:], in0=of[:, h, :], scalar=gng[:, l:l + 1], in1=sq[:], op0=ALU.mult, op1=ALU.mult),
                         reads=["C_of", "gng", sqk], writes=[tk])
                    P.op("pool", lambda e, h=h, tb=tb: e.tensor_tensor(out=mix[:, h, :], in0=tb[:], in1=gate[:, h, :], op=ALU.mult), reads=[tk, "C_gate"], writes=["C_mix:%d" % h])
                mixkeys = ["C_mix:%d" % h for h in range(H)] + ["C_mix:hy"]
                if t + 1 < g.ntiles:
                    load_oga(t + 1)
                for m in range(KC):
                    pt, pk = next_ps()
                    for k in range(KC):
                        P.op("pe", lambda e, pt=pt, k=k, m=m: e.matmul(pt[:], lhsT=w_out_sb[:, k, m * 128:(m + 1) * 128], rhs=mix[:, k, :], start=(k == 0), stop=(k == KC - 1)),
                             reads=["w_out_sb"] + mixkeys, writes=[pk])
                    P.op("dve", lambda e, pt=pt, m=m: e.scalar_tensor_tensor(out=xt[:, m, :], in0=pt[:], scalar=mods[:, l, j, 16 + m:16 + m + 1], in1=xt[:, m, :], op0=ALU.mult, op1=ALU.add),
                         reads=[pk, "mods", "C_xt"], writes=["C_xt"])
                if t + 1 < g.ntiles:
                    load_yh(t + 1)
                P.dma("C_xo", lambda e, t0=t0: e.dma_start(out=S["X_" + n][:, :, t0:t0 + NT].rearrange("k p t -> p k t"), in_=xt[:]),
                      reads=["C_xt"], writes=[xkeys(g, t)])
                rms_stats(xt, "C_xt", KC, D, rstd, "C_rstd", sqr)
                for k in range(KC):
                    tb, tk = tmp[k % 4]
                    P.op("dve", lambda e, tb=tb, k=k: e.scalar_tensor_tensor(out=tb[:], in0=xt[:, k, :], scalar=gmod[:, l, j, 1, k:k + 1], in1=rstd[:], op0=ALU.mult, op1=ALU.mult),
                         reads=["C_xt", "gmod", "C_rstd"], writes=[tk])
                    P.op("act", lambda e, tb=tb, k=k: e.activation(out=h2[:, k, :], in_=tb[:], func=AF.Identity, bias=mods[:, l, j, 24 + k:24 + k + 1], scale=1.0),
                         reads=[tk, "mods"], writes=["C_h2"])
                P.dma("C_h2o", lambda e, t0=t0: e.dma_start(out=S["H2_" + n][:, :, t0:t0 + NT].rearrange("k p t -> p k t"), in_=h2[:]),
                      reads=["C_h2"], writes=["H2_%s:%d" % (n, t)])
            P.release(m0)

        def phase_C2(l, g, w1_sb, w2_sb):
            n = g.name
            j = g.cond
            lastl = (l == depth - 1)
            m0 = P.mark()
            xt = P.tile("M_xt", [128, KC, NT], F32)
            h2 = P.tile("M_h2", [128, KC, NT], BF16)
            act = P.tile("M_act", [128, 16, NT], BF16)
            rl = [(P.tile("M_rl%d" % i, [128, NT], F32), "M_rl%d" % i) for i in range(3)]
            sqb = [(P.tile("M_sq%d" % i, [128, NT], BF16), "M_sq%d" % i) for i in range(2)]
            rstd = P.tile("M_rstd", [128, NT], F32)
            def load_h2(t):
                t0 = t * NT
                P.dma("M_h2", lambda e, t0=t0: e.dma_start(out=h2[:], in_=S["H2_" + n][:, :, t0:t0 + NT].rearrange("k p t -> p k t")), reads=["H2_%s:%d" % (n, t)], writes=["M_h2"])

            load_h2(0)
            for t in range(g.ntiles):
                t0 = t * NT
                P.dma("M_xt", lambda e, t0=t0: e.dma_start(out=xt[:], in_=S["X_" + n][:, :, t0:t0 + NT].rearrange("k p t -> p k t")), reads=[xkeys(g, t)], writes=["M_xt"])
                for half in range(2):
                    for mm in range(16):
                        col = (half * 16 + mm) * 128
                        pt, pk = next_ps()
                        for k in range(KC):
                            P.op("pe", lambda e, pt=pt, k=k, col=col: e.matmul(pt[:], lhsT=w1_sb[:, k, col:col + 128], rhs=h2[:, k, :], start=(k == 0), stop=(k == KC - 1)),
                                 reads=["w1_sb", "M_h2"], writes=[pk])
                        rb, rk = rl[mm % 3]
                        P.op("act", lambda e, pt=pt, rb=rb: e.activation(out=rb[:], in_=pt[:], func=AF.Relu), reads=[pk], writes=[rk])
                        P.op("pool", lambda e, rb=rb, mm=mm: e.tensor_tensor(out=act[:, mm, :], in0=rb[:], in1=rb[:], op=ALU.mult), reads=[rk], writes=["M_act:%d" % mm])
                    if half == 1 and t + 1 < g.ntiles:
                        load_h2(t + 1)
                    for m in range(KC):
                        pt, pk = next_ps()
                        for kk in range(16):
                            P.op("pe", lambda e, pt=pt, kk=kk, m=m, half=half: e.matmul(pt[:], lhsT=w2_sb[:, half * 16 + kk, m * 128:(m + 1) * 128], rhs=act[:, kk, :], start=(kk == 0), stop=(kk == 15)),
                                 reads=["w2_sb", "M_act:%d" % kk], writes=[pk])
                        P.op("dve", lambda e, pt=pt, m=m: e.scalar_tensor_tensor(out=xt[:, m, :], in0=pt[:], scalar=mods[:, l, j, 40 + m:40 + m + 1], in1=xt[:, m, :], op0=ALU.mult, op1=ALU.add),
                             reads=[pk, "mods", "M_xt"], writes=["M_xt"])
                if not lastl:
                    P.dma("M_xo", lambda e, t0=t0: e.dma_start(out=S["X_" + n][:, :, t0:t0 + NT].rearrange("k p t -> p k t"), in_=xt[:]), reads=["M_xt"], writes=[xkeys(g, t)])
                else:
                    rms_stats(xt, "M_xt", KC, D, rstd, "M_rstd", sqb)
                    for k in range(KC):
                        P.op("dve", lambda e, k=k: e.scalar_tensor_tensor(out=xt[:, k, :], in0=xt[:, k, :], scalar=finalg[:, k:k + 1], in1=rstd[:], op0=ALU.mult, op1=ALU.mult),
                             reads=["M_xt", "finalg", "M_rstd"], writes=["M_xt"])
                    d_ = P.dma("M_yo", lambda e, t0=t0: e.dma_start(out=O["y_" + n][:, :, t0:t0 + NT].rearrange("k p t -> p k t"), in_=xt[:]), reads=["M_xt"])
                    fin.append(d_.idx)
            P.release(m0)

        for l in range(depth):
            m0 = P.mark()
            w_in_sb = load_weight_bf16("w_in_sb", I["w_in"][l].rearrange("(k p) n -> p k n", p=128), [128, KC, IN_COLS])
            for g in groups:
                phase_A(l, g, w_in_sb)
            P.release(m0)
            for g in groups:
                phase_H(l, g)
            for g in groups:
                phase_G(l, g)
            m0 = P.mark()
            w_out_sb = load_weight_bf16("w_out_sb", I["w_out"][l].rearrange("(k p) n -> p k n", p=128), [128, KC, D])
            for g in groups:
                phase_C1(l, g, w_out_sb)
            P.release(m0)
            m0 = P.mark()
            w1_sb = load_weight_bf16("w1_sb", I["w_mlp1"][l].rearrange("(k p) n -> p k n", p=128), [128, KC, DFF])
            w2_sb = load_weight_bf16("w2_sb", I["w_mlp2"][l].rearrange("(k p) n -> p k n", p=128), [128, 32, D])
            for g in groups:
                phase_C2(l, g, w1_sb, w2_sb)
            P.release(m0)
        P.emit(final_wait_ops=fin)
    return nc


def _dft_tables(L, ntab):
    N = 2 * L
    a = np.arange(ntab, dtype=np.int64)
    ph = (a[:, None] * a[None, :]) % N
    ang = ph.astype(np.float64) * (2.0 * np.pi / N)
    c = np.cos(ang)
    s_ = np.sin(ang)
    s_[(ph % L) == 0] = 0.0
    def tl(a_, w_):
        return np.ascontiguousarray(a_.reshape(ntab // 128, 128, ntab // w_, w_).transpose(2, 1, 0, 3)).astype(ml_dtypes.bfloat16)
    return tl(c, FW), tl(s_, FW), tl(c, 128), tl(s_, 128)


def _zfeat(L):
    t = np.linspace(0.0, 1.0, L, dtype=np.float32)[:, None]
    bands = (HY_EMB - 1) // 2
    f = np.linspace(1e-4, bands - 1, bands, dtype=np.float32)[None, :]
    wpos = (np.float32(2.0 * math.pi) * np.arange(L, dtype=np.float32)[:, None] / np.float32(L)).astype(np.float32)
    z = np.concatenate([t, np.cos(f * wpos), -np.sin(f * wpos)], axis=-1).astype(np.float32)
    return np.ascontiguousarray(z.T), t[:, 0]


_PROG_CACHE = {}


def run_cfg(depth, LS, inputs, n_cores=8):
    f32 = np.float32
    groups = make_groups(LS)
    A = {k: np.asarray(v) for k, v in inputs.items()}
    shared = {}
    for g in groups:
        c, s_, cF, sF = _dft_tables(g.L, g.NTAB)
        shared["ctab_" + g.name] = c
        shared["stab_" + g.name] = s_
        shared["ctabF_" + g.name] = cF
        shared["stabF_" + g.name] = sF
        zf, t = _zfeat(g.L)
        shared["zf_" + g.name] = zf
        shared["negt_" + g.name] = np.ascontiguousarray((-t).reshape(g.nTB, 128).T).astype(f32)
        w = np.zeros(g.NF, f32)
        w[0:g.L + 1] = 2.0 / (2 * g.L)
        w[0] = 1.0 / (2 * g.L)
        w[g.L] = 1.0 / (2 * g.L)
        shared["wcol_" + g.name] = np.ascontiguousarray(w.reshape(g.NFB, 128).T).astype(f32)
    shared["w_ada"] = A["w_ada"][:depth].astype(f32)
    shared["b_ada"] = np.ascontiguousarray(A["b_ada"][:depth].reshape(depth, 48, 128).transpose(2, 0, 1)).astype(f32)
    ng = np.stack([A["norm1_g"][:depth], A["norm2_g"][:depth]], axis=1)
    shared["norm_g"] = np.ascontiguousarray(ng.reshape(depth, 2, KC, 128).transpose(3, 0, 1, 2)).astype(f32)
    shared["final_g"] = np.ascontiguousarray(A["final_g"].reshape(KC, 128).T).astype(f32)
    shared["w_in"] = A["w_in"][:depth].astype(f32)
    shared["gcw"] = np.ascontiguousarray(A["gdn_conv_w"][:depth].reshape(depth, 5, 12, 128).transpose(3, 0, 2, 1)).astype(f32)
    shared["hcw"] = np.ascontiguousarray(A["hy_conv_w"][:depth].reshape(depth, 3, 12, 128).transpose(3, 0, 2, 1)).astype(f32)
    ap_ = np.stack([A["gdn_a_log"][:depth].reshape(depth, 8), A["gdn_dt_bias"][:depth].reshape(depth, 8)], axis=-1)
    shared["a_par"] = np.ascontiguousarray(ap_.transpose(1, 0, 2)).astype(f32)
    shared["gng"] = np.ascontiguousarray(A["gdn_norm_g"][:depth].T).astype(f32)
    shared["hw1"] = A["hy_w1"][:depth].astype(f32)
    hv = np.stack([A["hy_b1"][:depth], A["hy_freq"][:depth], A["hy_b2"][:depth], np.zeros_like(A["hy_b1"][:depth])], axis=-1)
    shared["hvec"] = np.ascontiguousarray(hv.transpose(1, 0, 2)).astype(f32)
    shared["hw2"] = A["hy_w2"][:depth].astype(f32)
    shared["hw3e"] = np.ascontiguousarray(np.concatenate([A["hy_w3"][:depth], A["hy_b3"][:depth][:, None, :]], axis=1)).astype(f32)
    shared["hskip"] = np.ascontiguousarray(A["hy_skip"][:depth].reshape(depth, 4, 128).transpose(2, 0, 1)).astype(f32)
    shared["w_out"] = A["w_out"][:depth].astype(f32)
    shared["w_mlp1"] = A["w_mlp1"][:depth].astype(f32)
    shared["w_mlp2"] = A["w_mlp2"][:depth].astype(f32)
    ii = np.arange(64)
    mk = np.zeros((64, 5, 64), f32)
    mk[:, 0, :] = (ii[None, :] >= ii[:, None])
    mk[:, 1, :] = (ii[None, :] <= ii[:, None])
    mk[:, 2, :] = -1.0 * (ii[None, :] > ii[:, None])
    mk[:, 3, :] = -1.0 * (ii[None, :] < ii[:, None])
    mk[:, 4, :] = (ii[None, :] == ii[:, None])
    shared["masks"] = mk
    mk2 = np.zeros((128, 4, 128), f32)
    for bb in range(2):
        mk2[bb * 64:(bb + 1) * 64, :, bb * 64:(bb + 1) * 64] = mk[:, 0:4, :]
    shared["masks2"] = mk2
    sel = np.zeros((8, 8, 128), f32)
    for r in range(8):
        sel[r, r, :] = 1.0
    shared["sel"] = sel
    sm = np.ones((8, NT), f32)
    sm[:, ::CH] = 0.0
    shared["scanmask"] = sm
    min_decay = math.log(1e-2) / 1.5
    max_decay = math.log(1e-2) / 0.3
    shared["delta"] = np.abs(np.linspace(min_decay, max_decay, 512, dtype=f32)).astype(f32)

    n_s = A["x_sample"].shape[0]
    in_maps = []
    for i in range(n_cores):
        b = i % n_s
        m = dict(shared)
        xs = A["x_sample"][b]
        m["x_s"] = np.ascontiguousarray(xs.T.reshape(KC, 128, LS)).astype(f32)
        xp = A["x_prompt"][2 * i:2 * i + 2].reshape(512, D)
        m["x_p"] = np.ascontiguousarray(xp.T.reshape(KC, 128, 512)).astype(f32)
        m["s0"] = np.ascontiguousarray(A["state_gdn"][b][:depth]).astype(f32)
        cd = np.stack([A["c"][b], A["c_ctx"]], axis=-1)
        m["cond"] = np.ascontiguousarray(cd.reshape(KC, 128, 2).transpose(1, 0, 2)).astype(f32)
        in_maps.append(m)

    key = (depth, LS)
    if key not in _PROG_CACHE:
        _PROG_CACHE[key] = build_program(depth, LS)
    nc = _PROG_CACHE[key]
    res = run_bass_kernel_spmd(nc, in_maps, core_ids=list(range(n_cores)))
    R = res.results
    _LAST["R"] = R
    y_sample = np.stack([R[b]["y_s"].reshape(D, LS).T for b in range(n_s)], axis=0).astype(f32)
    y_prompt = np.concatenate([R[i]["y_p"].reshape(D, 512).T.reshape(2, 256, D) for i in range(n_cores)], axis=0).astype(f32)
    new_state = np.concatenate([R[i]["nstate"] for i in range(n_cores)], axis=0).astype(f32)
    return (y_prompt, y_sample, new_state)


def kernel(**inputs):
    return run_cfg(4, 4096, inputs)
```

```python
import contextlib
import math
import numpy as np
import ml_dtypes
import concourse.bass as bass
import concourse.mybir as mybir
from concourse.bass_utils import run_bass_kernel_spmd

F32 = mybir.dt.float32
BF16 = mybir.dt.bfloat16
I32 = mybir.dt.int32
AF = mybir.ActivationFunctionType
ALU = mybir.AluOpType
AX = mybir.AxisListType

ENGS = ("pe", "act", "dve", "pool", "sp")
STORE_ENG = "sp"


def _is_store(semname):
    return semname.endswith("_o") or semname in ("C_xo", "C_h2o", "M_xo", "M_yo") or semname.startswith("G_oT")
SEM_EPOCH = 30000
SBUF_BASE = 16640
SBUF_BYTES = 229000


def _dsize(dt):
    return 2 if dt == BF16 else 4


class Op:
    __slots__ = ("eng", "fn", "deps", "is_dma", "dsem", "dval", "signal", "sem", "val", "idx", "_rw")


class Prog:
    def __init__(self, nc, stack):
        self.nc = nc
        self.stack = stack
        self.ops = []
        self.key_w = {}
        self.key_r = {}
        self.dma_sems = {}
        self.bump = SBUF_BASE
        self.tiles = []
        self.tile_keys = {}
        self.tile_pending = {}
        self.uid = 0
        self.ps_rr = 0

    def tile(self, name, shape, dtype):
        nbytes = int(np.prod(shape[1:])) * _dsize(dtype)
        nbytes = (nbytes + 63) // 64 * 64
        off = self.bump
        assert off + nbytes <= SBUF_BYTES, ("SBUF overflow", name, off, nbytes)
        self.bump = off + nbytes
        self.uid += 1
        t = self.nc.alloc_sbuf_tensor_at("%s_%d" % (name, self.uid), list(shape), dtype, offset=off)
        pend = set()
        for (s, e, oname) in self.tiles:
            if s < off + nbytes and off < e:
                pend |= self.tile_pending.get(oname, set())
                for k in self.tile_keys.get(oname, ()):
                    w = self.key_w.get(k)
                    if w is not None:
                        pend.add(w)
                    pend.update(self.key_r.get(k, {}).values())
        self.tiles = [(s, e, n) for (s, e, n) in self.tiles if not (s >= off and e <= off + nbytes)] + [(off, off + nbytes, name)]
        self.tile_keys.setdefault(name, set())
        self.tile_pending[name] = self._compress(pend)
        return t

    def mark(self):
        return self.bump

    def release(self, m):
        self.bump = m

    def new_sem(self, name):
        return self.stack.enter_context(self.nc.semaphore(name))

    def _cls(self, d):
        o = self.ops[d]
        return ("d", id(o.dsem)) if o.is_dma else o.eng

    def _compress(self, deps):
        best = {}
        for d in deps:
            c = self._cls(d)
            b = best.get(c)
            if b is None or b < d:
                best[c] = d
        return set(best.values())

    def _deps(self, reads, writes):
        deps = set()
        for k in reads:
            base = k.split(":")[0]
            if base in self.tile_keys:
                self.tile_keys[base].add(k)
                deps |= self.tile_pending[base]
            w = self.key_w.get(k)
            if w is not None:
                deps.add(w)
        for k in writes:
            base = k.split(":")[0]
            if base in self.tile_keys:
                self.tile_keys[base].add(k)
                deps |= self.tile_pending[base]
            w = self.key_w.get(k)
            if w is not None:
                deps.add(w)
            deps.update(self.key_r.get(k, {}).values())
        return self._compress(deps)

    def _op(self, eng, fn, reads=(), writes=()):
        o = Op()
        o.eng = eng
        o.fn = fn
        o.is_dma = False
        o.dsem = None
        o.signal = False
        o.idx = len(self.ops)
        o.deps = self._deps(reads, writes)
        self.ops.append(o)
        o._rw = (reads, writes)
        return o

    def op(self, eng, fn, reads=(), writes=()):
        o = self._op(eng, fn, reads, writes)
        self._commit(o)
        return o

    def _commit(self, o):
        reads, writes = o._rw
        c = self._cls(o.idx)
        for k in reads:
            self.key_r.setdefault(k, {})[c] = o.idx
        for k in writes:
            self.key_w[k] = o.idx
            self.key_r[k] = {}
        o._rw = None

    def dma(self, semname, fn, reads=(), writes=(), eng="sp"):
        if eng == "sp" and _is_store(semname):
            eng = STORE_ENG
        o = self._op(eng, fn, reads, writes)
        o.is_dma = True
        if semname not in self.dma_sems:
            self.dma_sems[semname] = [self.new_sem("d%d" % len(self.dma_sems)), 0]
        ent = self.dma_sems[semname]
        ent[1] += 16
        o.dsem = ent[0]
        o.dval = ent[1]
        self._commit(o)
        return o

    def emit(self, final_wait_ops=()):
        nc = self.nc
        ops = self.ops

        def skip(do, o):
            return (not do.is_dma) and (not o.is_dma) and do.eng == o.eng and do.eng == "pe"

        for o in ops:
            for d in o.deps:
                do = ops[d]
                if do.is_dma or skip(do, o):
                    continue
                do.signal = True
        cur = {}
        for o in ops:
            if o.is_dma:
                o.sem, o.val = o.dsem, o.dval
                continue
            if not o.signal:
                continue
            ent = cur.get(o.eng)
            if ent is None or ent[1] >= SEM_EPOCH:
                ent = [self.new_sem("e%s%d" % (o.eng, o.idx)), 0]
                cur[o.eng] = ent
            ent[1] += 1
            o.sem, o.val = ent[0], ent[1]
        per_eng = {e: [o for o in ops if o.eng == e] for e in ENGS}
        final_ops = [ops[i] for i in final_wait_ops]

        def run(ename, e):
            waited = {}
            for o in per_eng[ename]:
                need = {}
                for d in o.deps:
                    do = ops[d]
                    if skip(do, o):
                        continue
                    sid = id(do.sem)
                    if sid not in need or need[sid][1] < do.val:
                        need[sid] = (do.sem, do.val)
                for sid, (s, v) in need.items():
                    if waited.get(sid, 0) >= v:
                        continue
                    e.wait_ge(s, v)
                    waited[sid] = v
                ins = o.fn(e)
                if o.is_dma:
                    ins.then_inc(o.sem, 16)
                elif o.signal:
                    ins.then_inc(o.sem, 1)
            if ename == "sp":
                for o in final_ops:
                    e.wait_ge(o.sem, o.val)

        with nc.Block() as block:
            @block.tensor
            def _(e):
                run("pe", e)

            @block.scalar
            def _(e):
                run("act", e)

            @block.vector
            def _(e):
                run("dve", e)

            @block.gpsimd
            def _(e):
                run("pool", e)

            @block.sync
            def _(e):
                run("sp", e)


D = 1024
KC = 8
NT = 512
H = 4
DK = 128
CH = 64
IN_COLS = 3600
OFF_A = 2048
OFF_B = 2056
OFF_HY = 2064
DFF = 4096
EPS = 1e-6
HY_EMB = 33
HY_FH = 64
NEWTON_STEPS = 1
FW = 256
TW = 256


class Group:
    def __init__(self, name, T, L, seg, has_s0, wstate, cond):
        self.name = name
        self.T = T
        self.L = L
        self.nseq = T // L
        self.seg = seg
        self.has_s0 = has_s0
        self.wstate = wstate
        self.cond = cond
        self.ntiles = T // NT
        self.NF = (L + 1 + FW - 1) // FW * FW
        self.NFB = self.NF // 128
        self.NTAB = max(self.NF, L)
        self.nTB = L // 128


def make_groups(LS):
    return [Group("s", LS, LS, 64, True, False, 0), Group("p", 512, 256, 256, False, True, 1)]


DEBUG_SCRATCH = False
_LAST = {}


def build_program(depth, LS):
    nc = bass.Bass("TRN2", target_bir_lowering=False)
    groups = make_groups(LS)

    def din(name, shape, dt=F32):
        return nc.dram_tensor(name, list(shape), dt, kind="ExternalInput").ap()

    def dout(name, shape, dt=F32):
        return nc.dram_tensor(name, list(shape), dt, kind="ExternalOutput").ap()

    def dscr(name, shape, dt=F32):
        return nc.dram_tensor(name, list(shape), dt, kind=("ExternalOutput" if DEBUG_SCRATCH else "Internal")).ap()

    I = {}
    for g in groups:
        I["x_" + g.name] = din("x_" + g.name, [KC, 128, g.T])
        I["ctab_" + g.name] = din("ctab_" + g.name, [g.NTAB // FW, 128, g.NTAB // 128, FW], BF16)
        I["stab_" + g.name] = din("stab_" + g.name, [g.NTAB // FW, 128, g.NTAB // 128, FW], BF16)
        I["zf_" + g.name] = din("zf_" + g.name, [HY_EMB, g.L])
        I["negt_" + g.name] = din("negt_" + g.name, [128, g.nTB])
        I["wcol_" + g.name] = din("wcol_" + g.name, [128, g.NFB])
        I["ctabF_" + g.name] = din("ctabF_" + g.name, [g.NTAB // 128, 128, g.NTAB // 128, 128], BF16)
        I["stabF_" + g.name] = din("stabF_" + g.name, [g.NTAB // 128, 128, g.NTAB // 128, 128], BF16)
    I["s0"] = din("s0", [depth, 2, H, DK, DK])
    I["cond"] = din("cond", [128, KC, 2])
    I["w_ada"] = din("w_ada", [depth, D, 6 * D])
    I["b_ada"] = din("b_ada", [128, depth, 48])
    I["norm_g"] = din("norm_g", [128, depth, 2, KC])
    I["final_g"] = din("final_g", [128, KC])
    I["w_in"] = din("w_in", [depth, D, IN_COLS])
    I["gcw"] = din("gcw", [128, depth, 12, 5])
    I["hcw"] = din("hcw", [128, depth, 12, 3])
    I["a_par"] = din("a_par", [8, depth, 2])
    I["gng"] = din("gng", [128, depth])
    I["hw1"] = din("hw1", [depth, HY_EMB, HY_FH])
    I["hvec"] = din("hvec", [HY_FH, depth, 4])
    I["hw2"] = din("hw2", [depth, HY_FH, HY_FH])
    I["hw3e"] = din("hw3e", [depth, HY_FH + 1, 2 * 512])
    I["hskip"] = din("hskip", [128, depth, 4])
    I["w_out"] = din("w_out", [depth, D, D])
    I["w_mlp1"] = din("w_mlp1", [depth, D, DFF])
    I["w_mlp2"] = din("w_mlp2", [depth, DFF, D])
    I["masks"] = din("masks", [64, 5, 64])
    I["masks2"] = din("masks2", [128, 4, 128])
    I["sel"] = din("sel", [8, 8, 128])
    I["scanmask"] = din("scanmask", [8, NT])
    I["delta"] = din("delta", [512])

    O = {}
    for g in groups:
        O["y_" + g.name] = dout("y_" + g.name, [KC, 128, g.T])
    O["nstate"] = dout("nstate", [2, depth, 2, H, DK, DK])

    S = {}
    for g in groups:
        n = g.name
        S["X_" + n] = dscr("X_" + n, [KC, 128, g.T])
        S["QKV_" + n] = dscr("QKV_" + n, [12, 128, g.T])
        S["GATE_" + n] = dscr("GATE_" + n, [4, 128, g.T])
        S["GB_" + n] = dscr("GB_" + n, [16, g.T])
        S["X0_" + n] = dscr("X0_" + n, [4, 128, g.T])
        S["ZZ_" + n] = dscr("ZZ_" + n, [4, 128, g.T])
        S["ZZT_" + n] = dscr("ZZT_" + n, [g.T, 512], BF16)
        S["O_" + n] = dscr("O_" + n, [2, 4, 128, g.T])
        S["YH_" + n] = dscr("YH_" + n, [4, 128, g.T], BF16)
        S["H2_" + n] = dscr("H2_" + n, [KC, 128, g.T], BF16)
        S["SPEC_" + n] = dscr("SPEC_" + n, [2, g.NFB, 128, 512])

    with contextlib.ExitStack() as st:
        P = Prog(nc, st)
        ps_t = [st.enter_context(nc.psum_tensor("ps%d" % i, [128, 512], F32)) for i in range(8)]

        ps_held = set()

        def next_ps(hold=False):
            while True:
                i = P.ps_rr
                P.ps_rr = (i + 1) % 8
                if i not in ps_held:
                    break
            if hold:
                ps_held.add(i)
            return ps_t[i], "ps%d" % i

        def ps_release(key):
            ps_held.discard(int(key[2:]))

        fin = []

        ident = P.tile("ident", [128, 128], F32)
        identb = P.tile("identb", [128, 128], BF16)
        ones = P.tile("ones", [128, 128], F32)
        onesb = P.tile("onesb", [128, 128], BF16)
        selb = P.tile("selb", [8, 8, 128], BF16)
        masks = P.tile("masks", [64, 5, 64], F32)
        masks2 = P.tile("masks2", [128, 4, 128], F32)
        sel = P.tile("sel", [8, 8, 128], F32)
        scanmask = P.tile("scanmask", [8, NT], F32)
        mods = P.tile("mods", [128, depth, 2, 48], F32)
        gmod = P.tile("gmod", [128, depth, 2, 2, KC], F32)
        normg = P.tile("normg", [128, depth, 2, KC], F32)
        finalg = P.tile("finalg", [128, KC], F32)
        gcw = P.tile("gcw", [128, depth, 12, 5], F32)
        hcw = P.tile("hcw", [128, depth, 12, 3], F32)
        apar = P.tile("apar", [8, depth, 2], F32)
        nega = P.tile("nega", [8, depth], F32)
        gng = P.tile("gng", [128, depth], F32)
        hskip = P.tile("hskip", [128, depth, 4], F32)
        hvec = P.tile("hvec", [HY_FH, depth, 4], F32)
        hf2p = P.tile("hf2p", [HY_FH, depth], F32)
        deltab = P.tile("deltab", [128, 512], F32)
        condt = P.tile("condt", [128, KC, 2], F32)
        bada = P.tile("bada", [128, depth, 48], F32)

        def ld(semname, t_ap, src, key, eng="sp"):
            return P.dma(semname, lambda e: e.dma_start(out=t_ap, in_=src), writes=[key], eng=eng)

        P.op("pool", lambda e: e.memset(ident[:], 0.0), writes=["ident"])
        P.op("pool", lambda e: e.affine_select(out=ident[:], in_=ident[:], pattern=[[-1, 128]], compare_op=ALU.not_equal,
                                               fill=1.0, base=0, channel_multiplier=1), reads=["ident"], writes=["ident"])
        P.op("dve", lambda e: e.tensor_copy(out=identb[:], in_=ident[:]), reads=["ident"], writes=["identb"])
        P.op("pool", lambda e: e.memset(ones[:], 1.0), writes=["ones"])
        P.op("pool", lambda e: e.memset(onesb[:], 1.0), writes=["onesb"])
        ld("c_masks", masks[:], I["masks"], "masks")
        ld("c_masks2", masks2[:], I["masks2"], "masks2")
        ld("c_sel", sel[:], I["sel"], "sel")
        P.op("dve", lambda e: e.tensor_copy(out=selb[:], in_=sel[:]), reads=["sel"], writes=["selb"])
        ld("c_scanmask", scanmask[:], I["scanmask"], "scanmask")
        ld("c_normg", normg[:], I["norm_g"], "normg")
        ld("c_finalg", finalg[:], I["final_g"], "finalg")
        ld("c_gcw", gcw[:], I["gcw"], "gcw")
        ld("c_hcw", hcw[:], I["hcw"], "hcw")
        ld("c_apar", apar[:], I["a_par"], "apar")
        ld("c_gng", gng[:], I["gng"], "gng")
        ld("c_hskip", hskip[:], I["hskip"], "hskip")
        ld("c_hvec", hvec[:], I["hvec"], "hvec")
        ld("c_delta", deltab[:], I["delta"].partition_broadcast(128), "deltab")
        ld("c_cond", condt[:], I["cond"], "condt")
        ld("c_bada", bada[:], I["b_ada"], "bada")
        P.op("act", lambda e: e.activation(out=nega[:], in_=apar[:, :, 0], func=AF.Exp), reads=["apar"], writes=["nega"])
        P.op("dve", lambda e: e.tensor_scalar(out=nega[:], in0=nega[:], scalar1=-1.0, scalar2=None, op0=ALU.mult), reads=["nega"], writes=["nega"])
        P.op("dve", lambda e: e.tensor_scalar(out=hf2p[:], in0=hvec[:, :, 1], scalar1=1.0 / (2 * math.pi), scalar2=None, op0=ALU.mult),
             reads=["hvec"], writes=["hf2p"])

        base_mark = P.mark()

        def prologue():
            m0 = P.mark()
            scond = P.tile("scond", [128, KC, 2], F32)
            P.op("act", lambda e: e.activation(out=scond[:], in_=condt[:], func=AF.Silu), reads=["condt"], writes=["scond"])
            wa = [P.tile("wa%d" % i, [128, KC, 512], F32) for i in range(2)]
            n = 0
            for l in range(depth):
                for cg in range(12):
                    wt = wa[n % 2]
                    wk = "wa%d" % (n % 2)
                    n += 1
                    src = I["w_ada"][l, :, cg * 512:(cg + 1) * 512].rearrange("(k p) n -> p k n", p=128)
                    ld(wk, wt[:], src, wk)
                    for mi in range(4):
                        chunk = cg * 4 + mi
                        pt, pk = next_ps()
                        for k in range(KC):
                            P.op("pe", lambda e, pt=pt, wt=wt, k=k, mi=mi: e.matmul(pt[:, 0:2], lhsT=wt[:, k, mi * 128:(mi + 1) * 128],
                                                                                   rhs=scond[:, k, :], start=(k == 0), stop=(k == KC - 1)),
                                 reads=[wk, "scond"], writes=[pk])
                        P.op("dve", lambda e, pt=pt, l=l, chunk=chunk: e.tensor_tensor(
                            out=mods[:, l, :, chunk], in0=pt[:, 0:2], in1=bada[:, l, chunk:chunk + 1].to_broadcast([128, 2]), op=ALU.add),
                            reads=[pk, "bada"], writes=["mods"])
            for l in range(depth):
                for j in range(2):
                    for w, sc0 in ((0, 8), (1, 32)):
                        P.op("dve", lambda e, l=l, j=j, w=w, sc0=sc0: e.scalar_tensor_tensor(
                            out=gmod[:, l, j, w, :], in0=mods[:, l, j, sc0:sc0 + KC], scalar=1.0, in1=normg[:, l, w, :],
                            op0=ALU.add, op1=ALU.mult), reads=["mods", "normg"], writes=["gmod"])
            P.release(m0)

        prologue()

        def load_weight_bf16(name, src3, shape):
            t = P.tile(name, shape, BF16)
            for a in range(shape[1]):
                P.dma(name, lambda e, a=a: e.dma_start(out=t[:, a, :], in_=src3[:, a, :]), writes=[name], eng="pool")
            return t

        def rms_stats(src_tile, src_key, nchunks, dim, rstd, rstd_key, sqbufs):
            pt, pk = next_ps()
            for k in range(nchunks):
                sq, sqk = sqbufs[k % len(sqbufs)]
                P.op("act", lambda e, sq=sq, k=k: e.activation(out=sq[:], in_=src_tile[:, k, :], func=AF.Square), reads=[src_key], writes=[sqk])
                P.op("pe", lambda e, sq=sq, k=k, pt=pt: e.matmul(pt[:], lhsT=onesb[:], rhs=sq[:], start=(k == 0), stop=(k == nchunks - 1)),
                     reads=[sqk, "onesb"], writes=[pk])
            P.op("act", lambda e, pt=pt: e.activation(out=rstd[:], in_=pt[:], func=AF.Ln, scale=1.0 / dim, bias=EPS), reads=[pk], writes=[rstd_key])
            P.op("act", lambda e: e.activation(out=rstd[:], in_=rstd[:], func=AF.Exp, scale=-0.5), reads=[rstd_key], writes=[rstd_key])

        def sumsq_hilo(src_ap, src_key, sqf, sqfk, hl, hlk, pt, pk):
            P.op("act", lambda e: e.activation(out=sqf[:], in_=src_ap, func=AF.Square), reads=[src_key], writes=[sqfk])
            P.op("act", lambda e: e.activation(out=hl[:, 0, :], in_=src_ap, func=AF.Square), reads=[src_key], writes=[hlk + ":0"])
            P.op("dve", lambda e: e.tensor_tensor(out=hl[:, 1, :], in0=sqf[:], in1=hl[:, 0, :], op=ALU.subtract), reads=[sqfk, hlk + ":0"], writes=[hlk + ":1"])
            P.op("pe", lambda e: e.matmul(pt[:], lhsT=onesb[:], rhs=hl[:, 0, :], start=True, stop=False), reads=[hlk + ":0", "onesb"], writes=[pk])
            P.op("pe", lambda e: e.matmul(pt[:], lhsT=onesb[:], rhs=hl[:, 1, :], start=False, stop=True), reads=[hlk + ":1", "onesb"], writes=[pk])

        def xkeys(g, t):
            return "X_%s:%d" % (g.name, t)

        def phase_A(l, g, w_in_sb):
            n = g.name
            j = g.cond
            nseg = NT // g.seg
            m0 = P.mark()
            xt = P.tile("A_xt", [128, KC, NT], F32)
            sqb = [(P.tile("A_sq%d" % i, [128, NT], F32), "A_sq%d" % i) for i in range(4)]
            sqr = [(P.tile("A_sqr%d" % i, [128, NT], BF16), "A_sqr%d" % i) for i in range(4)]
            hlb = [(P.tile("A_hl%d" % i, [128, 2, NT], BF16), "A_hl%d" % i) for i in range(4)]
            rstd = P.tile("A_rstd", [128, NT], F32)
            tmp = [(P.tile("A_tmp%d" % i, [128, NT], F32), "A_tmp%d" % i) for i in range(4)]
            hn = P.tile("A_hn", [128, KC, NT], BF16)
            pj = [(P.tile("A_pj%d" % i, [128, NT], F32), "A_pj%d" % i) for i in range(3)]
            qkv = P.tile("A_qkv", [128, 12, NT], F32)
            qkvb = P.tile("A_qkvb", [128, 8, NT], F32)
            gate = P.tile("A_gate", [128, 4, NT], F32)
            gb = P.tile("A_gb", [8, 2, NT], F32)
            zz = P.tile("A_zz", [128, 4, NT], F32)
            zzb = P.tile("A_zzb", [128, 4, NT], BF16)
            zzt = P.tile("A_zzt", [128, 4, 512], BF16)
            xsrc = I["x_" + n] if l == 0 else S["X_" + n]
            def load_xt(t):
                t0 = t * NT
                P.dma("A_xt", lambda e, t0=t0: e.dma_start(out=xt[:], in_=xsrc[:, :, t0:t0 + NT].rearrange("k p t -> p k t")),
                      reads=[xkeys(g, t)], writes=["A_xt"])

            load_xt(0)
            for t in range(g.ntiles):
                t0 = t * NT
                rms_stats(xt, "A_xt", KC, D, rstd, "A_rstd", sqr)
                for k in range(KC):
                    tb, tk = tmp[k % 4]
                    P.op("dve", lambda e, tb=tb, k=k: e.scalar_tensor_tensor(out=tb[:], in0=xt[:, k, :], scalar=gmod[:, l, j, 0, k:k + 1],
                                                                           in1=rstd[:], op0=ALU.mult, op1=ALU.mult),
                         reads=["A_xt", "gmod", "A_rstd"], writes=[tk])
                    P.op("act", lambda e, tb=tb, k=k: e.activation(out=hn[:, k, :], in_=tb[:], func=AF.Identity,
                                                                    bias=mods[:, l, j, 0 + k:0 + k + 1], scale=1.0),
                         reads=[tk, "mods"], writes=["A_hn:%d" % k])
                hn_keys = ["A_hn:%d" % k for k in range(KC)]
                if t + 1 < g.ntiles:
                    load_xt(t + 1)

                def proj(c0, ncols, pt, pk):
                    for k in range(KC):
                        P.op("pe", lambda e, k=k: e.matmul(pt[0:ncols, :], lhsT=w_in_sb[:, k, c0:c0 + ncols], rhs=hn[:, k, :],
                                                           start=(k == 0), stop=(k == KC - 1)),
                             reads=["w_in_sb", "A_hn:%d" % k], writes=[pk])

                def conv(dst, dkey, src, skey, wts, width, pt, pk):
                    pad = width // 2
                    d3 = dst.rearrange("p (s c) -> p s c", c=g.seg)
                    s3 = src.rearrange("p (s c) -> p s c", c=g.seg)
                    P.op("act", lambda e: e.activation(out=dst, in_=pt[:], func=AF.Copy, scale=wts[:, pad:pad + 1]),
                         reads=[pk, "gcw", "hcw"], writes=[dkey])
                    for jj in range(width):
                        o = jj - pad
                        if o == 0:
                            continue
                        lo_d, hi_d = max(0, -o), g.seg - max(0, o)
                        lo_s, hi_s = max(0, o), g.seg - max(0, -o)
                        P.op("dve", lambda e, jj=jj, lo_d=lo_d, hi_d=hi_d, lo_s=lo_s, hi_s=hi_s: e.scalar_tensor_tensor(
                            out=d3[:, :, lo_d:hi_d], in0=s3[:, :, lo_s:hi_s], scalar=wts[:, jj:jj + 1], in1=d3[:, :, lo_d:hi_d],
                            op0=ALU.mult, op1=ALU.add), reads=[skey, dkey, "gcw", "hcw"], writes=[dkey])

                for m in range(12):
                    pt, pk = next_ps()
                    proj(m * 128, 128, pt, pk)
                    pb, pbk = pj[m % 3]
                    P.op("act", lambda e, pt=pt, pb=pb: e.copy(out=pb[:], in_=pt[:]), reads=[pk], writes=[pbk])
                    conv(qkv[:, m, :], "A_qkv:%d" % m, pb[:], pbk, gcw[:, l, m, :], 5, pt, pk)
                gpts = []
                for m in range(4):
                    pt, pk = next_ps()
                    proj(1536 + m * 128, 128, pt, pk)
                    gpts.append((pt, pk))
                for m in range(12):
                    P.op("act", lambda e, m=m: e.activation(out=qkv[:, m, :], in_=qkv[:, m, :], func=AF.Silu),
                         reads=["A_qkv:%d" % m], writes=["A_qkv:%d" % m])
                for m in range(4):
                    pt, pk = gpts[m]
                    P.op("act", lambda e, m=m, pt=pt: e.activation(out=gate[:, m, :], in_=pt[:], func=AF.Silu), reads=[pk], writes=["A_gate"])
                P.dma("A_gate_o", lambda e, t0=t0: e.dma_start(out=S["GATE_" + n][:, :, t0:t0 + NT].rearrange("m p t -> p m t"), in_=gate[:]),
                      reads=["A_gate"], writes=["GATE_%s:%d" % (n, t)])
                pt, pk = next_ps()
                proj(OFF_B, 8, pt, pk)
                P.op("act", lambda e, pt=pt: e.activation(out=gb[:, 1, :], in_=pt[0:8, :], func=AF.Sigmoid), reads=[pk], writes=["A_gb:1"])
                pt, pk = next_ps()
                proj(OFF_A, 8, pt, pk)
                P.op("act", lambda e, pt=pt: e.activation(out=gb[:, 0, :], in_=pt[0:8, :], func=AF.Exp, bias=apar[:, l, 1:2], scale=1.0),
                     reads=[pk, "apar"], writes=["A_gb:0"])
                P.op("act", lambda e: e.activation(out=gb[:, 0, :], in_=gb[:, 0, :], func=AF.Ln, bias=1.0, scale=1.0), reads=["A_gb:0"], writes=["A_gb:0"])
                P.op("dve", lambda e: e.tensor_scalar(out=gb[:, 0, :], in0=gb[:, 0, :], scalar1=nega[:, l:l + 1], scalar2=None, op0=ALU.mult),
                     reads=["A_gb:0", "nega"], writes=["A_gb:0"])
                P.dma("A_gb_o", lambda e, t0=t0: e.dma_start(out=S["GB_" + n][:, t0:t0 + NT].rearrange("(a r) t -> r a t", a=2), in_=gb[:]),
                      reads=["A_gb:0", "A_gb:1"], writes=["GB_%s:%d" % (n, t)])
                for grp in range(2):
                    rn = {}
                    ms_ = range(grp * 4, grp * 4 + 4)
                    for m in ms_:
                        sq, sqk = sqb[m % 4]
                        hl, hlk = hlb[m % 4]
                        pt, pk = next_ps()
                        sumsq_hilo(qkv[:, m, :], "A_qkv:%d" % m, sq, sqk, hl, hlk, pt, pk)
                        rn[m] = (pt, pk)
                    for m in ms_:
                        sq, sqk = sqb[m % 4]
                        pt, pk = rn[m]
                        P.op("act", lambda e, pt=pt, sq=sq: e.activation(out=sq[:], in_=pt[:], func=AF.Ln, scale=1.0, bias=EPS), reads=[pk], writes=[sqk])
                    for m in ms_:
                        sq, sqk = sqb[m % 4]
                        P.op("act", lambda e, sq=sq: e.activation(out=sq[:], in_=sq[:], func=AF.Exp, scale=-0.5), reads=[sqk], writes=[sqk])
                    for m in ms_:
                        sq, sqk = sqb[m % 4]
                        sc = DK ** -0.5 if m < 4 else 1.0
                        P.op("dve", lambda e, m=m, sq=sq, sc=sc: e.scalar_tensor_tensor(out=qkvb[:, m, :], in0=qkv[:, m, :], scalar=sc, in1=sq[:],
                                                                                     op0=ALU.mult, op1=ALU.mult),
                             reads=["A_qkv:%d" % m, sqk], writes=["A_qkvb:%d" % m])
                P.dma("A_qkv_o", lambda e, t0=t0: e.dma_start(out=S["QKV_" + n][0:8, :, t0:t0 + NT].rearrange("m p t -> p m t"), in_=qkvb[:]),
                      reads=["A_qkvb:%d" % m for m in range(8)], writes=["QKV_%s:%d" % (n, t)])
                P.dma("A_v_o", lambda e, t0=t0: e.dma_start(out=S["QKV_" + n][8:12, :, t0:t0 + NT].rearrange("m p t -> p m t"), in_=qkv[:, 8:12, :]),
                      reads=["A_qkv:%d" % m for m in range(8, 12)], writes=["QKVv_%s:%d" % (n, t)])
                for m in range(12):
                    pt, pk = next_ps()
                    proj(OFF_HY + m * 128, 128, pt, pk)
                    pb, pbk = pj[m % 3]
                    P.op("act", lambda e, pt=pt, pb=pb: e.copy(out=pb[:], in_=pt[:]), reads=[pk], writes=[pbk])
                    conv(qkv[:, m, :], "A_qkv:%d" % m, pb[:], pbk, hcw[:, l, m, :], 3, pt, pk)
                P.dma("A_x0_o", lambda e, t0=t0: e.dma_start(out=S["X0_" + n][:, :, t0:t0 + NT].rearrange("m p t -> p m t"), in_=qkv[:, 0:4, :]),
                      reads=["A_qkv:%d" % m for m in range(4)], writes=["X0_%s:%d" % (n, t)])
                for c in range(4):
                    P.op("dve", lambda e, c=c: e.tensor_tensor(out=zz[:, c, :], in0=qkv[:, 4 + c, :], in1=qkv[:, 8 + c, :], op=ALU.mult),
                         reads=["A_qkv:%d" % (4 + c), "A_qkv:%d" % (8 + c)], writes=["A_zz:%d" % c])
                    P.op("pool", lambda e, c=c: e.tensor_copy(out=zzb[:, c, :], in_=zz[:, c, :]), reads=["A_zz:%d" % c], writes=["A_zzb:%d" % c])
                P.dma("A_zz_o", lambda e, t0=t0: e.dma_start(out=S["ZZ_" + n][:, :, t0:t0 + NT].rearrange("m p t -> p m t"), in_=zz[:]),
                      reads=["A_zz:%d" % c for c in range(4)], writes=["ZZ_%s:%d" % (n, t)])
                for tb4 in range(4):
                    pt, pk = next_ps()
                    ptb = pt[:].bitcast(BF16)
                    for c in range(4):
                        P.op("pe", lambda e, c=c, tb4=tb4, ptb=ptb: e.transpose(out=ptb[:, c * 128:(c + 1) * 128],
                                                                             in_=zzb[:, c, tb4 * 128:(tb4 + 1) * 128], identity=identb[:]),
                             reads=["A_zzb:%d" % c, "identb"], writes=[pk])
                    P.op("act", lambda e, tb4=tb4, ptb=ptb: e.copy(out=zzt[:, tb4, :], in_=ptb[:, 0:512]), reads=[pk], writes=["A_zzt:%d" % tb4])
                P.dma("A_zzt_o", lambda e, t0=t0: e.dma_start(out=S["ZZT_" + n][t0:t0 + NT, :].rearrange("(b p) c -> p b c", p=128), in_=zzt[:]),
                      reads=["A_zzt:%d" % b for b in range(4)], writes=["ZZT_%s:%d" % (n, t)])
            P.release(m0)

        def phase_H(l, g):
            n = g.name
            L, nTB, NF, NFB = g.L, g.nTB, g.NF, g.NFB
            ctab, stab = I["ctab_" + n], I["stab_" + n]
            m0 = P.mark()
            hsd = P.tile("H_hsd", [128, 2, nTB, 512], BF16)
            m1 = P.mark()
            w1 = P.tile("H_w1", [HY_EMB, HY_FH], F32)
            w2 = P.tile("H_w2", [HY_FH, HY_FH], F32)
            w3e = P.tile("H_w3e", [HY_FH + 1, 1024], F32)
            zf = P.tile("H_zf", [HY_EMB, L], F32)
            negt = P.tile("H_negt", [128, nTB], F32)
            h1 = P.tile("H_h1", [HY_FH, L], F32)
            h2e = P.tile("H_h2e", [HY_FH + 1, L], F32)
            ld("H_w1", w1[:], I["hw1"][l], "H_w1")
            ld("H_w2", w2[:], I["hw2"][l], "H_w2")
            ld("H_w3e", w3e[:], I["hw3e"][l], "H_w3e")
            ld("H_zf", zf[:], I["zf_" + n], "H_zf")
            ld("H_negt", negt[:], I["negt_" + n], "H_negt")
            CW = min(512, L)
            ua = P.tile("H_ua", [HY_FH, CW], F32)
            ui = P.tile("H_ui", [HY_FH, CW], I32)
            uf = P.tile("H_uf", [HY_FH, CW], F32)

            def sin_layer(wt, wkey, kdim, src, skey, bcol, dst, dkey):
                for c0 in range(0, L, CW):
                    pt, pk = next_ps()
                    P.op("pe", lambda e, c0=c0, pt=pt: e.matmul(pt[0:HY_FH, 0:CW], lhsT=wt[0:kdim, :], rhs=src[0:kdim, c0:c0 + CW], start=True, stop=True),
                         reads=[wkey, skey], writes=[pk])
                    P.op("dve", lambda e, pt=pt: e.tensor_scalar(out=ua[:], in0=pt[0:HY_FH, 0:CW], scalar1=hvec[:, l, bcol:bcol + 1],
                                                                scalar2=hf2p[:, l:l + 1], op0=ALU.add, op1=ALU.mult),
                         reads=[pk, "hvec", "hf2p"], writes=["H_ua"])
                    P.op("dve", lambda e: e.tensor_scalar(out=ua[:], in0=ua[:], scalar1=8.5, scalar2=None, op0=ALU.add), reads=["H_ua"], writes=["H_ua"])
                    P.op("dve", lambda e: e.tensor_copy(out=ui[:], in_=ua[:]), reads=["H_ua"], writes=["H_ui"])
                    P.op("dve", lambda e: e.tensor_copy(out=uf[:], in_=ui[:]), reads=["H_ui"], writes=["H_uf"])
                    P.op("dve", lambda e: e.tensor_tensor(out=ua[:], in0=ua[:], in1=uf[:], op=ALU.subtract), reads=["H_ua", "H_uf"], writes=["H_ua"])
                    P.op("dve", lambda e: e.tensor_scalar(out=uf[:], in0=ua[:], scalar1=0.5, scalar2=None, op0=ALU.is_gt), reads=["H_ua"], writes=["H_uf"])
                    P.op("dve", lambda e: e.tensor_tensor(out=ua[:], in0=ua[:], in1=uf[:], op=ALU.subtract), reads=["H_ua", "H_uf"], writes=["H_ua"])
                    P.op("act", lambda e, c0=c0: e.activation(out=dst[0:HY_FH, c0:c0 + CW], in_=ua[:], func=AF.Sin, scale=-2 * math.pi),
                         reads=["H_ua"], writes=[dkey])

            sin_layer(w1, "H_w1", HY_EMB, zf, "H_zf", 0, h1, "H_h1")
            P.op("pool", lambda e: e.memset(h2e[64:65, :], 1.0), writes=["H_h2e"])
            sin_layer(w2, "H_w2", HY_FH, h1, "H_h1", 2, h2e, "H_h2e")
            win = P.tile("H_win", [128, 512], F32)
            hfb = [(P.tile("H_hf%d" % i, [128, 512], F32), "H_hf%d" % i) for i in range(2)]
            for tb in range(nTB):
                P.op("act", lambda e, tb=tb: e.activation(out=win[:], in_=deltab[:], func=AF.Exp, scale=negt[:, tb:tb + 1]),
                     reads=["deltab", "H_negt"], writes=["H_win"])
                for d in range(2):
                    pt, pk = next_ps()
                    P.op("pe", lambda e, tb=tb, d=d, pt=pt: e.matmul(pt[:], lhsT=h2e[:, tb * 128:(tb + 1) * 128], rhs=w3e[:, d * 512:(d + 1) * 512],
                                                                     start=True, stop=True), reads=["H_h2e", "H_w3e"], writes=[pk])
                    hb, hk = hfb[d]
                    P.op("dve", lambda e, pt=pt, hb=hb: e.tensor_tensor(out=hb[:], in0=pt[:], in1=win[:], op=ALU.mult), reads=[pk, "H_win"], writes=[hk])
                if tb == 0:
                    P.op("pool", lambda e: e.memset(hfb[1][0][0:1, :], 0.0), reads=[hfb[1][1]], writes=[hfb[1][1]])
                P.op("dve", lambda e, tb=tb: e.tensor_tensor(out=hsd[:, 0, tb, :], in0=hfb[0][0][:], in1=hfb[1][0][:], op=ALU.add),
                     reads=[hfb[0][1], hfb[1][1]], writes=["H_hsd"])
                P.op("pool", lambda e, tb=tb: e.tensor_tensor(out=hsd[:, 1, tb, :], in0=hfb[0][0][:], in1=hfb[1][0][:], op=ALU.subtract),
                     reads=[hfb[0][1], hfb[1][1]], writes=["H_hsd"])
            P.release(m1)

            ctabF, stabF = I["ctabF_" + n], I["stabF_" + n]

            def fwd_slabs():
                sl = []
                for i in range(2):
                    sl.append((P.tile("H_cs%d" % i, [128, nTB, 128], BF16), "H_cs%d" % i, P.tile("H_ss%d" % i, [128, nTB, 128], BF16), "H_ss%d" % i))
                return sl

            def load_fwd_slab(sl, fb):
                cs, ck, ss, sk = sl[fb % 2]
                P.dma(ck, lambda e: e.dma_start(out=cs[:], in_=ctabF[fb, :, 0:nTB, :]), writes=[ck])
                P.dma(sk, lambda e: e.dma_start(out=ss[:], in_=stabF[fb, :, 0:nTB, :]), writes=[sk])
                return cs, ck, ss, sk

            m1 = P.mark()
            wcol = P.tile("H_wcol", [128, NFB], F32)
            ld("H_wcol", wcol[:], I["wcol_" + n], "H_wcol")
            sl = fwd_slabs()
            fst = [(P.tile("H_fst%d" % i, [128, 2, 512], F32), "H_fst%d" % i) for i in range(2)]
            for fb in range(NFB):
                cs, ck, ss, sk = load_fwd_slab(sl, fb)
                fs, fk = fst[fb % 2]
                for which, (slab, slk) in enumerate(((cs, ck), (ss, sk))):
                    pt, pk = next_ps()
                    for tb in range(nTB):
                        P.op("pe", lambda e, tb=tb, pt=pt, slab=slab, which=which: e.matmul(
                            pt[:], lhsT=slab[:, tb, :], rhs=hsd[:, which, tb, :], start=(tb == 0), stop=(tb == nTB - 1)),
                            reads=["H_hsd", slk], writes=[pk])
                    P.op("act", lambda e, pt=pt, fs=fs, which=which, fb=fb: e.activation(out=fs[:, which, :], in_=pt[:], func=AF.Copy, scale=wcol[:, fb:fb + 1]),
                         reads=[pk, "H_wcol"], writes=[fk])
                P.dma(fk + "o", lambda e, fs=fs, fb=fb: e.dma_start(out=S["SPEC_" + n][:, fb, :, :].rearrange("w p c -> p w c"), in_=fs[:]),
                      reads=[fk], writes=["SPEC_%s:%d" % (n, fb)])
            P.release(m1)
            P.release(m0)

            for sq_i in range(g.nseq):
                s0 = sq_i * L
                m0 = P.mark()
                yf = P.tile("H_yf", [128, 2 * NFB, 512], BF16)
                m1 = P.mark()
                zzt = P.tile("H_zzt", [128, nTB, 512], BF16)
                tiles_touched = sorted(set((s0 + i * 128) // NT for i in range(nTB)))
                P.dma("H_zzt", lambda e, s0=s0, zzt=zzt: e.dma_start(out=zzt[:], in_=S["ZZT_" + n][s0:s0 + L, :].rearrange("(b p) c -> p b c", p=128)),
                      reads=["ZZT_%s:%d" % (n, t) for t in tiles_touched], writes=["H_zzt"])
                sl = fwd_slabs()
                fsl = [(P.tile("H_fsl%d" % i, [128, 2, 512], F32), "H_fsl%d" % i) for i in range(2)]
                t1 = [(P.tile("H_t1%d" % i, [128, 512], F32), "H_t1%d" % i) for i in range(4)]
                for fb in range(NFB):
                    cs, ck, ss, sk = load_fwd_slab(sl, fb)
                    fs, fk = fsl[fb % 2]
                    P.dma(fk, lambda e, fs=fs, fb=fb: e.dma_start(out=fs[:], in_=S["SPEC_" + n][:, fb, :, :].rearrange("w p c -> p w c")),
                          reads=["SPEC_%s:%d" % (n, fb)], writes=[fk])
                    zps = []
                    for slab, slk in ((cs, ck), (ss, sk)):
                        pt, pk = next_ps()
                        for tb in range(nTB):
                            P.op("pe", lambda e, tb=tb, pt=pt, slab=slab, zzt=zzt: e.matmul(
                                pt[:], lhsT=slab[:, tb, :], rhs=zzt[:, tb, :], start=(tb == 0), stop=(tb == nTB - 1)),
                                reads=["H_zzt", slk], writes=[pk])
                        zps.append((pt, pk))
                    (zc, zck), (zs, zsk) = zps
                    a, ak = t1[0]
                    b, bk = t1[1]
                    c_, c_k = t1[2]
                    d_, d_k = t1[3]
                    P.op("dve", lambda e, zc=zc, fs=fs, a=a: e.tensor_tensor(out=a[:], in0=zc[:], in1=fs[:, 0, :], op=ALU.mult), reads=[zck, fk], writes=[ak])
                    P.op("dve", lambda e, zs=zs, fs=fs, b=b: e.tensor_tensor(out=b[:], in0=zs[:], in1=fs[:, 1, :], op=ALU.mult), reads=[zsk, fk], writes=[bk])
                    P.op("dve", lambda e, zc=zc, fs=fs, c_=c_: e.tensor_tensor(out=c_[:], in0=zc[:], in1=fs[:, 1, :], op=ALU.mult), reads=[zck, fk], writes=[c_k])
                    P.op("dve", lambda e, zs=zs, fs=fs, d_=d_: e.tensor_tensor(out=d_[:], in0=zs[:], in1=fs[:, 0, :], op=ALU.mult), reads=[zsk, fk], writes=[d_k])
                    P.op("pool", lambda e, a=a, b=b, fb=fb, yf=yf: e.tensor_tensor(out=yf[:, fb, :], in0=a[:], in1=b[:], op=ALU.subtract), reads=[ak, bk], writes=["H_yf"])
                    P.op("pool", lambda e, c_=c_, d_=d_, fb=fb, yf=yf: e.tensor_tensor(out=yf[:, NFB + fb, :], in0=c_[:], in1=d_[:], op=ALU.add), reads=[c_k, d_k], writes=["H_yf"])
                P.release(m1)
                TWg = min(TW, L)
                isl = [(P.tile("H_ci%d" % i, [128, NFB, TWg], BF16), "H_ci%d" % i, P.tile("H_si%d" % i, [128, NFB, TWg], BF16), "H_si%d" % i) for i in range(2)]
                x0b = [(P.tile("H_x0%d" % i, [128, TWg], F32), "H_x0%d" % i) for i in range(2)]
                zzb_ = [(P.tile("H_zb%d" % i, [128, TWg], F32), "H_zb%d" % i) for i in range(2)]
                yo = [(P.tile("H_yo%d" % i, [128, TWg], BF16), "H_yo%d" % i) for i in range(2)]
                tm = [(P.tile("H_tm%d" % i, [128, TWg], F32), "H_tm%d" % i) for i in range(2)]
                it = 0
                for ti in range(L // TWg):
                    tt0 = ti * TWg
                    ci, cik, si, sik = isl[ti % 2]
                    P.dma(cik, lambda e, ci=ci, ti=ti: e.dma_start(out=ci[:], in_=ctab[ti, :, 0:NFB, :]), writes=[cik])
                    P.dma(sik, lambda e, si=si, ti=ti: e.dma_start(out=si[:], in_=stab[ti, :, 0:NFB, :]), writes=[sik])
                    gt0 = s0 + tt0
                    tile_i = gt0 // NT
                    for cc in range(4):
                        xb, xk = x0b[it % 2]
                        zb, zk = zzb_[it % 2]
                        yb, yk = yo[it % 2]
                        tmb, tmk = tm[it % 2]
                        it += 1
                        P.dma(xk, lambda e, xb=xb, cc=cc, gt0=gt0: e.dma_start(out=xb[:], in_=S["X0_" + n][cc, :, gt0:gt0 + TWg]),
                              reads=["X0_%s:%d" % (n, tile_i)], writes=[xk])
                        P.dma(zk, lambda e, zb=zb, cc=cc, gt0=gt0: e.dma_start(out=zb[:], in_=S["ZZ_" + n][cc, :, gt0:gt0 + TWg]),
                              reads=["ZZ_%s:%d" % (n, tile_i)], writes=[zk])
                        pt, pk = next_ps()
                        nmm = 2 * NFB
                        i_mm = 0
                        for w, (slab, slk) in enumerate(((ci, cik), (si, sik))):
                            for fb in range(NFB):
                                P.op("pe", lambda e, pt=pt, w=w, fb=fb, slab=slab, cc=cc, i_mm=i_mm: e.matmul(
                                    pt[:, 0:TWg], lhsT=yf[:, w * NFB + fb, cc * 128:(cc + 1) * 128], rhs=slab[:, fb, :],
                                    start=(i_mm == 0), stop=(i_mm == nmm - 1)), reads=["H_yf", slk], writes=[pk])
                                i_mm += 1
                        P.op("dve", lambda e, pt=pt, zb=zb, tmb=tmb, cc=cc: e.scalar_tensor_tensor(
                            out=tmb[:], in0=zb[:], scalar=hskip[:, l, cc:cc + 1], in1=pt[:, 0:TWg], op0=ALU.mult, op1=ALU.add),
                            reads=[zk, pk, "hskip"], writes=[tmk])
                        P.op("pool", lambda e, tmb=tmb, xb=xb, yb=yb: e.tensor_tensor(out=yb[:], in0=tmb[:], in1=xb[:], op=ALU.mult), reads=[tmk, xk], writes=[yk])
                        P.dma(yk + "o", lambda e, yb=yb, cc=cc, gt0=gt0: e.dma_start(out=S["YH_" + n][cc, :, gt0:gt0 + TWg], in_=yb[:]),
                              reads=[yk], writes=["YH_%s:%d:%d:%d" % (n, tile_i, cc, (gt0 % NT) // TWg)])
                P.release(m0)

        def phase_G(l, g):
            n = g.name
            m0 = P.mark()
            NCH = NT // CH
            Sst = [[(P.tile("G_S%d%d" % (d, h), [128, 128], F32), "G_S%d%d" % (d, h)) for h in range(H)] for d in range(2)]
            Sbf = [[(P.tile("G_Sb%d%d" % (d, h), [128, 128], BF16), "G_Sb%d%d" % (d, h)) for h in range(H)] for d in range(2)]

            NP = NCH // 2

            def dir_tiles(d):
                p = "G_"
                W = {}
                W["qkv"] = P.tile(p + "qkv", [128, 12, NT], F32)
                W["gbr"] = P.tile(p + "gbr", [8, 2, NT], F32)
                W["gc"] = P.tile(p + "gc", [8, NT], F32)
                W["gtot"] = P.tile(p + "gtot", [8, NCH], F32)
                W["gcs"] = P.tile(p + "gcs", [8, 3, NT], BF16)
                W["bts"] = P.tile(p + "bts", [8, 3, NT], BF16)
                W["gcb"] = P.tile(p + "gcb", [128, NT], F32)
                W["btb"] = P.tile(p + "btb", [128, NT], F32)
                W["E"] = P.tile(p + "E", [128, NT], F32)
                W["tmp"] = P.tile(p + "tmp", [128, NT], F32)
                W["gcT"] = P.tile(p + "gcT", [128, NP], F32)
                W["btT"] = P.tile(p + "btT", [128, NP], F32)
                W["sc"] = P.tile(p + "sc", [128, 3, NP], F32)
                W["eglh"] = P.tile(p + "eglh", [128, H, NCH], F32)
                W["DT"] = P.tile(p + "DT", [128, NT], F32)
                W["XB"] = P.tile(p + "XB", [128, NT], F32)
                for h in range(H):
                    W["qd%d" % h] = P.tile(p + "qd%d" % h, [128, NT], BF16)
                    W["wT%d" % h] = P.tile(p + "wT%d" % h, [128, NT], F32)
                    W["kbg%d" % h] = P.tile(p + "kbg%d" % h, [128, NP, 128], F32)
                    W["kdec%d" % h] = P.tile(p + "kdec%d" % h, [128, NP, 128], BF16)
                    W["Xf%d" % h] = P.tile(p + "Xf%d" % h, [128, NP, 128], F32)
                    W["Xtf%d" % h] = P.tile(p + "Xtf%d" % h, [128, NP, 128], F32)
                    W["vbf%d" % h] = P.tile(p + "vbf%d" % h, [128, NP, 128], F32)
                    W["X%d" % h] = P.tile(p + "X%d" % h, [128, 2, NP, 128], BF16)
                    W["Xt%d" % h] = P.tile(p + "Xt%d" % h, [128, 2, NP, 128], BF16)
                    W["R%d" % h] = P.tile(p + "R%d" % h, [128, 2, NP, 128], BF16)
                    W["Rf%d" % h] = P.tile(p + "Rf%d" % h, [128, NP, 128], F32)
                    W["qkT%d" % h] = P.tile(p + "qkT%d" % h, [128, NP, 128], BF16)
                    W["u%d" % h] = P.tile(p + "u%d" % h, [128, NP, 128], F32)
                    W["vn%d" % h] = P.tile(p + "vn%d" % h, [128, 2, 128], BF16)
                    W["oT%d" % h] = P.tile(p + "oT%d" % h, [128, NT], F32)
                W["E2"] = W["X0"][:].bitcast(F32).rearrange("p a c i -> p (a c i)").rearrange("p (c i) -> p c i", i=128)
                W["RfT"] = W["Xt0"][:].bitcast(F32).rearrange("p a c i -> p (a c i)").rearrange("p (c i) -> p c i", i=128)
                W["p"] = p
                return W

            W0 = dir_tiles(0)
            Ws = [W0, W0]

            for d in range(2):
                for h in range(H):
                    St, Sk = Sst[d][h]
                    if g.has_s0:
                        P.dma(Sk, lambda e, St=St, d=d, h=h: e.dma_start(out=St[:], in_=I["s0"][l, d, h]), writes=[Sk])
                    else:
                        P.op("pool", lambda e, St=St: e.memset(St[:], 0.0), writes=[Sk])
                    Sb, Sbk = Sbf[d][h]
                    P.op("act", lambda e, St=St, Sb=Sb: e.copy(out=Sb[:], in_=St[:]), reads=[Sk], writes=[Sbk])

            def load_tile(d, t):
                W = Ws[d]
                p = W["p"]
                t0 = t * NT
                P.dma(p + "qkv", lambda e: e.dma_start(out=W["qkv"][:], in_=S["QKV_" + n][:, :, t0:t0 + NT].rearrange("m p t -> p m t")),
                      reads=["QKV_%s:%d" % (n, t), "QKVv_%s:%d" % (n, t)], writes=[p + "qkv"])
                P.dma(p + "gbr", lambda e: e.dma_start(out=W["gbr"][:], in_=S["GB_" + n][:, t0:t0 + NT].rearrange("(a r) t -> r a t", a=2)),
                      reads=["GB_%s:%d" % (n, t)], writes=[p + "gbr"])

            def v4(ap):
                return ap.rearrange("p (q k) -> p q k", k=128)

            def chunk_local(d, t):
                W = Ws[d]
                p = W["p"]
                mi, ms = (0, 2) if d == 0 else (1, 3)
                last = CH - 1 if d == 0 else 0
                P.op("dve", lambda e: e.tensor_tensor_scan(out=W["gc"][:], data0=scanmask[:], data1=W["gbr"][:, 0, :], initial=0.0, op0=ALU.mult, op1=ALU.add),
                     reads=[p + "gbr", "scanmask"], writes=[p + "gc"])
                if d == 1:
                    gc3 = W["gc"][:].rearrange("r (c i) -> r c i", i=CH)
                    P.op("dve", lambda e: e.tensor_copy(out=W["gtot"][:], in_=gc3[:, :, CH - 1]), reads=[p + "gc"], writes=[p + "gtot"])
                    P.op("dve", lambda e: e.tensor_tensor(out=W["gc"][:], in0=W["gbr"][:, 0, :], in1=W["gc"][:], op=ALU.subtract),
                         reads=[p + "gbr", p + "gc"], writes=[p + "gc"])
                    P.op("dve", lambda e: e.tensor_tensor(out=gc3, in0=gc3, in1=W["gtot"][:].unsqueeze(2).to_broadcast([8, NCH, CH]), op=ALU.add),
                         reads=[p + "gc", p + "gtot"], writes=[p + "gc"])

                def split3(src_ap, skey, dst, dkey):
                    sp0, sp1 = W["tmp"][0:8, :], W["DT"][0:8, :]
                    P.op("dve", lambda e: e.tensor_copy(out=dst[:, 0, :], in_=src_ap), reads=[skey], writes=[dkey])
                    P.op("dve", lambda e: e.tensor_tensor(out=sp0, in0=src_ap, in1=dst[:, 0, :], op=ALU.subtract), reads=[skey, dkey], writes=[p + "tmp"])
                    P.op("dve", lambda e: e.tensor_copy(out=dst[:, 1, :], in_=sp0), reads=[p + "tmp"], writes=[dkey])
                    P.op("dve", lambda e: e.tensor_tensor(out=sp1, in0=sp0, in1=dst[:, 1, :], op=ALU.subtract), reads=[p + "tmp", dkey], writes=[p + "DT"])
                    P.op("dve", lambda e: e.tensor_copy(out=dst[:, 2, :], in_=sp1), reads=[p + "DT"], writes=[dkey])

                split3(W["gc"][:], p + "gc", W["gcs"], p + "gcs")
                split3(W["gbr"][:, 1, :], p + "gbr", W["bts"], p + "bts")

                def bcast(pt_ap, pk_, r_, src, skey):
                    for i3 in range(3):
                        P.op("pe", lambda e, i3=i3: e.matmul(pt_ap, lhsT=selb[:, r_, :], rhs=src[:, i3, :], start=(i3 == 0), stop=(i3 == 2)),
                             reads=["selb", skey], writes=[pk_])

                def xt_part(h):
                    Xh, Xth, Rh = W["X%d" % h], W["Xt%d" % h], W["R%d" % h]
                    kX, kXt, kR = p + "X%d" % h, p + "Xt%d" % h, p + "R%d" % h
                    Xf, Xtf = W["Xf%d" % h], W["Xtf%d" % h]
                    ptt, pkt = next_ps()
                    for q in range(NP):
                        P.op("pe", lambda e, q=q, ptt=ptt, Xf=Xf: e.transpose(out=ptt[:, q * 128:(q + 1) * 128], in_=Xf[:, q, :], identity=ident[:]),
                             reads=[p + "Xf%d" % h, "ident"], writes=[pkt])
                    P.op("act", lambda e, ptt=ptt, Xtf=Xtf: e.copy(out=Xtf[:], in_=v4(ptt[:])), reads=[pkt], writes=[p + "Xtf%d" % h])
                    P.op("pool", lambda e, Xth=Xth, Xtf=Xtf: e.tensor_copy(out=Xth[:, 0, :, :], in_=Xtf[:]), reads=[p + "Xtf%d" % h], writes=[kXt + ":0"])
                    P.op("pool", lambda e, Xh=Xh, Rh=Rh: e.tensor_copy(out=Rh[:, 0, :, :], in_=Xh[:, 0, :, :]), reads=[kX + ":0"], writes=[kR + ":0"])
                    P.op("pool", lambda e, Xf=Xf, h=h: e.tensor_copy(out=W["Rf%d" % h][:], in_=Xf[:]), reads=[p + "Xf%d" % h], writes=[p + "Rf%d" % h])

                gcb4, btb4, tmp4, DT4, XB4 = v4(W["gcb"][:]), v4(W["btb"][:]), v4(W["tmp"][:]), v4(W["DT"][:]), v4(W["XB"][:])
                eye_b = ident[:].unsqueeze(1).to_broadcast([128, NP, 128])
                gcbc3 = W["gcb"][:].rearrange("p (c i) -> p c i", i=CH)
                for h in range(H):
                    r = d * 4 + h
                    qT = W["qkv"][:, h, :]
                    kT = W["qkv"][:, 4 + h, :]
                    vT = W["qkv"][:, 8 + h, :]
                    pt, pk = next_ps()
                    bcast(pt[:], pk, r, W["gcs"], p + "gcs")
                    P.op("act", lambda e, pt=pt: e.copy(out=W["gcb"][:], in_=pt[:]), reads=[pk], writes=[p + "gcb"])
                    P.op("act", lambda e, pt=pt: e.activation(out=W["E"][:], in_=pt[:], func=AF.Exp), reads=[pk], writes=[p + "E"])
                    pt2, pk2 = next_ps()
                    bcast(pt2[:], pk2, r, W["bts"], p + "bts")
                    P.op("act", lambda e, pt2=pt2: e.copy(out=W["btb"][:], in_=pt2[:]), reads=[pk2], writes=[p + "btb"])
                    P.op("dve", lambda e: e.tensor_tensor(out=tmp4, in0=gcb4, in1=eye_b, op=ALU.mult), reads=[p + "gcb", "ident"], writes=[p + "tmp"])
                    P.op("dve", lambda e: e.tensor_reduce(out=W["gcT"][:], in_=tmp4, axis=AX.X, op=ALU.add), reads=[p + "tmp"], writes=[p + "gcT"])
                    P.op("dve", lambda e: e.tensor_tensor(out=tmp4, in0=btb4, in1=eye_b, op=ALU.mult), reads=[p + "btb", "ident"], writes=[p + "tmp"])
                    P.op("dve", lambda e: e.tensor_reduce(out=W["btT"][:], in_=tmp4, axis=AX.X, op=ALU.add), reads=[p + "tmp"], writes=[p + "btT"])
                    P.op("act", lambda e: e.activation(out=W["sc"][:, 2, :], in_=W["gcT"][:], func=AF.Exp), reads=[p + "gcT"], writes=[p + "sc:2"])
                    P.op("dve", lambda e: e.tensor_tensor(out=W["sc"][:, 0, :], in0=W["sc"][:, 2, :], in1=W["btT"][:], op=ALU.mult),
                         reads=[p + "sc:2", p + "btT"], writes=[p + "sc:0"])
                    for c2 in range(2):
                        ps_ = slice(c2 * 64, (c2 + 1) * 64)
                        P.op("dve", lambda e, ps_=ps_, c2=c2: e.tensor_tensor(out=W["sc"][ps_, 1, :], in0=gcb4[ps_, :, c2 * 64 + last], in1=W["gcT"][ps_, :], op=ALU.subtract),
                             reads=[p + "gcb", p + "gcT"], writes=[p + "sc:1"])
                    P.op("act", lambda e: e.activation(out=W["sc"][:, 1, :], in_=W["sc"][:, 1, :], func=AF.Exp), reads=[p + "sc:1"], writes=[p + "sc:1"])
                    P.op("act", lambda e, h=h: e.activation(out=W["eglh"][:, h, :], in_=gcbc3[:, :, last], func=AF.Exp), reads=[p + "gcb"], writes=[p + "eglh"])
                    P.op("dve", lambda e: e.tensor_tensor(out=tmp4, in0=gcb4, in1=W["gcT"][:].unsqueeze(2).to_broadcast([128, NP, 128]), op=ALU.subtract),
                         reads=[p + "gcb", p + "gcT"], writes=[p + "tmp"])
                    P.op("dve", lambda e: e.tensor_scalar(out=W["tmp"][:], in0=W["tmp"][:], scalar1=0.0, scalar2=None, op0=ALU.min), reads=[p + "tmp"], writes=[p + "tmp"])
                    P.op("act", lambda e: e.activation(out=W["tmp"][:], in_=W["tmp"][:], func=AF.Exp), reads=[p + "tmp"], writes=[p + "tmp"])
                    P.op("dve", lambda e: e.tensor_tensor(out=DT4, in0=tmp4, in1=masks2[:, mi, :].unsqueeze(1).to_broadcast([128, NP, 128]), op=ALU.mult),
                         reads=[p + "tmp", "masks2"], writes=[p + "DT"])
                    P.op("pool", lambda e: e.tensor_tensor(out=XB4, in0=DT4, in1=masks2[:, ms, :].unsqueeze(1).to_broadcast([128, NP, 128]), op=ALU.mult),
                         reads=[p + "DT", "masks2"], writes=[p + "XB"])
                    P.op("dve", lambda e: e.tensor_tensor(out=W["XB"][:], in0=W["XB"][:], in1=W["btb"][:], op=ALU.mult), reads=[p + "XB", p + "btb"], writes=[p + "XB"])
                    P.op("dve", lambda e, h=h, qT=qT: e.tensor_tensor(out=W["qd%d" % h][:], in0=qT, in1=W["E"][:], op=ALU.mult),
                         reads=[p + "qkv", p + "E"], writes=[p + "qd%d" % h])
                    P.op("pool", lambda e, h=h, kT=kT: e.tensor_tensor(out=W["wT%d" % h][:], in0=kT, in1=W["E"][:], op=ALU.mult),
                         reads=[p + "qkv", p + "E"], writes=[p + "wT%d" % h])
                    P.op("pool", lambda e, h=h: e.tensor_tensor(out=W["wT%d" % h][:], in0=W["wT%d" % h][:], in1=W["btb"][:], op=ALU.mult),
                         reads=[p + "btb", p + "wT%d" % h], writes=[p + "wT%d" % h])
                    for kind, src in ((0, kT), (1, vT)):
                        pt4, pk4 = next_ps()
                        for q in range(NP):
                            P.op("pe", lambda e, pt4=pt4, q=q, src=src: e.transpose(out=pt4[:, q * 128:(q + 1) * 128], in_=src[:, q * 128:(q + 1) * 128], identity=ident[:]),
                                 reads=[p + "qkv", "ident"], writes=[pk4])
                        src3 = v4(pt4[:])
                        if kind == 0:
                            P.op("dve", lambda e, src3=src3, h=h: e.tensor_tensor(
                                out=W["kbg%d" % h][:], in0=src3, in1=W["sc"][:, 0, :].unsqueeze(2).to_broadcast([128, NP, 128]), op=ALU.mult),
                                reads=[pk4, p + "sc:0"], writes=[p + "kbg%d" % h])
                            P.op("dve", lambda e, src3=src3, h=h: e.tensor_tensor(
                                out=W["kdec%d" % h][:], in0=src3, in1=W["sc"][:, 1, :].unsqueeze(2).to_broadcast([128, NP, 128]), op=ALU.mult),
                                reads=[pk4, p + "sc:1"], writes=[p + "kdec%d" % h])
                        else:
                            P.op("dve", lambda e, src3=src3, h=h: e.tensor_tensor(
                                out=W["vbf%d" % h][:], in0=src3, in1=W["btT"][:].unsqueeze(2).to_broadcast([128, NP, 128]), op=ALU.mult),
                                reads=[pk4, p + "btT"], writes=[p + "vbf%d" % h])
                    ptk, pkk = next_ps()
                    ptq, pkq = next_ps()
                    for q in range(NP):
                        qs = slice(q * 128, (q + 1) * 128)
                        P.op("pe", lambda e, qs=qs, ptk=ptk, kT=kT: e.matmul(ptk[:, qs], lhsT=kT[:, qs], rhs=kT[:, qs], start=True, stop=True), reads=[p + "qkv"], writes=[pkk])
                        P.op("pe", lambda e, qs=qs, ptq=ptq, kT=kT, qT=qT: e.matmul(ptq[:, qs], lhsT=kT[:, qs], rhs=qT[:, qs], start=True, stop=True), reads=[p + "qkv"], writes=[pkq])
                    Xh = W["X%d" % h]
                    kX = p + "X%d" % h
                    Xf = W["Xf%d" % h]
                    P.op("dve", lambda e, ptk=ptk, Xf=Xf: e.tensor_tensor(out=Xf[:], in0=v4(ptk[:]), in1=XB4, op=ALU.mult), reads=[pkk, p + "XB"], writes=[p + "Xf%d" % h])
                    P.op("dve", lambda e, ptq=ptq, h=h: e.tensor_tensor(out=W["qkT%d" % h][:], in0=v4(ptq[:]), in1=DT4, op=ALU.mult), reads=[pkq, p + "DT"], writes=[p + "qkT%d" % h])
                    P.op("pool", lambda e, Xh=Xh, Xf=Xf: e.tensor_copy(out=Xh[:, 0, :, :], in_=Xf[:]), reads=[p + "Xf%d" % h], writes=[kX + ":0"])
                    if h > 0:
                        xt_part(h - 1)
                xt_part(H - 1)
                for it in range(4):
                    a, b = it % 2, (it + 1) % 2
                    for h in range(H):
                        Xh, Xth = W["X%d" % h], W["Xt%d" % h]
                        kX, kXt = p + "X%d" % h, p + "Xt%d" % h
                        pa, pka = next_ps()
                        for q in range(NP):
                            P.op("pe", lambda e, q=q, pa=pa, Xh=Xh, Xth=Xth, a=a: e.matmul(pa[:, q * 128:(q + 1) * 128], lhsT=Xth[:, a, q, :], rhs=Xh[:, a, q, :], start=True, stop=True),
                                 reads=[kX + ":%d" % a, kXt + ":%d" % a], writes=[pka])
                        P.op("act", lambda e, pa=pa, Xh=Xh, b=b: e.copy(out=Xh[:, b, :, :], in_=v4(pa[:])), reads=[pka], writes=[kX + ":%d" % b])
                        pb_, pkb = next_ps()
                        for q in range(NP):
                            P.op("pe", lambda e, q=q, pb_=pb_, Xh=Xh, Xth=Xth, a=a: e.matmul(pb_[:, q * 128:(q + 1) * 128], lhsT=Xh[:, a, q, :], rhs=Xth[:, a, q, :], start=True, stop=True),
                                 reads=[kX + ":%d" % a, kXt + ":%d" % a], writes=[pkb])
                        P.op("act", lambda e, pb_=pb_, Xth=Xth, b=b: e.copy(out=Xth[:, b, :, :], in_=v4(pb_[:])), reads=[pkb], writes=[kXt + ":%d" % b])
                    for h in range(H):
                        Xh, Xth, Rh = W["X%d" % h], W["Xt%d" % h], W["R%d" % h]
                        kX, kXt, kR = p + "X%d" % h, p + "Xt%d" % h, p + "R%d" % h
                        pr, pkr = next_ps()
                        for q in range(NP):
                            P.op("pe", lambda e, q=q, pr=pr, Rh=Rh, Xth=Xth, a=a, b=b: e.matmul(pr[:, q * 128:(q + 1) * 128], lhsT=Xth[:, b, q, :], rhs=Rh[:, a, q, :], start=True, stop=True),
                                 reads=[kXt + ":%d" % b, kR + ":%d" % a], writes=[pkr])
                        Rf = W["Rf%d" % h]
                        P.op("dve", lambda e, pr=pr, Rf=Rf: e.tensor_tensor(out=Rf[:], in0=v4(pr[:]), in1=Rf[:], op=ALU.add), reads=[pkr, p + "Rf%d" % h], writes=[p + "Rf%d" % h])
                        P.op("pool", lambda e, Rf=Rf, Xh=Xh, b=b: e.tensor_tensor(out=Rf[:], in0=Rf[:], in1=Xh[:, b, :, :], op=ALU.add),
                             reads=[p + "Rf%d" % h, kX + ":%d" % b], writes=[p + "Rf%d" % h])
                        P.op("act", lambda e, Rf=Rf, Rh=Rh, b=b: e.copy(out=Rh[:, b, :, :], in_=Rf[:]), reads=[p + "Rf%d" % h], writes=[kR + ":%d" % b])
                kE2 = [p + "X0:0", p + "X0:1"]
                kRT = [p + "Xt0:0", p + "Xt0:1"]
                for h in range(H):
                    Xf, Xtf, Rf = W["Xf%d" % h], W["Xtf%d" % h], W["Rf%d" % h]
                    kRf = p + "Rf%d" % h
                    for rstep in range(NEWTON_STEPS):
                        pe2, pke2 = next_ps()
                        for q in range(NP):
                            P.op("pe", lambda e, q=q, pe2=pe2, Xtf=Xtf, Rf=Rf: e.matmul(pe2[:, q * 128:(q + 1) * 128], lhsT=Xtf[:, q, :], rhs=Rf[:, q, :], start=True, stop=True),
                                 reads=[p + "Xtf%d" % h, kRf], writes=[pke2])
                        P.op("dve", lambda e, pe2=pe2, Xf=Xf: e.tensor_tensor(out=W["E2"], in0=v4(pe2[:]), in1=Xf[:], op=ALU.add), reads=[pke2, p + "Xf%d" % h], writes=kE2)
                        P.op("dve", lambda e, Rf=Rf: e.tensor_tensor(out=W["E2"], in0=W["E2"], in1=Rf[:], op=ALU.subtract), reads=kE2 + [kRf], writes=kE2)
                        prt, pkrt = next_ps()
                        for q in range(NP):
                            P.op("pe", lambda e, q=q, prt=prt, Rf=Rf: e.transpose(out=prt[:, q * 128:(q + 1) * 128], in_=Rf[:, q, :], identity=ident[:]), reads=[kRf, "ident"], writes=[pkrt])
                        P.op("act", lambda e, prt=prt: e.copy(out=W["RfT"], in_=v4(prt[:])), reads=[pkrt], writes=kRT)
                        pre, pkre = next_ps()
                        for q in range(NP):
                            P.op("pe", lambda e, q=q, pre=pre: e.matmul(pre[:, q * 128:(q + 1) * 128], lhsT=W["RfT"][:, q, :], rhs=W["E2"][:, q, :], start=True, stop=True),
                                 reads=kRT + kE2, writes=[pkre])
                        P.op("dve", lambda e, Rf=Rf: e.tensor_tensor(out=Rf[:], in0=Rf[:], in1=W["E2"], op=ALU.add), reads=[kRf] + kE2, writes=[kRf])
                        P.op("dve", lambda e, pre=pre, Rf=Rf: e.tensor_tensor(out=Rf[:], in0=v4(pre[:]), in1=Rf[:], op=ALU.add), reads=[pkre, kRf], writes=[kRf])
                for h in range(H):
                    Rf = W["Rf%d" % h]
                    kRf = p + "Rf%d" % h
                    pu, pku = next_ps()
                    for q in range(NP):
                        P.op("pe", lambda e, q=q, pu=pu, Rf=Rf, h=h: e.matmul(pu[:, q * 128:(q + 1) * 128], lhsT=Rf[:, q, :], rhs=W["vbf%d" % h][:, q, :], start=True, stop=True),
                             reads=[kRf, p + "vbf%d" % h], writes=[pku])
                    P.op("dve", lambda e, pu=pu, h=h: e.tensor_tensor(out=W["u%d" % h][:], in0=v4(pu[:]), in1=W["vbf%d" % h][:], op=ALU.add),
                         reads=[pku, p + "vbf%d" % h], writes=[p + "u%d" % h])
                    pw, pkw = next_ps()
                    for q in range(NP):
                        P.op("pe", lambda e, q=q, pw=pw, Rf=Rf, h=h: e.matmul(pw[:, q * 128:(q + 1) * 128], lhsT=W["kbg%d" % h][:, q, :], rhs=Rf[:, q, :], start=True, stop=True),
                             reads=[kRf, p + "kbg%d" % h], writes=[pkw])
                    P.op("dve", lambda e, pw=pw, h=h: e.tensor_tensor(out=W["wT%d" % h][:], in0=pw[:], in1=W["wT%d" % h][:], op=ALU.add),
                         reads=[pkw, p + "wT%d" % h], writes=[p + "wT%d" % h])

            def scan_tiles(pairs):
                orders = {}
                for d, t in pairs:
                    orders[d] = list(range(NCH)) if d == 0 else list(range(NCH - 1, -1, -1))
                for step in range(NCH):
                    for d, t in pairs:
                        W = Ws[d]
                        p = W["p"]
                        c = orders[d][step]
                        q, c2 = c // 2, c % 2
                        hs = slice(c2 * 64, (c2 + 1) * 64)
                        gpos = t * NT + c * CH
                        seq = gpos // g.L
                        is_start = (gpos % g.L == 0) if d == 0 else ((gpos + CH) % g.L == 0)
                        is_end = ((gpos + CH) % g.L == 0) if d == 0 else (gpos % g.L == 0)
                        par = step % 2
                        for h in range(H):
                            St, Sk = Sst[d][h]
                            Sb, Sbk = Sbf[d][h]
                            if is_start and not g.has_s0 and not (step == 0 and ((d == 0 and t == 0) or (d == 1 and t == g.ntiles - 1))):
                                P.op("pool", lambda e, St=St: e.memset(St[:], 0.0), reads=[Sk], writes=[Sk])
                                P.op("pool", lambda e, Sb=Sb: e.memset(Sb[:], 0.0), reads=[Sbk], writes=[Sbk])
                        pvs = []
                        for h in range(H):
                            St, Sk = Sst[d][h]
                            pv, pkv = next_ps()
                            P.op("pe", lambda e, pv=pv, q=q, h=h, St=St, W=W: e.matmul(pv[:, 0:128], lhsT=W["wT%d" % h][:, q * 128:(q + 1) * 128], rhs=St[:], start=True, stop=True),
                                 reads=[p + "wT%d" % h, Sk], writes=[pkv])
                            pvs.append((pv, pkv))
                        for h in range(H):
                            pv, pkv = pvs[h]
                            vn = W["vn%d" % h]
                            vk = p + "vn%d:%d" % (h, par)
                            P.op("dve", lambda e, pv=pv, q=q, h=h, vn=vn, par=par, hs=hs, W=W: e.tensor_tensor(out=vn[hs, par, :], in0=W["u%d" % h][hs, q, :], in1=pv[hs, 0:128], op=ALU.subtract),
                                 reads=[p + "u%d" % h, pkv], writes=[vk])
                        pss_l = []
                        pos = []
                        for h in range(H):
                            Sb, Sbk = Sbf[d][h]
                            vn = W["vn%d" % h]
                            vk = p + "vn%d:%d" % (h, par)
                            po, pko = next_ps()
                            P.op("pe", lambda e, po=po, c=c, h=h, Sb=Sb, W=W: e.matmul(po[:, 0:CH], lhsT=Sb[:], rhs=W["qd%d" % h][:, c * CH:(c + 1) * CH], start=True, stop=False),
                                 reads=[p + "qd%d" % h, Sbk], writes=[pko])
                            P.op("pe", lambda e, po=po, q=q, c2=c2, h=h, vn=vn, par=par, hs=hs, W=W: e.matmul(po[:, 0:CH], lhsT=vn[hs, par, :], rhs=W["qkT%d" % h][hs, q, c2 * 64:(c2 + 1) * 64], start=False, stop=True),
                                 reads=[p + "qkT%d" % h, vk], writes=[pko])
                            pos.append((po, pko))
                            pss, pks = next_ps()
                            P.op("pe", lambda e, pss=pss, q=q, h=h, vn=vn, par=par, hs=hs, W=W: e.matmul(pss[:, 0:128], lhsT=W["kdec%d" % h][hs, q, :], rhs=vn[hs, par, :], start=True, stop=True),
                                 reads=[p + "kdec%d" % h, vk], writes=[pks])
                            pss_l.append((pss, pks))
                        for h in range(H):
                            St, Sk = Sst[d][h]
                            Sb, Sbk = Sbf[d][h]
                            po, pko = pos[h]
                            pss, pks = pss_l[h]
                            P.op("act", lambda e, po=po, c=c, h=h, W=W: e.copy(out=W["oT%d" % h][:, c * CH:(c + 1) * CH], in_=po[:, 0:CH]), reads=[pko], writes=[p + "oT%d" % h])
                            P.op("dve", lambda e, pss=pss, c=c, h=h, St=St, W=W: e.scalar_tensor_tensor(out=St[:], in0=St[:], scalar=W["eglh"][:, h, c:c + 1], in1=pss[:, 0:128], op0=ALU.mult, op1=ALU.add),
                                 reads=[Sk, pks, p + "eglh"], writes=[Sk])
                            P.op("act", lambda e, St=St, Sb=Sb: e.copy(out=Sb[:], in_=St[:]), reads=[Sk], writes=[Sbk])
                            if is_end and g.wstate:
                                d_ = P.dma("nst_%d%d" % (d, h), lambda e, St=St, seq=seq, d=d, h=h: e.dma_start(out=O["nstate"][seq, l, d, h], in_=St[:]), reads=[Sk])
                                fin.append(d_.idx)
                for d, t in pairs:
                    W = Ws[d]
                    p = W["p"]
                    for h in range(H):
                        P.dma(p + "oT%d" % h, lambda e, W=W, h=h, d=d, t=t: e.dma_start(out=S["O_" + n][d, h, :, t * NT:(t + 1) * NT], in_=W["oT%d" % h][:]),
                              reads=[p + "oT%d" % h], writes=["O_%s:%d:%d:%d" % (n, d, h, t)])

            seq = [(0, t) for t in range(g.ntiles)] + [(1, t) for t in range(g.ntiles - 1, -1, -1)]
            load_tile(*seq[0])
            for i_, (d, t) in enumerate(seq):
                chunk_local(d, t)
                if i_ + 1 < len(seq):
                    load_tile(*seq[i_ + 1])
                scan_tiles([(d, t)])
            P.release(m0)

        def phase_C1(l, g, w_out_sb):
            n = g.name
            j = g.cond
            m0 = P.mark()
            xt = P.tile("C_xt", [128, KC, NT], F32)
            of = P.tile("C_of", [128, 4, NT], F32)
            ob = P.tile("C_ob", [128, 4, NT], F32)
            gate = P.tile("C_gate", [128, 4, NT], F32)
            mix = P.tile("C_mix", [128, KC, NT], BF16)
            sqb = [(P.tile("C_sq%d" % i, [128, NT], F32), "C_sq%d" % i) for i in range(4)]
            sqr = [(P.tile("C_sqr%d" % i, [128, NT], BF16), "C_sqr%d" % i) for i in range(8)]
            hlb = [(P.tile("C_hl%d" % i, [128, 2, NT], BF16), "C_hl%d" % i) for i in range(4)]
            tmp = [(P.tile("C_tmp%d" % i, [128, NT], F32), "C_tmp%d" % i) for i in range(4)]
            rstd = P.tile("C_rstd", [128, NT], F32)
            h2 = P.tile("C_h2", [128, KC, NT], BF16)
            def load_oga(t):
                t0 = t * NT
                P.dma("C_of", lambda e, t0=t0: e.dma_start(out=of[:], in_=S["O_" + n][0, :, :, t0:t0 + NT].rearrange("h p t -> p h t")),
                      reads=["O_%s:0:%d:%d" % (n, h, t) for h in range(H)], writes=["C_of"])
                P.dma("C_ob", lambda e, t0=t0: e.dma_start(out=ob[:], in_=S["O_" + n][1, :, :, t0:t0 + NT].rearrange("h p t -> p h t")),
                      reads=["O_%s:1:%d:%d" % (n, h, t) for h in range(H)], writes=["C_ob"])
                P.dma("C_gate", lambda e, t0=t0: e.dma_start(out=gate[:], in_=S["GATE_" + n][:, :, t0:t0 + NT].rearrange("m p t -> p m t")),
                      reads=["GATE_%s:%d" % (n, t)], writes=["C_gate"])

            def load_yh(t):
                t0 = t * NT
                nsub = NT // min(TW, g.L)
                P.dma("C_yh", lambda e, t0=t0: e.dma_start(out=mix[:, 4:8, :], in_=S["YH_" + n][:, :, t0:t0 + NT].rearrange("m p t -> p m t")),
                      reads=["YH_%s:%d:%d:%d" % (n, t, cc, s_) for cc in range(4) for s_ in range(nsub)], writes=["C_mix:hy"])

            for t in range(g.ntiles):
                t0 = t * NT
                P.dma("C_xt", lambda e, t0=t0: e.dma_start(out=xt[:], in_=(I["x_" + n] if l == 0 else S["X_" + n])[:, :, t0:t0 + NT].rearrange("k p t -> p k t")),
                      reads=[xkeys(g, t)], writes=["C_xt"])
                if t == 0:
                    load_oga(0)
                    load_yh(0)
                P.op("dve", lambda e: e.tensor_tensor(out=of[:], in0=of[:], in1=ob[:], op=ALU.add), reads=["C_of", "C_ob"], writes=["C_of"])
                gp = []
                for h in range(H):
                    sq, sqk = sqb[h]
                    hl, hlk = hlb[h]
                    pt, pk = next_ps()
                    sumsq_hilo(of[:, h, :], "C_of", sq, sqk, hl, hlk, pt, pk)
                    gp.append((pt, pk))
                for h in range(H):
                    sq, sqk = sqb[h]
                    pt, pk = gp[h]
                    P.op("act", lambda e, pt=pt, sq=sq: e.activation(out=sq[:], in_=pt[:], func=AF.Ln, scale=1.0 / DK, bias=EPS), reads=[pk], writes=[sqk])
                for h in range(H):
                    sq, sqk = sqb[h]
                    P.op("act", lambda e, sq=sq: e.activation(out=sq[:], in_=sq[:], func=AF.Exp, scale=-0.5), reads=[sqk], writes=[sqk])
                for h in range(H):
                    sq, sqk = sqb[h]
                    tb, tk = tmp[h]
                    P.op("dve", lambda e, h=h, sq=sq, tb=tb: e.scalar_tensor_tensor(out=tb[:], in0=of[:, h, :], scalar=gng[:, l:l + 1], in1=sq[:], op0=ALU.mult, op1=ALU.mult),
                         reads=["C_of", "gng", sqk], writes=[tk])
                    P.op("pool", lambda e, h=h, tb=tb: e.tensor_tensor(out=mix[:, h, :], in0=tb[:], in1=gate[:, h, :], op=ALU.mult), reads=[tk, "C_gate"], writes=["C_mix:%d" % h])
                mixkeys = ["C_mix:%d" % h for h in range(H)] + ["C_mix:hy"]
                if t + 1 < g.ntiles:
                    load_oga(t + 1)
                for m in range(KC):
                    pt, pk = next_ps()
                    for k in range(KC):
                        P.op("pe", lambda e, pt=pt, k=k, m=m: e.matmul(pt[:], lhsT=w_out_sb[:, k, m * 128:(m + 1) * 128], rhs=mix[:, k, :], start=(k == 0), stop=(k == KC - 1)),
                             reads=["w_out_sb"] + mixkeys, writes=[pk])
                    P.op("dve", lambda e, pt=pt, m=m: e.scalar_tensor_tensor(out=xt[:, m, :], in0=pt[:], scalar=mods[:, l, j, 16 + m:16 + m + 1], in1=xt[:, m, :], op0=ALU.mult, op1=ALU.add),
                         reads=[pk, "mods", "C_xt"], writes=["C_xt"])
                if t + 1 < g.ntiles:
                    load_yh(t + 1)
                P.dma("C_xo", lambda e, t0=t0: e.dma_start(out=S["X_" + n][:, :, t0:t0 + NT].rearrange("k p t -> p k t"), in_=xt[:]),
                      reads=["C_xt"], writes=[xkeys(g, t)])
                rms_stats(xt, "C_xt", KC, D, rstd, "C_rstd", sqr)
                for k in range(KC):
                    tb, tk = tmp[k % 4]
                    P.op("dve", lambda e, tb=tb, k=k: e.scalar_tensor_tensor(out=tb[:], in0=xt[:, k, :], scalar=gmod[:, l, j, 1, k:k + 1], in1=rstd[:], op0=ALU.mult, op1=ALU.mult),
                         reads=["C_xt", "gmod", "C_rstd"], writes=[tk])
                    P.op("act", lambda e, tb=tb, k=k: e.activation(out=h2[:, k, :], in_=tb[:], func=AF.Identity, bias=mods[:, l, j, 24 + k:24 + k + 1], scale=1.0),
                         reads=[tk, "mods"], writes=["C_h2"])
                P.dma("C_h2o", lambda e, t0=t0: e.dma_start(out=S["H2_" + n][:, :, t0:t0 + NT].rearrange("k p t -> p k t"), in_=h2[:]),
                      reads=["C_h2"], writes=["H2_%s:%d" % (n, t)])
            P.release(m0)

        def phase_C2(l, g, w1_sb, w2_sb):
            n = g.name
            j = g.cond
            lastl = (l == depth - 1)
            m0 = P.mark()
            xt = P.tile("M_xt", [128, KC, NT], F32)
            h2 = P.tile("M_h2", [128, KC, NT], BF16)
            act = P.tile("M_act", [128, 16, NT], BF16)
            rl = [(P.tile("M_rl%d" % i, [128, NT], F32), "M_rl%d" % i) for i in range(3)]
            sqb = [(P.tile("M_sq%d" % i, [128, NT], BF16), "M_sq%d" % i) for i in range(2)]
            rstd = P.tile("M_rstd", [128, NT], F32)
            def load_h2(t):
                t0 = t * NT
                P.dma("M_h2", lambda e, t0=t0: e.dma_start(out=h2[:], in_=S["H2_" + n][:, :, t0:t0 + NT].rearrange("k p t -> p k t")), reads=["H2_%s:%d" % (n, t)], writes=["M_h2"])

            load_h2(0)
            for t in range(g.ntiles):
                t0 = t * NT
                P.dma("M_xt", lambda e, t0=t0: e.dma_start(out=xt[:], in_=S["X_" + n][:, :, t0:t0 + NT].rearrange("k p t -> p k t")), reads=[xkeys(g, t)], writes=["M_xt"])
                for half in range(2):
                    for mm in range(16):
                        col = (half * 16 + mm) * 128
                        pt, pk = next_ps()
                        for k in range(KC):
                            P.op("pe", lambda e, pt=pt, k=k, col=col: e.matmul(pt[:], lhsT=w1_sb[:, k, col:col + 128], rhs=h2[:, k, :], start=(k == 0), stop=(k == KC - 1)),
                                 reads=["w1_sb", "M_h2"], writes=[pk])
                        rb, rk = rl[mm % 3]
                        P.op("act", lambda e, pt=pt, rb=rb: e.activation(out=rb[:], in_=pt[:], func=AF.Relu), reads=[pk], writes=[rk])
                        P.op("pool", lambda e, rb=rb, mm=mm: e.tensor_tensor(out=act[:, mm, :], in0=rb[:], in1=rb[:], op=ALU.mult), reads=[rk], writes=["M_act:%d" % mm])
                    if half == 1 and t + 1 < g.ntiles:
                        load_h2(t + 1)
                    for m in range(KC):
                        pt, pk = next_ps()
                        for kk in range(16):
                            P.op("pe", lambda e, pt=pt, kk=kk, m=m, half=half: e.matmul(pt[:], lhsT=w2_sb[:, half * 16 + kk, m * 128:(m + 1) * 128], rhs=act[:, kk, :], start=(kk == 0), stop=(kk == 15)),
                                 reads=["w2_sb", "M_act:%d" % kk], writes=[pk])
                        P.op("dve", lambda e, pt=pt, m=m: e.scalar_tensor_tensor(out=xt[:, m, :], in0=pt[:], scalar=mods[:, l, j, 40 + m:40 + m + 1], in1=xt[:, m, :], op0=ALU.mult, op1=ALU.add),
                             reads=[pk, "mods", "M_xt"], writes=["M_xt"])
                if not lastl:
                    P.dma("M_xo", lambda e, t0=t0: e.dma_start(out=S["X_" + n][:, :, t0:t0 + NT].rearrange("k p t -> p k t"), in_=xt[:]), reads=["M_xt"], writes=[xkeys(g, t)])
                else:
                    rms_stats(xt, "M_xt", KC, D, rstd, "M_rstd", sqb)
                    for k in range(KC):
                        P.op("dve", lambda e, k=k: e.scalar_tensor_tensor(out=xt[:, k, :], in0=xt[:, k, :], scalar=finalg[:, k:k + 1], in1=rstd[:], op0=ALU.mult, op1=ALU.mult),
                             reads=["M_xt", "finalg", "M_rstd"], writes=["M_xt"])
                    d_ = P.dma("M_yo", lambda e, t0=t0: e.dma_start(out=O["y_" + n][:, :, t0:t0 + NT].rearrange("k p t -> p k t"), in_=xt[:]), reads=["M_xt"])
                    fin.append(d_.idx)
            P.release(m0)

        for l in range(depth):
            m0 = P.mark()
            w_in_sb = load_weight_bf16("w_in_sb", I["w_in"][l].rearrange("(k p) n -> p k n", p=128), [128, KC, IN_COLS])
            for g in groups:
                phase_A(l, g, w_in_sb)
            P.release(m0)
            for g in groups:
                phase_H(l, g)
            for g in groups:
                phase_G(l, g)
            m0 = P.mark()
            w_out_sb = load_weight_bf16("w_out_sb", I["w_out"][l].rearrange("(k p) n -> p k n", p=128), [128, KC, D])
            for g in groups:
                phase_C1(l, g, w_out_sb)
            P.release(m0)
            m0 = P.mark()
            w1_sb = load_weight_bf16("w1_sb", I["w_mlp1"][l].rearrange("(k p) n -> p k n", p=128), [128, KC, DFF])
            w2_sb = load_weight_bf16("w2_sb", I["w_mlp2"][l].rearrange("(k p) n -> p k n", p=128), [128, 32, D])
            for g in groups:
                phase_C2(l, g, w1_sb, w2_sb)
            P.release(m0)
        P.emit(final_wait_ops=fin)
    return nc


def _dft_tables(L, ntab):
    N = 2 * L
    a = np.arange(ntab, dtype=np.int64)
    ph = (a[:, None] * a[None, :]) % N
    ang = ph.astype(np.float64) * (2.0 * np.pi / N)
    c = np.cos(ang)
    s_ = np.sin(ang)
    s_[(ph % L) == 0] = 0.0
    def tl(a_, w_):
        return np.ascontiguousarray(a_.reshape(ntab // 128, 128, ntab // w_, w_).transpose(2, 1, 0, 3)).astype(ml_dtypes.bfloat16)
    return tl(c, FW), tl(s_, FW), tl(c, 128), tl(s_, 128)


def _zfeat(L):
    t = np.linspace(0.0, 1.0, L, dtype=np.float32)[:, None]
    bands = (HY_EMB - 1) // 2
    f = np.linspace(1e-4, bands - 1, bands, dtype=np.float32)[None, :]
    wpos = (np.float32(2.0 * math.pi) * np.arange(L, dtype=np.float32)[:, None] / np.float32(L)).astype(np.float32)
    z = np.concatenate([t, np.cos(f * wpos), -np.sin(f * wpos)], axis=-1).astype(np.float32)
    return np.ascontiguousarray(z.T), t[:, 0]


_PROG_CACHE = {}


def run_cfg(depth, LS, inputs, n_cores=8):
    f32 = np.float32
    groups = make_groups(LS)
    A = {k: np.asarray(v) for k, v in inputs.items()}
    shared = {}
    for g in groups:
        c, s_, cF, sF = _dft_tables(g.L, g.NTAB)
        shared["ctab_" + g.name] = c
        shared["stab_" + g.name] = s_
        shared["ctabF_" + g.name] = cF
        shared["stabF_" + g.name] = sF
        zf, t = _zfeat(g.L)
        shared["zf_" + g.name] = zf
        shared["negt_" + g.name] = np.ascontiguousarray((-t).reshape(g.nTB, 128).T).astype(f32)
        w = np.zeros(g.NF, f32)
        w[0:g.L + 1] = 2.0 / (2 * g.L)
        w[0] = 1.0 / (2 * g.L)
        w[g.L] = 1.0 / (2 * g.L)
        shared["wcol_" + g.name] = np.ascontiguousarray(w.reshape(g.NFB, 128).T).astype(f32)
    shared["w_ada"] = A["w_ada"][:depth].astype(f32)
    shared["b_ada"] = np.ascontiguousarray(A["b_ada"][:depth].reshape(depth, 48, 128).transpose(2, 0, 1)).astype(f32)
    ng = np.stack([A["norm1_g"][:depth], A["norm2_g"][:depth]], axis=1)
    shared["norm_g"] = np.ascontiguousarray(ng.reshape(depth, 2, KC, 128).transpose(3, 0, 1, 2)).astype(f32)
    shared["final_g"] = np.ascontiguousarray(A["final_g"].reshape(KC, 128).T).astype(f32)
    shared["w_in"] = A["w_in"][:depth].astype(f32)
    shared["gcw"] = np.ascontiguousarray(A["gdn_conv_w"][:depth].reshape(depth, 5, 12, 128).transpose(3, 0, 2, 1)).astype(f32)
    shared["hcw"] = np.ascontiguousarray(A["hy_conv_w"][:depth].reshape(depth, 3, 12, 128).transpose(3, 0, 2, 1)).astype(f32)
    ap_ = np.stack([A["gdn_a_log"][:depth].reshape(depth, 8), A["gdn_dt_bias"][:depth].reshape(depth, 8)], axis=-1)
    shared["a_par"] = np.ascontiguousarray(ap_.transpose(1, 0, 2)).astype(f32)
    shared["gng"] = np.ascontiguousarray(A["gdn_norm_g"][:depth].T).astype(f32)
    shared["hw1"] = A["hy_w1"][:depth].astype(f32)
    hv = np.stack([A["hy_b1"][:depth], A["hy_freq"][:depth], A["hy_b2"][:depth], np.zeros_like(A["hy_b1"][:depth])], axis=-1)
    shared["hvec"] = np.ascontiguousarray(hv.transpose(1, 0, 2)).astype(f32)
    shared["hw2"] = A["hy_w2"][:depth].astype(f32)
    shared["hw3e"] = np.ascontiguousarray(np.concatenate([A["hy_w3"][:depth], A["hy_b3"][:depth][:, None, :]], axis=1)).astype(f32)
    shared["hskip"] = np.ascontiguousarray(A["hy_skip"][:depth].reshape(depth, 4, 128).transpose(2, 0, 1)).astype(f32)
    shared["w_out"] = A["w_out"][:depth].astype(f32)
    shared["w_mlp1"] = A["w_mlp1"][:depth].astype(f32)
    shared["w_mlp2"] = A["w_mlp2"][:depth].astype(f32)
    ii = np.arange(64)
    mk = np.zeros((64, 5, 64), f32)
    mk[:, 0, :] = (ii[None, :] >= ii[:, None])
    mk[:, 1, :] = (ii[None, :] <= ii[:, None])
    mk[:, 2, :] = -1.0 * (ii[None, :] > ii[:, None])
    mk[:, 3, :] = -1.0 * (ii[None, :] < ii[:, None])
    mk[:, 4, :] = (ii[None, :] == ii[:, None])
    shared["masks"] = mk
    mk2 = np.zeros((128, 4, 128), f32)
    for bb in range(2):
        mk2[bb * 64:(bb + 1) * 64, :, bb * 64:(bb + 1) * 64] = mk[:, 0:4, :]
    shared["masks2"] = mk2
    sel = np.zeros((8, 8, 128), f32)
    for r in range(8):
        sel[r, r, :] = 1.0
    shared["sel"] = sel
    sm = np.ones((8, NT), f32)
    sm[:, ::CH] = 0.0
    shared["scanmask"] = sm
    min_decay = math.log(1e-2) / 1.5
    max_decay = math.log(1e-2) / 0.3
    shared["delta"] = np.abs(np.linspace(min_decay, max_decay, 512, dtype=f32)).astype(f32)

    n_s = A["x_sample"].shape[0]
    in_maps = []
    for i in range(n_cores):
        b = i % n_s
        m = dict(shared)
        xs = A["x_sample"][b]
        m["x_s"] = np.ascontiguousarray(xs.T.reshape(KC, 128, LS)).astype(f32)
        xp = A["x_prompt"][2 * i:2 * i + 2].reshape(512, D)
        m["x_p"] = np.ascontiguousarray(xp.T.reshape(KC, 128, 512)).astype(f32)
        m["s0"] = np.ascontiguousarray(A["state_gdn"][b][:depth]).astype(f32)
        cd = np.stack([A["c"][b], A["c_ctx"]], axis=-1)
        m["cond"] = np.ascontiguousarray(cd.reshape(KC, 128, 2).transpose(1, 0, 2)).astype(f32)
        in_maps.append(m)

    key = (depth, LS)
    if key not in _PROG_CACHE:
        _PROG_CACHE[key] = build_program(depth, LS)
    nc = _PROG_CACHE[key]
    res = run_bass_kernel_spmd(nc, in_maps, core_ids=list(range(n_cores)))
    R = res.results
    _LAST["R"] = R
    y_sample = np.stack([R[b]["y_s"].reshape(D, LS).T for b in range(n_s)], axis=0).astype(f32)
    y_prompt = np.concatenate([R[i]["y_p"].reshape(D, 512).T.reshape(2, 256, D) for i in range(n_cores)], axis=0).astype(f32)
    new_state = np.concatenate([R[i]["nstate"] for i in range(n_cores)], axis=0).astype(f32)
    return (y_prompt, y_sample, new_state)


def kernel(**inputs):
    return run_cfg(4, 4096, inputs)
```
